# Optimizing a Trainium2 kernel written in Bass

```python
import math
import jax
import jax.numpy as jnp
from jax import lax
import numpy as np

D_MODEL = 2048
BATCH = 4
SEQ = 2048
DEPTH = 4
DEC_BATCH = 8
DEC_SEQ = 8
PAST_LEN = 16384
PAGE_SIZE = 128

D_MIX = D_MODEL
GROUP_W = D_MIX // 4
DN_HEADS = 4
DN_DK = GROUP_W // DN_HEADS
DN_DV = GROUP_W // DN_HEADS
DN_CONV = 4
DN_CHUNK = 64
SSM_CH = 16
SSM_GROUPS = GROUP_W // SSM_CH
SSM_STATE = 64
SWA_HEADS = 8
SWA_HD = GROUP_W // SWA_HEADS
DILATED = ((128, 1), (512, 4), (2048, 16))
WIN_MAX = 2048
SWA_QBLOCK = 128
REL_BUCKETS = 32
REL_MAX_DIST = 2048
LRU_BLOCKS = 8
LRU_BW = GROUP_W // LRU_BLOCKS
LRU_CONV = 4
LRU_C = 8.0
MEM_LEN = 256
MEM_HEADS = 4
MEM_HD = 128
D_FF = 5632
FFN_CONV = 3
EPS = 1e-6
IN_SIZES = (3 * GROUP_W, GROUP_W, DN_HEADS, DN_HEADS, GROUP_W, 3 * GROUP_W, GROUP_W, GROUP_W)
IN_W = sum(IN_SIZES)

kernel_name = 'hybrid_parallel_heads_decoder_step'


def in_split_points():
    pts, acc = [], 0
    for s in IN_SIZES[:-1]:
        acc += s
        pts.append(acc)
    return pts


def rmsnorm(x, g):
    x32 = x.astype(jnp.float32)
    y = x32 * lax.rsqrt(jnp.mean(x32 * x32, axis=-1, keepdims=True) + EPS)
    return (y * g.astype(jnp.float32)).astype(x.dtype)


def l2norm(x):
    return x * lax.rsqrt(jnp.sum(x * x, axis=-1, keepdims=True) + EPS)


def causal_dwconv(x, buf, w):
    kw, t = w.shape[0], x.shape[1]
    xx = jnp.concatenate([buf.astype(x.dtype), x], axis=1)
    y = xx[:, 0:t] * w[0]
    for j in range(1, kw):
        y = y + xx[:, j:j + t] * w[j]
    return y, xx[:, t:]


def linear_combine(e1, e2):
    a1, b1 = e1
    a2, b2 = e2
    return a2 * a1, a2 * b1 + b2


def complex_combine(e1, e2):
    a1r, a1i, b1r, b1i = e1
    a2r, a2i, b2r, b2i = e2
    return (a2r * a1r - a2i * a1i, a2r * a1i + a2i * a1r,
            a2r * b1r - a2i * b1i + b2r, a2r * b1i + a2i * b1r + b2i)


def chunk_gated_delta(q, k, v, g, beta, s0):
    bsz, t, h, dk = q.shape
    dv = v.shape[-1]
    c = DN_CHUNK
    n = -(-t // c)
    pad = n * c - t

    def to_chunks(z):
        z = jnp.pad(z, [(0, 0), (0, pad)] + [(0, 0)] * (z.ndim - 2))
        z = z.reshape((bsz, n, c) + z.shape[2:])
        return jnp.transpose(z, (1, 0, 3, 2) + tuple(range(4, z.ndim)))

    qc, kc, vc = to_chunks(q), to_chunks(k), to_chunks(v)
    gc = jnp.cumsum(to_chunks(g), axis=-1)
    bc = to_chunks(beta)
    kb = kc * bc[..., None]
    vb = vc * bc[..., None]
    tril = jnp.tril(jnp.ones((c, c), bool))
    strict = jnp.tril(jnp.ones((c, c), bool), -1)
    decay = jnp.exp(jnp.where(tril, gc[..., :, None] - gc[..., None, :], -jnp.inf))
    m = jnp.where(strict, jnp.einsum('nbhcd,nbhsd->nbhcs', kb, kc) * decay, 0.0)
    a_mat = m + jnp.eye(c, dtype=m.dtype)
    rhs = jnp.concatenate([vb, kb * jnp.exp(gc)[..., None]], axis=-1)
    sol = lax.linalg.triangular_solve(a_mat, rhs, left_side=True, lower=True, unit_diagonal=True)
    u, w = sol[..., :dv], sol[..., dv:]
    qk = jnp.einsum('nbhcd,nbhsd->nbhcs', qc, kc) * decay

    def step(s, xs):
        qi, ki, ui, wi, gi, qki = xs
        v_new = ui - jnp.einsum('bhcd,bhde->bhce', wi, s)
        o = (jnp.einsum('bhcd,bhde->bhce', qi * jnp.exp(gi)[..., None], s)
             + jnp.einsum('bhcs,bhse->bhce', qki, v_new))
        g_last = gi[..., -1]
        s = (s * jnp.exp(g_last)[..., None, None]
             + jnp.einsum('bhcd,bhce->bhde', ki * jnp.exp(g_last[..., None] - gi)[..., None], v_new))
        return s, o

    s_fin, o = lax.scan(step, s0, (qc, kc, u, w, gc, qk))
    o = jnp.transpose(o, (1, 0, 3, 2, 4)).reshape(bsz, n * c, h, dv)[:, :t]
    return o, s_fin


def gated_deltanet(qkv, z, b_raw, a_raw, conv_buf, s0, conv_w, a_log, dt_bias, norm_g):
    bsz, t, _ = qkv.shape
    dt = qkv.dtype
    f32 = jnp.float32
    qkv_c, new_buf = causal_dwconv(qkv, conv_buf, conv_w)
    qkv_c = jax.nn.silu(qkv_c.astype(f32))
    q, k, v = jnp.split(qkv_c, 3, axis=-1)
    q = l2norm(q.reshape(bsz, t, DN_HEADS, DN_DK)) * (DN_DK ** -0.5)
    k = l2norm(k.reshape(bsz, t, DN_HEADS, DN_DK))
    v = v.reshape(bsz, t, DN_HEADS, DN_DV)
    beta = jax.nn.sigmoid(b_raw.astype(f32))
    g = -jnp.exp(a_log.astype(f32)) * jax.nn.softplus(a_raw.astype(f32) + dt_bias.astype(f32))
    o, s_new = chunk_gated_delta(q, k, v, g, beta, s0.astype(f32))
    o = rmsnorm(o, norm_g) * jax.nn.silu(z.astype(f32).reshape(bsz, t, DN_HEADS, DN_DV))
    return o.reshape(bsz, t, GROUP_W).astype(dt), s_new.astype(s0.dtype), new_buf


def s5_mixer(u, h0_re, h0_im, a_re, a_im, log_dt, b_re, b_im, c_re, c_im, d_skip, w_glu, b_glu):
    bsz, t, _ = u.shape
    f32 = jnp.float32
    u32 = u.astype(f32)
    ug = u32.reshape(bsz, t, SSM_GROUPS, SSM_CH)
    ar, ai = a_re.astype(f32), a_im.astype(f32)
    step = jnp.exp(log_dt.astype(f32))[:, None]
    mag = jnp.exp(ar * step)
    ab_re, ab_im = mag * jnp.cos(ai * step), mag * jnp.sin(ai * step)
    den = ar * ar + ai * ai
    cr = ((ab_re - 1.0) * ar + ab_im * ai) / den
    ci = (ab_im * ar - (ab_re - 1.0) * ai) / den
    br, bi = b_re.astype(f32), b_im.astype(f32)
    bb_re = cr[..., None] * br - ci[..., None] * bi
    bb_im = cr[..., None] * bi + ci[..., None] * br
    x_re = jnp.einsum('btgc,gnc->btgn', ug, bb_re)
    x_im = jnp.einsum('btgc,gnc->btgn', ug, bb_im)
    h0r, h0i = h0_re.astype(f32), h0_im.astype(f32)
    x_re = x_re.at[:, 0].add(ab_re * h0r - ab_im * h0i)
    x_im = x_im.at[:, 0].add(ab_re * h0i + ab_im * h0r)
    a_seq_re = jnp.broadcast_to(ab_re, x_re.shape)
    a_seq_im = jnp.broadcast_to(ab_im, x_im.shape)
    _, _, h_re, h_im = lax.associative_scan(complex_combine, (a_seq_re, a_seq_im, x_re, x_im), axis=1)
    y = (jnp.einsum('gcn,btgn->btgc', c_re.astype(f32), h_re)
         - jnp.einsum('gcn,btgn->btgc', c_im.astype(f32), h_im)).reshape(bsz, t, GROUP_W)
    y = y + d_skip.astype(f32) * u32
    y = jax.nn.gelu(y)
    y = y * jax.nn.sigmoid(y @ w_glu.astype(f32) + b_glu.astype(f32))
    return y.astype(u.dtype), h_re[:, -1].astype(h0_re.dtype), h_im[:, -1].astype(h0_im.dtype)


def t5_bucket(dist):
    exact = REL_BUCKETS // 2
    d = jnp.maximum(dist.astype(jnp.float32), 1.0)
    large = exact + (jnp.log(d / exact) / math.log(REL_MAX_DIST / exact) * (REL_BUCKETS - exact)).astype(jnp.int32)
    large = jnp.minimum(large, REL_BUCKETS - 1)
    return jnp.where(dist < exact, dist, large)


def dilated_swa(q, k_new, v_new, k_buf, v_buf, p0, rel_bias):
    bsz, t, h, hd = q.shape
    lb = k_buf.shape[1]
    k_ext = jnp.concatenate([k_buf.astype(k_new.dtype), k_new], axis=1)
    v_ext = jnp.concatenate([v_buf.astype(v_new.dtype), v_new], axis=1)
    qb = min(SWA_QBLOCK, t)
    nb = -(-t // qb)
    qp = jnp.pad(q, ((0, 0), (0, nb * qb - t), (0, 0), (0, 0)))
    q_blocks = jnp.moveaxis(qp.reshape(bsz, nb, qb, h, hd), 1, 0)
    scale = hd ** -0.5
    groups = []
    for win, dil in DILATED:
        dist = jnp.arange(win // dil + 1, dtype=jnp.int32) * dil
        bias = rel_bias[t5_bucket(dist)].T.astype(jnp.float32)
        groups.append((dist, bias))

    def block_fn(args):
        ib, qi = args
        loc = ib * qb + jnp.arange(qb, dtype=jnp.int32)
        qi32 = qi.astype(jnp.float32)
        outs, maxs, sums = [], [], []
        for dist, bias in groups:
            rel = loc[:, None] - dist[None, :]
            valid = (p0 + rel) >= 0
            idx = jnp.clip(lb + rel, 0, lb + t - 1)
            kg = k_ext[:, idx].astype(jnp.float32)
            vg = v_ext[:, idx].astype(jnp.float32)
            logits = jnp.einsum('bqhd,bqshd->bhqs', qi32, kg) * scale + bias[None, :, None, :]
            logits = jnp.where(valid[None, None], logits, -jnp.inf)
            mx = jnp.max(logits, axis=-1, keepdims=True)
            e = jnp.exp(logits - mx)
            s = jnp.sum(e, axis=-1, keepdims=True)
            outs.append(jnp.einsum('bhqs,bqshd->bhqd', e, vg) / s)
            maxs.append(mx)
            sums.append(s)
        ms = jnp.stack(maxs)
        wts = jnp.stack(sums) * jnp.exp(ms - jnp.max(ms, axis=0, keepdims=True))
        o = jnp.sum(wts * jnp.stack(outs), axis=0) / jnp.sum(wts, axis=0)
        return jnp.transpose(o, (0, 2, 1, 3))

    o = lax.map(block_fn, (jnp.arange(nb, dtype=jnp.int32), q_blocks))
    o = jnp.moveaxis(o, 0, 1).reshape(bsz, nb * qb, h, hd)[:, :t]
    return o.astype(q.dtype)


def rglru_mixer(xb, gate, conv_buf, h0, conv_w, conv_b, w_a, b_a, w_x, b_x, lam):
    bsz, t, _ = xb.shape
    f32 = jnp.float32
    xc, new_buf = causal_dwconv(xb, conv_buf, conv_w)
    xc = (xc + conv_b).astype(f32)
    xr = xc.reshape(bsz, t, LRU_BLOCKS, LRU_BW)
    r = jax.nn.sigmoid(jnp.einsum('btkc,kcd->btkd', xr, w_a.astype(f32)).reshape(bsz, t, GROUP_W) + b_a.astype(f32))
    i = jax.nn.sigmoid(jnp.einsum('btkc,kcd->btkd', xr, w_x.astype(f32)).reshape(bsz, t, GROUP_W) + b_x.astype(f32))
    log_a = -LRU_C * r * jax.nn.softplus(-lam.astype(f32))
    a = jnp.exp(log_a)
    mult = jnp.sqrt(jnp.maximum(-jnp.expm1(2.0 * log_a), 0.0))
    bx = mult * i * xc
    bx = bx.at[:, 0].add(a[:, 0] * h0.astype(f32))
    _, hs = lax.associative_scan(linear_combine, (a, bx), axis=1)
    y = hs * jax.nn.gelu(gate.astype(f32))
    return y.astype(xb.dtype), hs[:, -1].astype(h0.dtype), new_buf


def memory_kv(mem, g_mem, w_k, w_v):
    bsz, m, _ = mem.shape
    mn = rmsnorm(mem, g_mem)
    return ((mn @ w_k).reshape(bsz, m, MEM_HEADS, MEM_HD), (mn @ w_v).reshape(bsz, m, MEM_HEADS, MEM_HD))


def memory_attn(xn, mem_k, mem_v, w_q, w_o):
    bsz, t, _ = xn.shape
    q = (xn @ w_q).reshape(bsz, t, MEM_HEADS, MEM_HD).astype(jnp.float32)
    logits = jnp.einsum('bthd,bmhd->bhtm', q, mem_k.astype(jnp.float32)) * (MEM_HD ** -0.5)
    p = jax.nn.softmax(logits, axis=-1)
    o = jnp.einsum('bhtm,bmhd->bthd', p, mem_v.astype(jnp.float32)).reshape(bsz, t, MEM_HEADS * MEM_HD)
    return o.astype(xn.dtype) @ w_o


def conv_ffn(xn, buf, w_up, conv_w, w_down):
    hup = xn @ w_up
    hc, new_buf = causal_dwconv(hup, buf, conv_w)
    u, g = jnp.split(hc, 2, axis=-1)
    return (jax.nn.silu(g) * u) @ w_down, new_buf


def fresh_state(bsz, dtype):
    return (jnp.zeros((bsz, 0, SWA_HEADS, SWA_HD), dtype), jnp.zeros((bsz, 0, SWA_HEADS, SWA_HD), dtype),
            jnp.zeros((bsz, DN_HEADS, DN_DK, DN_DV), jnp.float32), jnp.zeros((bsz, DN_CONV - 1, 3 * GROUP_W), dtype),
            jnp.zeros((bsz, SSM_GROUPS, SSM_STATE), jnp.float32), jnp.zeros((bsz, SSM_GROUPS, SSM_STATE), jnp.float32),
            jnp.zeros((bsz, GROUP_W), jnp.float32), jnp.zeros((bsz, LRU_CONV - 1, GROUP_W), dtype),
            jnp.zeros((bsz, FFN_CONV - 1, 2 * D_FF), dtype))


def trunk_layer(x, mem_k, mem_v, win_k, win_v, dn_state, dn_buf, ssm_re, ssm_im, lru_h, lru_buf, ffn_buf,
                p0, rel_bias, lp):
    bsz, t, _ = x.shape
    dt = x.dtype
    proj = rmsnorm(x, lp['g_mix']) @ lp['w_in']
    dn_qkv, dn_z, dn_b, dn_a, ssm_u, swa_qkv, lru_x, lru_g = jnp.split(proj, in_split_points(), axis=-1)
    o_a, dn_state_new, dn_buf_new = gated_deltanet(dn_qkv, dn_z, dn_b, dn_a, dn_buf, dn_state, lp['dn_conv_w'],
                                                   lp['dn_a_log'], lp['dn_dt_bias'], lp['dn_norm_g'])
    o_b, ssm_re_new, ssm_im_new = s5_mixer(ssm_u, ssm_re, ssm_im, lp['ssm_a_re'], lp['ssm_a_im'], lp['ssm_log_dt'],
                                           lp['ssm_b_re'], lp['ssm_b_im'], lp['ssm_c_re'], lp['ssm_c_im'],
                                           lp['ssm_d'], lp['ssm_w_glu'], lp['ssm_b_glu'])
    q, k, v = [zz.reshape(bsz, t, SWA_HEADS, SWA_HD) for zz in jnp.split(swa_qkv, 3, axis=-1)]
    o_c = dilated_swa(q, k, v, win_k, win_v, p0, rel_bias).reshape(bsz, t, GROUP_W)
    o_d, lru_h_new, lru_buf_new = rglru_mixer(lru_x, lru_g, lru_buf, lru_h, lp['lru_conv_w'], lp['lru_conv_b'],
                                              lp['lru_w_a'], lp['lru_b_a'], lp['lru_w_x'], lp['lru_b_x'], lp['lru_lam'])
    mix = jnp.concatenate([o_a, o_b, o_c, o_d], axis=-1).astype(dt)
    x = x + (mix @ lp['w_out']).astype(dt)
    x = x + memory_attn(rmsnorm(x, lp['g_cross']), mem_k, mem_v, lp['w_mem_q'], lp['w_mem_o']).astype(dt)
    f, ffn_buf_new = conv_ffn(rmsnorm(x, lp['g_ffn']), ffn_buf, lp['w_up'], lp['ffn_conv_w'], lp['w_down'])
    x = x + f.astype(dt)
    return x, k, v, dn_state_new, dn_buf_new, ssm_re_new, ssm_im_new, lru_h_new, lru_buf_new, ffn_buf_new


def setup_inputs(seed: int = 0) -> dict:
    key = jax.random.key(seed)
    ks = iter(jax.random.split(key, 80))
    f32 = jnp.float32

    def nrm(shape, scale=1.0):
        return jax.random.normal(next(ks), shape, f32) * scale

    def unif(shape, lo, hi):
        return jax.random.uniform(next(ks), shape, f32, lo, hi)

    def gain(shape):
        return 1.0 + nrm(shape, 0.01)

    win_buf = min(WIN_MAX, PAST_LEN)
    dn_dt = jnp.exp(unif((DEPTH, DN_HEADS), math.log(1e-3), math.log(1e-1)))
    lam_s = unif((DEPTH, GROUP_W), 0.9, 0.999) ** (1.0 / LRU_C)
    return {
        'x_prompt': nrm((BATCH, SEQ, D_MODEL)),
        'x_sample': nrm((DEC_BATCH, DEC_SEQ, D_MODEL)),
        'mem_prompt': nrm((BATCH, MEM_LEN, D_MODEL)),
        'cache_mem_k': nrm((DEPTH, DEC_BATCH, MEM_LEN, MEM_HEADS, MEM_HD)),
        'cache_mem_v': nrm((DEPTH, DEC_BATCH, MEM_LEN, MEM_HEADS, MEM_HD)),
        'cache_win_k': nrm((DEPTH, DEC_BATCH, win_buf, SWA_HEADS, SWA_HD)),
        'cache_win_v': nrm((DEPTH, DEC_BATCH, win_buf, SWA_HEADS, SWA_HD)),
        'state_delta': nrm((DEPTH, DEC_BATCH, DN_HEADS, DN_DK, DN_DV), 0.3),
        'state_delta_conv': nrm((DEPTH, DEC_BATCH, DN_CONV - 1, 3 * GROUP_W)),
        'state_ssm_re': nrm((DEPTH, DEC_BATCH, SSM_GROUPS, SSM_STATE), 0.5),
        'state_ssm_im': nrm((DEPTH, DEC_BATCH, SSM_GROUPS, SSM_STATE), 0.5),
        'state_lru': nrm((DEPTH, DEC_BATCH, GROUP_W), 0.5),
        'state_lru_conv': nrm((DEPTH, DEC_BATCH, LRU_CONV - 1, GROUP_W)),
        'state_ffn_conv': nrm((DEPTH, DEC_BATCH, FFN_CONV - 1, 2 * D_FF)),
        'rel_bias': nrm((REL_BUCKETS, SWA_HEADS), 0.1),
        'g_mix': gain((DEPTH, D_MODEL)),
        'w_in': nrm((DEPTH, D_MODEL, IN_W), D_MODEL ** -0.5),
        'dn_conv_w': nrm((DEPTH, DN_CONV, 3 * GROUP_W), 0.5),
        'dn_a_log': jnp.log(unif((DEPTH, DN_HEADS), 1.0, 16.0)),
        'dn_dt_bias': dn_dt + jnp.log(-jnp.expm1(-dn_dt)),
        'dn_norm_g': gain((DEPTH, DN_DV)),
        'ssm_a_re': -0.5 + nrm((DEPTH, SSM_GROUPS, SSM_STATE), 0.01),
        'ssm_a_im': math.pi * jnp.arange(SSM_STATE, dtype=f32) + nrm((DEPTH, SSM_GROUPS, SSM_STATE), 0.01),
        'ssm_log_dt': unif((DEPTH, SSM_GROUPS), math.log(1e-3), math.log(1e-1)),
        'ssm_b_re': nrm((DEPTH, SSM_GROUPS, SSM_STATE, SSM_CH), (2 * SSM_CH) ** -0.5),
        'ssm_b_im': nrm((DEPTH, SSM_GROUPS, SSM_STATE, SSM_CH), (2 * SSM_CH) ** -0.5),
        'ssm_c_re': nrm((DEPTH, SSM_GROUPS, SSM_CH, SSM_STATE), SSM_STATE ** -0.5),
        'ssm_c_im': nrm((DEPTH, SSM_GROUPS, SSM_CH, SSM_STATE), SSM_STATE ** -0.5),
        'ssm_d': nrm((DEPTH, GROUP_W)),
        'ssm_w_glu': nrm((DEPTH, GROUP_W, GROUP_W), GROUP_W ** -0.5),
        'ssm_b_glu': nrm((DEPTH, GROUP_W), 0.01),
        'lru_conv_w': nrm((DEPTH, LRU_CONV, GROUP_W), 0.5),
        'lru_conv_b': nrm((DEPTH, GROUP_W), 0.01),
        'lru_w_a': nrm((DEPTH, LRU_BLOCKS, LRU_BW, LRU_BW), LRU_BW ** -0.5),
        'lru_b_a': nrm((DEPTH, GROUP_W), 0.01),
        'lru_w_x': nrm((DEPTH, LRU_BLOCKS, LRU_BW, LRU_BW), LRU_BW ** -0.5),
        'lru_b_x': nrm((DEPTH, GROUP_W), 0.01),
        'lru_lam': jnp.log(lam_s) - jnp.log1p(-lam_s),
        'w_out': nrm((DEPTH, D_MIX, D_MODEL), D_MIX ** -0.5),
        'g_cross': gain((DEPTH, D_MODEL)),
        'g_mem': gain((DEPTH, D_MODEL)),
        'w_mem_q': nrm((DEPTH, D_MODEL, MEM_HEADS * MEM_HD), D_MODEL ** -0.5),
        'w_mem_k': nrm((DEPTH, D_MODEL, MEM_HEADS * MEM_HD), D_MODEL ** -0.5),
        'w_mem_v': nrm((DEPTH, D_MODEL, MEM_HEADS * MEM_HD), D_MODEL ** -0.5),
        'w_mem_o': nrm((DEPTH, MEM_HEADS * MEM_HD, D_MODEL), (MEM_HEADS * MEM_HD) ** -0.5),
        'g_ffn': gain((DEPTH, D_MODEL)),
        'w_up': nrm((DEPTH, D_MODEL, 2 * D_FF), D_MODEL ** -0.5),
        'ffn_conv_w': nrm((DEPTH, FFN_CONV, 2 * D_FF), FFN_CONV ** -0.5),
        'w_down': nrm((DEPTH, D_FF, D_MODEL), D_FF ** -0.5),
        'g_final': gain((D_MODEL,)),
    }


def reference(x_prompt, x_sample, mem_prompt, cache_mem_k, cache_mem_v, cache_win_k, cache_win_v,
              state_delta, state_delta_conv, state_ssm_re, state_ssm_im, state_lru, state_lru_conv,
              state_ffn_conv, rel_bias, g_mix, w_in, dn_conv_w, dn_a_log, dn_dt_bias, dn_norm_g,
              ssm_a_re, ssm_a_im, ssm_log_dt, ssm_b_re, ssm_b_im, ssm_c_re, ssm_c_im, ssm_d,
              ssm_w_glu, ssm_b_glu, lru_conv_w, lru_conv_b, lru_w_a, lru_b_a, lru_w_x, lru_b_x,
              lru_lam, w_out, g_cross, g_mem, w_mem_q, w_mem_k, w_mem_v, w_mem_o, g_ffn, w_up,
              ffn_conv_w, w_down, g_final):
    bp = x_prompt.shape[0]
    yp, ys = x_prompt, x_sample
    names = ('mem_k', 'mem_v', 'win_k', 'win_v', 'delta', 'delta_conv', 'ssm_re', 'ssm_im', 'lru', 'lru_conv', 'ffn_conv')
    P = {nm: [] for nm in names}
    S = {nm: [] for nm in names[2:]}
    for l in range(DEPTH):
        lp = {'g_mix': g_mix[l], 'w_in': w_in[l], 'dn_conv_w': dn_conv_w[l], 'dn_a_log': dn_a_log[l],
              'dn_dt_bias': dn_dt_bias[l], 'dn_norm_g': dn_norm_g[l], 'ssm_a_re': ssm_a_re[l],
              'ssm_a_im': ssm_a_im[l], 'ssm_log_dt': ssm_log_dt[l], 'ssm_b_re': ssm_b_re[l],
              'ssm_b_im': ssm_b_im[l], 'ssm_c_re': ssm_c_re[l], 'ssm_c_im': ssm_c_im[l], 'ssm_d': ssm_d[l],
              'ssm_w_glu': ssm_w_glu[l], 'ssm_b_glu': ssm_b_glu[l], 'lru_conv_w': lru_conv_w[l],
              'lru_conv_b': lru_conv_b[l], 'lru_w_a': lru_w_a[l], 'lru_b_a': lru_b_a[l], 'lru_w_x': lru_w_x[l],
              'lru_b_x': lru_b_x[l], 'lru_lam': lru_lam[l], 'w_out': w_out[l], 'g_cross': g_cross[l],
              'w_mem_q': w_mem_q[l], 'w_mem_o': w_mem_o[l], 'g_ffn': g_ffn[l], 'w_up': w_up[l],
              'ffn_conv_w': ffn_conv_w[l], 'w_down': w_down[l]}
        mk, mv = memory_kv(mem_prompt, g_mem[l], w_mem_k[l], w_mem_v[l])
        yp, kp, vp, dsp, dcp, srp, sip, lhp, lcp, fcp = trunk_layer(
            yp, mk, mv, *fresh_state(bp, x_prompt.dtype), 0, rel_bias, lp)
        keep = min(WIN_MAX, kp.shape[1])
        for nm, val in (('mem_k', mk), ('mem_v', mv), ('win_k', kp[:, kp.shape[1] - keep:]),
                        ('win_v', vp[:, vp.shape[1] - keep:]), ('delta', dsp), ('delta_conv', dcp),
                        ('ssm_re', srp), ('ssm_im', sip), ('lru', lhp), ('lru_conv', lcp), ('ffn_conv', fcp)):
            P[nm].append(val)
        ys, ks_, vs_, dss, dcs, srs, sis, lhs, lcs, fcs = trunk_layer(
            ys, cache_mem_k[l], cache_mem_v[l], cache_win_k[l], cache_win_v[l], state_delta[l],
            state_delta_conv[l], state_ssm_re[l], state_ssm_im[l], state_lru[l], state_lru_conv[l],
            state_ffn_conv[l], PAST_LEN, rel_bias, lp)
        for nm, val in (('win_k', ks_), ('win_v', vs_), ('delta', dss), ('delta_conv', dcs), ('ssm_re', srs),
                        ('ssm_im', sis), ('lru', lhs), ('lru_conv', lcs), ('ffn_conv', fcs)):
            S[nm].append(val)
    y_prompt = rmsnorm(yp, g_final)
    y_sample = rmsnorm(ys, g_final)
    p_mem_k = jnp.stack(P['mem_k'])
    p_mem_v = jnp.stack(P['mem_v'])
    p_win_k = jnp.stack(P['win_k'])
    p_win_v = jnp.stack(P['win_v'])
    p_delta = jnp.stack(P['delta'])
    p_delta_conv = jnp.stack(P['delta_conv'])
    p_ssm_re = jnp.stack(P['ssm_re'])
    p_ssm_im = jnp.stack(P['ssm_im'])
    p_lru = jnp.stack(P['lru'])
    p_lru_conv = jnp.stack(P['lru_conv'])
    p_ffn_conv = jnp.stack(P['ffn_conv'])
    s_win_k = jnp.stack(S['win_k'])
    s_win_v = jnp.stack(S['win_v'])
    s_delta = jnp.stack(S['delta'])
    s_delta_conv = jnp.stack(S['delta_conv'])
    s_ssm_re = jnp.stack(S['ssm_re'])
    s_ssm_im = jnp.stack(S['ssm_im'])
    s_lru = jnp.stack(S['lru'])
    s_lru_conv = jnp.stack(S['lru_conv'])
    s_ffn_conv = jnp.stack(S['ffn_conv'])
    return (y_prompt, y_sample, p_mem_k, p_mem_v, p_win_k, p_win_v, p_delta, p_delta_conv, p_ssm_re, p_ssm_im,
            p_lru, p_lru_conv, p_ffn_conv, s_win_k, s_win_v, s_delta, s_delta_conv, s_ssm_re, s_ssm_im, s_lru,
            s_lru_conv, s_ffn_conv)
```

```python
import contextlib
import math
import numpy as np
import concourse.bass as bass
import concourse.mybir as mybir
from concourse.bass_utils import run_bass_kernel_spmd

F32 = mybir.dt.float32
BF16 = mybir.dt.bfloat16
AF = mybir.ActivationFunctionType
ALU = mybir.AluOpType
AX = mybir.AxisListType

ENGS = ("pe", "act", "dve", "pool", "sp")
NDMASEM = 16
NEG = -30000.0
D = 2048
TP = 2048
TS = 8
TT = TP + TS
DFF = 5632
EPS = 1e-6
CH = [(0, 512), (512, 512), (1024, 512), (1536, 512), (2048, 8)]
GRP = [(0, TP), (TP, TS)]
VL = 2600
SW = 2440


class Prog:
    def __init__(self, nc):
        self.nc = nc
        self.ops = {e: [] for e in ENGS}
        self.last_w = {}
        self.readers = {}
        self.pend = {e: set() for e in ENGS}
        self.dmas = []

    def op(self, eng, fn, reads=(), writes=(), dma=False):
        idx = len(self.ops[eng])
        deps = set(self.pend[eng])
        self.pend[eng] = set()
        for k in reads:
            w = self.last_w.get(k)
            if w is not None:
                deps.add(w)
        for k in writes:
            w = self.last_w.get(k)
            if w is not None:
                deps.add(w)
            for e_, r in self.readers.get(k, {}).items():
                if e_ == "_dma":
                    deps.update(r)
                else:
                    deps.add(r)
        me = (eng, idx)
        deps.discard(me)
        if eng == "pe":
            deps = {d for d in deps if d[0] != "pe"}
        self.ops[eng].append(dict(fn=fn, deps=deps, dma=dma, sig=False))
        for k in reads:
            rd = self.readers.setdefault(k, {})
            if dma:
                rd.setdefault("_dma", []).append(me)
            else:
                rd[eng] = me
        for k in writes:
            self.last_w[k] = me
            self.readers[k] = {}
        if dma:
            self.dmas.append(me)
        return me

    def dma(self, eng, out, in_, reads=(), writes=(), **kw):
        return self.op(eng, lambda e: e.dma_start(out=out, in_=in_, **kw),
                       reads=reads, writes=writes, dma=True)

    def barrier(self):
        lasts = set(self.dmas)
        for e in ENGS:
            n = len(self.ops[e])
            if n:
                lasts.add((e, n - 1))
        for e in ENGS:
            self.pend[e] |= lasts
        self.dmas = []
        self.last_w = {}
        self.readers = {}

    def emit(self):
        nc = self.nc
        ops = self.ops
        for e in ENGS:
            for o in ops[e]:
                if o["dma"]:
                    o["sig"] = True
                for (pe_, pi) in o["deps"]:
                    ops[pe_][pi]["sig"] = True
        with contextlib.ExitStack() as st:
            EPOCH = 30000
            csem = {e: [st.enter_context(nc.semaphore("c_%s%d" % (e, i))) for i in range(4)] for e in ENGS}
            dsem = {e: [st.enter_context(nc.semaphore("d_%s%d" % (e, i))) for i in range(NDMASEM)]
                    for e in ("sp", "act", "pool")}
            finals = {}
            for e in ENGS:
                cnt = 0
                dcnt = [0] * NDMASEM
                rr = 0
                for o in ops[e]:
                    if not o["sig"]:
                        continue
                    if o["dma"]:
                        s = rr % NDMASEM
                        rr += 1
                        o["prev"] = (dsem[e][s], dcnt[s])
                        dcnt[s] += 16
                        o["sem"] = (dsem[e][s], dcnt[s], 16)
                    else:
                        o["sem"] = (csem[e][cnt // EPOCH], cnt % EPOCH + 1, 1)
                        cnt += 1
                finals[e] = [(dsem[e][s], dcnt[s]) for s in range(NDMASEM) if dcnt[s] > 0] if e in dsem else []
            block = st.enter_context(nc.Block())

            def run(e, eng):
                waited = {}

                def wait(sem, val):
                    if val <= 0:
                        return
                    key = id(sem)
                    if waited.get(key, 0) >= val:
                        return
                    waited[key] = val
                    eng.wait_ge(sem, val)

                for o in ops[e]:
                    mx = {}
                    for (pe_, pi) in o["deps"]:
                        s = ops[pe_][pi]["sem"]
                        if mx.get(id(s[0]), (None, 0))[1] < s[1]:
                            mx[id(s[0])] = (s[0], s[1])
                    for (s0, v) in mx.values():
                        wait(s0, v)
                    if o["dma"]:
                        wait(*o["prev"])
                    ins = o["fn"](eng)
                    if o["sig"]:
                        ins.then_inc(o["sem"][0], o["sem"][2])
                for (s, v) in finals[e]:
                    wait(s, v)

            @block.tensor
            def _(eng):
                run("pe", eng)

            @block.scalar
            def _(eng):
                run("act", eng)

            @block.vector
            def _(eng):
                run("dve", eng)

            @block.gpsimd
            def _(eng):
                run("pool", eng)

            @block.sync
            def _(eng):
                run("sp", eng)


def _t5_bucket(d):
    if d < 16:
        return d
    v = np.log(np.float32(d) / np.float32(16)) / np.float32(math.log(2048 / 16)) * np.float32(16)
    return min(31, 16 + int(np.float32(v).astype(np.int32)))


def host_consts():
    c = {}
    oh = np.zeros((32, VL), np.float32)
    cv = np.full((1, VL), NEG, np.float32)
    for y in range(VL):
        d = y - 511
        if d < 0 or d > 2048:
            continue
        m = sum(1 for (win, dil) in ((128, 1), (512, 4), (2048, 16)) if d % dil == 0 and d <= win)
        if m == 0:
            continue
        oh[_t5_bucket(d), y] = 1.0
        cv[0, y] = math.log(m)
    c["c_oh"] = oh
    c["c_cv"] = cv
    c["c_ident"] = np.eye(128, dtype=np.float32)
    c["c_anti"] = np.eye(128, dtype=np.float32)[::-1].copy()
    c["c_ones"] = np.ones((128, 128), np.float32)
    i = np.arange(128)
    c["c_neglow"] = np.where(i[None, :] <= i[:, None], 0.0, NEG).astype(np.float32)
    c["c_negup"] = np.where(i[:, None] <= i[None, :], 0.0, NEG).astype(np.float32)
    c["c_strict"] = (i[None, :] < i[:, None]).astype(np.float32)
    sel = np.zeros((4, 4, 128), np.float32)
    for h in range(4):
        sel[h, h, :] = 1.0
    c["c_sel"] = sel.reshape(4, 512)
    return c


CONST_SHAPES = {"c_oh": (32, VL), "c_cv": (1, VL), "c_ident": (128, 128), "c_anti": (128, 128),
                "c_ones": (128, 128), "c_neglow": (128, 128), "c_negup": (128, 128),
                "c_strict": (128, 128), "c_sel": (4, 512)}

W_SHAPES = {
    "rel_bias": (32, 8), "g_mix": ("L", D), "w_in": ("L", D, 5128), "dn_conv_w": ("L", 4, 1536),
    "dn_a_log": ("L", 4), "dn_dt_bias": ("L", 4), "dn_norm_g": ("L", 128),
    "ssm_a_re": ("L", 32, 64), "ssm_a_im": ("L", 32, 64), "ssm_log_dt": ("L", 32),
    "ssm_b_re": ("L", 32, 64, 16), "ssm_b_im": ("L", 32, 64, 16), "ssm_c_re": ("L", 32, 16, 64),
    "ssm_c_im": ("L", 32, 16, 64), "ssm_d": ("L", 512), "ssm_w_glu": ("L", 512, 512), "ssm_b_glu": ("L", 512),
    "lru_conv_w": ("L", 4, 512), "lru_conv_b": ("L", 512), "lru_w_a": ("L", 8, 64, 64), "lru_b_a": ("L", 512),
    "lru_w_x": ("L", 8, 64, 64), "lru_b_x": ("L", 512), "lru_lam": ("L", 512), "w_out": ("L", D, D),
    "g_cross": ("L", D), "g_mem": ("L", D), "w_mem_q": ("L", D, 512), "w_mem_k": ("L", D, 512),
    "w_mem_v": ("L", D, 512), "w_mem_o": ("L", 512, D), "g_ffn": ("L", D), "w_up": ("L", D, 2 * DFF),
    "ffn_conv_w": ("L", 3, 2 * DFF), "w_down": ("L", DFF, D), "g_final": (1, D),
}
IN_SHAPES = {
    "x_prompt": (TP, D), "x_sample": (TS, D), "mem_prompt": (256, D),
    "cache_mem_k": ("L", 256, 512), "cache_mem_v": ("L", 256, 512),
    "cache_win_k": ("L", 2048, 512), "cache_win_v": ("L", 2048, 512),
    "state_delta": ("L", 4, 128, 128), "state_delta_conv": ("L", 3, 1536),
    "state_ssm_re": ("L", 32, 64), "state_ssm_im": ("L", 32, 64), "state_lru": ("L", 512),
    "state_lru_conv": ("L", 3, 512), "state_ffn_conv": ("L", 2, 2 * DFF),
}
OUT_SHAPES = [
    ("y_prompt", (TP, D)), ("y_sample", (TS, D)),
    ("p_mem_k", ("L", 256, 512)), ("p_mem_v", ("L", 256, 512)),
    ("p_win_k", ("L", 2048, 512)), ("p_win_v", ("L", 2048, 512)),
    ("p_delta", ("L", 4, 128, 128)), ("p_delta_conv", ("L", 3, 1536)),
    ("p_ssm_re", ("L", 32, 64)), ("p_ssm_im", ("L", 32, 64)), ("p_lru", ("L", 512)),
    ("p_lru_conv", ("L", 3, 512)), ("p_ffn_conv", ("L", 2, 2 * DFF)),
    ("s_win_k", ("L", 8, 512)), ("s_win_v", ("L", 8, 512)),
    ("s_delta", ("L", 4, 128, 128)), ("s_delta_conv", ("L", 3, 1536)),
    ("s_ssm_re", ("L", 32, 64)), ("s_ssm_im", ("L", 32, 64)), ("s_lru", ("L", 512)),
    ("s_lru_conv", ("L", 3, 512)), ("s_ffn_conv", ("L", 2, 2 * DFF)),
]


def _shape(s, L):
    return [L if v == "L" else v for v in s]


class K:
    def __init__(self, NL, stages="all"):
        self.NL = NL
        self.stages = stages
        nc = self.nc = bass.Bass("TRN2", target_bir_lowering=False)
        self.P = Prog(nc)
        self.i = {}
        for n, s in list(W_SHAPES.items()) + list(IN_SHAPES.items()) + list(CONST_SHAPES.items()):
            self.i[n] = nc.dram_tensor(n, _shape(s, NL), F32, kind="ExternalInput").ap()
        self.o = {}
        for n, s in OUT_SHAPES:
            self.o[n] = nc.dram_tensor(n, _shape(s, NL), F32, kind="ExternalOutput").ap()
        self.dbg = {}
        if stages == "dbg":
            for n, shp in [("dbg_xT", [D, TT]), ("dbg_swaT", [1536, TT]), ("dbg_rs", [128, 512]), ("dbg_xn", [128, 512]), ("dbg_g", [128, 64])]:
                self.dbg[n] = nc.dram_tensor(n, shp, F32, kind="ExternalOutput").ap()
            self.dbg["dbg_mix"] = nc.dram_tensor("dbg_mix", [D, TT], BF16, kind="ExternalOutput").ap()
        self.d = {}
        for n, s, dt in [("xT", [D, TT], F32), ("dnqkvT", [1536, TT], F32), ("dnzT", [512, TT], F32),
                         ("dnbaT", [8, TT], F32), ("ssmuT", [512, TT], F32), ("swaT", [1536, TT], F32),
                         ("lruxT", [512, TT], F32), ("lrugT", [512, TT], F32), ("mixT", [D, TT], BF16),
                         ("memT", [D, 256], F32), ("bvec", [8, VL], F32), ("strip", [8, 128, SW], F32)]:
            self.d[n] = nc.dram_tensor("scr_" + n, s, dt, kind="Internal").ap()
        self.off = 16384
        self.seq = 0

    def sb(self, name, shape, dt):
        nb = int(np.prod(shape[1:])) * (4 if dt == F32 else 2)
        nb = (nb + 63) // 64 * 64
        self.seq += 1
        t = self.nc.alloc_sbuf_tensor_at("%s_%d" % (name, self.seq), shape, dt, offset=self.off)
        self.off += nb
        assert self.off <= 229376, (name, self.off)
        return t

    def mm(self, out, lhsT, rhs, start, stop, reads, writes):
        self.P.op("pe", lambda e: e.matmul(out, lhsT=lhsT, rhs=rhs, start=start, stop=stop), reads, writes)

    def tr(self, out, in_, reads, writes):
        ident = self.ident
        n = in_.shape[0]
        self.P.op("pe", lambda e: e.transpose(out=out, in_=in_, identity=ident[0:n, 0:n]), reads, writes)

    def act(self, out, in_, func, reads, writes, scale=1.0, bias=0.0, accum_out=None, eng="act"):
        kw = {}
        if accum_out is not None:
            kw["accum_out"] = accum_out
        self.P.op("act", lambda e: e.activation(out=out, in_=in_, func=func, scale=scale, bias=bias, **kw), reads, writes)

    def cp(self, eng, out, in_, reads, writes):
        if eng == "act":
            self.P.op("act", lambda e: e.activation(out=out, in_=in_, func=AF.Copy), reads, writes)
        else:
            self.P.op(eng, lambda e: e.tensor_copy(out=out, in_=in_), reads, writes)

    def tt(self, eng, out, in0, in1, op, reads, writes):
        self.P.op(eng, lambda e: e.tensor_tensor(out=out, in0=in0, in1=in1, op=op), reads, writes)

    def ts(self, eng, out, in0, s1, op0, reads, writes, s2=None, op1=None):
        if op1 is None:
            self.P.op(eng, lambda e: e.tensor_scalar(out=out, in0=in0, scalar1=s1, scalar2=None, op0=op0), reads, writes)
        else:
            self.P.op(eng, lambda e: e.tensor_scalar(out=out, in0=in0, scalar1=s1, scalar2=s2, op0=op0, op1=op1), reads, writes)

    def stt(self, out, in0, scalar, in1, op0, op1, reads, writes):
        self.P.op("dve", lambda e: e.scalar_tensor_tensor(out=out, in0=in0, scalar=scalar, in1=in1, op0=op0, op1=op1), reads, writes)

    def recip(self, out, in_, reads, writes):
        self.P.op("dve", lambda e: e.reciprocal(out=out, in_=in_), reads, writes)

    def scan(self, out, d0, d1, init, reads, writes):
        self.P.op("dve", lambda e: e.tensor_tensor_scan(out=out, data0=d0, data1=d1, initial=init, op0=ALU.mult, op1=ALU.add), reads, writes)

    def memset(self, eng, ap, v, writes):
        self.P.op(eng, lambda e: e.memset(ap, v), (), writes)

    def dma(self, out, in_, reads, writes, eng="sp", **kw):
        self.P.dma(eng, out, in_, reads, writes, **kw)

    def psum(self):
        self.psi = (self.psi + 1) % len(self.psr)
        b = self.psr[self.psi]
        return self.ps[b], ("ps", b)

    def setup(self):
        nc = self.nc
        self.ps = [self.stack.enter_context(nc.psum_tensor("ps%d" % b, [128, 512], F32)) for b in range(8)]
        self.psr = [0, 1, 2, 3]
        self.psi = 0
        self.ident = self.sb("ident", [128, 128], F32)
        self.anti = self.sb("anti", [128, 128], F32)
        self.ones = self.sb("ones", [128, 128], F32)
        self.onesb = self.sb("onesb", [128, 128], BF16)
        self.neglow = self.sb("neglow", [128, 128], F32)
        self.negup = self.sb("negup", [128, 128], F32)
        self.strict = self.sb("strict", [128, 128], F32)
        self.sel = self.sb("sel", [4, 512], F32)
        for t, n in [(self.ident, "c_ident"), (self.anti, "c_anti"), (self.ones, "c_ones"), (self.neglow, "c_neglow"),
                     (self.negup, "c_negup"), (self.strict, "c_strict"), (self.sel, "c_sel")]:
            self.dma(t[:], self.i[n], [], [n])
        self.cp("dve", self.onesb[:], self.ones[:], ["c_ones"], ["onesb"])
        self.wb = []
        for j in range(3):
            o = self.off
            a = nc.alloc_sbuf_tensor_at("wA%d" % j, [128, 16, 256], BF16, offset=o)
            b = nc.alloc_sbuf_tensor_at("wB%d" % j, [128, 44, 128], BF16, offset=o)
            c = nc.alloc_sbuf_tensor_at("wC%d" % j, [128, 4, 256], BF16, offset=o)
            self.wb.append((a, b, c, ("w", j)))
            self.off += 11264
        self.wbi = 0
        self.base = self.off
        self.P.barrier()

    def next_w(self):
        self.wbi = (self.wbi + 1) % 3
        return self.wb[self.wbi]

    def colload(self, dst, src_rows, n, key):
        st = self.cl_st[self.cl_i % 2]
        sk = ("clst", self.cl_i % 2)
        self.cl_i += 1
        self.dma(st[0:n, :], src_rows, [], [sk])
        ps, pk = self.psum()
        self.tr(ps[:, 0:n], st[0:n, :], [sk, "c_ident"], [pk])
        self.cp("dve", dst, ps[:, 0:n], [pk], [key])

    def stage_input(self):
        self.off = self.lbase
        xin = [self.sb("xin", [128, D], F32) for _ in range(2)]
        xo = [self.sb("xo", [128, 16, 128], F32) for _ in range(2)]
        tiles = [(self.i["x_prompt"], t * 128, 128, t * 128, "xT") for t in range(16)]
        tiles.append((self.i["x_sample"], 0, TS, TP, "xT"))
        tiles += [(self.i["mem_prompt"], t * 128, 128, t * 128, "memT") for t in range(2)]
        for n, (src, r0, nr, c0, dst) in enumerate(tiles):
            xi, xk = xin[n % 2], ("xin", n % 2)
            xot, ok = xo[n % 2], ("xo", n % 2)
            self.dma(xi[0:nr, :], src[r0:r0 + nr, :], [], [xk])
            for f4 in range(4):
                ps, pk = self.psum()
                for j in range(4):
                    f = f4 * 4 + j
                    self.tr(ps[:, j * 128:j * 128 + nr], xi[0:nr, f * 128:(f + 1) * 128], [xk, "c_ident"], [pk])
                src_ps = ps[:].rearrange("p (j t) -> p j t", j=4)[:, :, 0:nr]
                self.cp("dve" if f4 % 2 else "act", xot[:, f4 * 4:f4 * 4 + 4, 0:nr], src_ps, [pk], [ok])
            self.dma(self.d[dst].rearrange("(f p) t -> p f t", p=128)[:, :, c0:c0 + nr], xot[:, :, 0:nr], [ok], [self.d[dst].name])
        self.P.barrier()

    def norm_T(self, src, gcol, chunks, dst, dkey, tbase=0):
        NS = 256
        xl = [self.sb("nxl", [128, 16, NS], F32) for _ in range(2)]
        sq = [self.sb("nsq", [128, NS], F32) for _ in range(2)]
        rs = self.sb("nrs", [128, NS], F32)
        ci = 0
        for (T0, TN) in chunks:
            for t0 in range(T0, T0 + TN, NS):
                tn = min(NS, T0 + TN - t0)
                x, xk = xl[ci % 2], ("nxl", ci % 2)
                ci += 1
                self.dma(x[:, :, 0:tn], src.rearrange("(f p) t -> p f t", p=128)[:, :, t0:t0 + tn], [src.name], [xk])
                ps, pk = self.psum()
                for f in range(16):
                    s, sk = sq[f % 2], ("nsq", f % 2)
                    self.act(s[:, 0:tn], x[:, f, 0:tn], AF.Square, [xk], [sk])
                    self.mm(ps[:, 0:tn], self.ones[:], s[:, 0:tn], f == 0, f == 15, [sk, "c_ones"], [pk])
                self.act(rs[:, 0:tn], ps[:, 0:tn], AF.Sqrt, [pk], ["nrs"], scale=1.0 / D, bias=self.epsc[:, 0:1])
                self.recip(rs[:, 0:tn], rs[:, 0:tn], ["nrs"], ["nrs"])
                for f in range(16):
                    self.stt(dst[:, f, t0 - tbase:t0 - tbase + tn], x[:, f, 0:tn], gcol[:, f:f + 1], rs[:, 0:tn],
                             ALU.mult, ALU.mult, [xk, "nrs", "gcols"], [(dkey, T0)])

    def dense(self, Wd, nK, groups, act, akey, chunks, consume, wv=0):
        for (c0, w) in groups:
            wt = self.next_w()
            wtile, wk = wt[wv], wt[3]
            self.dma(wtile[:, 0:nK, 0:w], Wd[:, c0:c0 + w].rearrange("(k p) c -> p k c", p=128), [], [wk], eng="pool")
            for m0 in range(0, w, 128):
                mw = min(128, w - m0)
                for (t0, tn, a0) in chunks:
                    ps, pk = self.psum()
                    for k in range(nK):
                        self.mm(ps[0:mw, 0:tn], wtile[:, k, m0:m0 + mw], act[:, k, a0:a0 + tn], k == 0, k == nK - 1,
                                [wk, (akey, t0), akey], [pk])
                    consume(c0 + m0, mw, t0, tn, ps, pk)

    def to_dram(self, dst, r0):
        def f(c, mw, t0, tn, ps, pk):
            i = self.evi % 4
            self.evi += 1
            s, sk = self.evs[i], ("evs", i)
            self.cp("act" if i % 2 else "dve", s[0:mw, 0:tn], ps[0:mw, 0:tn], [pk], [sk])
            self.dma(dst[c - r0:c - r0 + mw, t0:t0 + tn], s[0:mw, 0:tn], [sk], [dst.name])
        return f

    def load_gcols(self, l):
        for j, n in enumerate(["g_mix", "g_cross", "g_ffn", "g_mem"]):
            self.colload(self.gcols[:, j, :], self.i[n][l].rearrange("(f p) -> f p", p=128), 16, "gcols")

    def stage_in(self, l):
        self.off = self.lbase
        xn = self.sb("xnT", [128, 16, TT], BF16)
        self.evs = [self.sb("evs", [128, 512], F32) for _ in range(4)]
        self.evi = 0
        self.norm_T(self.d["xT"], self.gcols[:, 0, :], CH, xn, "xnT")
        if self.dbg:
            self.P.barrier()
            dt = self.sb("dbgt", [128, 512], F32)
            self.cp("dve", dt[:], xn[:, 0, 0:512], [], ["dbgt"])
            self.dma(self.dbg["dbg_xn"], dt[:], ["dbgt"], [])
            self.dma(self.dbg["dbg_rs"], self.last_rs[:], [], [])
            self.dma(self.dbg["dbg_g"], self.gcols[:].rearrange("p a b -> p (a b)"), [], [])
            self.dma(self.dbg["dbg_xT"], self.d["xT"], [], [])
            self.P.barrier()
        W = self.i["w_in"][l]
        ch3 = [(t0, tn, t0) for (t0, tn) in CH]
        segs = [(0, 1536, "dnqkvT"), (1536, 512, "dnzT"), (2048, 8, "dnbaT"), (2056, 512, "ssmuT"),
                (2568, 1536, "swaT"), (4104, 512, "lruxT"), (4616, 512, "lrugT")]
        for (c0, n, dn) in segs:
            groups = [(c0 + g, min(256, n - g)) for g in range(0, n, 256)]
            self.dense(W, 16, groups, xn, "xnT", ch3, self.to_dram(self.d[dn], c0))
        self.P.barrier()
        if self.dbg:
            self.dma(self.dbg["dbg_swaT"], self.d["swaT"], [], [])
            self.P.barrier()

    def tok_major_out(self, srcT, r0, ncol, t0, nt, dst, c0, extra=None):
        assert ncol % 128 == 0
        nf = ncol // 128
        st = [self.sb("tmi", [128, nf, 512], F32) for _ in range(2)]
        so = [self.sb("tmo", [128, ncol], F32) for _ in range(2)]
        n = 0
        for tc in range(0, nt, 512):
            tw = min(512, nt - tc)
            s, sk = st[(tc // 512) % 2], ("tmi", (tc // 512) % 2)
            self.dma(s[:, :, 0:tw], srcT[r0:r0 + ncol, :].rearrange("(f p) t -> p f t", p=128)[:, :, t0 + tc:t0 + tc + tw],
                     [srcT.name], [sk])
            for tb in range(0, tw, 128):
                bw = min(128, tw - tb)
                o, ok = so[n % 2], ("tmo", n % 2)
                n += 1
                for f4 in range(0, nf, 4):
                    ps, pk = self.psum()
                    for j in range(min(4, nf - f4)):
                        self.tr(ps[0:bw, j * 128:(j + 1) * 128], s[:, f4 + j, tb:tb + bw], [sk, "c_ident"], [pk])
                    w = min(4, nf - f4) * 128
                    self.cp("act" if (f4 // 4) % 2 else "dve", o[0:bw, f4 * 128:f4 * 128 + w], ps[0:bw, 0:w], [pk], [ok])
                self.dma(dst[tc + tb:tc + tb + bw, c0:c0 + ncol], o[0:bw, :], [ok], [dst.name])
                if extra is not None:
                    extra(tc + tb, bw, o, ok)

    def stage_kvout(self, l):
        self.off = self.lbase
        for (t0, nt), pre in zip(GRP, ("p", "s")):
            self.tok_major_out(self.d["swaT"], 512, 512, t0, nt, self.o[pre + "_win_k"][l], 0)
            self.off = self.lbase
            self.tok_major_out(self.d["swaT"], 1024, 512, t0, nt, self.o[pre + "_win_v"][l], 0)
            self.off = self.lbase
        self.P.barrier()

    def conv_tail_out(self, srcT, nrow, ntail, dstp, dsts, l):
        for (t0, nt), dst in zip(GRP, (dstp, dsts)):
            self.off = self.lbase
            for c0 in range(0, nrow, 512):
                self.tok_major_out(srcT, c0, min(512, nrow - c0), t0 + nt - ntail, ntail, dst[l], c0)
                self.off = self.lbase
        self.P.barrier()

    def stage_memkv(self, l):
        self.off = self.lbase
        mn = self.sb("mnT", [128, 16, 256], BF16)
        self.norm_T(self.d["memT"], self.gcols[:, 3, :], [(0, 256)], mn, "mnT")
        so = [self.sb("mko", [128, 512], F32) for _ in range(2)]
        n = 0
        for wname, oname in (("w_mem_k", "p_mem_k"), ("w_mem_v", "p_mem_v")):
            W = self.i[wname][l]
            wts = []
            for half in range(2):
                wt = self.next_w()
                self.dma(wt[0][:, 0:16, 0:256], W[:, half * 256:(half + 1) * 256].rearrange("(k p) c -> p k c", p=128),
                         [], [wt[3]], eng="pool")
                wts.append(wt)
            for tt in range(2):
                o, ok = so[n % 2], ("mko", n % 2)
                n += 1
                for half in range(2):
                    ps, pk = self.psum()
                    for k in range(16):
                        self.mm(ps[:, 0:256], mn[:, k, tt * 128:(tt + 1) * 128], wts[half][0][:, k, 0:256], k == 0, k == 15,
                                [wts[half][3], ("mnT", 0)], [pk])
                    self.cp("dve" if half else "act", o[:, half * 256:(half + 1) * 256], ps[:, 0:256], [pk], [ok])
                self.dma(self.o[oname][l][tt * 128:(tt + 1) * 128, :], o[:], [ok], [self.o[oname].name])
        self.P.barrier()

    def stage_lru(self, l):
        self.off = self.lbase
        I = self.i
        lp = self.sb("lrup", [128, 40], F32)
        self.colload(lp[:, 0:16], I["lru_conv_w"][l].rearrange("j (f p) -> (j f) p", p=128), 16, "lrup")
        for j, n in enumerate(["lru_conv_b", "lru_b_a", "lru_b_x", "lru_lam"]):
            self.colload(lp[:, 16 + 4 * j:20 + 4 * j], I[n][l].rearrange("(f p) -> f p", p=128), 4, "lrup")
        self.act(lp[:, 32:36], lp[:, 28:32], AF.Exp, ["lrup"], ["lrup"], scale=-1.0)
        self.act(lp[:, 32:36], lp[:, 32:36], AF.Ln, ["lrup"], ["lrup"], bias=self.onec[:, 0:1])
        self.ts("dve", lp[:, 32:36], lp[:, 32:36], -8.0, ALU.mult, ["lrup"], ["lrup"])
        WA = self.sb("lruWA", [128, 4, 128], F32)
        WX = self.sb("lruWX", [128, 4, 128], F32)
        self.memset("dve", WA[:], 0.0, ["lruW"])
        self.memset("dve", WX[:], 0.0, ["lruW"])
        for k in range(8):
            hb = (k % 2) * 64
            self.dma(WA[hb:hb + 64, k // 2, hb:hb + 64], I["lru_w_a"][l][k], [], ["lruW"])
            self.dma(WX[hb:hb + 64, k // 2, hb:hb + 64], I["lru_w_x"][l][k], [], ["lruW"])
        T = TP
        xpad = self.sb("l_xpad", [128, 3 + T], F32)
        xc = self.sb("l_xc", [128, T], F32)
        r = self.sb("l_r", [128, T], F32)
        ig = self.sb("l_ig", [128, T], F32)
        a = self.sb("l_a", [128, T], F32)
        bx = self.sb("l_bx", [128, T], F32)
        g = self.sb("l_g", [128, T], F32)
        u = self.sb("l_u", [128, T], F32)
        yb = self.sb("l_yb", [128, T], BF16)
        h0 = self.sb("l_h0", [128, 4], F32)
        self.colload(h0[:, 0:4], I["state_lru"][l].rearrange("(f p) -> f p", p=128), 4, "l_h0")
        for i in range(4):
            for gi, (t0, T) in enumerate(GRP):
                pre = "ps"[gi] + "_"
                if gi == 0:
                    self.memset("pool", xpad[:, 0:3], 0.0, ["l_xpad"])
                else:
                    self.colload(xpad[:, 0:3], I["state_lru_conv"][l][:, i * 128:(i + 1) * 128], 3, "l_xpad")
                self.dma(xpad[:, 3:3 + T], self.d["lruxT"][i * 128:(i + 1) * 128, t0:t0 + T], [], ["l_xpad"])
                self.dma(g[:, 0:T], self.d["lrugT"][i * 128:(i + 1) * 128, t0:t0 + T], [], ["l_g"])
                self.ts("dve", xc[:, 0:T], xpad[:, 0:T], lp[:, i:i + 1], ALU.mult, ["l_xpad", "lrup"], ["l_xc"],
                        s2=lp[:, 16 + i:17 + i], op1=ALU.add)
                for j in range(1, 4):
                    self.stt(xc[:, 0:T], xpad[:, j:j + T], lp[:, 4 * j + i:4 * j + i + 1], xc[:, 0:T], ALU.mult, ALU.add,
                             ["l_xpad", "lrup", "l_xc"], ["l_xc"])
                for c0 in range(0, T, 512):
                    cw = min(512, T - c0)
                    ps, pk = self.psum()
                    self.mm(ps[:, 0:cw], WA[:, i, :], xc[:, c0:c0 + cw], True, True, ["lruW", "l_xc"], [pk])
                    self.act(r[:, c0:c0 + cw], ps[:, 0:cw], AF.Sigmoid, [pk, "lrup"], ["l_r"], bias=lp[:, 20 + i:21 + i])
                    ps, pk = self.psum()
                    self.mm(ps[:, 0:cw], WX[:, i, :], xc[:, c0:c0 + cw], True, True, ["lruW", "l_xc"], [pk])
                    self.act(ig[:, c0:c0 + cw], ps[:, 0:cw], AF.Sigmoid, [pk, "lrup"], ["l_ig"], bias=lp[:, 24 + i:25 + i])
                self.act(a[:, 0:T], r[:, 0:T], AF.Exp, ["l_r", "lrup"], ["l_a"], scale=lp[:, 32 + i:33 + i])
                self.tt("pool", bx[:, 0:T], a[:, 0:T], a[:, 0:T], ALU.mult, ["l_a"], ["l_bx"])
                self.ts("dve", bx[:, 0:T], bx[:, 0:T], -1.0, ALU.mult, ["l_bx"], ["l_bx"], s2=1.0, op1=ALU.add)
                self.ts("dve", bx[:, 0:T], bx[:, 0:T], 0.0, ALU.max, ["l_bx"], ["l_bx"])
                self.act(bx[:, 0:T], bx[:, 0:T], AF.Sqrt, ["l_bx"], ["l_bx"])
                self.tt("pool", ig[:, 0:T], ig[:, 0:T], xc[:, 0:T], ALU.mult, ["l_ig", "l_xc"], ["l_ig"])
                self.tt("dve", bx[:, 0:T], bx[:, 0:T], ig[:, 0:T], ALU.mult, ["l_bx", "l_ig"], ["l_bx"])
                init = 0.0 if gi == 0 else h0[:, i:i + 1]
                self.scan(r[:, 0:T], a[:, 0:T], bx[:, 0:T], init, ["l_a", "l_bx", "l_h0", "l_r"], ["l_r"])
                self.dma(self.o[pre + "lru"][l][i * 128:(i + 1) * 128].rearrange("(p o) -> p o", o=1), r[:, T - 1:T],
                         ["l_r"], [self.o[pre + "lru"].name])
                self.tt("pool", u[:, 0:T], g[:, 0:T], g[:, 0:T], ALU.mult, ["l_g"], ["l_u"])
                self.ts("dve", u[:, 0:T], u[:, 0:T], 0.044715, ALU.mult, ["l_u"], ["l_u"], s2=1.0, op1=ALU.add)
                self.tt("dve", u[:, 0:T], u[:, 0:T], g[:, 0:T], ALU.mult, ["l_u", "l_g"], ["l_u"])
                self.act(u[:, 0:T], u[:, 0:T], AF.Sigmoid, ["l_u"], ["l_u"], scale=1.5957691216057308)
                self.tt("pool", u[:, 0:T], u[:, 0:T], g[:, 0:T], ALU.mult, ["l_u", "l_g"], ["l_u"])
                self.tt("dve", yb[:, 0:T], u[:, 0:T], r[:, 0:T], ALU.mult, ["l_u", "l_r"], ["l_yb"])
                self.dma(self.d["mixT"][1536 + i * 128:1536 + (i + 1) * 128, t0:t0 + T], yb[:, 0:T], ["l_yb"], ["mixT"])
        self.P.barrier()

    def gelu(self, out, x, tmp, kx, kt, ko):
        self.tt("pool", tmp, x, x, ALU.mult, [kx], [kt])
        self.ts("dve", tmp, tmp, 0.044715, ALU.mult, [kt], [kt], s2=1.0, op1=ALU.add)
        self.tt("dve", tmp, tmp, x, ALU.mult, [kt, kx], [kt])
        self.act(tmp, tmp, AF.Sigmoid, [kt], [kt], scale=1.5957691216057308)
        self.tt("dve", out, tmp, x, ALU.mult, [kt, kx], [ko] if ko != kt else [kt])

    def stage_ssm(self, l):
        self.off = self.lbase
        I = self.i
        PI = math.pi
        sp = self.sb("ssp", [128, 24, 16], F32)
        names = ["are", "aim", "ldt", "step", "ars", "ais", "mag", "c1", "s1", "abr", "abi", "den", "cr", "ci",
                 "t0", "t1", "h0r", "h0i", "g0r", "g0i", "t2", "t3"]
        ix = {n: k for k, n in enumerate(names)}
        P_ = lambda n: sp[:, ix[n], :]
        K_ = "ssp"
        self.colload(P_("are"), I["ssm_a_re"][l].rearrange("(j a) n -> j (a n)", a=2), 16, K_)
        self.colload(P_("aim"), I["ssm_a_im"][l].rearrange("(j a) n -> j (a n)", a=2), 16, K_)
        self.colload(P_("h0r"), I["state_ssm_re"][l].rearrange("(j a) n -> j (a n)", a=2), 16, K_)
        self.colload(P_("h0i"), I["state_ssm_im"][l].rearrange("(j a) n -> j (a n)", a=2), 16, K_)
        ldr = self.sb("ldr", [1, 32], F32)
        self.dma(ldr[:], I["ssm_log_dt"][l:l + 1, :], [], ["ldr"])
        ps, pk = self.psum()
        self.mm(ps[:, 0:32], self.ones[0:1, :], ldr[0:1, :], True, True, ["ldr"], [pk])
        pv = ps[:, 0:32].rearrange("p (j a) -> p j a", a=2)
        self.cp("dve", P_("ldt")[0:64, :], pv[0:64, :, 0], [pk], [K_])
        self.cp("dve", P_("ldt")[64:128, :], pv[64:128, :, 1], [pk], [K_])
        self.act(P_("step"), P_("ldt"), AF.Exp, [K_], [K_])
        self.tt("dve", P_("ars"), P_("are"), P_("step"), ALU.mult, [K_], [K_])
        self.tt("dve", P_("ais"), P_("aim"), P_("step"), ALU.mult, [K_], [K_])
        self.act(P_("mag"), P_("ars"), AF.Exp, [K_], [K_])
        it = self.sb("ssi", [128, 16], mybir.dt.int32)

        def sin_of(dst, src, shift):
            self.ts("dve", P_("t0"), src, shift, ALU.add, [K_], [K_])
            self.ts("dve", P_("t1"), P_("t0"), 1.0 / (2 * PI), ALU.mult, [K_], [K_])
            self.cp("dve", it[:], P_("t1"), [K_], ["ssi"])
            self.cp("dve", P_("t1"), it[:], ["ssi"], [K_])
            self.stt(P_("t0"), P_("t1"), -2 * PI, P_("t0"), ALU.mult, ALU.add, [K_], [K_])
            self.ts("dve", P_("t1"), P_("t0"), PI, ALU.is_gt, [K_], [K_])
            self.stt(P_("t0"), P_("t1"), -2 * PI, P_("t0"), ALU.mult, ALU.add, [K_], [K_])
            self.ts("dve", P_("t1"), P_("t0"), -PI, ALU.is_lt, [K_], [K_])
            self.stt(P_("t0"), P_("t1"), 2 * PI, P_("t0"), ALU.mult, ALU.add, [K_], [K_])
            self.act(dst, P_("t0"), AF.Sin, [K_], [K_])
        sin_of(P_("s1"), P_("ais"), 0.0)
        sin_of(P_("c1"), P_("ais"), PI / 2)
        self.tt("dve", P_("abr"), P_("mag"), P_("c1"), ALU.mult, [K_], [K_])
        self.tt("dve", P_("abi"), P_("mag"), P_("s1"), ALU.mult, [K_], [K_])
        self.tt("dve", P_("den"), P_("are"), P_("are"), ALU.mult, [K_], [K_])
        self.tt("dve", P_("t0"), P_("aim"), P_("aim"), ALU.mult, [K_], [K_])
        self.tt("dve", P_("den"), P_("den"), P_("t0"), ALU.add, [K_], [K_])
        self.recip(P_("den"), P_("den"), [K_], [K_])
        self.ts("dve", P_("t2"), P_("abr"), -1.0, ALU.add, [K_], [K_])
        self.tt("dve", P_("t0"), P_("t2"), P_("are"), ALU.mult, [K_], [K_])
        self.tt("dve", P_("t1"), P_("abi"), P_("aim"), ALU.mult, [K_], [K_])
        self.tt("dve", P_("cr"), P_("t0"), P_("t1"), ALU.add, [K_], [K_])
        self.tt("dve", P_("cr"), P_("cr"), P_("den"), ALU.mult, [K_], [K_])
        self.tt("dve", P_("t0"), P_("abi"), P_("are"), ALU.mult, [K_], [K_])
        self.tt("dve", P_("t1"), P_("t2"), P_("aim"), ALU.mult, [K_], [K_])
        self.tt("dve", P_("ci"), P_("t0"), P_("t1"), ALU.subtract, [K_], [K_])
        self.tt("dve", P_("ci"), P_("ci"), P_("den"), ALU.mult, [K_], [K_])
        self.tt("dve", P_("t0"), P_("c1"), P_("h0r"), ALU.mult, [K_], [K_])
        self.tt("dve", P_("t1"), P_("s1"), P_("h0i"), ALU.mult, [K_], [K_])
        self.tt("dve", P_("g0r"), P_("t0"), P_("t1"), ALU.subtract, [K_], [K_])
        self.tt("dve", P_("t0"), P_("c1"), P_("h0i"), ALU.mult, [K_], [K_])
        self.tt("dve", P_("t1"), P_("s1"), P_("h0r"), ALU.mult, [K_], [K_])
        self.tt("dve", P_("g0i"), P_("t0"), P_("t1"), ALU.add, [K_], [K_])
        pwc = self.sb("spwc", [128, 12, 16], F32)
        pws = self.sb("spws", [128, 12, 16], F32)
        self.cp("dve", pwc[:, 0, :], P_("c1"), [K_], ["spw"])
        self.cp("dve", pws[:, 0, :], P_("s1"), [K_], ["spw"])
        for k in range(11):
            self.tt("dve", P_("t0"), pwc[:, k, :], pwc[:, k, :], ALU.mult, ["spw", K_], [K_])
            self.tt("dve", P_("t1"), pws[:, k, :], pws[:, k, :], ALU.mult, ["spw", K_], [K_])
            self.tt("dve", pwc[:, k + 1, :], P_("t0"), P_("t1"), ALU.subtract, [K_], ["spw"])
            self.tt("dve", P_("t0"), pwc[:, k, :], pws[:, k, :], ALU.mult, ["spw", K_], [K_])
            self.ts("dve", pws[:, k + 1, :], P_("t0"), 2.0, ALU.mult, [K_], ["spw"])
        BR = self.sb("sBR", [128, 16, 16], F32)
        BI = self.sb("sBI", [128, 16, 16], F32)
        QR = self.sb("sQR", [128, 16, 16], F32)
        QI = self.sb("sQI", [128, 16, 16], F32)
        TM = self.sb("sTM", [128, 16, 16], F32)
        self.dma(BR[:], I["ssm_b_re"][l].rearrange("(j a) n c -> (a n) j c", a=2), [], ["sB"])
        self.dma(BI[:], I["ssm_b_im"][l].rearrange("(j a) n c -> (a n) j c", a=2), [], ["sB"])
        crb = P_("cr").unsqueeze(2).to_broadcast([128, 16, 16])
        cib = P_("ci").unsqueeze(2).to_broadcast([128, 16, 16])
        self.tt("dve", QR[:], BR[:], crb, ALU.mult, ["sB", K_], ["sQ"])
        self.tt("dve", TM[:], BI[:], cib, ALU.mult, ["sB", K_], ["sTM"])
        self.tt("dve", QR[:], QR[:], TM[:], ALU.subtract, ["sQ", "sTM"], ["sQ"])
        self.tt("dve", QI[:], BI[:], crb, ALU.mult, ["sB", K_], ["sQ"])
        self.tt("dve", TM[:], BR[:], cib, ALU.mult, ["sB", K_, "sQ"], ["sTM"])
        self.tt("dve", QI[:], QI[:], TM[:], ALU.add, ["sQ", "sTM"], ["sQ"])
        LB = [self.sb("sLB%d" % q, [128, 16, 128], F32) for q in range(2)]
        WC = [self.sb("sWC%d" % q, [128, 16, 128], F32) for q in range(2)]
        Z = self.sb("sZ", [128, 128], F32)
        for q, Q in enumerate((QR, QI)):
            for j in range(16):
                c0 = 32 * (j % 4)
                self.memset("pool", Z[:], 0.0, ["sZ"])
                self.cp("pool", Z[0:64, c0:c0 + 16], Q[0:64, j, :], ["sQ", "sZ"], ["sZ"])
                self.cp("pool", Z[64:128, c0 + 16:c0 + 32], Q[64:128, j, :], ["sQ", "sZ"], ["sZ"])
                ps, pk = self.psum()
                self.tr(ps[:, 0:128], Z[:], ["sZ"], [pk])
                self.cp("dve", LB[q][:, j, :], ps[:, 0:128], [pk], ["sLB"])
        V = self.sb("sV", [32, 16, 128], F32)
        for q, nm in enumerate(("ssm_c_re", "ssm_c_im")):
            self.memset("pool", V[:], 0.0, ["sV"])
            self.memset("pool", WC[q][:], 0.0, ["sWC"])
            cv = I[nm][l].rearrange("(j a) c n -> a c j n", a=2)
            self.dma(V[0:16, :, 0:64], cv[0], ["sV"], ["sV"])
            self.dma(V[16:32, :, 64:128], cv[1], ["sV"], ["sV"])
            for j in range(16):
                c0 = 32 * (j % 4)
                ps, pk = self.psum()
                self.tr(ps[:, 0:32], V[:, j, :], ["sV"], [pk])
                if q == 0:
                    self.cp("dve", WC[q][:, j, c0:c0 + 32], ps[:, 0:32], [pk, "sWC"], ["sWC"])
                else:
                    self.ts("dve", WC[q][:, j, c0:c0 + 32], ps[:, 0:32], -1.0, ALU.mult, [pk, "sWC"], ["sWC"])
        dcol = self.sb("sdc", [128, 8], F32)
        self.colload(dcol[:, 0:4], I["ssm_d"][l].rearrange("(f p) -> f p", p=128), 4, "sdc")
        self.colload(dcol[:, 4:8], I["ssm_b_glu"][l].rearrange("(f p) -> f p", p=128), 4, "sdc")
        T = TP
        big = {n: self.sb("s_" + n, [128, T], F32) for n in ("xr", "xi", "ec", "es", "t1", "t2", "tm")}
        u = self.sb("s_u", [128, TT], F32)
        yg = self.sb("s_yg", [128, 4, TT], F32)
        ygb = self.sb("s_ygb", [128, 4, TT], BF16)
        xr, xi, ec, es, t1, t2, tm = (big[n] for n in ("xr", "xi", "ec", "es", "t1", "t2", "tm"))
        for yi in range(4):
            self.dma(u[:], self.d["ssmuT"][yi * 128:(yi + 1) * 128, :], [], ["s_u"])
            for gi, (t0, T) in enumerate(GRP):
                pre = "ps"[gi] + "_"
                nch = [(c0, min(512, T - c0)) for c0 in range(0, T, 512)]
                for jj in range(4):
                    j = 4 * yi + jj
                    for (dst, q, kk) in ((xr, 0, "s_xr"), (xi, 1, "s_xi")):
                        for (c0, cw) in nch:
                            ps, pk = self.psum()
                            self.mm(ps[:, 0:cw], LB[q][:, j, :], u[:, t0 + c0:t0 + c0 + cw], True, True, ["sLB", "s_u"], [pk])
                            self.cp("act", dst[:, c0:c0 + cw], ps[:, 0:cw], [pk], [kk])
                    self.memset("pool", ec[:, 0:1], 1.0, ["s_ec"])
                    self.memset("pool", es[:, 0:1], 0.0, ["s_es"])
                    n = 1
                    k = 0
                    while n < T:
                        cn, sn = pwc[:, k, j:j + 1], pws[:, k, j:j + 1]
                        self.ts("dve", tm[:, 0:n], es[:, 0:n], sn, ALU.mult, ["s_es", "spw"], ["s_tm"])
                        self.stt(ec[:, n:2 * n], ec[:, 0:n], cn, tm[:, 0:n], ALU.mult, ALU.subtract, ["s_ec", "s_tm", "spw"], ["s_ec2"])
                        self.ts("dve", tm[:, 0:n], ec[:, 0:n], sn, ALU.mult, ["s_ec", "spw", "s_ec2"], ["s_tm"])
                        self.stt(es[:, n:2 * n], es[:, 0:n], cn, tm[:, 0:n], ALU.mult, ALU.add, ["s_es", "s_tm", "spw"], ["s_es"])
                        self.P.last_w["s_ec"] = self.P.last_w["s_ec2"]
                        n *= 2
                        k += 1
                    self.tt("pool", t1[:, 0:T], ec[:, 0:T], xr[:, 0:T], ALU.mult, ["s_ec", "s_xr"], ["s_t1"])
                    self.tt("dve", tm[:, 0:T], es[:, 0:T], xi[:, 0:T], ALU.mult, ["s_es", "s_xi"], ["s_tm"])
                    self.tt("dve", t1[:, 0:T], t1[:, 0:T], tm[:, 0:T], ALU.add, ["s_t1", "s_tm"], ["s_t1"])
                    self.tt("pool", t2[:, 0:T], ec[:, 0:T], xi[:, 0:T], ALU.mult, ["s_ec", "s_xi"], ["s_t2"])
                    self.tt("dve", tm[:, 0:T], es[:, 0:T], xr[:, 0:T], ALU.mult, ["s_es", "s_xr", "s_t1"], ["s_tm"])
                    self.tt("dve", t2[:, 0:T], t2[:, 0:T], tm[:, 0:T], ALU.subtract, ["s_t2", "s_tm"], ["s_t2"])
                    rho = P_("mag")[:, j:j + 1].to_broadcast([128, T])
                    ir = 0.0 if gi == 0 else P_("g0r")[:, j:j + 1]
                    ii = 0.0 if gi == 0 else P_("g0i")[:, j:j + 1]
                    self.scan(xr[:, 0:T], rho, t1[:, 0:T], ir, [K_, "s_t1", "s_xr"], ["s_xr"])
                    self.scan(xi[:, 0:T], rho, t2[:, 0:T], ii, [K_, "s_t2", "s_xi"], ["s_xi"])
                    self.tt("pool", t1[:, 0:T], ec[:, 0:T], xr[:, 0:T], ALU.mult, ["s_ec", "s_xr", "s_t1"], ["s_t1"])
                    self.tt("dve", tm[:, 0:T], es[:, 0:T], xi[:, 0:T], ALU.mult, ["s_es", "s_xi", "s_t2"], ["s_tm"])
                    self.tt("dve", t1[:, 0:T], t1[:, 0:T], tm[:, 0:T], ALU.subtract, ["s_t1", "s_tm"], ["s_t1"])
                    self.tt("pool", t2[:, 0:T], ec[:, 0:T], xi[:, 0:T], ALU.mult, ["s_ec", "s_xi", "s_t2"], ["s_t2"])
                    self.tt("dve", tm[:, 0:T], es[:, 0:T], xr[:, 0:T], ALU.mult, ["s_es", "s_xr", "s_t1"], ["s_tm"])
                    self.tt("dve", t2[:, 0:T], t2[:, 0:T], tm[:, 0:T], ALU.add, ["s_t2", "s_tm"], ["s_t2"])
                    for (src, nm, kk) in ((t1, "ssm_re", "s_t1"), (t2, "ssm_im", "s_t2")):
                        o = self.o[pre + nm][l].rearrange("g n -> (g n)")[j * 128:(j + 1) * 128].rearrange("(p o) -> p o", o=1)
                        self.dma(o, src[:, T - 1:T], [kk], [self.o[pre + nm].name])
                    for ci, (c0, cw) in enumerate(nch):
                        yps, yk = self.ps[4 + ci], ("ps", 4 + ci)
                        self.mm(yps[:, 0:cw], WC[0][:, j, :], t1[:, c0:c0 + cw], jj == 0, False, ["sWC", "s_t1"], [yk])
                        self.mm(yps[:, 0:cw], WC[1][:, j, :], t2[:, c0:c0 + cw], False, jj == 3, ["sWC", "s_t2"], [yk])
                for ci, (c0, cw) in enumerate(nch):
                    yps, yk = self.ps[4 + ci], ("ps", 4 + ci)
                    self.stt(yg[:, yi, t0 + c0:t0 + c0 + cw], u[:, t0 + c0:t0 + c0 + cw], dcol[:, yi:yi + 1], yps[:, 0:cw],
                             ALU.mult, ALU.add, ["s_u", "sdc", yk], [("s_yg", yi, gi)])
                self.gelu(yg[:, yi, t0:t0 + T], yg[:, yi, t0:t0 + T], tm[:, 0:T], ("s_yg", yi, gi), "s_tm", ("s_yg", yi, gi))
                self.cp("pool", ygb[:, yi, t0:t0 + T], yg[:, yi, t0:t0 + T], [("s_yg", yi, gi)], ["s_ygb"])
        ob = [self.sb("s_ob", [128, 512], BF16) for _ in range(2)]
        sg = [self.sb("s_sg", [128, 512], F32) for _ in range(2)]
        self.gli = 0

        def glu(c, mw, t0, tn, ps, pk):
            i = c // 128
            b = self.gli % 2
            self.gli += 1
            self.act(sg[b][:, 0:tn], ps[:, 0:tn], AF.Sigmoid, [pk, "sdc"], [("s_sg", b)], bias=dcol[:, 4 + i:5 + i])
            gi = 0 if t0 < TP else 1
            self.tt("dve", ob[b][:, 0:tn], sg[b][:, 0:tn], yg[:, i, t0:t0 + tn], ALU.mult, [("s_sg", b), ("s_yg", i, gi)], [("s_ob", b)])
            self.dma(self.d["mixT"][512 + c:512 + c + mw, t0:t0 + tn], ob[b][:, 0:tn], [("s_ob", b)], ["mixT"])
        self.dense(I["ssm_w_glu"][l], 4, [(0, 256), (256, 256)], ygb, "s_ygb", [(t0, tn, t0) for (t0, tn) in CH], glu, wv=2)
        self.P.barrier()

    def stage_dn(self, l):
        self.off = self.lbase
        I = self.i
        self.psr = list(range(8))
        cw = self.sb("dcw", [128, 48], F32)
        self.colload(cw[:, 0:48], I["dn_conv_w"][l].rearrange("j (f p) -> (j f) p", p=128), 48, "dcw")
        ngc_ = self.sb("dng", [128, 1], F32)
        self.dma(ngc_[:], I["dn_norm_g"][l].rearrange("(p o) -> p o", o=1), [], ["dng"])
        hp = self.sb("dhp", [4, 4], F32)
        self.dma(hp[:, 0:1], I["dn_a_log"][l].rearrange("(h o) -> h o", o=1), [], ["dhp"])
        self.dma(hp[:, 1:2], I["dn_dt_bias"][l].rearrange("(h o) -> h o", o=1), [], ["dhp"])
        self.act(hp[:, 2:3], hp[:, 0:1], AF.Exp, ["dhp"], ["dhp"])
        self.ts("dve", hp[:, 2:3], hp[:, 2:3], -1.0, ALU.mult, ["dhp"], ["dhp"])
        T = TP
        rows = {n: self.sb("dr_" + n, [4, T], F32) for n in ("b", "g", "gc")}
        rows["ngc"] = rows["g"]
        colB = self.sb("dcB", [128, 16, 4], F32)
        colG = self.sb("dcG", [128, 16, 4], F32)
        colNB = self.sb("dcNB", [128, 16, 4], F32)
        colNEB = self.sb("dcNEB", [128, 16, 4], F32)
        qT = [self.sb("dq%d" % h, [128, T], F32) for h in range(4)]
        kT = [self.sb("dk%d" % h, [128, T], F32) for h in range(4)]
        vT = [self.sb("dv%d" % h, [128, T], F32) for h in range(4)]
        oT = vT
        S = [self.sb("dS%d" % h, [128, 128], F32) for h in range(4)]
        xpad = self.sb("dxp", [128, 3 + T], F32)
        rs = self.sb("drs", [128, 512], F32)
        sq = self.sb("dsq", [128, 512], F32)
        wn = ("dec", "decT", "N", "NT", "X0", "X1", "Y0", "Y1", "AT", "vb", "kd", "R", "vn", "QK", "qg")
        wt = [{n: self.sb("dw%d%s" % (h, n), [128, 128], F32) for n in wn} for h in range(4)]
        egl = [self.sb("degl%d" % h, [128, 1], F32) for h in range(4)]
        ob = self.sb("dob", [128, 512], BF16)
        for gi, (t0, T) in enumerate(GRP):
            pre = "ps"[gi] + "_"
            C = 128 if gi == 0 else 8
            NCH = T // C
            L = int(round(math.log2(C))) - 1
            b, g, gc, ngc = (rows[n] for n in ("b", "g", "gc", "ngc"))
            self.dma(b[:, 0:T], self.d["dnbaT"][0:4, t0:t0 + T], [], ["dr_b"])
            self.dma(g[:, 0:T], self.d["dnbaT"][4:8, t0:t0 + T], [], ["dr_g"])
            self.act(b[:, 0:T], b[:, 0:T], AF.Sigmoid, ["dr_b"], ["dr_b"])
            self.act(g[:, 0:T], g[:, 0:T], AF.Exp, ["dr_g", "dhp"], ["dr_g"], bias=hp[:, 1:2])
            self.act(g[:, 0:T], g[:, 0:T], AF.Ln, ["dr_g"], ["dr_g"], bias=self.onec[0:4, 0:1])
            self.ts("dve", g[:, 0:T], g[:, 0:T], hp[:, 2:3], ALU.mult, ["dr_g", "dhp"], ["dr_g"])
            for n in range(NCH):
                self.scan(gc[:, n * C:(n + 1) * C], self.ones[0:4, 0:C], g[:, n * C:(n + 1) * C], 0.0, ["dr_g", "dr_gc"], ["dr_gc"])
            self.ts("dve", ngc[:, 0:T], gc[:, 0:T], -1.0, ALU.mult, ["dr_gc"], ["dr_g"])
            for n in range(NCH):
                ps, pk = self.psum()
                self.tr(ps[0:C, 0:4], b[:, n * C:(n + 1) * C], ["dr_b"], [pk])
                self.tr(ps[0:C, 4:8], gc[:, n * C:(n + 1) * C], ["dr_gc"], [pk])
                self.cp("dve", colB[0:C, n, :], ps[0:C, 0:4], [pk], ["dcol"])
                self.cp("dve", colG[0:C, n, :], ps[0:C, 4:8], [pk], ["dcol"])
            self.act(colG[0:C, 0:NCH, :], colG[0:C, 0:NCH, :], AF.Exp, ["dcol"], ["dcol"])
            self.ts("dve", colNB[0:C, 0:NCH, :], colB[0:C, 0:NCH, :], -1.0, ALU.mult, ["dcol"], ["dcol"])
            self.tt("dve", colNEB[0:C, 0:NCH, :], colG[0:C, 0:NCH, :], colNB[0:C, 0:NCH, :], ALU.mult, ["dcol"], ["dcol"])
            for h in range(4):
                for (dst, part, kk) in ((qT[h], 0, ("dq", h)), (kT[h], 1, ("dk", h)), (vT[h], 2, ("dv", h))):
                    f = part * 4 + h
                    if gi == 0:
                        self.memset("pool", xpad[:, 0:3], 0.0, ["dxp"])
                    else:
                        self.colload(xpad[:, 0:3], I["state_delta_conv"][l][:, f * 128:(f + 1) * 128], 3, "dxp")
                    self.dma(xpad[:, 3:3 + T], self.d["dnqkvT"][f * 128:(f + 1) * 128, t0:t0 + T], [], ["dxp"])
                    self.ts("dve", dst[:, 0:T], xpad[:, 0:T], cw[:, f:f + 1], ALU.mult, ["dxp", "dcw"], [kk])
                    for j in range(1, 4):
                        self.stt(dst[:, 0:T], xpad[:, j:j + T], cw[:, 12 * j + f:12 * j + f + 1], dst[:, 0:T], ALU.mult, ALU.add,
                                 ["dxp", "dcw", kk], [kk])
                    self.act(dst[:, 0:T], dst[:, 0:T], AF.Silu, [kk], [kk])
                    if part < 2:
                        for c0 in range(0, T, 512):
                            w = min(512, T - c0)
                            self.act(sq[:, 0:w], dst[:, c0:c0 + w], AF.Square, [kk], ["dsq"])
                            ps, pk = self.psum()
                            self.mm(ps[:, 0:w], self.ones[:], sq[:, 0:w], True, True, ["dsq"], [pk])
                            self.act(rs[:, 0:w], ps[:, 0:w], AF.Sqrt, [pk], ["drs"], bias=self.epsc[:, 0:1])
                            self.recip(rs[:, 0:w], rs[:, 0:w], ["drs"], ["drs"])
                            self.stt(dst[:, c0:c0 + w], dst[:, c0:c0 + w], (128 ** -0.5) if part == 0 else 1.0, rs[:, 0:w],
                                     ALU.mult, ALU.mult, [kk, "drs"], [kk])
                if gi == 0:
                    self.memset("pool", S[h][:], 0.0, [("dS", h)])
                else:
                    self.dma(S[h][:], I["state_delta"][l][h], [], [("dS", h)])
            for n in range(NCH):
                cs = slice(n * C, (n + 1) * C)
                for h in range(4):
                    w = wt[h]
                    W = lambda nm: ("dw", h, nm)
                    selM = self.sel[0:4, h * 128:h * 128 + C]
                    sel128 = self.sel[0:4, h * 128:(h + 1) * 128]
                    kq, kk_, kv = ("dq", h), ("dk", h), ("dv", h)
                    ps, pk = self.psum()
                    self.mm(ps[0:C, 0:C], gc[0:4, cs], selM, True, False, ["dr_gc"], [pk])
                    self.mm(ps[0:C, 0:C], selM, ngc[0:4, cs], False, False, ["dr_g"], [pk])
                    self.mm(ps[0:C, 0:C], self.ident[0:C, 0:C], self.neglow[0:C, 0:C], False, True, [], [pk])
                    self.act(w["dec"][0:C, 0:C], ps[0:C, 0:C], AF.Exp, [pk], [W("dec")])
                    ps, pk = self.psum()
                    self.mm(ps[0:C, 0:C], selM, gc[0:4, cs], True, False, ["dr_gc"], [pk])
                    self.mm(ps[0:C, 0:C], ngc[0:4, cs], selM, False, False, ["dr_g"], [pk])
                    self.mm(ps[0:C, 0:C], self.ident[0:C, 0:C], self.negup[0:C, 0:C], False, True, [], [pk])
                    self.act(w["decT"][0:C, 0:C], ps[0:C, 0:C], AF.Exp, [pk], [W("decT")])
                    ps, pk = self.psum()
                    self.mm(ps[0:C, 0:C], kT[h][:, cs], kT[h][:, cs], True, True, [kk_], [pk])
                    self.tt("dve", w["N"][0:C, 0:C], ps[0:C, 0:C], w["dec"][0:C, 0:C], ALU.mult, [pk, W("dec")], [W("N")])
                    self.stt(w["N"][0:C, 0:C], w["N"][0:C, 0:C], colNB[0:C, n, h:h + 1], self.strict[0:C, 0:C], ALU.mult, ALU.mult,
                             [W("N"), "dcol"], [W("N")])
                    ps, pk = self.psum()
                    self.tr(ps[0:C, 0:C], w["N"][0:C, 0:C], [W("N")], [pk])
                    self.cp("act", w["NT"][0:C, 0:C], ps[0:C, 0:C], [pk], [W("NT")])
                    self.tt("dve", w["AT"][0:C, 0:C], w["NT"][0:C, 0:C], self.ident[0:C, 0:C], ALU.add, [W("NT")], [W("AT")])
                    Xc, Yc = "N", "NT"
                    for i in range(L):
                        Xn, Yn = "X%d" % (i % 2), "Y%d" % (i % 2)
                        ps, pk = self.psum()
                        self.mm(ps[0:C, 0:C], w[Yc][0:C, 0:C], w[Xc][0:C, 0:C], True, True, [W(Xc), W(Yc)], [pk])
                        self.cp("act", w[Xn][0:C, 0:C], ps[0:C, 0:C], [pk], [W(Xn)])
                        if i < L - 1:
                            ps, pk = self.psum()
                            self.mm(ps[0:C, 0:C], w[Xc][0:C, 0:C], w[Yc][0:C, 0:C], True, True, [W(Xc), W(Yc)], [pk])
                            self.cp("dve", w[Yn][0:C, 0:C], ps[0:C, 0:C], [pk], [W(Yn)])
                        ps, pk = self.psum()
                        self.mm(ps[0:C, 0:C], w[Xn][0:C, 0:C], w["AT"][0:C, 0:C], True, True, [W(Xn), W("AT")], [pk])
                        self.tt("dve", w["AT"][0:C, 0:C], w["AT"][0:C, 0:C], ps[0:C, 0:C], ALU.add, [pk, W("AT")], [W("AT")])
                        Xc, Yc = Xn, Yn
                    ps, pk = self.psum()
                    self.tr(ps[0:C, 0:128], vT[h][:, cs], [kv], [pk])
                    self.ts("dve", w["vb"][0:C, :], ps[0:C, 0:128], colB[0:C, n, h:h + 1], ALU.mult, [pk, "dcol"], [W("vb")])
                    ps, pk = self.psum()
                    self.tr(ps[0:C, 0:128], kT[h][:, cs], [kk_], [pk])
                    self.ts("dve", w["kd"][0:C, :], ps[0:C, 0:128], w["decT"][0:C, C - 1:C], ALU.mult, [pk, W("decT")], [W("kd")])
                    ps, pk = self.psum()
                    self.mm(ps[0:C, 0:128], kT[h][:, cs], S[h][:], True, True, [kk_, ("dS", h)], [pk])
                    self.stt(w["R"][0:C, :], ps[0:C, 0:128], colNEB[0:C, n, h:h + 1], w["vb"][0:C, :], ALU.mult, ALU.add,
                             [pk, "dcol", W("vb")], [W("R")])
                    ps, pk = self.psum()
                    self.mm(ps[0:C, 0:128], w["AT"][0:C, 0:C], w["R"][0:C, :], True, True, [W("AT"), W("R")], [pk])
                    self.cp("act", w["vn"][0:C, :], ps[0:C, 0:128], [pk], [W("vn")])
                    ps, pk = self.psum()
                    self.mm(ps[:, 0:C], sel128, gc[0:4, cs], True, True, ["dr_gc"], [pk])
                    self.act(w["qg"][:, 0:C], ps[:, 0:C], AF.Exp, [pk], [W("qg")])
                    self.tt("dve", w["qg"][:, 0:C], qT[h][:, cs], w["qg"][:, 0:C], ALU.mult, [kq, W("qg")], [W("qg")])
                    ps, pk = self.psum()
                    self.mm(ps[0:C, 0:C], kT[h][:, cs], qT[h][:, cs], True, True, [kk_, kq], [pk])
                    self.tt("dve", w["QK"][0:C, 0:C], ps[0:C, 0:C], w["decT"][0:C, 0:C], ALU.mult, [pk, W("decT")], [W("QK")])
                    ps, pk = self.psum()
                    self.mm(ps[:, 0:C], S[h][:], w["qg"][:, 0:C], True, False, [("dS", h), W("qg")], [pk])
                    self.mm(ps[:, 0:C], w["vn"][0:C, :], w["QK"][0:C, 0:C], False, True, [W("vn"), W("QK")], [pk])
                    self.cp("act", oT[h][:, cs], ps[:, 0:C], [pk], [("dv", h)])
                    ps, pk = self.psum()
                    self.mm(ps[:, 0:1], sel128, gc[0:4, (n + 1) * C - 1:(n + 1) * C], True, True, ["dr_gc"], [pk])
                    self.act(egl[h][:], ps[:, 0:1], AF.Exp, [pk], [("degl", h)])
                    ps, pk = self.psum()
                    self.mm(ps[:, 0:128], w["kd"][0:C, :], w["vn"][0:C, :], True, True, [W("kd"), W("vn")], [pk])
                    self.stt(S[h][:], S[h][:], egl[h][:, 0:1], ps[:, 0:128], ALU.mult, ALU.add, [("dS", h), ("degl", h), pk], [("dS", h)])
            for h in range(4):
                self.dma(self.o[pre + "delta"][l][h], S[h][:], [("dS", h)], [self.o[pre + "delta"].name])
                z = xpad
                self.dma(z[:, 0:T], self.d["dnzT"][h * 128:(h + 1) * 128, t0:t0 + T], [], ["dxp"])
                self.act(z[:, 0:T], z[:, 0:T], AF.Silu, ["dxp"], ["dxp"])
                for c0 in range(0, T, 512):
                    wd = min(512, T - c0)
                    self.act(sq[:, 0:wd], oT[h][:, c0:c0 + wd], AF.Square, [("dv", h)], ["dsq"])
                    ps, pk = self.psum()
                    self.mm(ps[:, 0:wd], self.ones[:], sq[:, 0:wd], True, True, ["dsq"], [pk])
                    self.act(rs[:, 0:wd], ps[:, 0:wd], AF.Sqrt, [pk], ["drs"], scale=1.0 / 128, bias=self.epsc[:, 0:1])
                    self.recip(rs[:, 0:wd], rs[:, 0:wd], ["drs"], ["drs"])
                    self.stt(sq[:, 0:wd], oT[h][:, c0:c0 + wd], ngc_[:, 0:1], rs[:, 0:wd], ALU.mult, ALU.mult,
                             [("dv", h), "dng", "drs", "dsq"], ["dsq"])
                    self.tt("dve", ob[:, 0:wd], sq[:, 0:wd], z[:, c0:c0 + wd], ALU.mult, ["dsq", "dxp"], ["dob"])
                    self.dma(self.d["mixT"][h * 128:(h + 1) * 128, t0 + c0:t0 + c0 + wd], ob[:, 0:wd], ["dob"], ["mixT"])
        self.psr = [0, 1, 2, 3]
        self.P.barrier()

    def stage_bias(self):
        self.off = self.lbase
        I = self.i
        rb = self.sb("rb", [32, 8], F32)
        oh = self.sb("oh", [32, VL], F32)
        cv = self.sb("cv", [1, VL], F32)
        bv = self.sb("bv", [8, VL], F32)
        self.dma(rb[:], I["rel_bias"], [], ["rb"])
        self.dma(oh[:], I["c_oh"], [], ["oh"])
        self.dma(cv[:], I["c_cv"], [], ["cv"])
        for c0 in range(0, VL, 512):
            w = min(512, VL - c0)
            ps, pk = self.psum()
            self.mm(ps[0:8, 0:w], rb[:, :], oh[:, c0:c0 + w], True, False, ["rb", "oh"], [pk])
            self.mm(ps[0:8, 0:w], self.ones[0:1, 0:8], cv[0:1, c0:c0 + w], False, True, ["cv"], [pk])
            self.cp("dve", bv[:, c0:c0 + w], ps[0:8, 0:w], [pk], ["bv"])
        self.dma(self.d["bvec"], bv[:], ["bv"], ["bvec"])
        self.P.barrier()
        U = [self.sb("bU", [128, SW], F32) for _ in range(2)]
        Tt = [self.sb("bT", [128, SW], F32) for _ in range(2)]
        for h in range(8):
            u, uk = U[h % 2], ("bU", h % 2)
            t, tk = Tt[h % 2], ("bT", h % 2)
            self.dma(u[:], bass.AP(self.d["bvec"].tensor, h * VL, [[1, 128], [1, SW]]), [], [uk])
            for c0 in range(0, SW, 512):
                w = min(512, SW - c0)
                ps, pk = self.psum()
                self.mm(ps[:, 0:w], self.anti[:], u[:, c0:c0 + w], True, True, [uk], [pk])
                self.cp("act" if (c0 // 512) % 2 else "dve", t[:, c0:c0 + w], ps[:, 0:w], [pk], [tk])
            self.dma(self.d["strip"][h], t[:], [tk], ["strip"])
        self.P.barrier()

    def stage_swa(self, l):
        self.off = self.lbase
        I = self.i
        qT = self.sb("aq", [128, 4, TT], BF16)
        kT = self.sb("ak", [128, 4, TT], BF16)
        V = self.sb("av", [128, 16, 512], BF16)
        Vn = self.sb("avn", [8, 512], BF16)
        kcT = self.sb("akc", [128, 4, 2048], BF16)
        Vc = self.sb("avc", [128, 16, 512], BF16)
        strip = [self.sb("ast", [128, SW], F32) for _ in range(2)]
        Lw = [self.sb("aL", [128, 512], F32) for _ in range(2)]
        PT = [self.sb("aP", [128, 512], BF16) for _ in range(2)]
        rinv = self.sb("ari", [128, 512], F32)
        ob = [self.sb("aob", [128, 512], BF16) for _ in range(2)]
        cst = self.sb("acs", [128, 4, 512], F32)
        sw = self.d["swaT"]
        self.dma(qT[:], sw[0:512, :].rearrange("(f p) t -> p f t", p=128), [], ["aq"], eng="pool")
        self.dma(kT[:], sw[512:1024, :].rearrange("(f p) t -> p f t", p=128), [], ["ak"], eng="pool")
        self.dma(V[:], self.o["p_win_v"][l].rearrange("(j p) c -> p j c", p=128), [], ["av"], eng="pool")
        self.dma(Vn[:], self.o["s_win_v"][l], [], ["avn"], eng="pool")
        self.dma(Vc[:], I["cache_win_v"][l].rearrange("(j p) c -> p j c", p=128), [], ["avc"], eng="pool")
        for j4 in range(4):
            self.dma(cst[:], I["cache_win_k"][l][j4 * 512:(j4 + 1) * 512, :].rearrange("(j p) c -> p j c", p=128), [], ["acs"])
            for jj in range(4):
                j = j4 * 4 + jj
                ps, pk = self.psum()
                for i in range(4):
                    self.tr(ps[:, i * 128:(i + 1) * 128], cst[:, jj, i * 128:(i + 1) * 128], ["acs"], [pk])
                self.cp("act" if jj % 2 else "dve", kcT[:, :, j * 128:(j + 1) * 128],
                        ps[:].rearrange("p (i t) -> p i t", i=4), [pk], ["akc"])
        un = 0
        li = 0
        for h in range(8):
            i, hb = h // 2, (h % 2) * 64
            st, sk = strip[h % 2], ("ast", h % 2)
            self.dma(st[:], self.d["strip"][h], [], [sk])
            units = [(0, c * 512, 512, [(kT, V, j, 128, c * 512 - 128 * j) for j in range(4 * c + 4)]) for c in range(4)]
            units.append((1, TP, TS, [(kcT, Vc, j, 128, 2048 - 128 * j) for j in range(16)] + [(kT, Vn, None, 8, 0)]))
            for (gi, q0, N, keys) in units:
                ops_, okk = self.ps[4 + 2 * (un % 2)], ("ps", 4 + 2 * (un % 2))
                sps, skk = self.ps[5 + 2 * (un % 2)], ("ps", 5 + 2 * (un % 2))
                un += 1
                for ki, (KT, VV, j, kn, dl) in enumerate(keys):
                    first, last = ki == 0, ki == len(keys) - 1
                    if j is None:
                        lk = kT[hb:hb + 64, i, TP:TP + TS]
                        vv = Vn[0:kn, i * 128:(i + 1) * 128]
                        kkey, vkey = "ak", "avn"
                    else:
                        lk = KT[hb:hb + 64, i, j * 128:(j + 1) * 128]
                        vv = VV[:, j, i * 128:(i + 1) * 128]
                        kkey, vkey = ("ak", "av") if gi == 0 else ("akc", "avc")
                    b = li % 2
                    li += 1
                    ps, pk = self.psum()
                    self.mm(ps[0:kn, 0:N], lk, qT[hb:hb + 64, i, q0:q0 + N], True, True, [kkey, "aq"], [pk])
                    y0 = dl + 384
                    self.stt(Lw[b][0:kn, 0:N], ps[0:kn, 0:N], 0.125, st[0:kn, y0:y0 + N], ALU.mult, ALU.add, [pk, sk], [("aL", b)])
                    self.act(PT[b][0:kn, 0:N], Lw[b][0:kn, 0:N], AF.Exp, [("aL", b)], [("aP", b)])
                    self.mm(ops_[:, 0:N], vv, PT[b][0:kn, 0:N], first, last, [vkey, ("aP", b)], [okk])
                    self.mm(sps[:, 0:N], self.onesb[0:kn, :], PT[b][0:kn, 0:N], first, last, [("aP", b)], [skk])
                self.recip(rinv[hb:hb + 64, 0:N], sps[hb:hb + 64, 0:N], [skk], ["ari"])
                o = ob[un % 2]
                self.tt("dve", o[hb:hb + 64, 0:N], ops_[hb:hb + 64, 0:N], rinv[hb:hb + 64, 0:N], ALU.mult, [okk, "ari"], [("aob", un % 2)])
                self.dma(self.d["mixT"][1024 + h * 64:1024 + (h + 1) * 64, q0:q0 + N], o[hb:hb + 64, 0:N], [("aob", un % 2)], ["mixT"])
        self.P.barrier()

    def resid(self):
        def f(c, mw, t0, tn, ps, pk):
            i = self.evi % 4
            self.evi += 1
            s, sk = self.evs[i], ("evs", i)
            rk = ("xTr", c, t0)
            self.dma(s[0:mw, 0:tn], self.d["xT"][c:c + mw, t0:t0 + tn], [rk], [sk])
            self.tt("dve", s[0:mw, 0:tn], s[0:mw, 0:tn], ps[0:mw, 0:tn], ALU.add, [sk, pk], [sk])
            self.dma(self.d["xT"][c:c + mw, t0:t0 + tn], s[0:mw, 0:tn], [sk], [rk])
        return f

    def stage_wout(self, l):
        self.off = self.lbase
        mx = self.sb("mixS", [128, 16, TT], BF16)
        self.evs = [self.sb("evs", [128, 512], F32) for _ in range(4)]
        self.evi = 0
        self.dma(mx[:], self.d["mixT"].rearrange("(f p) t -> p f t", p=128), [], ["mixS"])
        self.dense(self.i["w_out"][l], 16, [(g, 256) for g in range(0, D, 256)], mx, "mixS",
                   [(t0, tn, t0) for (t0, tn) in CH], self.resid())
        self.P.barrier()

    def stage_cross(self, l):
        self.off = self.lbase
        I = self.i
        xn = self.sb("xnT", [128, 16, TT], BF16)
        self.evs = [self.sb("evs", [128, 512], F32) for _ in range(4)]
        self.evi = 0
        qT = self.sb("cq", [128, 4, TT], BF16)
        at = self.sb("cat", [128, 4, TT], BF16)
        kT = [self.sb("ckT%d" % g, [128, 4, 256], BF16) for g in range(2)]
        Vm = [self.sb("cV%d" % g, [128, 2, 512], BF16) for g in range(2)]
        kst = self.sb("ckst", [128, 2, 512], F32)
        PT = [self.sb("cP", [128, 512], BF16) for _ in range(2)]
        rinv = self.sb("cri", [128, 512], F32)
        ksrc = [self.o["p_mem_k"][l], I["cache_mem_k"][l]]
        vsrc = [self.o["p_mem_v"][l], I["cache_mem_v"][l]]
        for g in range(2):
            self.dma(Vm[g][:], vsrc[g].rearrange("(m p) c -> p m c", p=128), [], [("cV", g)], eng="pool")
            self.dma(kst[:], ksrc[g].rearrange("(m p) c -> p m c", p=128), [], ["ckst"])
            for m in range(2):
                ps, pk = self.psum()
                for h in range(4):
                    self.tr(ps[:, h * 128:(h + 1) * 128], kst[:, m, h * 128:(h + 1) * 128], ["ckst"], [pk])
                self.cp("dve", kT[g][:, :, m * 128:(m + 1) * 128], ps[:].rearrange("p (h t) -> p h t", h=4), [pk], [("ckT", g)])
        self.norm_T(self.d["xT"], self.gcols[:, 1, :], CH, xn, "xnT")

        def qcons(c, mw, t0, tn, ps, pk):
            self.act(qT[:, c // 128, t0:t0 + tn], ps[:, 0:tn], AF.Copy, [pk], [("cq", t0)], scale=128 ** -0.5)
        self.dense(I["w_mem_q"][l], 16, [(0, 256), (256, 256)], xn, "xnT", [(t0, tn, t0) for (t0, tn) in CH], qcons)
        un = 0
        li = 0
        for (t0, tn) in CH:
            g = 0 if t0 < TP else 1
            for h in range(4):
                ops_, okk = self.ps[4 + 2 * (un % 2)], ("ps", 4 + 2 * (un % 2))
                sps, skk = self.ps[5 + 2 * (un % 2)], ("ps", 5 + 2 * (un % 2))
                un += 1
                for m in range(2):
                    b = li % 2
                    li += 1
                    ps, pk = self.psum()
                    self.mm(ps[:, 0:tn], kT[g][:, h, m * 128:(m + 1) * 128], qT[:, h, t0:t0 + tn], True, True, [("ckT", g), ("cq", t0)], [pk])
                    self.act(PT[b][:, 0:tn], ps[:, 0:tn], AF.Exp, [pk], [("cP", b)])
                    self.mm(ops_[:, 0:tn], Vm[g][:, m, h * 128:(h + 1) * 128], PT[b][:, 0:tn], m == 0, m == 1, [("cV", g), ("cP", b)], [okk])
                    self.mm(sps[:, 0:tn], self.onesb[:], PT[b][:, 0:tn], m == 0, m == 1, [("cP", b)], [skk])
                self.recip(rinv[:, 0:tn], sps[:, 0:tn], [skk], ["cri"])
                self.tt("dve", at[:, h, t0:t0 + tn], ops_[:, 0:tn], rinv[:, 0:tn], ALU.mult, [okk, "cri"], [("cat", t0)])
        self.dense(I["w_mem_o"][l], 4, [(g_, 256) for g_ in range(0, D, 256)], at, "cat", [(t0, tn, t0) for (t0, tn) in CH],
                   self.resid(), wv=2)
        self.P.barrier()

    def stage_ffn(self, l):
        self.off = self.lbase
        I = self.i
        NJ = DFF // 128
        cwc = self.sb("fcw", [128, 3, 88], F32)
        stc = self.sb("fst", [128, 2, 88], F32)
        for r in range(3):
            self.colload(cwc[:, r, :], I["ffn_conv_w"][l][r].rearrange("(f p) -> f p", p=128), 88, "fcw")
        for r in range(2):
            self.colload(stc[:, r, :], I["state_ffn_conv"][l][r].rearrange("(f p) -> f p", p=128), 88, "fst")
        tails = self.sb("ftl", [128, 88, 2], F32)
        otl = [self.sb("fot%d" % g, [128, 2, 88], F32) for g in range(2)]
        self.evs = [self.sb("evs", [128, 512], F32) for _ in range(4)]
        self.evi = 0
        GW = 1032
        xn = self.sb("fxn", [128, 16, GW], BF16)
        a_off = self.off
        actT = self.sb("fact", [128, NJ, GW], BF16)
        hub = [self.sb("fhub", [128, 2 + 512], F32) for _ in range(4)]
        cvt = [self.sb("fcv", [128, 512], F32) for _ in range(2)]
        W = I["w_up"][l]
        hi = 0
        for (tb, chunks) in ((0, [(0, 512), (512, 512)]), (1024, [(1024, 512), (1536, 512), (2048, 8)])):
            e_off = self.off
            self.off = a_off
            self.P.barrier()
            self.norm_T(self.d["xT"], self.gcols[:, 2, :], chunks, xn, ("fxn", tb), tbase=tb)
            self.P.barrier()
            self.off = e_off
            for jp in range(NJ // 2):
                wu = self.next_w()
                self.dma(wu[0][:, 0:16, 0:256], W[:, jp * 256:(jp + 1) * 256].rearrange("(k p) c -> p k c", p=128), [], [wu[3]], eng="pool")
                wg = self.next_w()
                self.dma(wg[0][:, 0:16, 0:256], W[:, DFF + jp * 256:DFF + (jp + 1) * 256].rearrange("(k p) c -> p k c", p=128), [], [wg[3]], eng="pool")
                for jj in range(2):
                    j = 2 * jp + jj
                    for (t0, tn) in chunks:
                        a0 = t0 - tb
                        res = []
                        for half, wt_ in enumerate((wu, wg)):
                            ft = half * NJ + j
                            ps, pk = self.psum()
                            for k in range(16):
                                self.mm(ps[:, 0:tn], wt_[0][:, k, jj * 128:(jj + 1) * 128], xn[:, k, a0:a0 + tn], k == 0, k == 15,
                                        [wt_[3], (("fxn", tb), t0)], [pk])
                            hb_, hk = hub[hi % 4], ("fhub", hi % 4)
                            hi += 1
                            if t0 == 0:
                                self.memset("pool", hb_[:, 0:2], 0.0, [hk])
                            elif t0 == TP:
                                self.cp("pool", hb_[:, 0:2], stc[:, :, ft], ["fst", hk], [hk])
                            else:
                                self.cp("pool", hb_[:, 0:2], tails[:, ft, :], [("ftl", ft), hk], [hk])
                            self.cp("act", hb_[:, 2:2 + tn], ps[:, 0:tn], [pk, hk], [hk])
                            if t0 + tn == TP or t0 == TP:
                                self.cp("pool", otl[0 if t0 < TP else 1][:, :, ft], hb_[:, tn:tn + 2], [hk], [("fot", ft)])
                            if t0 < TP:
                                self.cp("pool", tails[:, ft, :], hb_[:, tn:tn + 2], [hk], [("ftl", ft)])
                            cv, ck = cvt[half], ("fcv", half)
                            self.ts("dve", cv[:, 0:tn], hb_[:, 0:tn], cwc[:, 0, ft:ft + 1], ALU.mult, [hk, "fcw"], [ck])
                            self.stt(cv[:, 0:tn], hb_[:, 1:1 + tn], cwc[:, 1, ft:ft + 1], cv[:, 0:tn], ALU.mult, ALU.add, [hk, "fcw", ck], [ck])
                            self.stt(cv[:, 0:tn], hb_[:, 2:2 + tn], cwc[:, 2, ft:ft + 1], cv[:, 0:tn], ALU.mult, ALU.add, [hk, "fcw", ck], [ck])
                        self.act(cvt[1][:, 0:tn], cvt[1][:, 0:tn], AF.Silu, [("fcv", 1)], [("fcv", 1)])
                        self.tt("dve", actT[:, j, a0:a0 + tn], cvt[1][:, 0:tn], cvt[0][:, 0:tn], ALU.mult, [("fcv", 0), ("fcv", 1)], [(("fact", tb), t0)])
            self.dense(I["w_down"][l], NJ, [(g_, 128) for g_ in range(0, D, 128)], actT, ("fact", tb),
                       [(t0, tn, t0 - tb) for (t0, tn) in chunks], self.resid(), wv=1)
        for g, pre in enumerate(("p_", "s_")):
            for r in range(2):
                ps, pk = self.psum()
                self.tr(ps[0:88, 0:128], otl[g][:, r, :], [("fot", ft) for ft in range(88)], [pk])
                i = self.evi % 4
                self.evi += 1
                sv, sk = self.evs[i], ("evs", i)
                self.cp("dve", sv[0:88, 0:128], ps[0:88, 0:128], [pk], [sk])
                self.dma(self.o[pre + "ffn_conv"][l][r].rearrange("(f p) -> f p", p=128), sv[0:88, 0:128], [sk], [self.o[pre + "ffn_conv"].name])
        self.P.barrier()

    def stage_final(self):
        self.off = self.lbase
        I = self.i
        gr = self.sb("zgr", [1, D], F32)
        gb = self.sb("zgb", [128, D], F32)
        self.dma(gr[:], I["g_final"], [], ["zgr"])
        for c in range(4):
            ps, pk = self.psum()
            self.mm(ps[:, :], self.ones[0:1, :], gr[0:1, c * 512:(c + 1) * 512], True, True, ["zgr"], [pk])
            self.cp("dve", gb[:, c * 512:(c + 1) * 512], ps[:, :], [pk], ["zgb"])
        xi = [self.sb("zxi", [128, 16, 128], F32) for _ in range(2)]
        xt = [self.sb("zxt", [128, D], F32) for _ in range(2)]
        sqt = self.sb("zsq", [128, D], F32)
        ss = self.sb("zss", [128, 2], F32)
        tiles = [(t * 128, 128, self.o["y_prompt"], t * 128) for t in range(16)] + [(TP, TS, self.o["y_sample"], 0)]
        for n, (t0, nt, dst, r0) in enumerate(tiles):
            a, ak = xi[n % 2], ("zxi", n % 2)
            b, bk = xt[n % 2], ("zxt", n % 2)
            self.dma(a[:, :, 0:nt], self.d["xT"].rearrange("(f p) t -> p f t", p=128)[:, :, t0:t0 + nt], [], [ak])
            for f4 in range(4):
                ps, pk = self.psum()
                for j in range(4):
                    self.tr(ps[0:nt, j * 128:(j + 1) * 128], a[:, f4 * 4 + j, 0:nt], [ak], [pk])
                self.cp("act" if f4 % 2 else "dve", b[0:nt, f4 * 512:(f4 + 1) * 512], ps[0:nt, :], [pk], [bk])
            self.act(sqt[0:nt, :], b[0:nt, :], AF.Square, [bk], ["zsq"])
            self.P.op("dve", (lambda o_, i_: (lambda e: e.reduce_sum(out=o_, in_=i_, axis=AX.X)))(ss[0:nt, 0:1], sqt[0:nt, :]), ["zsq"], ["zss"])
            self.act(ss[0:nt, 1:2], ss[0:nt, 0:1], AF.Sqrt, ["zss"], ["zss"], scale=1.0 / D, bias=self.epsc[0:nt, 0:1])
            self.recip(ss[0:nt, 1:2], ss[0:nt, 1:2], ["zss"], ["zss"])
            self.stt(b[0:nt, :], b[0:nt, :], ss[0:nt, 1:2], gb[0:nt, :], ALU.mult, ALU.mult, [bk, "zss", "zgb"], [bk])
            self.dma(dst[r0:r0 + nt, :], b[0:nt, :], [bk], [dst.name])
        self.P.barrier()

    def build(self):
        with contextlib.ExitStack() as st:
            self.stack = st
            self.setup()
            self.gcols = self.sb("gcols", [128, 4, 16], F32)
            self.epsc = self.sb("epsc", [128, 1], F32)
            self.memset("dve", self.epsc[:], EPS, ["epsc"])
            self.onec = self.sb("onec", [128, 1], F32)
            self.memset("dve", self.onec[:], 1.0, ["onec"])
            self.cl_st = [self.sb("clst", [128, 128], F32) for _ in range(2)]
            self.cl_i = 0
            self.lbase = self.off
            self.stage_input()
            self.stage_bias()
            for l in range(self.NL):
                self.load_gcols(l)
                self.P.barrier()
                self.stage_memkv(l)
                self.stage_in(l)
                self.stage_kvout(l)
                self.stage_lru(l)
                self.stage_ssm(l)
                self.stage_dn(l)
                self.stage_swa(l)
                if self.dbg:
                    self.dma(self.dbg["dbg_mix"], self.d["mixT"], [], [])
                    self.P.barrier()
                self.conv_tail_out(self.d["dnqkvT"], 1536, 3, self.o["p_delta_conv"], self.o["s_delta_conv"], l)
                self.conv_tail_out(self.d["lruxT"], 512, 3, self.o["p_lru_conv"], self.o["s_lru_conv"], l)
                self.stage_wout(l)
                self.stage_cross(l)
                self.stage_ffn(l)
            self.stage_final()
            self.P.emit()
        return self.nc


_CACHE = {}


def run(inputs, NL, stages="all", ncores=8):
    key = (NL, stages)
    if key not in _CACHE:
        _CACHE[key] = K(NL, stages).build()
    nc = _CACHE[key]
    consts = host_consts()
    f = lambda a: np.ascontiguousarray(np.asarray(a, dtype=np.float32))
    in_maps = []
    for c in range(ncores):
        b = c % 4
        m = dict(consts)
        for n, s in W_SHAPES.items():
            a = f(inputs[n])
            if n == "g_final":
                a = a.reshape(1, D)
            elif s[0] == "L":
                a = a[:NL]
            m[n] = np.ascontiguousarray(a)
        m["x_prompt"] = f(inputs["x_prompt"][b])
        m["x_sample"] = f(inputs["x_sample"][c])
        m["mem_prompt"] = f(inputs["mem_prompt"][b])
        for n, s in IN_SHAPES.items():
            if s[0] == "L":
                a = np.asarray(inputs[n])[:NL, c]
                m[n] = np.ascontiguousarray(a.reshape(_shape(s, NL)).astype(np.float32))
        in_maps.append(m)
    res = run_bass_kernel_spmd(nc, in_maps, core_ids=list(range(ncores)))
    return res.results


def assemble(results, NL):
    outs = []
    full = {"y_prompt": (4, TP, D), "y_sample": (8, TS, D)}
    for n, s in OUT_SHAPES:
        if n.startswith("y_p"):
            outs.append(np.stack([results[c][n] for c in range(4)]).reshape(4, TP, D))
        elif n.startswith("y_s"):
            outs.append(np.stack([results[c][n] for c in range(8)]).reshape(8, TS, D))
        else:
            nb = 4 if n.startswith("p_") else 8
            a = np.stack([results[c][n] for c in range(nb)], axis=1)
            outs.append(a)
    shp = {"mem_k": (256, 4, 128), "mem_v": (256, 4, 128), "win_k": (-1, 8, 64), "win_v": (-1, 8, 64)}
    res = []
    for (n, s), a in zip(OUT_SHAPES, outs):
        base = n[2:]
        if base in shp and not n.startswith("y_"):
            a = a.reshape(a.shape[0], a.shape[1], *([a.shape[2]] if shp[base][0] == -1 else [shp[base][0]]), *shp[base][1:])
        res.append(np.ascontiguousarray(a.astype(np.float32)))
    return tuple(res)


def kernel(**inputs):
    results = run(inputs, 4)
    return assemble(results, 4)
```

```python
import contextlib
import math
import numpy as np
import concourse.bass as bass
import concourse.mybir as mybir
from concourse.bass_utils import run_bass_kernel_spmd

F32 = mybir.dt.float32
BF16 = mybir.dt.bfloat16
AF = mybir.ActivationFunctionType
ALU = mybir.AluOpType
AX = mybir.AxisListType

ENGS = ("pe", "act", "dve", "pool", "sp")
NDMASEM = 16
NEG = -30000.0
D = 2048
TP = 2048
TS = 8
TT = TP + TS
DFF = 5632
EPS = 1e-6
CH = [(0, 512), (512, 512), (1024, 512), (1536, 512), (2048, 8)]
GRP = [(0, TP), (TP, TS)]
VL = 2600
SW = 2440


class Prog:
    def __init__(self, nc):
        self.nc = nc
        self.ops = {e: [] for e in ENGS}
        self.last_w = {}
        self.readers = {}
        self.pend = {e: set() for e in ENGS}
        self.dmas = []

    def op(self, eng, fn, reads=(), writes=(), dma=False):
        idx = len(self.ops[eng])
        deps = set(self.pend[eng])
        self.pend[eng] = set()
        for k in reads:
            w = self.last_w.get(k)
            if w is not None:
                deps.add(w)
        for k in writes:
            w = self.last_w.get(k)
            if w is not None:
                deps.add(w)
            for e_, r in self.readers.get(k, {}).items():
                if e_ == "_dma":
                    deps.update(r)
                else:
                    deps.add(r)
        me = (eng, idx)
        deps.discard(me)
        if eng == "pe":
            deps = {d for d in deps if d[0] != "pe"}
        self.ops[eng].append(dict(fn=fn, deps=deps, dma=dma, sig=False))
        for k in reads:
            rd = self.readers.setdefault(k, {})
            if dma:
                rd.setdefault("_dma", []).append(me)
            else:
                rd[eng] = me
        for k in writes:
            self.last_w[k] = me
            self.readers[k] = {}
        if dma:
            self.dmas.append(me)
        return me

    def dma(self, eng, out, in_, reads=(), writes=(), **kw):
        return self.op(eng, lambda e: e.dma_start(out=out, in_=in_, **kw),
                       reads=reads, writes=writes, dma=True)

    def barrier(self):
        lasts = set(self.dmas)
        for e in ENGS:
            n = len(self.ops[e])
            if n:
                lasts.add((e, n - 1))
        for e in ENGS:
            self.pend[e] |= lasts
        self.dmas = []
        self.last_w = {}
        self.readers = {}

    def emit(self):
        nc = self.nc
        ops = self.ops
        for e in ENGS:
            for o in ops[e]:
                if o["dma"]:
                    o["sig"] = True
                for (pe_, pi) in o["deps"]:
                    ops[pe_][pi]["sig"] = True
        with contextlib.ExitStack() as st:
            EPOCH = 30000
            csem = {e: [st.enter_context(nc.semaphore("c_%s%d" % (e, i))) for i in range(4)] for e in ENGS}
            dsem = {e: [st.enter_context(nc.semaphore("d_%s%d" % (e, i))) for i in range(NDMASEM)]
                    for e in ("sp", "act", "pool")}
            finals = {}
            for e in ENGS:
                cnt = 0
                dcnt = [0] * NDMASEM
                rr = 0
                for o in ops[e]:
                    if not o["sig"]:
                        continue
                    if o["dma"]:
                        s = rr % NDMASEM
                        rr += 1
                        o["prev"] = (dsem[e][s], dcnt[s])
                        dcnt[s] += 16
                        o["sem"] = (dsem[e][s], dcnt[s], 16)
                    else:
                        o["sem"] = (csem[e][cnt // EPOCH], cnt % EPOCH + 1, 1)
                        cnt += 1
                finals[e] = [(dsem[e][s], dcnt[s]) for s in range(NDMASEM) if dcnt[s] > 0] if e in dsem else []
            block = st.enter_context(nc.Block())

            def run(e, eng):
                waited = {}

                def wait(sem, val):
                    if val <= 0:
                        return
                    key = id(sem)
                    if waited.get(key, 0) >= val:
                        return
                    waited[key] = val
                    eng.wait_ge(sem, val)

                for o in ops[e]:
                    mx = {}
                    for (pe_, pi) in o["deps"]:
                        s = ops[pe_][pi]["sem"]
                        if mx.get(id(s[0]), (None, 0))[1] < s[1]:
                            mx[id(s[0])] = (s[0], s[1])
                    for (s0, v) in mx.values():
                        wait(s0, v)
                    if o["dma"]:
                        wait(*o["prev"])
                    ins = o["fn"](eng)
                    if o["sig"]:
                        ins.then_inc(o["sem"][0], o["sem"][2])
                for (s, v) in finals[e]:
                    wait(s, v)

            @block.tensor
            def _(eng):
                run("pe", eng)

            @block.scalar
            def _(eng):
                run("act", eng)

            @block.vector
            def _(eng):
                run("dve", eng)

            @block.gpsimd
            def _(eng):
                run("pool", eng)

            @block.sync
            def _(eng):
                run("sp", eng)


def _t5_bucket(d):
    if d < 16:
        return d
    v = np.log(np.float32(d) / np.float32(16)) / np.float32(math.log(2048 / 16)) * np.float32(16)
    return min(31, 16 + int(np.float32(v).astype(np.int32)))


def host_consts():
    c = {}
    oh = np.zeros((32, VL), np.float32)
    cv = np.full((1, VL), NEG, np.float32)
    for y in range(VL):
        d = y - 511
        if d < 0 or d > 2048:
            continue
        m = sum(1 for (win, dil) in ((128, 1), (512, 4), (2048, 16)) if d % dil == 0 and d <= win)
        if m == 0:
            continue
        oh[_t5_bucket(d), y] = 1.0
        cv[0, y] = math.log(m)
    c["c_oh"] = oh
    c["c_cv"] = cv
    c["c_ident"] = np.eye(128, dtype=np.float32)
    c["c_anti"] = np.eye(128, dtype=np.float32)[::-1].copy()
    c["c_ones"] = np.ones((128, 128), np.float32)
    i = np.arange(128)
    c["c_neglow"] = np.where(i[None, :] <= i[:, None], 0.0, NEG).astype(np.float32)
    c["c_negup"] = np.where(i[:, None] <= i[None, :], 0.0, NEG).astype(np.float32)
    c["c_strict"] = (i[None, :] < i[:, None]).astype(np.float32)
    sel = np.zeros((4, 4, 128), np.float32)
    for h in range(4):
        sel[h, h, :] = 1.0
    c["c_sel"] = sel.reshape(4, 512)
    return c


CONST_SHAPES = {"c_oh": (32, VL), "c_cv": (1, VL), "c_ident": (128, 128), "c_anti": (128, 128),
                "c_ones": (128, 128), "c_neglow": (128, 128), "c_negup": (128, 128),
                "c_strict": (128, 128), "c_sel": (4, 512)}

W_SHAPES = {
    "rel_bias": (32, 8), "g_mix": ("L", D), "w_in": ("L", D, 5128), "dn_conv_w": ("L", 4, 1536),
    "dn_a_log": ("L", 4), "dn_dt_bias": ("L", 4), "dn_norm_g": ("L", 128),
    "ssm_a_re": ("L", 32, 64), "ssm_a_im": ("L", 32, 64), "ssm_log_dt": ("L", 32),
    "ssm_b_re": ("L", 32, 64, 16), "ssm_b_im": ("L", 32, 64, 16), "ssm_c_re": ("L", 32, 16, 64),
    "ssm_c_im": ("L", 32, 16, 64), "ssm_d": ("L", 512), "ssm_w_glu": ("L", 512, 512), "ssm_b_glu": ("L", 512),
    "lru_conv_w": ("L", 4, 512), "lru_conv_b": ("L", 512), "lru_w_a": ("L", 8, 64, 64), "lru_b_a": ("L", 512),
    "lru_w_x": ("L", 8, 64, 64), "lru_b_x": ("L", 512), "lru_lam": ("L", 512), "w_out": ("L", D, D),
    "g_cross": ("L", D), "g_mem": ("L", D), "w_mem_q": ("L", D, 512), "w_mem_k": ("L", D, 512),
    "w_mem_v": ("L", D, 512), "w_mem_o": ("L", 512, D), "g_ffn": ("L", D), "w_up": ("L", D, 2 * DFF),
    "ffn_conv_w": ("L", 3, 2 * DFF), "w_down": ("L", DFF, D), "g_final": (1, D),
}
IN_SHAPES = {
    "x_prompt": (TP, D), "x_sample": (TS, D), "mem_prompt": (256, D),
    "cache_mem_k": ("L", 256, 512), "cache_mem_v": ("L", 256, 512),
    "cache_win_k": ("L", 2048, 512), "cache_win_v": ("L", 2048, 512),
    "state_delta": ("L", 4, 128, 128), "state_delta_conv": ("L", 3, 1536),
    "state_ssm_re": ("L", 32, 64), "state_ssm_im": ("L", 32, 64), "state_lru": ("L", 512),
    "state_lru_conv": ("L", 3, 512), "state_ffn_conv": ("L", 2, 2 * DFF),
}
OUT_SHAPES = [
    ("y_prompt", (TP, D)), ("y_sample", (TS, D)),
    ("p_mem_k", ("L", 256, 512)), ("p_mem_v", ("L", 256, 512)),
    ("p_win_k", ("L", 2048, 512)), ("p_win_v", ("L", 2048, 512)),
    ("p_delta", ("L", 4, 128, 128)), ("p_delta_conv", ("L", 3, 1536)),
    ("p_ssm_re", ("L", 32, 64)), ("p_ssm_im", ("L", 32, 64)), ("p_lru", ("L", 512)),
    ("p_lru_conv", ("L", 3, 512)), ("p_ffn_conv", ("L", 2, 2 * DFF)),
    ("s_win_k", ("L", 8, 512)), ("s_win_v", ("L", 8, 512)),
    ("s_delta", ("L", 4, 128, 128)), ("s_delta_conv", ("L", 3, 1536)),
    ("s_ssm_re", ("L", 32, 64)), ("s_ssm_im", ("L", 32, 64)), ("s_lru", ("L", 512)),
    ("s_lru_conv", ("L", 3, 512)), ("s_ffn_conv", ("L", 2, 2 * DFF)),
]


def _shape(s, L):
    return [L if v == "L" else v for v in s]


class K:
    def __init__(self, NL, stages="all"):
        self.NL = NL
        self.stages = stages
        nc = self.nc = bass.Bass("TRN2", target_bir_lowering=False)
        self.P = Prog(nc)
        self.i = {}
        for n, s in list(W_SHAPES.items()) + list(IN_SHAPES.items()) + list(CONST_SHAPES.items()):
            self.i[n] = nc.dram_tensor(n, _shape(s, NL), F32, kind="ExternalInput").ap()
        self.o = {}
        for n, s in OUT_SHAPES:
            self.o[n] = nc.dram_tensor(n, _shape(s, NL), F32, kind="ExternalOutput").ap()
        self.dbg = {}
        if stages == "dbg":
            for n, shp in [("dbg_xT", [D, TT]), ("dbg_swaT", [1536, TT]), ("dbg_rs", [128, 512]), ("dbg_xn", [128, 512]), ("dbg_g", [128, 64])]:
                self.dbg[n] = nc.dram_tensor(n, shp, F32, kind="ExternalOutput").ap()
            self.dbg["dbg_mix"] = nc.dram_tensor("dbg_mix", [D, TT], BF16, kind="ExternalOutput").ap()
        self.d = {}
        for n, s, dt in [("xT", [D, TT], F32), ("dnqkvT", [1536, TT], F32), ("dnzT", [512, TT], F32),
                         ("dnbaT", [8, TT], F32), ("ssmuT", [512, TT], F32), ("swaT", [1536, TT], F32),
                         ("lruxT", [512, TT], F32), ("lrugT", [512, TT], F32), ("mixT", [D, TT], BF16),
                         ("memT", [D, 256], F32), ("bvec", [8, VL], F32), ("strip", [8, 128, SW], F32)]:
            self.d[n] = nc.dram_tensor("scr_" + n, s, dt, kind="Internal").ap()
        self.off = 16384
        self.seq = 0

    def sb(self, name, shape, dt):
        nb = int(np.prod(shape[1:])) * (4 if dt == F32 else 2)
        nb = (nb + 63) // 64 * 64
        self.seq += 1
        t = self.nc.alloc_sbuf_tensor_at("%s_%d" % (name, self.seq), shape, dt, offset=self.off)
        self.off += nb
        assert self.off <= 229376, (name, self.off)
        return t

    def mm(self, out, lhsT, rhs, start, stop, reads, writes):
        self.P.op("pe", lambda e: e.matmul(out, lhsT=lhsT, rhs=rhs, start=start, stop=stop), reads, writes)

    def tr(self, out, in_, reads, writes):
        ident = self.ident
        n = in_.shape[0]
        self.P.op("pe", lambda e: e.transpose(out=out, in_=in_, identity=ident[0:n, 0:n]), reads, writes)

    def act(self, out, in_, func, reads, writes, scale=1.0, bias=0.0, accum_out=None, eng="act"):
        kw = {}
        if accum_out is not None:
            kw["accum_out"] = accum_out
        self.P.op("act", lambda e: e.activation(out=out, in_=in_, func=func, scale=scale, bias=bias, **kw), reads, writes)

    def cp(self, eng, out, in_, reads, writes):
        if eng == "act":
            self.P.op("act", lambda e: e.activation(out=out, in_=in_, func=AF.Copy), reads, writes)
        else:
            self.P.op(eng, lambda e: e.tensor_copy(out=out, in_=in_), reads, writes)

    def tt(self, eng, out, in0, in1, op, reads, writes):
        self.P.op(eng, lambda e: e.tensor_tensor(out=out, in0=in0, in1=in1, op=op), reads, writes)

    def ts(self, eng, out, in0, s1, op0, reads, writes, s2=None, op1=None):
        if op1 is None:
            self.P.op(eng, lambda e: e.tensor_scalar(out=out, in0=in0, scalar1=s1, scalar2=None, op0=op0), reads, writes)
        else:
            self.P.op(eng, lambda e: e.tensor_scalar(out=out, in0=in0, scalar1=s1, scalar2=s2, op0=op0, op1=op1), reads, writes)

    def stt(self, out, in0, scalar, in1, op0, op1, reads, writes):
        self.P.op("dve", lambda e: e.scalar_tensor_tensor(out=out, in0=in0, scalar=scalar, in1=in1, op0=op0, op1=op1), reads, writes)

    def recip(self, out, in_, reads, writes):
        self.P.op("dve", lambda e: e.reciprocal(out=out, in_=in_), reads, writes)

    def scan(self, out, d0, d1, init, reads, writes):
        self.P.op("dve", lambda e: e.tensor_tensor_scan(out=out, data0=d0, data1=d1, initial=init, op0=ALU.mult, op1=ALU.add), reads, writes)

    def memset(self, eng, ap, v, writes):
        self.P.op(eng, lambda e: e.memset(ap, v), (), writes)

    def dma(self, out, in_, reads, writes, eng="sp", **kw):
        self.P.dma(eng, out, in_, reads, writes, **kw)

    def psum(self):
        self.psi = (self.psi + 1) % len(self.psr)
        b = self.psr[self.psi]
        return self.ps[b], ("ps", b)

    def setup(self):
        nc = self.nc
        self.ps = [self.stack.enter_context(nc.psum_tensor("ps%d" % b, [128, 512], F32)) for b in range(8)]
        self.psr = [0, 1, 2, 3]
        self.psi = 0
        self.ident = self.sb("ident", [128, 128], F32)
        self.anti = self.sb("anti", [128, 128], F32)
        self.ones = self.sb("ones", [128, 128], F32)
        self.onesb = self.sb("onesb", [128, 128], BF16)
        self.neglow = self.sb("neglow", [128, 128], F32)
        self.negup = self.sb("negup", [128, 128], F32)
        self.strict = self.sb("strict", [128, 128], F32)
        self.sel = self.sb("sel", [4, 512], F32)
        for t, n in [(self.ident, "c_ident"), (self.anti, "c_anti"), (self.ones, "c_ones"), (self.neglow, "c_neglow"),
                     (self.negup, "c_negup"), (self.strict, "c_strict"), (self.sel, "c_sel")]:
            self.dma(t[:], self.i[n], [], [n])
        self.cp("dve", self.onesb[:], self.ones[:], ["c_ones"], ["onesb"])
        self.wb = []
        for j in range(3):
            o = self.off
            a = nc.alloc_sbuf_tensor_at("wA%d" % j, [128, 16, 256], BF16, offset=o)
            b = nc.alloc_sbuf_tensor_at("wB%d" % j, [128, 44, 128], BF16, offset=o)
            c = nc.alloc_sbuf_tensor_at("wC%d" % j, [128, 4, 256], BF16, offset=o)
            self.wb.append((a, b, c, ("w", j)))
            self.off += 11264
        self.wbi = 0
        self.base = self.off
        self.P.barrier()

    def next_w(self):
        self.wbi = (self.wbi + 1) % 3
        return self.wb[self.wbi]

    def colload(self, dst, src_rows, n, key):
        st = self.cl_st[self.cl_i % 2]
        sk = ("clst", self.cl_i % 2)
        self.cl_i += 1
        self.dma(st[0:n, :], src_rows, [], [sk])
        ps, pk = self.psum()
        self.tr(ps[:, 0:n], st[0:n, :], [sk, "c_ident"], [pk])
        self.cp("dve", dst, ps[:, 0:n], [pk], [key])

    def stage_input(self):
        self.off = self.lbase
        xin = [self.sb("xin", [128, D], F32) for _ in range(2)]
        xo = [self.sb("xo", [128, 16, 128], F32) for _ in range(2)]
        tiles = [(self.i["x_prompt"], t * 128, 128, t * 128, "xT") for t in range(16)]
        tiles.append((self.i["x_sample"], 0, TS, TP, "xT"))
        tiles += [(self.i["mem_prompt"], t * 128, 128, t * 128, "memT") for t in range(2)]
        for n, (src, r0, nr, c0, dst) in enumerate(tiles):
            xi, xk = xin[n % 2], ("xin", n % 2)
            xot, ok = xo[n % 2], ("xo", n % 2)
            self.dma(xi[0:nr, :], src[r0:r0 + nr, :], [], [xk])
            for f4 in range(4):
                ps, pk = self.psum()
                for j in range(4):
                    f = f4 * 4 + j
                    self.tr(ps[:, j * 128:j * 128 + nr], xi[0:nr, f * 128:(f + 1) * 128], [xk, "c_ident"], [pk])
                src_ps = ps[:].rearrange("p (j t) -> p j t", j=4)[:, :, 0:nr]
                self.cp("dve" if f4 % 2 else "act", xot[:, f4 * 4:f4 * 4 + 4, 0:nr], src_ps, [pk], [ok])
            self.dma(self.d[dst].rearrange("(f p) t -> p f t", p=128)[:, :, c0:c0 + nr], xot[:, :, 0:nr], [ok], [self.d[dst].name])
        self.P.barrier()

    def norm_T(self, src, gcol, chunks, dst, dkey, tbase=0):
        NS = 256
        xl = [self.sb("nxl", [128, 16, NS], F32) for _ in range(2)]
        sq = [self.sb("nsq", [128, NS], F32) for _ in range(2)]
        rs = self.sb("nrs", [128, NS], F32)
        ci = 0
        for (T0, TN) in chunks:
            for t0 in range(T0, T0 + TN, NS):
                tn = min(NS, T0 + TN - t0)
                x, xk = xl[ci % 2], ("nxl", ci % 2)
                ci += 1
                self.dma(x[:, :, 0:tn], src.rearrange("(f p) t -> p f t", p=128)[:, :, t0:t0 + tn], [src.name], [xk])
                ps, pk = self.psum()
                for f in range(16):
                    s, sk = sq[f % 2], ("nsq", f % 2)
                    self.act(s[:, 0:tn], x[:, f, 0:tn], AF.Square, [xk], [sk])
                    self.mm(ps[:, 0:tn], self.ones[:], s[:, 0:tn], f == 0, f == 15, [sk, "c_ones"], [pk])
                self.act(rs[:, 0:tn], ps[:, 0:tn], AF.Sqrt, [pk], ["nrs"], scale=1.0 / D, bias=self.epsc[:, 0:1])
                self.recip(rs[:, 0:tn], rs[:, 0:tn], ["nrs"], ["nrs"])
                for f in range(16):
                    self.stt(dst[:, f, t0 - tbase:t0 - tbase + tn], x[:, f, 0:tn], gcol[:, f:f + 1], rs[:, 0:tn],
                             ALU.mult, ALU.mult, [xk, "nrs", "gcols"], [(dkey, T0)])

    def dense(self, Wd, nK, groups, act, akey, chunks, consume, wv=0, pre=None):
        for (c0, w) in groups:
            wt = self.next_w()
            wtile, wk = wt[wv], wt[3]
            self.dma(wtile[:, 0:nK, 0:w], Wd[:, c0:c0 + w].rearrange("(k p) c -> p k c", p=128), [], [wk], eng="pool")
            tiles = [(m0, min(128, w - m0), t0, tn, a0) for m0 in range(0, w, 128) for (t0, tn, a0) in chunks]
            if pre is not None:
                pre(c0 + tiles[0][0], tiles[0][1], tiles[0][2], tiles[0][3])
            for ti, (m0, mw, t0, tn, a0) in enumerate(tiles):
                ps, pk = self.psum()
                for k in range(nK):
                    self.mm(ps[0:mw, 0:tn], wtile[:, k, m0:m0 + mw], act[:, k, a0:a0 + tn], k == 0, k == nK - 1,
                            [wk, (akey, t0), akey], [pk])
                if pre is not None and ti + 1 < len(tiles):
                    n_ = tiles[ti + 1]
                    pre(c0 + n_[0], n_[1], n_[2], n_[3])
                consume(c0 + m0, mw, t0, tn, ps, pk)

    def to_dram(self, dst, r0):
        def f(c, mw, t0, tn, ps, pk):
            i = self.evi % 4
            self.evi += 1
            s, sk = self.evs[i], ("evs", i)
            self.cp("act" if i % 2 else "dve", s[0:mw, 0:tn], ps[0:mw, 0:tn], [pk], [sk])
            self.dma(dst[c - r0:c - r0 + mw, t0:t0 + tn], s[0:mw, 0:tn], [sk], [dst.name])
        return f

    def load_gcols(self, l):
        for j, n in enumerate(["g_mix", "g_cross", "g_ffn", "g_mem"]):
            self.colload(self.gcols[:, j, :], self.i[n][l].rearrange("(f p) -> f p", p=128), 16, "gcols")

    def stage_in(self, l):
        self.off = self.lbase
        xn = self.sb("xnT", [128, 16, TT], BF16)
        self.evs = [self.sb("evs", [128, 512], F32) for _ in range(4)]
        self.evi = 0
        self.norm_T(self.d["xT"], self.gcols[:, 0, :], CH, xn, "xnT")
        if self.dbg:
            self.P.barrier()
            dt = self.sb("dbgt", [128, 512], F32)
            self.cp("dve", dt[:], xn[:, 0, 0:512], [], ["dbgt"])
            self.dma(self.dbg["dbg_xn"], dt[:], ["dbgt"], [])
            self.dma(self.dbg["dbg_rs"], self.last_rs[:], [], [])
            self.dma(self.dbg["dbg_g"], self.gcols[:].rearrange("p a b -> p (a b)"), [], [])
            self.dma(self.dbg["dbg_xT"], self.d["xT"], [], [])
            self.P.barrier()
        W = self.i["w_in"][l]
        ch3 = [(t0, tn, t0) for (t0, tn) in CH]
        segs = [(0, 1536, "dnqkvT"), (1536, 512, "dnzT"), (2048, 8, "dnbaT"), (2056, 512, "ssmuT"),
                (2568, 1536, "swaT"), (4104, 512, "lruxT"), (4616, 512, "lrugT")]
        for (c0, n, dn) in segs:
            groups = [(c0 + g, min(256, n - g)) for g in range(0, n, 256)]
            self.dense(W, 16, groups, xn, "xnT", ch3, self.to_dram(self.d[dn], c0))
        self.P.barrier()
        if self.dbg:
            self.dma(self.dbg["dbg_swaT"], self.d["swaT"], [], [])
            self.P.barrier()

    def tok_major_out(self, srcT, r0, ncol, t0, nt, dst, c0, extra=None):
        assert ncol % 128 == 0
        nf = ncol // 128
        st = [self.sb("tmi", [128, nf, 512], F32) for _ in range(2)]
        so = [self.sb("tmo", [128, ncol], F32) for _ in range(2)]
        n = 0
        for tc in range(0, nt, 512):
            tw = min(512, nt - tc)
            s, sk = st[(tc // 512) % 2], ("tmi", (tc // 512) % 2)
            self.dma(s[:, :, 0:tw], srcT[r0:r0 + ncol, :].rearrange("(f p) t -> p f t", p=128)[:, :, t0 + tc:t0 + tc + tw],
                     [srcT.name], [sk])
            for tb in range(0, tw, 128):
                bw = min(128, tw - tb)
                o, ok = so[n % 2], ("tmo", n % 2)
                n += 1
                for f4 in range(0, nf, 4):
                    ps, pk = self.psum()
                    for j in range(min(4, nf - f4)):
                        self.tr(ps[0:bw, j * 128:(j + 1) * 128], s[:, f4 + j, tb:tb + bw], [sk, "c_ident"], [pk])
                    w = min(4, nf - f4) * 128
                    self.cp("act" if (f4 // 4) % 2 else "dve", o[0:bw, f4 * 128:f4 * 128 + w], ps[0:bw, 0:w], [pk], [ok])
                self.dma(dst[tc + tb:tc + tb + bw, c0:c0 + ncol], o[0:bw, :], [ok], [dst.name])
                if extra is not None:
                    extra(tc + tb, bw, o, ok)

    def stage_kvout(self, l):
        self.off = self.lbase
        for (t0, nt), pre in zip(GRP, ("p", "s")):
            self.tok_major_out(self.d["swaT"], 512, 512, t0, nt, self.o[pre + "_win_k"][l], 0)
            self.off = self.lbase
            self.tok_major_out(self.d["swaT"], 1024, 512, t0, nt, self.o[pre + "_win_v"][l], 0)
            self.off = self.lbase
        self.P.barrier()

    def conv_tail_out(self, srcT, nrow, ntail, dstp, dsts, l):
        for (t0, nt), dst in zip(GRP, (dstp, dsts)):
            self.off = self.lbase
            for c0 in range(0, nrow, 512):
                self.tok_major_out(srcT, c0, min(512, nrow - c0), t0 + nt - ntail, ntail, dst[l], c0)
                self.off = self.lbase
        self.P.barrier()

    def stage_memkv(self, l):
        self.off = self.lbase
        mn = self.sb("mnT", [128, 16, 256], BF16)
        self.norm_T(self.d["memT"], self.gcols[:, 3, :], [(0, 256)], mn, "mnT")
        so = [self.sb("mko", [128, 512], F32) for _ in range(2)]
        n = 0
        for wname, oname in (("w_mem_k", "p_mem_k"), ("w_mem_v", "p_mem_v")):
            W = self.i[wname][l]
            wts = []
            for half in range(2):
                wt = self.next_w()
                self.dma(wt[0][:, 0:16, 0:256], W[:, half * 256:(half + 1) * 256].rearrange("(k p) c -> p k c", p=128),
                         [], [wt[3]], eng="pool")
                wts.append(wt)
            for tt in range(2):
                o, ok = so[n % 2], ("mko", n % 2)
                n += 1
                for half in range(2):
                    ps, pk = self.psum()
                    for k in range(16):
                        self.mm(ps[:, 0:256], mn[:, k, tt * 128:(tt + 1) * 128], wts[half][0][:, k, 0:256], k == 0, k == 15,
                                [wts[half][3], ("mnT", 0)], [pk])
                    self.cp("dve" if half else "act", o[:, half * 256:(half + 1) * 256], ps[:, 0:256], [pk], [ok])
                self.dma(self.o[oname][l][tt * 128:(tt + 1) * 128, :], o[:], [ok], [self.o[oname].name])
        self.P.barrier()

    def stage_lru(self, l):
        self.off = self.lbase
        I = self.i
        lp = self.sb("lrup", [128, 40], F32)
        self.colload(lp[:, 0:16], I["lru_conv_w"][l].rearrange("j (f p) -> (j f) p", p=128), 16, "lrup")
        for j, n in enumerate(["lru_conv_b", "lru_b_a", "lru_b_x", "lru_lam"]):
            self.colload(lp[:, 16 + 4 * j:20 + 4 * j], I[n][l].rearrange("(f p) -> f p", p=128), 4, "lrup")
        self.act(lp[:, 32:36], lp[:, 28:32], AF.Exp, ["lrup"], ["lrup"], scale=-1.0)
        self.act(lp[:, 32:36], lp[:, 32:36], AF.Ln, ["lrup"], ["lrup"], bias=self.onec[:, 0:1])
        self.ts("dve", lp[:, 32:36], lp[:, 32:36], -8.0, ALU.mult, ["lrup"], ["lrup"])
        WA = self.sb("lruWA", [128, 4, 128], F32)
        WX = self.sb("lruWX", [128, 4, 128], F32)
        self.memset("dve", WA[:], 0.0, ["lruW"])
        self.memset("dve", WX[:], 0.0, ["lruW"])
        for k in range(8):
            hb = (k % 2) * 64
            self.dma(WA[hb:hb + 64, k // 2, hb:hb + 64], I["lru_w_a"][l][k], [], ["lruW"])
            self.dma(WX[hb:hb + 64, k // 2, hb:hb + 64], I["lru_w_x"][l][k], [], ["lruW"])
        T = TP
        xpad = self.sb("l_xpad", [128, 3 + T], F32)
        xc = self.sb("l_xc", [128, T], F32)
        r = self.sb("l_r", [128, T], F32)
        ig = self.sb("l_ig", [128, T], F32)
        a = self.sb("l_a", [128, T], F32)
        bx = self.sb("l_bx", [128, T], F32)
        g = self.sb("l_g", [128, T], F32)
        u = self.sb("l_u", [128, T], F32)
        yb = self.sb("l_yb", [128, T], BF16)
        h0 = self.sb("l_h0", [128, 4], F32)
        self.colload(h0[:, 0:4], I["state_lru"][l].rearrange("(f p) -> f p", p=128), 4, "l_h0")
        for i in range(4):
            for gi, (t0, T) in enumerate(GRP):
                pre = "ps"[gi] + "_"
                if gi == 0:
                    self.memset("pool", xpad[:, 0:3], 0.0, ["l_xpad"])
                else:
                    self.colload(xpad[:, 0:3], I["state_lru_conv"][l][:, i * 128:(i + 1) * 128], 3, "l_xpad")
                self.dma(xpad[:, 3:3 + T], self.d["lruxT"][i * 128:(i + 1) * 128, t0:t0 + T], [], ["l_xpad"])
                self.dma(g[:, 0:T], self.d["lrugT"][i * 128:(i + 1) * 128, t0:t0 + T], [], ["l_g"])
                self.ts("dve", xc[:, 0:T], xpad[:, 0:T], lp[:, i:i + 1], ALU.mult, ["l_xpad", "lrup"], ["l_xc"],
                        s2=lp[:, 16 + i:17 + i], op1=ALU.add)
                for j in range(1, 4):
                    self.stt(xc[:, 0:T], xpad[:, j:j + T], lp[:, 4 * j + i:4 * j + i + 1], xc[:, 0:T], ALU.mult, ALU.add,
                             ["l_xpad", "lrup", "l_xc"], ["l_xc"])
                for c0 in range(0, T, 512):
                    cw = min(512, T - c0)
                    ps, pk = self.psum()
                    self.mm(ps[:, 0:cw], WA[:, i, :], xc[:, c0:c0 + cw], True, True, ["lruW", "l_xc"], [pk])
                    self.act(r[:, c0:c0 + cw], ps[:, 0:cw], AF.Sigmoid, [pk, "lrup"], ["l_r"], bias=lp[:, 20 + i:21 + i])
                    ps, pk = self.psum()
                    self.mm(ps[:, 0:cw], WX[:, i, :], xc[:, c0:c0 + cw], True, True, ["lruW", "l_xc"], [pk])
                    self.act(ig[:, c0:c0 + cw], ps[:, 0:cw], AF.Sigmoid, [pk, "lrup"], ["l_ig"], bias=lp[:, 24 + i:25 + i])
                self.act(a[:, 0:T], r[:, 0:T], AF.Exp, ["l_r", "lrup"], ["l_a"], scale=lp[:, 32 + i:33 + i])
                self.tt("pool", bx[:, 0:T], a[:, 0:T], a[:, 0:T], ALU.mult, ["l_a"], ["l_bx"])
                self.ts("dve", bx[:, 0:T], bx[:, 0:T], -1.0, ALU.mult, ["l_bx"], ["l_bx"], s2=1.0, op1=ALU.add)
                self.ts("dve", bx[:, 0:T], bx[:, 0:T], 0.0, ALU.max, ["l_bx"], ["l_bx"])
                self.act(bx[:, 0:T], bx[:, 0:T], AF.Sqrt, ["l_bx"], ["l_bx"])
                self.tt("pool", ig[:, 0:T], ig[:, 0:T], xc[:, 0:T], ALU.mult, ["l_ig", "l_xc"], ["l_ig"])
                self.tt("dve", bx[:, 0:T], bx[:, 0:T], ig[:, 0:T], ALU.mult, ["l_bx", "l_ig"], ["l_bx"])
                init = 0.0 if gi == 0 else h0[:, i:i + 1]
                self.scan(r[:, 0:T], a[:, 0:T], bx[:, 0:T], init, ["l_a", "l_bx", "l_h0", "l_r"], ["l_r"])
                self.dma(self.o[pre + "lru"][l][i * 128:(i + 1) * 128].rearrange("(p o) -> p o", o=1), r[:, T - 1:T],
                         ["l_r"], [self.o[pre + "lru"].name])
                self.tt("pool", u[:, 0:T], g[:, 0:T], g[:, 0:T], ALU.mult, ["l_g"], ["l_u"])
                self.ts("dve", u[:, 0:T], u[:, 0:T], 0.044715, ALU.mult, ["l_u"], ["l_u"], s2=1.0, op1=ALU.add)
                self.tt("dve", u[:, 0:T], u[:, 0:T], g[:, 0:T], ALU.mult, ["l_u", "l_g"], ["l_u"])
                self.act(u[:, 0:T], u[:, 0:T], AF.Sigmoid, ["l_u"], ["l_u"], scale=1.5957691216057308)
                self.tt("pool", u[:, 0:T], u[:, 0:T], g[:, 0:T], ALU.mult, ["l_u", "l_g"], ["l_u"])
                self.tt("dve", yb[:, 0:T], u[:, 0:T], r[:, 0:T], ALU.mult, ["l_u", "l_r"], ["l_yb"])
                self.dma(self.d["mixT"][1536 + i * 128:1536 + (i + 1) * 128, t0:t0 + T], yb[:, 0:T], ["l_yb"], ["mixT"])
        self.P.barrier()

    def gelu(self, out, x, tmp, kx, kt, ko):
        self.tt("pool", tmp, x, x, ALU.mult, [kx], [kt])
        self.ts("dve", tmp, tmp, 0.044715, ALU.mult, [kt], [kt], s2=1.0, op1=ALU.add)
        self.tt("dve", tmp, tmp, x, ALU.mult, [kt, kx], [kt])
        self.act(tmp, tmp, AF.Sigmoid, [kt], [kt], scale=1.5957691216057308)
        self.tt("dve", out, tmp, x, ALU.mult, [kt, kx], [ko] if ko != kt else [kt])

    def stage_ssm(self, l):
        self.off = self.lbase
        I = self.i
        PI = math.pi
        sp = self.sb("ssp", [128, 24, 16], F32)
        names = ["are", "aim", "ldt", "step", "ars", "ais", "mag", "c1", "s1", "abr", "abi", "den", "cr", "ci",
                 "t0", "t1", "h0r", "h0i", "g0r", "g0i", "t2", "t3"]
        ix = {n: k for k, n in enumerate(names)}
        P_ = lambda n: sp[:, ix[n], :]
        K_ = "ssp"
        self.colload(P_("are"), I["ssm_a_re"][l].rearrange("(j a) n -> j (a n)", a=2), 16, K_)
        self.colload(P_("aim"), I["ssm_a_im"][l].rearrange("(j a) n -> j (a n)", a=2), 16, K_)
        self.colload(P_("h0r"), I["state_ssm_re"][l].rearrange("(j a) n -> j (a n)", a=2), 16, K_)
        self.colload(P_("h0i"), I["state_ssm_im"][l].rearrange("(j a) n -> j (a n)", a=2), 16, K_)
        ldr = self.sb("ldr", [1, 32], F32)
        self.dma(ldr[:], I["ssm_log_dt"][l:l + 1, :], [], ["ldr"])
        ps, pk = self.psum()
        self.mm(ps[:, 0:32], self.ones[0:1, :], ldr[0:1, :], True, True, ["ldr"], [pk])
        pv = ps[:, 0:32].rearrange("p (j a) -> p j a", a=2)
        self.cp("dve", P_("ldt")[0:64, :], pv[0:64, :, 0], [pk], [K_])
        self.cp("dve", P_("ldt")[64:128, :], pv[64:128, :, 1], [pk], [K_])
        self.act(P_("step"), P_("ldt"), AF.Exp, [K_], [K_])
        self.tt("dve", P_("ars"), P_("are"), P_("step"), ALU.mult, [K_], [K_])
        self.tt("dve", P_("ais"), P_("aim"), P_("step"), ALU.mult, [K_], [K_])
        self.act(P_("mag"), P_("ars"), AF.Exp, [K_], [K_])
        it = self.sb("ssi", [128, 16], mybir.dt.int32)

        def sin_of(dst, src, shift):
            self.ts("dve", P_("t0"), src, shift, ALU.add, [K_], [K_])
            self.ts("dve", P_("t1"), P_("t0"), 1.0 / (2 * PI), ALU.mult, [K_], [K_])
            self.cp("dve", it[:], P_("t1"), [K_], ["ssi"])
            self.cp("dve", P_("t1"), it[:], ["ssi"], [K_])
            self.stt(P_("t0"), P_("t1"), -2 * PI, P_("t0"), ALU.mult, ALU.add, [K_], [K_])
            self.ts("dve", P_("t1"), P_("t0"), PI, ALU.is_gt, [K_], [K_])
            self.stt(P_("t0"), P_("t1"), -2 * PI, P_("t0"), ALU.mult, ALU.add, [K_], [K_])
            self.ts("dve", P_("t1"), P_("t0"), -PI, ALU.is_lt, [K_], [K_])
            self.stt(P_("t0"), P_("t1"), 2 * PI, P_("t0"), ALU.mult, ALU.add, [K_], [K_])
            self.act(dst, P_("t0"), AF.Sin, [K_], [K_])
        sin_of(P_("s1"), P_("ais"), 0.0)
        sin_of(P_("c1"), P_("ais"), PI / 2)
        self.tt("dve", P_("abr"), P_("mag"), P_("c1"), ALU.mult, [K_], [K_])
        self.tt("dve", P_("abi"), P_("mag"), P_("s1"), ALU.mult, [K_], [K_])
        self.tt("dve", P_("den"), P_("are"), P_("are"), ALU.mult, [K_], [K_])
        self.tt("dve", P_("t0"), P_("aim"), P_("aim"), ALU.mult, [K_], [K_])
        self.tt("dve", P_("den"), P_("den"), P_("t0"), ALU.add, [K_], [K_])
        self.recip(P_("den"), P_("den"), [K_], [K_])
        self.ts("dve", P_("t2"), P_("abr"), -1.0, ALU.add, [K_], [K_])
        self.tt("dve", P_("t0"), P_("t2"), P_("are"), ALU.mult, [K_], [K_])
        self.tt("dve", P_("t1"), P_("abi"), P_("aim"), ALU.mult, [K_], [K_])
        self.tt("dve", P_("cr"), P_("t0"), P_("t1"), ALU.add, [K_], [K_])
        self.tt("dve", P_("cr"), P_("cr"), P_("den"), ALU.mult, [K_], [K_])
        self.tt("dve", P_("t0"), P_("abi"), P_("are"), ALU.mult, [K_], [K_])
        self.tt("dve", P_("t1"), P_("t2"), P_("aim"), ALU.mult, [K_], [K_])
        self.tt("dve", P_("ci"), P_("t0"), P_("t1"), ALU.subtract, [K_], [K_])
        self.tt("dve", P_("ci"), P_("ci"), P_("den"), ALU.mult, [K_], [K_])
        self.tt("dve", P_("t0"), P_("c1"), P_("h0r"), ALU.mult, [K_], [K_])
        self.tt("dve", P_("t1"), P_("s1"), P_("h0i"), ALU.mult, [K_], [K_])
        self.tt("dve", P_("g0r"), P_("t0"), P_("t1"), ALU.subtract, [K_], [K_])
        self.tt("dve", P_("t0"), P_("c1"), P_("h0i"), ALU.mult, [K_], [K_])
        self.tt("dve", P_("t1"), P_("s1"), P_("h0r"), ALU.mult, [K_], [K_])
        self.tt("dve", P_("g0i"), P_("t0"), P_("t1"), ALU.add, [K_], [K_])
        pwc = self.sb("spwc", [128, 12, 16], F32)
        pws = self.sb("spws", [128, 12, 16], F32)
        self.cp("dve", pwc[:, 0, :], P_("c1"), [K_], ["spw"])
        self.cp("dve", pws[:, 0, :], P_("s1"), [K_], ["spw"])
        for k in range(11):
            self.tt("dve", P_("t0"), pwc[:, k, :], pwc[:, k, :], ALU.mult, ["spw", K_], [K_])
            self.tt("dve", P_("t1"), pws[:, k, :], pws[:, k, :], ALU.mult, ["spw", K_], [K_])
            self.tt("dve", pwc[:, k + 1, :], P_("t0"), P_("t1"), ALU.subtract, [K_], ["spw"])
            self.tt("dve", P_("t0"), pwc[:, k, :], pws[:, k, :], ALU.mult, ["spw", K_], [K_])
            self.ts("dve", pws[:, k + 1, :], P_("t0"), 2.0, ALU.mult, [K_], ["spw"])
        BR = self.sb("sBR", [128, 16, 16], F32)
        BI = self.sb("sBI", [128, 16, 16], F32)
        QR = self.sb("sQR", [128, 16, 16], F32)
        QI = self.sb("sQI", [128, 16, 16], F32)
        TM = self.sb("sTM", [128, 16, 16], F32)
        self.dma(BR[:], I["ssm_b_re"][l].rearrange("(j a) n c -> (a n) j c", a=2), [], ["sB"])
        self.dma(BI[:], I["ssm_b_im"][l].rearrange("(j a) n c -> (a n) j c", a=2), [], ["sB"])
        crb = P_("cr").unsqueeze(2).to_broadcast([128, 16, 16])
        cib = P_("ci").unsqueeze(2).to_broadcast([128, 16, 16])
        self.tt("dve", QR[:], BR[:], crb, ALU.mult, ["sB", K_], ["sQ"])
        self.tt("dve", TM[:], BI[:], cib, ALU.mult, ["sB", K_], ["sTM"])
        self.tt("dve", QR[:], QR[:], TM[:], ALU.subtract, ["sQ", "sTM"], ["sQ"])
        self.tt("dve", QI[:], BI[:], crb, ALU.mult, ["sB", K_], ["sQ"])
        self.tt("dve", TM[:], BR[:], cib, ALU.mult, ["sB", K_, "sQ"], ["sTM"])
        self.tt("dve", QI[:], QI[:], TM[:], ALU.add, ["sQ", "sTM"], ["sQ"])
        LB = [self.sb("sLB%d" % q, [128, 16, 128], F32) for q in range(2)]
        WC = [self.sb("sWC%d" % q, [128, 16, 128], F32) for q in range(2)]
        Z = self.sb("sZ", [128, 128], F32)
        for q, Q in enumerate((QR, QI)):
            for j in range(16):
                c0 = 32 * (j % 4)
                self.memset("pool", Z[:], 0.0, ["sZ"])
                self.cp("pool", Z[0:64, c0:c0 + 16], Q[0:64, j, :], ["sQ", "sZ"], ["sZ"])
                self.cp("pool", Z[64:128, c0 + 16:c0 + 32], Q[64:128, j, :], ["sQ", "sZ"], ["sZ"])
                ps, pk = self.psum()
                self.tr(ps[:, 0:128], Z[:], ["sZ"], [pk])
                self.cp("dve", LB[q][:, j, :], ps[:, 0:128], [pk], ["sLB"])
        V = self.sb("sV", [32, 16, 128], F32)
        for q, nm in enumerate(("ssm_c_re", "ssm_c_im")):
            self.memset("pool", V[:], 0.0, ["sV"])
            self.memset("pool", WC[q][:], 0.0, ["sWC"])
            cv = I[nm][l].rearrange("(j a) c n -> a c j n", a=2)
            self.dma(V[0:16, :, 0:64], cv[0], ["sV"], ["sV"])
            self.dma(V[16:32, :, 64:128], cv[1], ["sV"], ["sV"])
            for j in range(16):
                c0 = 32 * (j % 4)
                ps, pk = self.psum()
                self.tr(ps[:, 0:32], V[:, j, :], ["sV"], [pk])
                if q == 0:
                    self.cp("dve", WC[q][:, j, c0:c0 + 32], ps[:, 0:32], [pk, "sWC"], ["sWC"])
                else:
                    self.ts("dve", WC[q][:, j, c0:c0 + 32], ps[:, 0:32], -1.0, ALU.mult, [pk, "sWC"], ["sWC"])
        dcol = self.sb("sdc", [128, 8], F32)
        self.colload(dcol[:, 0:4], I["ssm_d"][l].rearrange("(f p) -> f p", p=128), 4, "sdc")
        self.colload(dcol[:, 4:8], I["ssm_b_glu"][l].rearrange("(f p) -> f p", p=128), 4, "sdc")
        T = TP
        big = {n: self.sb("s_" + n, [128, T], F32) for n in ("xr", "xi", "ec", "es", "t1", "t2", "tm")}
        u = self.sb("s_u", [128, TT], F32)
        yg = self.sb("s_yg", [128, 4, TT], F32)
        ygb = self.sb("s_ygb", [128, 4, TT], BF16)
        xr, xi, ec, es, t1, t2, tm = (big[n] for n in ("xr", "xi", "ec", "es", "t1", "t2", "tm"))
        for yi in range(4):
            self.dma(u[:], self.d["ssmuT"][yi * 128:(yi + 1) * 128, :], [], ["s_u"])
            for gi, (t0, T) in enumerate(GRP):
                pre = "ps"[gi] + "_"
                nch = [(c0, min(512, T - c0)) for c0 in range(0, T, 512)]
                for jj in range(4):
                    j = 4 * yi + jj
                    for (dst, q, kk) in ((xr, 0, "s_xr"), (xi, 1, "s_xi")):
                        for (c0, cw) in nch:
                            ps, pk = self.psum()
                            self.mm(ps[:, 0:cw], LB[q][:, j, :], u[:, t0 + c0:t0 + c0 + cw], True, True, ["sLB", "s_u"], [pk])
                            self.cp("act", dst[:, c0:c0 + cw], ps[:, 0:cw], [pk], [kk])
                    self.memset("pool", ec[:, 0:1], 1.0, ["s_ec"])
                    self.memset("pool", es[:, 0:1], 0.0, ["s_es"])
                    n = 1
                    k = 0
                    while n < T:
                        cn, sn = pwc[:, k, j:j + 1], pws[:, k, j:j + 1]
                        self.ts("dve", tm[:, 0:n], es[:, 0:n], sn, ALU.mult, ["s_es", "spw"], ["s_tm"])
                        self.stt(ec[:, n:2 * n], ec[:, 0:n], cn, tm[:, 0:n], ALU.mult, ALU.subtract, ["s_ec", "s_tm", "spw"], ["s_ec2"])
                        self.ts("dve", tm[:, 0:n], ec[:, 0:n], sn, ALU.mult, ["s_ec", "spw", "s_ec2"], ["s_tm"])
                        self.stt(es[:, n:2 * n], es[:, 0:n], cn, tm[:, 0:n], ALU.mult, ALU.add, ["s_es", "s_tm", "spw"], ["s_es"])
                        self.P.last_w["s_ec"] = self.P.last_w["s_ec2"]
                        n *= 2
                        k += 1
                    self.tt("pool", t1[:, 0:T], ec[:, 0:T], xr[:, 0:T], ALU.mult, ["s_ec", "s_xr"], ["s_t1"])
                    self.tt("dve", tm[:, 0:T], es[:, 0:T], xi[:, 0:T], ALU.mult, ["s_es", "s_xi"], ["s_tm"])
                    self.tt("pool", t1[:, 0:T], t1[:, 0:T], tm[:, 0:T], ALU.add, ["s_t1", "s_tm"], ["s_t1"])
                    self.tt("pool", t2[:, 0:T], ec[:, 0:T], xi[:, 0:T], ALU.mult, ["s_ec", "s_xi"], ["s_t2"])
                    self.tt("dve", tm[:, 0:T], es[:, 0:T], xr[:, 0:T], ALU.mult, ["s_es", "s_xr", "s_t1"], ["s_tm"])
                    self.tt("dve", t2[:, 0:T], t2[:, 0:T], tm[:, 0:T], ALU.subtract, ["s_t2", "s_tm"], ["s_t2"])
                    rho = P_("mag")[:, j:j + 1].to_broadcast([128, T])
                    ir = 0.0 if gi == 0 else P_("g0r")[:, j:j + 1]
                    ii = 0.0 if gi == 0 else P_("g0i")[:, j:j + 1]
                    self.scan(xr[:, 0:T], rho, t1[:, 0:T], ir, [K_, "s_t1", "s_xr"], ["s_xr"])
                    self.scan(xi[:, 0:T], rho, t2[:, 0:T], ii, [K_, "s_t2", "s_xi"], ["s_xi"])
                    self.tt("pool", t1[:, 0:T], ec[:, 0:T], xr[:, 0:T], ALU.mult, ["s_ec", "s_xr", "s_t1"], ["s_t1"])
                    self.tt("dve", tm[:, 0:T], es[:, 0:T], xi[:, 0:T], ALU.mult, ["s_es", "s_xi", "s_t2"], ["s_tm"])
                    self.tt("pool", t1[:, 0:T], t1[:, 0:T], tm[:, 0:T], ALU.subtract, ["s_t1", "s_tm"], ["s_t1"])
                    self.tt("pool", t2[:, 0:T], ec[:, 0:T], xi[:, 0:T], ALU.mult, ["s_ec", "s_xi", "s_t2"], ["s_t2"])
                    self.tt("dve", tm[:, 0:T], es[:, 0:T], xr[:, 0:T], ALU.mult, ["s_es", "s_xr", "s_t1"], ["s_tm"])
                    self.tt("dve", t2[:, 0:T], t2[:, 0:T], tm[:, 0:T], ALU.add, ["s_t2", "s_tm"], ["s_t2"])
                    for (src, nm, kk) in ((t1, "ssm_re", "s_t1"), (t2, "ssm_im", "s_t2")):
                        o = self.o[pre + nm][l].rearrange("g n -> (g n)")[j * 128:(j + 1) * 128].rearrange("(p o) -> p o", o=1)
                        self.dma(o, src[:, T - 1:T], [kk], [self.o[pre + nm].name])
                    for ci, (c0, cw) in enumerate(nch):
                        yps, yk = self.ps[4 + ci], ("ps", 4 + ci)
                        self.mm(yps[:, 0:cw], WC[0][:, j, :], t1[:, c0:c0 + cw], jj == 0, False, ["sWC", "s_t1"], [yk])
                        self.mm(yps[:, 0:cw], WC[1][:, j, :], t2[:, c0:c0 + cw], False, jj == 3, ["sWC", "s_t2"], [yk])
                for ci, (c0, cw) in enumerate(nch):
                    yps, yk = self.ps[4 + ci], ("ps", 4 + ci)
                    self.stt(yg[:, yi, t0 + c0:t0 + c0 + cw], u[:, t0 + c0:t0 + c0 + cw], dcol[:, yi:yi + 1], yps[:, 0:cw],
                             ALU.mult, ALU.add, ["s_u", "sdc", yk], [("s_yg", yi, gi)])
                self.gelu(yg[:, yi, t0:t0 + T], yg[:, yi, t0:t0 + T], tm[:, 0:T], ("s_yg", yi, gi), "s_tm", ("s_yg", yi, gi))
                self.cp("pool", ygb[:, yi, t0:t0 + T], yg[:, yi, t0:t0 + T], [("s_yg", yi, gi)], ["s_ygb"])
        ob = [self.sb("s_ob", [128, 512], BF16) for _ in range(2)]
        sg = [self.sb("s_sg", [128, 512], F32) for _ in range(2)]
        self.gli = 0

        def glu(c, mw, t0, tn, ps, pk):
            i = c // 128
            b = self.gli % 2
            self.gli += 1
            self.act(sg[b][:, 0:tn], ps[:, 0:tn], AF.Sigmoid, [pk, "sdc"], [("s_sg", b)], bias=dcol[:, 4 + i:5 + i])
            gi = 0 if t0 < TP else 1
            self.tt("dve", ob[b][:, 0:tn], sg[b][:, 0:tn], yg[:, i, t0:t0 + tn], ALU.mult, [("s_sg", b), ("s_yg", i, gi)], [("s_ob", b)])
            self.dma(self.d["mixT"][512 + c:512 + c + mw, t0:t0 + tn], ob[b][:, 0:tn], [("s_ob", b)], ["mixT"])
        self.dense(I["ssm_w_glu"][l], 4, [(0, 256), (256, 256)], ygb, "s_ygb", [(t0, tn, t0) for (t0, tn) in CH], glu, wv=2)
        self.P.barrier()

    def stage_dn(self, l):
        self.off = self.lbase
        I = self.i
        self.psr = list(range(8))
        cw = self.sb("dcw", [128, 48], F32)
        self.colload(cw[:, 0:48], I["dn_conv_w"][l].rearrange("j (f p) -> (j f) p", p=128), 48, "dcw")
        ngc_ = self.sb("dng", [128, 1], F32)
        self.dma(ngc_[:], I["dn_norm_g"][l].rearrange("(p o) -> p o", o=1), [], ["dng"])
        hp = self.sb("dhp", [4, 4], F32)
        self.dma(hp[:, 0:1], I["dn_a_log"][l].rearrange("(h o) -> h o", o=1), [], ["dhp"])
        self.dma(hp[:, 1:2], I["dn_dt_bias"][l].rearrange("(h o) -> h o", o=1), [], ["dhp"])
        self.act(hp[:, 2:3], hp[:, 0:1], AF.Exp, ["dhp"], ["dhp"])
        self.ts("dve", hp[:, 2:3], hp[:, 2:3], -1.0, ALU.mult, ["dhp"], ["dhp"])
        T = TP
        rows = {n: self.sb("dr_" + n, [4, T], F32) for n in ("b", "g", "gc")}
        rows["ngc"] = rows["g"]
        colB = self.sb("dcB", [128, 16, 4], F32)
        colG = self.sb("dcG", [128, 16, 4], F32)
        colNB = self.sb("dcNB", [128, 16, 4], F32)
        colNEB = self.sb("dcNEB", [128, 16, 4], F32)
        qT = [self.sb("dq%d" % h, [128, T], F32) for h in range(4)]
        kT = [self.sb("dk%d" % h, [128, T], F32) for h in range(4)]
        vT = [self.sb("dv%d" % h, [128, T], F32) for h in range(4)]
        oT = vT
        S = [self.sb("dS%d" % h, [128, 128], F32) for h in range(4)]
        xpad = self.sb("dxp", [128, 3 + T], F32)
        rs = self.sb("drs", [128, 512], F32)
        sq = self.sb("dsq", [128, 512], F32)
        wn = ("dec", "decT", "N", "NT", "X0", "X1", "Y0", "Y1", "AT", "vb", "kd", "R", "vn", "QK", "qg")
        wt = [{n: self.sb("dw%d%s" % (h, n), [128, 128], F32) for n in wn} for h in range(4)]
        egl = [self.sb("degl%d" % h, [128, 1], F32) for h in range(4)]
        ob = self.sb("dob", [128, 512], BF16)
        for gi, (t0, T) in enumerate(GRP):
            pre = "ps"[gi] + "_"
            C = 128 if gi == 0 else 8
            NCH = T // C
            L = int(round(math.log2(C))) - 1
            b, g, gc, ngc = (rows[n] for n in ("b", "g", "gc", "ngc"))
            self.dma(b[:, 0:T], self.d["dnbaT"][0:4, t0:t0 + T], [], ["dr_b"])
            self.dma(g[:, 0:T], self.d["dnbaT"][4:8, t0:t0 + T], [], ["dr_g"])
            self.act(b[:, 0:T], b[:, 0:T], AF.Sigmoid, ["dr_b"], ["dr_b"])
            self.act(g[:, 0:T], g[:, 0:T], AF.Exp, ["dr_g", "dhp"], ["dr_g"], bias=hp[:, 1:2])
            self.act(g[:, 0:T], g[:, 0:T], AF.Ln, ["dr_g"], ["dr_g"], bias=self.onec[0:4, 0:1])
            self.ts("dve", g[:, 0:T], g[:, 0:T], hp[:, 2:3], ALU.mult, ["dr_g", "dhp"], ["dr_g"])
            for n in range(NCH):
                self.scan(gc[:, n * C:(n + 1) * C], self.ones[0:4, 0:C], g[:, n * C:(n + 1) * C], 0.0, ["dr_g", "dr_gc"], ["dr_gc"])
            self.ts("dve", ngc[:, 0:T], gc[:, 0:T], -1.0, ALU.mult, ["dr_gc"], ["dr_g"])
            for n in range(NCH):
                ps, pk = self.psum()
                self.tr(ps[0:C, 0:4], b[:, n * C:(n + 1) * C], ["dr_b"], [pk])
                self.tr(ps[0:C, 4:8], gc[:, n * C:(n + 1) * C], ["dr_gc"], [pk])
                self.cp("dve", colB[0:C, n, :], ps[0:C, 0:4], [pk], ["dcol"])
                self.cp("dve", colG[0:C, n, :], ps[0:C, 4:8], [pk], ["dcol"])
            self.act(colG[0:C, 0:NCH, :], colG[0:C, 0:NCH, :], AF.Exp, ["dcol"], ["dcol"])
            self.ts("dve", colNB[0:C, 0:NCH, :], colB[0:C, 0:NCH, :], -1.0, ALU.mult, ["dcol"], ["dcol"])
            self.tt("dve", colNEB[0:C, 0:NCH, :], colG[0:C, 0:NCH, :], colNB[0:C, 0:NCH, :], ALU.mult, ["dcol"], ["dcol"])
            for h in range(4):
                for (dst, part, kk) in ((qT[h], 0, ("dq", h)), (kT[h], 1, ("dk", h)), (vT[h], 2, ("dv", h))):
                    f = part * 4 + h
                    if gi == 0:
                        self.memset("pool", xpad[:, 0:3], 0.0, ["dxp"])
                    else:
                        self.colload(xpad[:, 0:3], I["state_delta_conv"][l][:, f * 128:(f + 1) * 128], 3, "dxp")
                    self.dma(xpad[:, 3:3 + T], self.d["dnqkvT"][f * 128:(f + 1) * 128, t0:t0 + T], [], ["dxp"])
                    self.ts("dve", dst[:, 0:T], xpad[:, 0:T], cw[:, f:f + 1], ALU.mult, ["dxp", "dcw"], [kk])
                    for j in range(1, 4):
                        self.stt(dst[:, 0:T], xpad[:, j:j + T], cw[:, 12 * j + f:12 * j + f + 1], dst[:, 0:T], ALU.mult, ALU.add,
                                 ["dxp", "dcw", kk], [kk])
                    self.act(dst[:, 0:T], dst[:, 0:T], AF.Silu, [kk], [kk])
                    if part < 2:
                        for c0 in range(0, T, 512):
                            w = min(512, T - c0)
                            self.act(sq[:, 0:w], dst[:, c0:c0 + w], AF.Square, [kk], ["dsq"])
                            ps, pk = self.psum()
                            self.mm(ps[:, 0:w], self.ones[:], sq[:, 0:w], True, True, ["dsq"], [pk])
                            self.act(rs[:, 0:w], ps[:, 0:w], AF.Sqrt, [pk], ["drs"], bias=self.epsc[:, 0:1])
                            self.recip(rs[:, 0:w], rs[:, 0:w], ["drs"], ["drs"])
                            self.stt(dst[:, c0:c0 + w], dst[:, c0:c0 + w], (128 ** -0.5) if part == 0 else 1.0, rs[:, 0:w],
                                     ALU.mult, ALU.mult, [kk, "drs"], [kk])
                if gi == 0:
                    self.memset("pool", S[h][:], 0.0, [("dS", h)])
                else:
                    self.dma(S[h][:], I["state_delta"][l][h], [], [("dS", h)])
            for n in range(NCH):
                cs = slice(n * C, (n + 1) * C)
                for h in range(4):
                    w = wt[h]
                    W = lambda nm: ("dw", h, nm)
                    selM = self.sel[0:4, h * 128:h * 128 + C]
                    sel128 = self.sel[0:4, h * 128:(h + 1) * 128]
                    kq, kk_, kv = ("dq", h), ("dk", h), ("dv", h)
                    ps, pk = self.psum()
                    self.mm(ps[0:C, 0:C], gc[0:4, cs], selM, True, False, ["dr_gc"], [pk])
                    self.mm(ps[0:C, 0:C], selM, ngc[0:4, cs], False, False, ["dr_g"], [pk])
                    self.mm(ps[0:C, 0:C], self.ident[0:C, 0:C], self.neglow[0:C, 0:C], False, True, [], [pk])
                    self.act(w["dec"][0:C, 0:C], ps[0:C, 0:C], AF.Exp, [pk], [W("dec")])
                    ps, pk = self.psum()
                    self.mm(ps[0:C, 0:C], selM, gc[0:4, cs], True, False, ["dr_gc"], [pk])
                    self.mm(ps[0:C, 0:C], ngc[0:4, cs], selM, False, False, ["dr_g"], [pk])
                    self.mm(ps[0:C, 0:C], self.ident[0:C, 0:C], self.negup[0:C, 0:C], False, True, [], [pk])
                    self.act(w["decT"][0:C, 0:C], ps[0:C, 0:C], AF.Exp, [pk], [W("decT")])
                    ps, pk = self.psum()
                    self.mm(ps[0:C, 0:C], kT[h][:, cs], kT[h][:, cs], True, True, [kk_], [pk])
                    self.tt("dve", w["N"][0:C, 0:C], ps[0:C, 0:C], w["dec"][0:C, 0:C], ALU.mult, [pk, W("dec")], [W("N")])
                    self.stt(w["N"][0:C, 0:C], w["N"][0:C, 0:C], colNB[0:C, n, h:h + 1], self.strict[0:C, 0:C], ALU.mult, ALU.mult,
                             [W("N"), "dcol"], [W("N")])
                    ps, pk = self.psum()
                    self.tr(ps[0:C, 0:C], w["N"][0:C, 0:C], [W("N")], [pk])
                    self.cp("act", w["NT"][0:C, 0:C], ps[0:C, 0:C], [pk], [W("NT")])
                    self.tt("dve", w["AT"][0:C, 0:C], w["NT"][0:C, 0:C], self.ident[0:C, 0:C], ALU.add, [W("NT")], [W("AT")])
                    Xc, Yc = "N", "NT"
                    for i in range(L):
                        Xn, Yn = "X%d" % (i % 2), "Y%d" % (i % 2)
                        ps, pk = self.psum()
                        self.mm(ps[0:C, 0:C], w[Yc][0:C, 0:C], w[Xc][0:C, 0:C], True, True, [W(Xc), W(Yc)], [pk])
                        self.cp("act", w[Xn][0:C, 0:C], ps[0:C, 0:C], [pk], [W(Xn)])
                        if i < L - 1:
                            ps, pk = self.psum()
                            self.mm(ps[0:C, 0:C], w[Xc][0:C, 0:C], w[Yc][0:C, 0:C], True, True, [W(Xc), W(Yc)], [pk])
                            self.cp("dve", w[Yn][0:C, 0:C], ps[0:C, 0:C], [pk], [W(Yn)])
                        ps, pk = self.psum()
                        self.mm(ps[0:C, 0:C], w[Xn][0:C, 0:C], w["AT"][0:C, 0:C], True, True, [W(Xn), W("AT")], [pk])
                        self.tt("dve", w["AT"][0:C, 0:C], w["AT"][0:C, 0:C], ps[0:C, 0:C], ALU.add, [pk, W("AT")], [W("AT")])
                        Xc, Yc = Xn, Yn
                    ps, pk = self.psum()
                    self.tr(ps[0:C, 0:128], vT[h][:, cs], [kv], [pk])
                    self.ts("dve", w["vb"][0:C, :], ps[0:C, 0:128], colB[0:C, n, h:h + 1], ALU.mult, [pk, "dcol"], [W("vb")])
                    ps, pk = self.psum()
                    self.tr(ps[0:C, 0:128], kT[h][:, cs], [kk_], [pk])
                    self.ts("dve", w["kd"][0:C, :], ps[0:C, 0:128], w["decT"][0:C, C - 1:C], ALU.mult, [pk, W("decT")], [W("kd")])
                    ps, pk = self.psum()
                    self.mm(ps[0:C, 0:128], kT[h][:, cs], S[h][:], True, True, [kk_, ("dS", h)], [pk])
                    self.stt(w["R"][0:C, :], ps[0:C, 0:128], colNEB[0:C, n, h:h + 1], w["vb"][0:C, :], ALU.mult, ALU.add,
                             [pk, "dcol", W("vb")], [W("R")])
                    ps, pk = self.psum()
                    self.mm(ps[0:C, 0:128], w["AT"][0:C, 0:C], w["R"][0:C, :], True, True, [W("AT"), W("R")], [pk])
                    self.cp("act", w["vn"][0:C, :], ps[0:C, 0:128], [pk], [W("vn")])
                    ps, pk = self.psum()
                    self.mm(ps[:, 0:C], sel128, gc[0:4, cs], True, True, ["dr_gc"], [pk])
                    self.act(w["qg"][:, 0:C], ps[:, 0:C], AF.Exp, [pk], [W("qg")])
                    self.tt("dve", w["qg"][:, 0:C], qT[h][:, cs], w["qg"][:, 0:C], ALU.mult, [kq, W("qg")], [W("qg")])
                    ps, pk = self.psum()
                    self.mm(ps[0:C, 0:C], kT[h][:, cs], qT[h][:, cs], True, True, [kk_, kq], [pk])
                    self.tt("dve", w["QK"][0:C, 0:C], ps[0:C, 0:C], w["decT"][0:C, 0:C], ALU.mult, [pk, W("decT")], [W("QK")])
                    ps, pk = self.psum()
                    self.mm(ps[:, 0:C], S[h][:], w["qg"][:, 0:C], True, False, [("dS", h), W("qg")], [pk])
                    self.mm(ps[:, 0:C], w["vn"][0:C, :], w["QK"][0:C, 0:C], False, True, [W("vn"), W("QK")], [pk])
                    self.cp("act", oT[h][:, cs], ps[:, 0:C], [pk], [("dv", h)])
                    ps, pk = self.psum()
                    self.mm(ps[:, 0:1], sel128, gc[0:4, (n + 1) * C - 1:(n + 1) * C], True, True, ["dr_gc"], [pk])
                    self.act(egl[h][:], ps[:, 0:1], AF.Exp, [pk], [("degl", h)])
                    ps, pk = self.psum()
                    self.mm(ps[:, 0:128], w["kd"][0:C, :], w["vn"][0:C, :], True, True, [W("kd"), W("vn")], [pk])
                    self.stt(S[h][:], S[h][:], egl[h][:, 0:1], ps[:, 0:128], ALU.mult, ALU.add, [("dS", h), ("degl", h), pk], [("dS", h)])
            for h in range(4):
                self.dma(self.o[pre + "delta"][l][h], S[h][:], [("dS", h)], [self.o[pre + "delta"].name])
                z = xpad
                self.dma(z[:, 0:T], self.d["dnzT"][h * 128:(h + 1) * 128, t0:t0 + T], [], ["dxp"])
                self.act(z[:, 0:T], z[:, 0:T], AF.Silu, ["dxp"], ["dxp"])
                for c0 in range(0, T, 512):
                    wd = min(512, T - c0)
                    self.act(sq[:, 0:wd], oT[h][:, c0:c0 + wd], AF.Square, [("dv", h)], ["dsq"])
                    ps, pk = self.psum()
                    self.mm(ps[:, 0:wd], self.ones[:], sq[:, 0:wd], True, True, ["dsq"], [pk])
                    self.act(rs[:, 0:wd], ps[:, 0:wd], AF.Sqrt, [pk], ["drs"], scale=1.0 / 128, bias=self.epsc[:, 0:1])
                    self.recip(rs[:, 0:wd], rs[:, 0:wd], ["drs"], ["drs"])
                    self.stt(sq[:, 0:wd], oT[h][:, c0:c0 + wd], ngc_[:, 0:1], rs[:, 0:wd], ALU.mult, ALU.mult,
                             [("dv", h), "dng", "drs", "dsq"], ["dsq"])
                    self.tt("dve", ob[:, 0:wd], sq[:, 0:wd], z[:, c0:c0 + wd], ALU.mult, ["dsq", "dxp"], ["dob"])
                    self.dma(self.d["mixT"][h * 128:(h + 1) * 128, t0 + c0:t0 + c0 + wd], ob[:, 0:wd], ["dob"], ["mixT"])
        self.psr = [0, 1, 2, 3]
        self.P.barrier()

    def stage_bias(self):
        self.off = self.lbase
        I = self.i
        rb = self.sb("rb", [32, 8], F32)
        oh = self.sb("oh", [32, VL], F32)
        cv = self.sb("cv", [1, VL], F32)
        bv = self.sb("bv", [8, VL], F32)
        self.dma(rb[:], I["rel_bias"], [], ["rb"])
        self.dma(oh[:], I["c_oh"], [], ["oh"])
        self.dma(cv[:], I["c_cv"], [], ["cv"])
        for c0 in range(0, VL, 512):
            w = min(512, VL - c0)
            ps, pk = self.psum()
            self.mm(ps[0:8, 0:w], rb[:, :], oh[:, c0:c0 + w], True, False, ["rb", "oh"], [pk])
            self.mm(ps[0:8, 0:w], self.ones[0:1, 0:8], cv[0:1, c0:c0 + w], False, True, ["cv"], [pk])
            self.cp("dve", bv[:, c0:c0 + w], ps[0:8, 0:w], [pk], ["bv"])
        self.dma(self.d["bvec"], bv[:], ["bv"], ["bvec"])
        self.P.barrier()
        U = [self.sb("bU", [128, SW], F32) for _ in range(2)]
        Tt = [self.sb("bT", [128, SW], F32) for _ in range(2)]
        for h in range(8):
            u, uk = U[h % 2], ("bU", h % 2)
            t, tk = Tt[h % 2], ("bT", h % 2)
            self.dma(u[:], bass.AP(self.d["bvec"].tensor, h * VL, [[1, 128], [1, SW]]), [], [uk])
            for c0 in range(0, SW, 512):
                w = min(512, SW - c0)
                ps, pk = self.psum()
                self.mm(ps[:, 0:w], self.anti[:], u[:, c0:c0 + w], True, True, [uk], [pk])
                self.cp("act" if (c0 // 512) % 2 else "dve", t[:, c0:c0 + w], ps[:, 0:w], [pk], [tk])
            self.dma(self.d["strip"][h], t[:], [tk], ["strip"])
        self.P.barrier()

    def stage_swa(self, l):
        self.off = self.lbase
        I = self.i
        qT = self.sb("aq", [128, 4, TT], BF16)
        kT = self.sb("ak", [128, 4, TT], BF16)
        V = self.sb("av", [128, 16, 512], BF16)
        Vn = self.sb("avn", [8, 512], BF16)
        kcT = self.sb("akc", [128, 4, 2048], BF16)
        Vc = self.sb("avc", [128, 16, 512], BF16)
        strip = [self.sb("ast", [128, SW], F32) for _ in range(2)]
        Lw = [self.sb("aL", [128, 512], F32) for _ in range(4)]
        PT = [self.sb("aP", [128, 512], BF16) for _ in range(4)]
        rinv = self.sb("ari", [128, 512], F32)
        ob = [self.sb("aob", [128, 512], BF16) for _ in range(2)]
        cst = self.sb("acs", [128, 4, 512], F32)
        sw = self.d["swaT"]
        self.dma(qT[:], sw[0:512, :].rearrange("(f p) t -> p f t", p=128), [], ["aq"], eng="pool")
        self.dma(kT[:], sw[512:1024, :].rearrange("(f p) t -> p f t", p=128), [], ["ak"], eng="pool")
        self.dma(V[:], self.o["p_win_v"][l].rearrange("(j p) c -> p j c", p=128), [], ["av"], eng="pool")
        self.dma(Vn[:], self.o["s_win_v"][l], [], ["avn"], eng="pool")
        self.dma(Vc[:], I["cache_win_v"][l].rearrange("(j p) c -> p j c", p=128), [], ["avc"], eng="pool")
        for j4 in range(4):
            self.dma(cst[:], I["cache_win_k"][l][j4 * 512:(j4 + 1) * 512, :].rearrange("(j p) c -> p j c", p=128), [], ["acs"])
            for jj in range(4):
                j = j4 * 4 + jj
                ps, pk = self.psum()
                for i in range(4):
                    self.tr(ps[:, i * 128:(i + 1) * 128], cst[:, jj, i * 128:(i + 1) * 128], ["acs"], [pk])
                self.cp("act" if jj % 2 else "dve", kcT[:, :, j * 128:(j + 1) * 128],
                        ps[:].rearrange("p (i t) -> p i t", i=4), [pk], ["akc"])
        LA = 2
        tasks = []
        un = 0
        for h in range(8):
            i, hb = h // 2, (h % 2) * 64
            units = [(0, c * 512, 512, [(kT, V, j, 128, c * 512 - 128 * j) for j in range(4 * c + 4)]) for c in range(4)]
            units.append((1, TP, TS, [(kcT, Vc, j, 128, 2048 - 128 * j) for j in range(16)] + [(kT, Vn, None, 8, 0)]))
            for (gi, q0, N, keys) in units:
                for ki, key in enumerate(keys):
                    tasks.append((h, i, hb, gi, q0, N, key, ki == 0, ki == len(keys) - 1, un, ki == 0 and q0 == 0))
                un += 1

        def A(ti):
            (h, i, hb, gi, q0, N, (KT, VV, j, kn, dl), first, last, u, newhead) = tasks[ti]
            st, sk = strip[h % 2], ("ast", h % 2)
            if newhead:
                self.dma(st[:], self.d["strip"][h], [], [sk])
            if j is None:
                lk, kkey = kT[hb:hb + 64, i, TP:TP + TS], "ak"
            else:
                lk, kkey = KT[hb:hb + 64, i, j * 128:(j + 1) * 128], ("ak" if gi == 0 else "akc")
            b = ti % 4
            ps, pk = self.psum()
            self.mm(ps[0:kn, 0:N], lk, qT[hb:hb + 64, i, q0:q0 + N], True, True, [kkey, "aq"], [pk])
            y0 = dl + 384
            self.stt(Lw[b][0:kn, 0:N], ps[0:kn, 0:N], 0.125, st[0:kn, y0:y0 + N], ALU.mult, ALU.add, [pk, sk], [("aL", b)])
            self.act(PT[b][0:kn, 0:N], Lw[b][0:kn, 0:N], AF.Exp, [("aL", b)], [("aP", b)])

        def B(ti):
            (h, i, hb, gi, q0, N, (KT, VV, j, kn, dl), first, last, u, newhead) = tasks[ti]
            b = ti % 4
            ops_, okk = self.ps[4 + 2 * (u % 2)], ("ps", 4 + 2 * (u % 2))
            sps, skk = self.ps[5 + 2 * (u % 2)], ("ps", 5 + 2 * (u % 2))
            if j is None:
                vv, vkey = Vn[0:kn, i * 128:(i + 1) * 128], "avn"
            else:
                vv, vkey = VV[:, j, i * 128:(i + 1) * 128], ("av" if gi == 0 else "avc")
            self.mm(ops_[:, 0:N], vv, PT[b][0:kn, 0:N], first, last, [vkey, ("aP", b)], [okk])
            self.mm(sps[:, 0:N], self.onesb[0:kn, :], PT[b][0:kn, 0:N], first, last, [("aP", b)], [skk])
            if last:
                self.recip(rinv[hb:hb + 64, 0:N], sps[hb:hb + 64, 0:N], [skk], ["ari"])
                o = ob[u % 2]
                self.tt("dve", o[hb:hb + 64, 0:N], ops_[hb:hb + 64, 0:N], rinv[hb:hb + 64, 0:N], ALU.mult, [okk, "ari"], [("aob", u % 2)])
                self.dma(self.d["mixT"][1024 + h * 64:1024 + (h + 1) * 64, q0:q0 + N], o[hb:hb + 64, 0:N], [("aob", u % 2)], ["mixT"])
        for ti in range(len(tasks) + LA):
            if ti < len(tasks):
                A(ti)
            if ti - LA >= 0:
                B(ti - LA)
        self.P.barrier()

    def resid(self):
        fifo = []

        def pre(c, mw, t0, tn):
            i = self.evi % 4
            self.evi += 1
            s, sk = self.evs[i], ("evs", i)
            rk = ("xTr", c, t0)
            self.dma(s[0:mw, 0:tn], self.d["xT"][c:c + mw, t0:t0 + tn], [rk], [sk])
            fifo.append((s, sk, rk))

        def f(c, mw, t0, tn, ps, pk):
            s, sk, rk = fifo.pop(0)
            self.tt("dve", s[0:mw, 0:tn], s[0:mw, 0:tn], ps[0:mw, 0:tn], ALU.add, [sk, pk], [sk])
            self.dma(self.d["xT"][c:c + mw, t0:t0 + tn], s[0:mw, 0:tn], [sk], [rk])
        return pre, f

    def stage_wout(self, l):
        self.off = self.lbase
        mx = self.sb("mixS", [128, 16, TT], BF16)
        self.evs = [self.sb("evs", [128, 512], F32) for _ in range(4)]
        self.evi = 0
        self.dma(mx[:], self.d["mixT"].rearrange("(f p) t -> p f t", p=128), [], ["mixS"])
        rp, rc = self.resid()
        self.dense(self.i["w_out"][l], 16, [(g, 256) for g in range(0, D, 256)], mx, "mixS",
                   [(t0, tn, t0) for (t0, tn) in CH], rc, pre=rp)
        self.P.barrier()

    def stage_cross(self, l):
        self.off = self.lbase
        I = self.i
        xn = self.sb("xnT", [128, 16, TT], BF16)
        self.evs = [self.sb("evs", [128, 512], F32) for _ in range(4)]
        self.evi = 0
        qT = self.sb("cq", [128, 4, TT], BF16)
        at = self.sb("cat", [128, 4, TT], BF16)
        kT = [self.sb("ckT%d" % g, [128, 4, 256], BF16) for g in range(2)]
        Vm = [self.sb("cV%d" % g, [128, 2, 512], BF16) for g in range(2)]
        kst = self.sb("ckst", [128, 2, 512], F32)
        PT = [self.sb("cP", [128, 512], BF16) for _ in range(4)]
        rinv = self.sb("cri", [128, 512], F32)
        ksrc = [self.o["p_mem_k"][l], I["cache_mem_k"][l]]
        vsrc = [self.o["p_mem_v"][l], I["cache_mem_v"][l]]
        for g in range(2):
            self.dma(Vm[g][:], vsrc[g].rearrange("(m p) c -> p m c", p=128), [], [("cV", g)], eng="pool")
            self.dma(kst[:], ksrc[g].rearrange("(m p) c -> p m c", p=128), [], ["ckst"])
            for m in range(2):
                ps, pk = self.psum()
                for h in range(4):
                    self.tr(ps[:, h * 128:(h + 1) * 128], kst[:, m, h * 128:(h + 1) * 128], ["ckst"], [pk])
                self.cp("dve", kT[g][:, :, m * 128:(m + 1) * 128], ps[:].rearrange("p (h t) -> p h t", h=4), [pk], [("ckT", g)])
        self.norm_T(self.d["xT"], self.gcols[:, 1, :], CH, xn, "xnT")

        def qcons(c, mw, t0, tn, ps, pk):
            self.act(qT[:, c // 128, t0:t0 + tn], ps[:, 0:tn], AF.Copy, [pk], [("cq", t0)], scale=128 ** -0.5)
        self.dense(I["w_mem_q"][l], 16, [(0, 256), (256, 256)], xn, "xnT", [(t0, tn, t0) for (t0, tn) in CH], qcons)
        LA = 0
        tasks = []
        un = 0
        for (t0, tn) in CH:
            g = 0 if t0 < TP else 1
            for h in range(4):
                for m in range(2):
                    tasks.append((t0, tn, g, h, m, un))
                un += 1

        def A(ti):
            (t0, tn, g, h, m, u) = tasks[ti]
            b = ti % 4
            ps, pk = self.psum()
            self.mm(ps[:, 0:tn], kT[g][:, h, m * 128:(m + 1) * 128], qT[:, h, t0:t0 + tn], True, True, [("ckT", g), ("cq", t0)], [pk])
            self.act(PT[b][:, 0:tn], ps[:, 0:tn], AF.Exp, [pk], [("cP", b)])

        def B(ti):
            (t0, tn, g, h, m, u) = tasks[ti]
            b = ti % 4
            ops_, okk = self.ps[4 + 2 * (u % 2)], ("ps", 4 + 2 * (u % 2))
            sps, skk = self.ps[5 + 2 * (u % 2)], ("ps", 5 + 2 * (u % 2))
            self.mm(ops_[:, 0:tn], Vm[g][:, m, h * 128:(h + 1) * 128], PT[b][:, 0:tn], m == 0, m == 1, [("cV", g), ("cP", b)], [okk])
            self.mm(sps[:, 0:tn], self.onesb[:], PT[b][:, 0:tn], m == 0, m == 1, [("cP", b)], [skk])
            if m == 1:
                self.recip(rinv[:, 0:tn], sps[:, 0:tn], [skk], ["cri"])
                self.tt("dve", at[:, h, t0:t0 + tn], ops_[:, 0:tn], rinv[:, 0:tn], ALU.mult, [okk, "cri"], [("cat", t0)])
        for ti in range(len(tasks) + LA):
            if ti < len(tasks):
                A(ti)
            if ti - LA >= 0:
                B(ti - LA)
        rp, rc = self.resid()
        self.dense(I["w_mem_o"][l], 4, [(g_, 256) for g_ in range(0, D, 256)], at, "cat", [(t0, tn, t0) for (t0, tn) in CH],
                   rc, wv=2, pre=rp)
        self.P.barrier()

    def stage_ffn(self, l):
        self.off = self.lbase
        I = self.i
        NJ = DFF // 128
        cwc = self.sb("fcw", [128, 3, 88], F32)
        stc = self.sb("fst", [128, 2, 88], F32)
        for r in range(3):
            self.colload(cwc[:, r, :], I["ffn_conv_w"][l][r].rearrange("(f p) -> f p", p=128), 88, "fcw")
        for r in range(2):
            self.colload(stc[:, r, :], I["state_ffn_conv"][l][r].rearrange("(f p) -> f p", p=128), 88, "fst")
        tails = self.sb("ftl", [128, 88, 2], F32)
        otl = [self.sb("fot%d" % g, [128, 2, 88], F32) for g in range(2)]
        self.evs = [self.sb("evs", [128, 512], F32) for _ in range(4)]
        self.evi = 0
        GW = 1032
        xn = self.sb("fxn", [128, 16, GW], BF16)
        a_off = self.off
        actT = self.sb("fact", [128, NJ, GW], BF16)
        hub = [self.sb("fhub", [128, 2 + 512], F32) for _ in range(4)]
        cvt = [self.sb("fcv", [128, 512], F32) for _ in range(2)]
        W = I["w_up"][l]
        hi = 0
        for (tb, chunks) in ((0, [(0, 512), (512, 512)]), (1024, [(1024, 512), (1536, 512), (2048, 8)])):
            e_off = self.off
            self.off = a_off
            self.P.barrier()
            self.norm_T(self.d["xT"], self.gcols[:, 2, :], chunks, xn, ("fxn", tb), tbase=tb)
            self.P.barrier()
            self.off = e_off
            for jp in range(NJ // 2):
                wu = self.next_w()
                self.dma(wu[0][:, 0:16, 0:256], W[:, jp * 256:(jp + 1) * 256].rearrange("(k p) c -> p k c", p=128), [], [wu[3]], eng="pool")
                wg = self.next_w()
                self.dma(wg[0][:, 0:16, 0:256], W[:, DFF + jp * 256:DFF + (jp + 1) * 256].rearrange("(k p) c -> p k c", p=128), [], [wg[3]], eng="pool")
                for jj in range(2):
                    j = 2 * jp + jj
                    for (t0, tn) in chunks:
                        a0 = t0 - tb
                        res = []
                        for half, wt_ in enumerate((wu, wg)):
                            ft = half * NJ + j
                            ps, pk = self.psum()
                            for k in range(16):
                                self.mm(ps[:, 0:tn], wt_[0][:, k, jj * 128:(jj + 1) * 128], xn[:, k, a0:a0 + tn], k == 0, k == 15,
                                        [wt_[3], (("fxn", tb), t0)], [pk])
                            hb_, hk = hub[hi % 4], ("fhub", hi % 4)
                            hi += 1
                            if t0 == 0:
                                self.memset("pool", hb_[:, 0:2], 0.0, [hk])
                            elif t0 == TP:
                                self.cp("pool", hb_[:, 0:2], stc[:, :, ft], ["fst", hk], [hk])
                            else:
                                self.cp("pool", hb_[:, 0:2], tails[:, ft, :], [("ftl", ft), hk], [hk])
                            self.cp("act", hb_[:, 2:2 + tn], ps[:, 0:tn], [pk, hk], [hk])
                            if t0 + tn == TP or t0 == TP:
                                self.cp("pool", otl[0 if t0 < TP else 1][:, :, ft], hb_[:, tn:tn + 2], [hk], [("fot", ft)])
                            if t0 < TP:
                                self.cp("pool", tails[:, ft, :], hb_[:, tn:tn + 2], [hk], [("ftl", ft)])
                            cv, ck = cvt[half], ("fcv", half)
                            self.ts("dve", cv[:, 0:tn], hb_[:, 0:tn], cwc[:, 0, ft:ft + 1], ALU.mult, [hk, "fcw"], [ck])
                            self.stt(cv[:, 0:tn], hb_[:, 1:1 + tn], cwc[:, 1, ft:ft + 1], cv[:, 0:tn], ALU.mult, ALU.add, [hk, "fcw", ck], [ck])
                            self.stt(cv[:, 0:tn], hb_[:, 2:2 + tn], cwc[:, 2, ft:ft + 1], cv[:, 0:tn], ALU.mult, ALU.add, [hk, "fcw", ck], [ck])
                        self.act(cvt[1][:, 0:tn], cvt[1][:, 0:tn], AF.Silu, [("fcv", 1)], [("fcv", 1)])
                        self.tt("dve", actT[:, j, a0:a0 + tn], cvt[1][:, 0:tn], cvt[0][:, 0:tn], ALU.mult, [("fcv", 0), ("fcv", 1)], [(("fact", tb), t0)])
            rp, rc = self.resid()
            self.dense(I["w_down"][l], NJ, [(g_, 128) for g_ in range(0, D, 128)], actT, ("fact", tb),
                       [(t0, tn, t0 - tb) for (t0, tn) in chunks], rc, wv=1, pre=rp)
        for g, pre in enumerate(("p_", "s_")):
            for r in range(2):
                ps, pk = self.psum()
                self.tr(ps[0:88, 0:128], otl[g][:, r, :], [("fot", ft) for ft in range(88)], [pk])
                i = self.evi % 4
                self.evi += 1
                sv, sk = self.evs[i], ("evs", i)
                self.cp("dve", sv[0:88, 0:128], ps[0:88, 0:128], [pk], [sk])
                self.dma(self.o[pre + "ffn_conv"][l][r].rearrange("(f p) -> f p", p=128), sv[0:88, 0:128], [sk], [self.o[pre + "ffn_conv"].name])
        self.P.barrier()

    def stage_final(self):
        self.off = self.lbase
        I = self.i
        gr = self.sb("zgr", [1, D], F32)
        gb = self.sb("zgb", [128, D], F32)
        self.dma(gr[:], I["g_final"], [], ["zgr"])
        for c in range(4):
            ps, pk = self.psum()
            self.mm(ps[:, :], self.ones[0:1, :], gr[0:1, c * 512:(c + 1) * 512], True, True, ["zgr"], [pk])
            self.cp("dve", gb[:, c * 512:(c + 1) * 512], ps[:, :], [pk], ["zgb"])
        xi = [self.sb("zxi", [128, 16, 128], F32) for _ in range(2)]
        xt = [self.sb("zxt", [128, D], F32) for _ in range(2)]
        sqt = self.sb("zsq", [128, D], F32)
        ss = self.sb("zss", [128, 2], F32)
        tiles = [(t * 128, 128, self.o["y_prompt"], t * 128) for t in range(16)] + [(TP, TS, self.o["y_sample"], 0)]
        for n, (t0, nt, dst, r0) in enumerate(tiles):
            a, ak = xi[n % 2], ("zxi", n % 2)
            b, bk = xt[n % 2], ("zxt", n % 2)
            self.dma(a[:, :, 0:nt], self.d["xT"].rearrange("(f p) t -> p f t", p=128)[:, :, t0:t0 + nt], [], [ak])
            for f4 in range(4):
                ps, pk = self.psum()
                for j in range(4):
                    self.tr(ps[0:nt, j * 128:(j + 1) * 128], a[:, f4 * 4 + j, 0:nt], [ak], [pk])
                self.cp("act" if f4 % 2 else "dve", b[0:nt, f4 * 512:(f4 + 1) * 512], ps[0:nt, :], [pk], [bk])
            self.act(sqt[0:nt, :], b[0:nt, :], AF.Square, [bk], ["zsq"])
            self.P.op("dve", (lambda o_, i_: (lambda e: e.reduce_sum(out=o_, in_=i_, axis=AX.X)))(ss[0:nt, 0:1], sqt[0:nt, :]), ["zsq"], ["zss"])
            self.act(ss[0:nt, 1:2], ss[0:nt, 0:1], AF.Sqrt, ["zss"], ["zss"], scale=1.0 / D, bias=self.epsc[0:nt, 0:1])
            self.recip(ss[0:nt, 1:2], ss[0:nt, 1:2], ["zss"], ["zss"])
            self.stt(b[0:nt, :], b[0:nt, :], ss[0:nt, 1:2], gb[0:nt, :], ALU.mult, ALU.mult, [bk, "zss", "zgb"], [bk])
            self.dma(dst[r0:r0 + nt, :], b[0:nt, :], [bk], [dst.name])
        self.P.barrier()

    def build(self):
        with contextlib.ExitStack() as st:
            self.stack = st
            self.setup()
            self.gcols = self.sb("gcols", [128, 4, 16], F32)
            self.epsc = self.sb("epsc", [128, 1], F32)
            self.memset("dve", self.epsc[:], EPS, ["epsc"])
            self.onec = self.sb("onec", [128, 1], F32)
            self.memset("dve", self.onec[:], 1.0, ["onec"])
            self.cl_st = [self.sb("clst", [128, 128], F32) for _ in range(2)]
            self.cl_i = 0
            self.lbase = self.off
            self.stage_input()
            self.stage_bias()
            for l in range(self.NL):
                self.load_gcols(l)
                self.P.barrier()
                self.stage_memkv(l)
                self.stage_in(l)
                self.stage_kvout(l)
                self.stage_lru(l)
                self.stage_ssm(l)
                self.stage_dn(l)
                self.stage_swa(l)
                if self.dbg:
                    self.dma(self.dbg["dbg_mix"], self.d["mixT"], [], [])
                    self.P.barrier()
                self.conv_tail_out(self.d["dnqkvT"], 1536, 3, self.o["p_delta_conv"], self.o["s_delta_conv"], l)
                self.conv_tail_out(self.d["lruxT"], 512, 3, self.o["p_lru_conv"], self.o["s_lru_conv"], l)
                self.stage_wout(l)
                self.stage_cross(l)
                self.stage_ffn(l)
            self.stage_final()
            self.P.emit()
        return self.nc


_CACHE = {}


def run(inputs, NL, stages="all", ncores=8):
    key = (NL, stages)
    if key not in _CACHE:
        _CACHE[key] = K(NL, stages).build()
    nc = _CACHE[key]
    consts = host_consts()
    f = lambda a: np.ascontiguousarray(np.asarray(a, dtype=np.float32))
    in_maps = []
    for c in range(ncores):
        b = c % 4
        m = dict(consts)
        for n, s in W_SHAPES.items():
            a = f(inputs[n])
            if n == "g_final":
                a = a.reshape(1, D)
            elif s[0] == "L":
                a = a[:NL]
            m[n] = np.ascontiguousarray(a)
        m["x_prompt"] = f(inputs["x_prompt"][b])
        m["x_sample"] = f(inputs["x_sample"][c])
        m["mem_prompt"] = f(inputs["mem_prompt"][b])
        for n, s in IN_SHAPES.items():
            if s[0] == "L":
                a = np.asarray(inputs[n])[:NL, c]
                m[n] = np.ascontiguousarray(a.reshape(_shape(s, NL)).astype(np.float32))
        in_maps.append(m)
    res = run_bass_kernel_spmd(nc, in_maps, core_ids=list(range(ncores)))
    return res.results


def assemble(results, NL):
    outs = []
    full = {"y_prompt": (4, TP, D), "y_sample": (8, TS, D)}
    for n, s in OUT_SHAPES:
        if n.startswith("y_p"):
            outs.append(np.stack([results[c][n] for c in range(4)]).reshape(4, TP, D))
        elif n.startswith("y_s"):
            outs.append(np.stack([results[c][n] for c in range(8)]).reshape(8, TS, D))
        else:
            nb = 4 if n.startswith("p_") else 8
            a = np.stack([results[c][n] for c in range(nb)], axis=1)
            outs.append(a)
    shp = {"mem_k": (256, 4, 128), "mem_v": (256, 4, 128), "win_k": (-1, 8, 64), "win_v": (-1, 8, 64)}
    res = []
    for (n, s), a in zip(OUT_SHAPES, outs):
        base = n[2:]
        if base in shp and not n.startswith("y_"):
            a = a.reshape(a.shape[0], a.shape[1], *([a.shape[2]] if shp[base][0] == -1 else [shp[base][0]]), *shp[base][1:])
        res.append(np.ascontiguousarray(a.astype(np.float32)))
    return tuple(res)


def kernel(**inputs):
    results = run(inputs, 4)
    return assemble(results, 4)
```

```python
import contextlib
import math
import numpy as np
import concourse.bass as bass
import concourse.mybir as mybir
from concourse.bass_utils import run_bass_kernel_spmd

F32 = mybir.dt.float32
BF16 = mybir.dt.bfloat16
AF = mybir.ActivationFunctionType
ALU = mybir.AluOpType
AX = mybir.AxisListType

ENGS = ("pe", "act", "dve", "pool", "sp")
NDMASEM = 16
NEG = -30000.0
D = 2048
TP = 2048
TS = 8
TT = TP + TS
DFF = 5632
EPS = 1e-6
CH = [(0, 512), (512, 512), (1024, 512), (1536, 512), (2048, 8)]
GRP = [(0, TP), (TP, TS)]
VL = 2600
SW = 2440
NWB = 3
FFN_XW = 2


class Prog:
    def __init__(self, nc):
        self.nc = nc
        self.ops = {e: [] for e in ENGS}
        self.last_w = {}
        self.readers = {}
        self.pend = {e: set() for e in ENGS}
        self.dmas = []

    def op(self, eng, fn, reads=(), writes=(), dma=False):
        idx = len(self.ops[eng])
        deps = set(self.pend[eng])
        self.pend[eng] = set()
        for k in reads:
            w = self.last_w.get(k)
            if w is not None:
                deps.add(w)
        for k in writes:
            w = self.last_w.get(k)
            if w is not None:
                deps.add(w)
            for e_, r in self.readers.get(k, {}).items():
                if e_ == "_dma":
                    deps.update(r)
                else:
                    deps.add(r)
        me = (eng, idx)
        deps.discard(me)
        if eng == "pe":
            deps = {d for d in deps if d[0] != "pe"}
        self.ops[eng].append(dict(fn=fn, deps=deps, dma=dma, sig=False))
        for k in reads:
            rd = self.readers.setdefault(k, {})
            if dma:
                rd.setdefault("_dma", []).append(me)
            else:
                rd[eng] = me
        for k in writes:
            self.last_w[k] = me
            self.readers[k] = {}
        if dma:
            self.dmas.append(me)
        return me

    def dma(self, eng, out, in_, reads=(), writes=(), **kw):
        return self.op(eng, lambda e: e.dma_start(out=out, in_=in_, **kw),
                       reads=reads, writes=writes, dma=True)

    def barrier(self):
        lasts = set(self.dmas)
        for e in ENGS:
            n = len(self.ops[e])
            if n:
                lasts.add((e, n - 1))
        for e in ENGS:
            self.pend[e] |= lasts
        self.dmas = []
        self.last_w = {}
        self.readers = {}

    def emit(self):
        nc = self.nc
        ops = self.ops
        for e in ENGS:
            for o in ops[e]:
                if o["dma"]:
                    o["sig"] = True
                for (pe_, pi) in o["deps"]:
                    ops[pe_][pi]["sig"] = True
        with contextlib.ExitStack() as st:
            EPOCH = 30000
            csem = {e: [st.enter_context(nc.semaphore("c_%s%d" % (e, i))) for i in range(4)] for e in ENGS}
            dsem = {e: [st.enter_context(nc.semaphore("d_%s%d" % (e, i))) for i in range(NDMASEM)]
                    for e in ("sp", "act", "pool")}
            finals = {}
            for e in ENGS:
                cnt = 0
                dcnt = [0] * NDMASEM
                rr = 0
                for o in ops[e]:
                    if not o["sig"]:
                        continue
                    if o["dma"]:
                        s = rr % NDMASEM
                        rr += 1
                        o["prev"] = (dsem[e][s], dcnt[s])
                        dcnt[s] += 16
                        o["sem"] = (dsem[e][s], dcnt[s], 16)
                    else:
                        o["sem"] = (csem[e][cnt // EPOCH], cnt % EPOCH + 1, 1)
                        cnt += 1
                finals[e] = [(dsem[e][s], dcnt[s]) for s in range(NDMASEM) if dcnt[s] > 0] if e in dsem else []
            block = st.enter_context(nc.Block())

            def run(e, eng):
                waited = {}

                def wait(sem, val):
                    if val <= 0:
                        return
                    key = id(sem)
                    if waited.get(key, 0) >= val:
                        return
                    waited[key] = val
                    eng.wait_ge(sem, val)

                for o in ops[e]:
                    mx = {}
                    for (pe_, pi) in o["deps"]:
                        s = ops[pe_][pi]["sem"]
                        if mx.get(id(s[0]), (None, 0))[1] < s[1]:
                            mx[id(s[0])] = (s[0], s[1])
                    for (s0, v) in mx.values():
                        wait(s0, v)
                    if o["dma"]:
                        wait(*o["prev"])
                    ins = o["fn"](eng)
                    if o["sig"]:
                        ins.then_inc(o["sem"][0], o["sem"][2])
                for (s, v) in finals[e]:
                    wait(s, v)

            @block.tensor
            def _(eng):
                run("pe", eng)

            @block.scalar
            def _(eng):
                run("act", eng)

            @block.vector
            def _(eng):
                run("dve", eng)

            @block.gpsimd
            def _(eng):
                run("pool", eng)

            @block.sync
            def _(eng):
                run("sp", eng)


def _t5_bucket(d):
    if d < 16:
        return d
    v = np.log(np.float32(d) / np.float32(16)) / np.float32(math.log(2048 / 16)) * np.float32(16)
    return min(31, 16 + int(np.float32(v).astype(np.int32)))


def host_consts():
    c = {}
    oh = np.zeros((32, VL), np.float32)
    cv = np.full((1, VL), NEG, np.float32)
    for y in range(VL):
        d = y - 511
        if d < 0 or d > 2048:
            continue
        m = sum(1 for (win, dil) in ((128, 1), (512, 4), (2048, 16)) if d % dil == 0 and d <= win)
        if m == 0:
            continue
        oh[_t5_bucket(d), y] = 1.0
        cv[0, y] = math.log(m)
    c["c_oh"] = oh
    c["c_cv"] = cv
    c["c_ident"] = np.eye(128, dtype=np.float32)
    c["c_anti"] = np.eye(128, dtype=np.float32)[::-1].copy()
    c["c_ones"] = np.ones((128, 128), np.float32)
    i = np.arange(128)
    c["c_neglow"] = np.where(i[None, :] <= i[:, None], 0.0, NEG).astype(np.float32)
    c["c_negup"] = np.where(i[:, None] <= i[None, :], 0.0, NEG).astype(np.float32)
    c["c_strict"] = (i[None, :] < i[:, None]).astype(np.float32)
    sel = np.zeros((4, 4, 128), np.float32)
    for h in range(4):
        sel[h, h, :] = 1.0
    c["c_sel"] = sel.reshape(4, 512)
    return c


CONST_SHAPES = {"c_oh": (32, VL), "c_cv": (1, VL), "c_ident": (128, 128), "c_anti": (128, 128),
                "c_ones": (128, 128), "c_neglow": (128, 128), "c_negup": (128, 128),
                "c_strict": (128, 128), "c_sel": (4, 512)}

W_SHAPES = {
    "rel_bias": (32, 8), "g_mix": ("L", D), "w_in": ("L", D, 5128), "dn_conv_w": ("L", 4, 1536),
    "dn_a_log": ("L", 4), "dn_dt_bias": ("L", 4), "dn_norm_g": ("L", 128),
    "ssm_a_re": ("L", 32, 64), "ssm_a_im": ("L", 32, 64), "ssm_log_dt": ("L", 32),
    "ssm_b_re": ("L", 32, 64, 16), "ssm_b_im": ("L", 32, 64, 16), "ssm_c_re": ("L", 32, 16, 64),
    "ssm_c_im": ("L", 32, 16, 64), "ssm_d": ("L", 512), "ssm_w_glu": ("L", 512, 512), "ssm_b_glu": ("L", 512),
    "lru_conv_w": ("L", 4, 512), "lru_conv_b": ("L", 512), "lru_w_a": ("L", 8, 64, 64), "lru_b_a": ("L", 512),
    "lru_w_x": ("L", 8, 64, 64), "lru_b_x": ("L", 512), "lru_lam": ("L", 512), "w_out": ("L", D, D),
    "g_cross": ("L", D), "g_mem": ("L", D), "w_mem_q": ("L", D, 512), "w_mem_k": ("L", D, 512),
    "w_mem_v": ("L", D, 512), "w_mem_o": ("L", 512, D), "g_ffn": ("L", D), "w_up": ("L", D, 2 * DFF),
    "ffn_conv_w": ("L", 3, 2 * DFF), "w_down": ("L", DFF, D), "g_final": (1, D),
}
IN_SHAPES = {
    "x_prompt": (TP, D), "x_sample": (TS, D), "mem_prompt": (256, D),
    "cache_mem_k": ("L", 256, 512), "cache_mem_v": ("L", 256, 512),
    "cache_win_k": ("L", 2048, 512), "cache_win_v": ("L", 2048, 512),
    "state_delta": ("L", 4, 128, 128), "state_delta_conv": ("L", 3, 1536),
    "state_ssm_re": ("L", 32, 64), "state_ssm_im": ("L", 32, 64), "state_lru": ("L", 512),
    "state_lru_conv": ("L", 3, 512), "state_ffn_conv": ("L", 2, 2 * DFF),
}
OUT_SHAPES = [
    ("y_prompt", (TP, D)), ("y_sample", (TS, D)),
    ("p_mem_k", ("L", 256, 512)), ("p_mem_v", ("L", 256, 512)),
    ("p_win_k", ("L", 2048, 512)), ("p_win_v", ("L", 2048, 512)),
    ("p_delta", ("L", 4, 128, 128)), ("p_delta_conv", ("L", 3, 1536)),
    ("p_ssm_re", ("L", 32, 64)), ("p_ssm_im", ("L", 32, 64)), ("p_lru", ("L", 512)),
    ("p_lru_conv", ("L", 3, 512)), ("p_ffn_conv", ("L", 2, 2 * DFF)),
    ("s_win_k", ("L", 8, 512)), ("s_win_v", ("L", 8, 512)),
    ("s_delta", ("L", 4, 128, 128)), ("s_delta_conv", ("L", 3, 1536)),
    ("s_ssm_re", ("L", 32, 64)), ("s_ssm_im", ("L", 32, 64)), ("s_lru", ("L", 512)),
    ("s_lru_conv", ("L", 3, 512)), ("s_ffn_conv", ("L", 2, 2 * DFF)),
]


def _shape(s, L):
    return [L if v == "L" else v for v in s]


class K:
    def __init__(self, NL, stages="all"):
        self.NL = NL
        self.stages = stages
        nc = self.nc = bass.Bass("TRN2", target_bir_lowering=False)
        self.P = Prog(nc)
        self.i = {}
        for n, s in list(W_SHAPES.items()) + list(IN_SHAPES.items()) + list(CONST_SHAPES.items()):
            self.i[n] = nc.dram_tensor(n, _shape(s, NL), F32, kind="ExternalInput").ap()
        self.o = {}
        for n, s in OUT_SHAPES:
            self.o[n] = nc.dram_tensor(n, _shape(s, NL), F32, kind="ExternalOutput").ap()
        self.dbg = {}
        if stages == "dbg":
            for n, shp in [("dbg_xT", [D, TT]), ("dbg_swaT", [1536, TT]), ("dbg_rs", [128, 512]), ("dbg_xn", [128, 512]), ("dbg_g", [128, 64])]:
                self.dbg[n] = nc.dram_tensor(n, shp, F32, kind="ExternalOutput").ap()
            self.dbg["dbg_mix"] = nc.dram_tensor("dbg_mix", [D, TT], BF16, kind="ExternalOutput").ap()
        self.d = {}
        for n, s, dt in [("xT", [D, TT], F32), ("dnqkvT", [1536, TT], F32), ("dnzT", [512, TT], F32),
                         ("dnbaT", [8, TT], F32), ("ssmuT", [512, TT], F32), ("swaT", [1536, TT], F32),
                         ("lruxT", [512, TT], F32), ("lrugT", [512, TT], F32), ("mixT", [D, TT], BF16),
                         ("memT", [D, 256], F32), ("bvec", [8, VL], F32), ("strip", [8, 128, SW], F32)]:
            self.d[n] = nc.dram_tensor("scr_" + n, s, dt, kind="Internal").ap()
        self.off = 16384
        self.seq = 0

    def sb(self, name, shape, dt):
        nb = int(np.prod(shape[1:])) * (4 if dt == F32 else 2)
        nb = (nb + 63) // 64 * 64
        self.seq += 1
        t = self.nc.alloc_sbuf_tensor_at("%s_%d" % (name, self.seq), shape, dt, offset=self.off)
        self.off += nb
        assert self.off <= 229376, (name, self.off)
        return t

    def mm(self, out, lhsT, rhs, start, stop, reads, writes):
        self.P.op("pe", lambda e: e.matmul(out, lhsT=lhsT, rhs=rhs, start=start, stop=stop), reads, writes)

    def tr(self, out, in_, reads, writes):
        ident = self.ident
        n = in_.shape[0]
        self.P.op("pe", lambda e: e.transpose(out=out, in_=in_, identity=ident[0:n, 0:n]), reads, writes)

    def act(self, out, in_, func, reads, writes, scale=1.0, bias=0.0, accum_out=None, eng="act"):
        kw = {}
        if accum_out is not None:
            kw["accum_out"] = accum_out
        self.P.op("act", lambda e: e.activation(out=out, in_=in_, func=func, scale=scale, bias=bias, **kw), reads, writes)

    def cp(self, eng, out, in_, reads, writes):
        if eng == "act":
            self.P.op("act", lambda e: e.activation(out=out, in_=in_, func=AF.Copy), reads, writes)
        else:
            self.P.op(eng, lambda e: e.tensor_copy(out=out, in_=in_), reads, writes)

    def tt(self, eng, out, in0, in1, op, reads, writes):
        self.P.op(eng, lambda e: e.tensor_tensor(out=out, in0=in0, in1=in1, op=op), reads, writes)

    def ts(self, eng, out, in0, s1, op0, reads, writes, s2=None, op1=None):
        if op1 is None:
            self.P.op(eng, lambda e: e.tensor_scalar(out=out, in0=in0, scalar1=s1, scalar2=None, op0=op0), reads, writes)
        else:
            self.P.op(eng, lambda e: e.tensor_scalar(out=out, in0=in0, scalar1=s1, scalar2=s2, op0=op0, op1=op1), reads, writes)

    def stt(self, out, in0, scalar, in1, op0, op1, reads, writes):
        self.P.op("dve", lambda e: e.scalar_tensor_tensor(out=out, in0=in0, scalar=scalar, in1=in1, op0=op0, op1=op1), reads, writes)

    def recip(self, out, in_, reads, writes):
        self.P.op("dve", lambda e: e.reciprocal(out=out, in_=in_), reads, writes)

    def scan(self, out, d0, d1, init, reads, writes):
        self.P.op("dve", lambda e: e.tensor_tensor_scan(out=out, data0=d0, data1=d1, initial=init, op0=ALU.mult, op1=ALU.add), reads, writes)

    def memset(self, eng, ap, v, writes):
        self.P.op(eng, lambda e: e.memset(ap, v), (), writes)

    def dma(self, out, in_, reads, writes, eng="sp", **kw):
        self.P.dma(eng, out, in_, reads, writes, **kw)

    def psum(self):
        self.psi = (self.psi + 1) % len(self.psr)
        b = self.psr[self.psi]
        return self.ps[b], ("ps", b)

    def setup(self):
        nc = self.nc
        self.ps = [self.stack.enter_context(nc.psum_tensor("ps%d" % b, [128, 512], F32)) for b in range(8)]
        self.psr = [0, 1, 2, 3]
        self.psi = 0
        self.ident = self.sb("ident", [128, 128], F32)
        self.anti = self.sb("anti", [128, 128], F32)
        self.ones = self.sb("ones", [128, 128], F32)
        self.onesb = self.sb("onesb", [128, 128], BF16)
        self.neglow = self.sb("neglow", [128, 128], F32)
        self.negup = self.sb("negup", [128, 128], F32)
        self.strict = self.sb("strict", [128, 128], F32)
        self.sel = self.sb("sel", [4, 512], F32)
        for t, n in [(self.ident, "c_ident"), (self.anti, "c_anti"), (self.ones, "c_ones"), (self.neglow, "c_neglow"),
                     (self.negup, "c_negup"), (self.strict, "c_strict"), (self.sel, "c_sel")]:
            self.dma(t[:], self.i[n], [], [n])
        self.cp("dve", self.onesb[:], self.ones[:], ["c_ones"], ["onesb"])
        self.wb = []
        for j in range(NWB):
            o = self.off
            a = nc.alloc_sbuf_tensor_at("wA%d" % j, [128, 16, 256], BF16, offset=o)
            b = nc.alloc_sbuf_tensor_at("wB%d" % j, [128, 44, 128], BF16, offset=o)
            c = nc.alloc_sbuf_tensor_at("wC%d" % j, [128, 4, 256], BF16, offset=o)
            self.wb.append((a, b, c, ("w", j)))
            self.off += 11264
        self.wbi = 0
        self.wring = self.wb
        self.base = self.off
        self.P.barrier()

    def next_w(self):
        self.wbi = (self.wbi + 1) % len(self.wring)
        return self.wring[self.wbi]

    def ring_extra(self, n):
        self.wring = list(self.wb)
        for j in range(n):
            o = self.off
            self.seq += 1
            a = self.nc.alloc_sbuf_tensor_at("wAx%d_%d" % (j, self.seq), [128, 16, 256], BF16, offset=o)
            b = self.nc.alloc_sbuf_tensor_at("wBx%d_%d" % (j, self.seq), [128, 44, 128], BF16, offset=o)
            c = self.nc.alloc_sbuf_tensor_at("wCx%d_%d" % (j, self.seq), [128, 4, 256], BF16, offset=o)
            self.wring.append((a, b, c, ("w", NWB + j)))
            self.off += 11264
            assert self.off <= 229376, ("ring_extra", self.off)

    def colload(self, dst, src_rows, n, key):
        st = self.cl_st[self.cl_i % 2]
        sk = ("clst", self.cl_i % 2)
        self.cl_i += 1
        self.dma(st[0:n, :], src_rows, [], [sk])
        ps, pk = self.psum()
        self.tr(ps[:, 0:n], st[0:n, :], [sk, "c_ident"], [pk])
        self.cp("dve", dst, ps[:, 0:n], [pk], [key])

    def stage_input(self):
        self.off = self.lbase
        xin = [self.sb("xin", [128, D], F32) for _ in range(2)]
        xo = [self.sb("xo", [128, 16, 128], F32) for _ in range(2)]
        tiles = [(self.i["x_prompt"], t * 128, 128, t * 128, "xT") for t in range(16)]
        tiles.append((self.i["x_sample"], 0, TS, TP, "xT"))
        tiles += [(self.i["mem_prompt"], t * 128, 128, t * 128, "memT") for t in range(2)]
        for n, (src, r0, nr, c0, dst) in enumerate(tiles):
            xi, xk = xin[n % 2], ("xin", n % 2)
            xot, ok = xo[n % 2], ("xo", n % 2)
            self.dma(xi[0:nr, :], src[r0:r0 + nr, :], [], [xk])
            for f4 in range(4):
                ps, pk = self.psum()
                for j in range(4):
                    f = f4 * 4 + j
                    self.tr(ps[:, j * 128:j * 128 + nr], xi[0:nr, f * 128:(f + 1) * 128], [xk, "c_ident"], [pk])
                src_ps = ps[:].rearrange("p (j t) -> p j t", j=4)[:, :, 0:nr]
                self.cp("dve" if f4 % 2 else "act", xot[:, f4 * 4:f4 * 4 + 4, 0:nr], src_ps, [pk], [ok])
            self.dma(self.d[dst].rearrange("(f p) t -> p f t", p=128)[:, :, c0:c0 + nr], xot[:, :, 0:nr], [ok], [self.d[dst].name])
        self.P.barrier()

    def norm_T(self, src, gcol, chunks, dst, dkey, tbase=0):
        NS = 256
        xl = [self.sb("nxl", [128, 16, NS], F32) for _ in range(2)]
        sq = [self.sb("nsq", [128, NS], F32) for _ in range(2)]
        rs = self.sb("nrs", [128, NS], F32)
        ci = 0
        for (T0, TN) in chunks:
            for t0 in range(T0, T0 + TN, NS):
                tn = min(NS, T0 + TN - t0)
                x, xk = xl[ci % 2], ("nxl", ci % 2)
                ci += 1
                self.dma(x[:, :, 0:tn], src.rearrange("(f p) t -> p f t", p=128)[:, :, t0:t0 + tn], [src.name], [xk])
                ps, pk = self.psum()
                for f in range(16):
                    s, sk = sq[f % 2], ("nsq", f % 2)
                    self.act(s[:, 0:tn], x[:, f, 0:tn], AF.Square, [xk], [sk])
                    self.mm(ps[:, 0:tn], self.ones[:], s[:, 0:tn], f == 0, f == 15, [sk, "c_ones"], [pk])
                self.act(rs[:, 0:tn], ps[:, 0:tn], AF.Sqrt, [pk], ["nrs"], scale=1.0 / D, bias=self.epsc[:, 0:1])
                self.recip(rs[:, 0:tn], rs[:, 0:tn], ["nrs"], ["nrs"])
                for f in range(16):
                    self.stt(dst[:, f, t0 - tbase:t0 - tbase + tn], x[:, f, 0:tn], gcol[:, f:f + 1], rs[:, 0:tn],
                             ALU.mult, ALU.mult, [xk, "nrs", "gcols"], [(dkey, T0)])

    def dense(self, Wd, nK, groups, act, akey, chunks, consume, wv=0, pre=None):
        for (c0, w) in groups:
            wt = self.next_w()
            wtile, wk = wt[wv], wt[3]
            self.dma(wtile[:, 0:nK, 0:w], Wd[:, c0:c0 + w].rearrange("(k p) c -> p k c", p=128), [], [wk], eng="pool")
            tiles = [(m0, min(128, w - m0), t0, tn, a0) for m0 in range(0, w, 128) for (t0, tn, a0) in chunks]
            if pre is not None:
                pre(c0 + tiles[0][0], tiles[0][1], tiles[0][2], tiles[0][3])
            for ti, (m0, mw, t0, tn, a0) in enumerate(tiles):
                ps, pk = self.psum()
                for k in range(nK):
                    self.mm(ps[0:mw, 0:tn], wtile[:, k, m0:m0 + mw], act[:, k, a0:a0 + tn], k == 0, k == nK - 1,
                            [wk, (akey, t0), akey], [pk])
                if pre is not None and ti + 1 < len(tiles):
                    n_ = tiles[ti + 1]
                    pre(c0 + n_[0], n_[1], n_[2], n_[3])
                consume(c0 + m0, mw, t0, tn, ps, pk)

    def to_dram(self, dst, r0):
        def f(c, mw, t0, tn, ps, pk):
            i = self.evi % 4
            self.evi += 1
            s, sk = self.evs[i], ("evs", i)
            self.cp("act" if i % 2 else "dve", s[0:mw, 0:tn], ps[0:mw, 0:tn], [pk], [sk])
            self.dma(dst[c - r0:c - r0 + mw, t0:t0 + tn], s[0:mw, 0:tn], [sk], [dst.name])
        return f

    def load_gcols(self, l):
        for j, n in enumerate(["g_mix", "g_cross", "g_ffn", "g_mem"]):
            self.colload(self.gcols[:, j, :], self.i[n][l].rearrange("(f p) -> f p", p=128), 16, "gcols")

    def stage_in(self, l):
        self.off = self.lbase
        xn = self.sb("xnT", [128, 16, TT], BF16)
        self.evs = [self.sb("evs", [128, 512], F32) for _ in range(4)]
        self.evi = 0
        self.norm_T(self.d["xT"], self.gcols[:, 0, :], CH, xn, "xnT")
        if self.dbg:
            self.P.barrier()
            dt = self.sb("dbgt", [128, 512], F32)
            self.cp("dve", dt[:], xn[:, 0, 0:512], [], ["dbgt"])
            self.dma(self.dbg["dbg_xn"], dt[:], ["dbgt"], [])
            self.dma(self.dbg["dbg_rs"], self.last_rs[:], [], [])
            self.dma(self.dbg["dbg_g"], self.gcols[:].rearrange("p a b -> p (a b)"), [], [])
            self.dma(self.dbg["dbg_xT"], self.d["xT"], [], [])
            self.P.barrier()
        W = self.i["w_in"][l]
        ch3 = [(t0, tn, t0) for (t0, tn) in CH]
        segs = [(0, 1536, "dnqkvT"), (1536, 512, "dnzT"), (2048, 8, "dnbaT"), (2056, 512, "ssmuT"),
                (2568, 1536, "swaT"), (4104, 512, "lruxT"), (4616, 512, "lrugT")]
        for (c0, n, dn) in segs:
            groups = [(c0 + g, min(256, n - g)) for g in range(0, n, 256)]
            self.dense(W, 16, groups, xn, "xnT", ch3, self.to_dram(self.d[dn], c0))
        self.P.barrier()
        if self.dbg:
            self.dma(self.dbg["dbg_swaT"], self.d["swaT"], [], [])
            self.P.barrier()

    def tok_major_out(self, srcT, r0, ncol, t0, nt, dst, c0, extra=None):
        assert ncol % 128 == 0
        nf = ncol // 128
        st = [self.sb("tmi", [128, nf, 512], F32) for _ in range(2)]
        so = [self.sb("tmo", [128, ncol], F32) for _ in range(2)]
        n = 0
        for tc in range(0, nt, 512):
            tw = min(512, nt - tc)
            s, sk = st[(tc // 512) % 2], ("tmi", (tc // 512) % 2)
            self.dma(s[:, :, 0:tw], srcT[r0:r0 + ncol, :].rearrange("(f p) t -> p f t", p=128)[:, :, t0 + tc:t0 + tc + tw],
                     [srcT.name], [sk])
            for tb in range(0, tw, 128):
                bw = min(128, tw - tb)
                o, ok = so[n % 2], ("tmo", n % 2)
                n += 1
                for f4 in range(0, nf, 4):
                    ps, pk = self.psum()
                    for j in range(min(4, nf - f4)):
                        self.tr(ps[0:bw, j * 128:(j + 1) * 128], s[:, f4 + j, tb:tb + bw], [sk, "c_ident"], [pk])
                    w = min(4, nf - f4) * 128
                    self.cp("act" if (f4 // 4) % 2 else "dve", o[0:bw, f4 * 128:f4 * 128 + w], ps[0:bw, 0:w], [pk], [ok])
                self.dma(dst[tc + tb:tc + tb + bw, c0:c0 + ncol], o[0:bw, :], [ok], [dst.name])
                if extra is not None:
                    extra(tc + tb, bw, o, ok)

    def stage_kvout(self, l):
        self.off = self.lbase
        for (t0, nt), pre in zip(GRP, ("p", "s")):
            self.tok_major_out(self.d["swaT"], 512, 512, t0, nt, self.o[pre + "_win_k"][l], 0)
            self.off = self.lbase
            self.tok_major_out(self.d["swaT"], 1024, 512, t0, nt, self.o[pre + "_win_v"][l], 0)
            self.off = self.lbase
        self.P.barrier()

    def conv_tail_out(self, srcT, nrow, ntail, dstp, dsts, l):
        for (t0, nt), dst in zip(GRP, (dstp, dsts)):
            self.off = self.lbase
            for c0 in range(0, nrow, 512):
                self.tok_major_out(srcT, c0, min(512, nrow - c0), t0 + nt - ntail, ntail, dst[l], c0)
                self.off = self.lbase
        self.P.barrier()

    def stage_memkv(self, l):
        self.off = self.lbase
        mn = self.sb("mnT", [128, 16, 256], BF16)
        self.norm_T(self.d["memT"], self.gcols[:, 3, :], [(0, 256)], mn, "mnT")
        so = [self.sb("mko", [128, 512], F32) for _ in range(2)]
        n = 0
        for wname, oname in (("w_mem_k", "p_mem_k"), ("w_mem_v", "p_mem_v")):
            W = self.i[wname][l]
            wts = []
            for half in range(2):
                wt = self.next_w()
                self.dma(wt[0][:, 0:16, 0:256], W[:, half * 256:(half + 1) * 256].rearrange("(k p) c -> p k c", p=128),
                         [], [wt[3]], eng="pool")
                wts.append(wt)
            for tt in range(2):
                o, ok = so[n % 2], ("mko", n % 2)
                n += 1
                for half in range(2):
                    ps, pk = self.psum()
                    for k in range(16):
                        self.mm(ps[:, 0:256], mn[:, k, tt * 128:(tt + 1) * 128], wts[half][0][:, k, 0:256], k == 0, k == 15,
                                [wts[half][3], ("mnT", 0)], [pk])
                    self.cp("dve" if half else "act", o[:, half * 256:(half + 1) * 256], ps[:, 0:256], [pk], [ok])
                self.dma(self.o[oname][l][tt * 128:(tt + 1) * 128, :], o[:], [ok], [self.o[oname].name])
        self.P.barrier()

    def stage_lru(self, l):
        self.off = self.lbase
        I = self.i
        lp = self.sb("lrup", [128, 40], F32)
        self.colload(lp[:, 0:16], I["lru_conv_w"][l].rearrange("j (f p) -> (j f) p", p=128), 16, "lrup")
        for j, n in enumerate(["lru_conv_b", "lru_b_a", "lru_b_x", "lru_lam"]):
            self.colload(lp[:, 16 + 4 * j:20 + 4 * j], I[n][l].rearrange("(f p) -> f p", p=128), 4, "lrup")
        self.act(lp[:, 32:36], lp[:, 28:32], AF.Exp, ["lrup"], ["lrup"], scale=-1.0)
        self.act(lp[:, 32:36], lp[:, 32:36], AF.Ln, ["lrup"], ["lrup"], bias=self.onec[:, 0:1])
        self.ts("dve", lp[:, 32:36], lp[:, 32:36], -8.0, ALU.mult, ["lrup"], ["lrup"])
        WA = self.sb("lruWA", [128, 4, 128], F32)
        WX = self.sb("lruWX", [128, 4, 128], F32)
        self.memset("dve", WA[:], 0.0, ["lruW"])
        self.memset("dve", WX[:], 0.0, ["lruW"])
        for k in range(8):
            hb = (k % 2) * 64
            self.dma(WA[hb:hb + 64, k // 2, hb:hb + 64], I["lru_w_a"][l][k], [], ["lruW"])
            self.dma(WX[hb:hb + 64, k // 2, hb:hb + 64], I["lru_w_x"][l][k], [], ["lruW"])
        T = TP
        xpad = self.sb("l_xpad", [128, 3 + T], F32)
        xc = self.sb("l_xc", [128, T], F32)
        r = self.sb("l_r", [128, T], F32)
        ig = self.sb("l_ig", [128, T], F32)
        a = self.sb("l_a", [128, T], F32)
        bx = self.sb("l_bx", [128, T], F32)
        g = self.sb("l_g", [128, T], F32)
        u = self.sb("l_u", [128, T], F32)
        yb = self.sb("l_yb", [128, T], BF16)
        h0 = self.sb("l_h0", [128, 4], F32)
        self.colload(h0[:, 0:4], I["state_lru"][l].rearrange("(f p) -> f p", p=128), 4, "l_h0")
        for i in range(4):
            for gi, (t0, T) in enumerate(GRP):
                pre = "ps"[gi] + "_"
                if gi == 0:
                    self.memset("pool", xpad[:, 0:3], 0.0, ["l_xpad"])
                else:
                    self.colload(xpad[:, 0:3], I["state_lru_conv"][l][:, i * 128:(i + 1) * 128], 3, "l_xpad")
                self.dma(xpad[:, 3:3 + T], self.d["lruxT"][i * 128:(i + 1) * 128, t0:t0 + T], [], ["l_xpad"])
                self.dma(g[:, 0:T], self.d["lrugT"][i * 128:(i + 1) * 128, t0:t0 + T], [], ["l_g"])
                self.ts("dve", xc[:, 0:T], xpad[:, 0:T], lp[:, i:i + 1], ALU.mult, ["l_xpad", "lrup"], ["l_xc"],
                        s2=lp[:, 16 + i:17 + i], op1=ALU.add)
                for j in range(1, 4):
                    self.stt(xc[:, 0:T], xpad[:, j:j + T], lp[:, 4 * j + i:4 * j + i + 1], xc[:, 0:T], ALU.mult, ALU.add,
                             ["l_xpad", "lrup", "l_xc"], ["l_xc"])
                for c0 in range(0, T, 512):
                    cw = min(512, T - c0)
                    ps, pk = self.psum()
                    self.mm(ps[:, 0:cw], WA[:, i, :], xc[:, c0:c0 + cw], True, True, ["lruW", "l_xc"], [pk])
                    self.act(r[:, c0:c0 + cw], ps[:, 0:cw], AF.Sigmoid, [pk, "lrup"], ["l_r"], bias=lp[:, 20 + i:21 + i])
                    ps, pk = self.psum()
                    self.mm(ps[:, 0:cw], WX[:, i, :], xc[:, c0:c0 + cw], True, True, ["lruW", "l_xc"], [pk])
                    self.act(ig[:, c0:c0 + cw], ps[:, 0:cw], AF.Sigmoid, [pk, "lrup"], ["l_ig"], bias=lp[:, 24 + i:25 + i])
                self.act(a[:, 0:T], r[:, 0:T], AF.Exp, ["l_r", "lrup"], ["l_a"], scale=lp[:, 32 + i:33 + i])
                self.tt("pool", bx[:, 0:T], a[:, 0:T], a[:, 0:T], ALU.mult, ["l_a"], ["l_bx"])
                self.ts("dve", bx[:, 0:T], bx[:, 0:T], -1.0, ALU.mult, ["l_bx"], ["l_bx"], s2=1.0, op1=ALU.add)
                self.ts("dve", bx[:, 0:T], bx[:, 0:T], 0.0, ALU.max, ["l_bx"], ["l_bx"])
                self.act(bx[:, 0:T], bx[:, 0:T], AF.Sqrt, ["l_bx"], ["l_bx"])
                self.tt("pool", ig[:, 0:T], ig[:, 0:T], xc[:, 0:T], ALU.mult, ["l_ig", "l_xc"], ["l_ig"])
                self.tt("dve", bx[:, 0:T], bx[:, 0:T], ig[:, 0:T], ALU.mult, ["l_bx", "l_ig"], ["l_bx"])
                init = 0.0 if gi == 0 else h0[:, i:i + 1]
                self.scan(r[:, 0:T], a[:, 0:T], bx[:, 0:T], init, ["l_a", "l_bx", "l_h0", "l_r"], ["l_r"])
                self.dma(self.o[pre + "lru"][l][i * 128:(i + 1) * 128].rearrange("(p o) -> p o", o=1), r[:, T - 1:T],
                         ["l_r"], [self.o[pre + "lru"].name])
                self.tt("pool", u[:, 0:T], g[:, 0:T], g[:, 0:T], ALU.mult, ["l_g"], ["l_u"])
                self.ts("dve", u[:, 0:T], u[:, 0:T], 0.044715, ALU.mult, ["l_u"], ["l_u"], s2=1.0, op1=ALU.add)
                self.tt("dve", u[:, 0:T], u[:, 0:T], g[:, 0:T], ALU.mult, ["l_u", "l_g"], ["l_u"])
                self.act(u[:, 0:T], u[:, 0:T], AF.Sigmoid, ["l_u"], ["l_u"], scale=1.5957691216057308)
                self.tt("pool", u[:, 0:T], u[:, 0:T], g[:, 0:T], ALU.mult, ["l_u", "l_g"], ["l_u"])
                self.tt("dve", yb[:, 0:T], u[:, 0:T], r[:, 0:T], ALU.mult, ["l_u", "l_r"], ["l_yb"])
                self.dma(self.d["mixT"][1536 + i * 128:1536 + (i + 1) * 128, t0:t0 + T], yb[:, 0:T], ["l_yb"], ["mixT"])
        self.P.barrier()

    def gelu(self, out, x, tmp, kx, kt, ko):
        self.tt("pool", tmp, x, x, ALU.mult, [kx], [kt])
        self.ts("dve", tmp, tmp, 0.044715, ALU.mult, [kt], [kt], s2=1.0, op1=ALU.add)
        self.tt("dve", tmp, tmp, x, ALU.mult, [kt, kx], [kt])
        self.act(tmp, tmp, AF.Sigmoid, [kt], [kt], scale=1.5957691216057308)
        self.tt("dve", out, tmp, x, ALU.mult, [kt, kx], [ko] if ko != kt else [kt])

    def stage_ssm(self, l):
        self.off = self.lbase
        I = self.i
        PI = math.pi
        sp = self.sb("ssp", [128, 24, 16], F32)
        names = ["are", "aim", "ldt", "step", "ars", "ais", "mag", "c1", "s1", "abr", "abi", "den", "cr", "ci",
                 "t0", "t1", "h0r", "h0i", "g0r", "g0i", "t2", "t3"]
        ix = {n: k for k, n in enumerate(names)}
        P_ = lambda n: sp[:, ix[n], :]
        K_ = "ssp"
        self.colload(P_("are"), I["ssm_a_re"][l].rearrange("(j a) n -> j (a n)", a=2), 16, K_)
        self.colload(P_("aim"), I["ssm_a_im"][l].rearrange("(j a) n -> j (a n)", a=2), 16, K_)
        self.colload(P_("h0r"), I["state_ssm_re"][l].rearrange("(j a) n -> j (a n)", a=2), 16, K_)
        self.colload(P_("h0i"), I["state_ssm_im"][l].rearrange("(j a) n -> j (a n)", a=2), 16, K_)
        ldr = self.sb("ldr", [1, 32], F32)
        self.dma(ldr[:], I["ssm_log_dt"][l:l + 1, :], [], ["ldr"])
        ps, pk = self.psum()
        self.mm(ps[:, 0:32], self.ones[0:1, :], ldr[0:1, :], True, True, ["ldr"], [pk])
        pv = ps[:, 0:32].rearrange("p (j a) -> p j a", a=2)
        self.cp("dve", P_("ldt")[0:64, :], pv[0:64, :, 0], [pk], [K_])
        self.cp("dve", P_("ldt")[64:128, :], pv[64:128, :, 1], [pk], [K_])
        self.act(P_("step"), P_("ldt"), AF.Exp, [K_], [K_])
        self.tt("dve", P_("ars"), P_("are"), P_("step"), ALU.mult, [K_], [K_])
        self.tt("dve", P_("ais"), P_("aim"), P_("step"), ALU.mult, [K_], [K_])
        self.act(P_("mag"), P_("ars"), AF.Exp, [K_], [K_])
        it = self.sb("ssi", [128, 16], mybir.dt.int32)

        def sin_of(dst, src, shift):
            self.ts("dve", P_("t0"), src, shift, ALU.add, [K_], [K_])
            self.ts("dve", P_("t1"), P_("t0"), 1.0 / (2 * PI), ALU.mult, [K_], [K_])
            self.cp("dve", it[:], P_("t1"), [K_], ["ssi"])
            self.cp("dve", P_("t1"), it[:], ["ssi"], [K_])
            self.stt(P_("t0"), P_("t1"), -2 * PI, P_("t0"), ALU.mult, ALU.add, [K_], [K_])
            self.ts("dve", P_("t1"), P_("t0"), PI, ALU.is_gt, [K_], [K_])
            self.stt(P_("t0"), P_("t1"), -2 * PI, P_("t0"), ALU.mult, ALU.add, [K_], [K_])
            self.ts("dve", P_("t1"), P_("t0"), -PI, ALU.is_lt, [K_], [K_])
            self.stt(P_("t0"), P_("t1"), 2 * PI, P_("t0"), ALU.mult, ALU.add, [K_], [K_])
            self.act(dst, P_("t0"), AF.Sin, [K_], [K_])
        sin_of(P_("s1"), P_("ais"), 0.0)
        sin_of(P_("c1"), P_("ais"), PI / 2)
        self.tt("dve", P_("abr"), P_("mag"), P_("c1"), ALU.mult, [K_], [K_])
        self.tt("dve", P_("abi"), P_("mag"), P_("s1"), ALU.mult, [K_], [K_])
        self.tt("dve", P_("den"), P_("are"), P_("are"), ALU.mult, [K_], [K_])
        self.tt("dve", P_("t0"), P_("aim"), P_("aim"), ALU.mult, [K_], [K_])
        self.tt("dve", P_("den"), P_("den"), P_("t0"), ALU.add, [K_], [K_])
        self.recip(P_("den"), P_("den"), [K_], [K_])
        self.ts("dve", P_("t2"), P_("abr"), -1.0, ALU.add, [K_], [K_])
        self.tt("dve", P_("t0"), P_("t2"), P_("are"), ALU.mult, [K_], [K_])
        self.tt("dve", P_("t1"), P_("abi"), P_("aim"), ALU.mult, [K_], [K_])
        self.tt("dve", P_("cr"), P_("t0"), P_("t1"), ALU.add, [K_], [K_])
        self.tt("dve", P_("cr"), P_("cr"), P_("den"), ALU.mult, [K_], [K_])
        self.tt("dve", P_("t0"), P_("abi"), P_("are"), ALU.mult, [K_], [K_])
        self.tt("dve", P_("t1"), P_("t2"), P_("aim"), ALU.mult, [K_], [K_])
        self.tt("dve", P_("ci"), P_("t0"), P_("t1"), ALU.subtract, [K_], [K_])
        self.tt("dve", P_("ci"), P_("ci"), P_("den"), ALU.mult, [K_], [K_])
        self.tt("dve", P_("t0"), P_("c1"), P_("h0r"), ALU.mult, [K_], [K_])
        self.tt("dve", P_("t1"), P_("s1"), P_("h0i"), ALU.mult, [K_], [K_])
        self.tt("dve", P_("g0r"), P_("t0"), P_("t1"), ALU.subtract, [K_], [K_])
        self.tt("dve", P_("t0"), P_("c1"), P_("h0i"), ALU.mult, [K_], [K_])
        self.tt("dve", P_("t1"), P_("s1"), P_("h0r"), ALU.mult, [K_], [K_])
        self.tt("dve", P_("g0i"), P_("t0"), P_("t1"), ALU.add, [K_], [K_])
        pwc = self.sb("spwc", [128, 12, 16], F32)
        pws = self.sb("spws", [128, 12, 16], F32)
        self.cp("dve", pwc[:, 0, :], P_("c1"), [K_], ["spw"])
        self.cp("dve", pws[:, 0, :], P_("s1"), [K_], ["spw"])
        for k in range(11):
            self.tt("dve", P_("t0"), pwc[:, k, :], pwc[:, k, :], ALU.mult, ["spw", K_], [K_])
            self.tt("dve", P_("t1"), pws[:, k, :], pws[:, k, :], ALU.mult, ["spw", K_], [K_])
            self.tt("dve", pwc[:, k + 1, :], P_("t0"), P_("t1"), ALU.subtract, [K_], ["spw"])
            self.tt("dve", P_("t0"), pwc[:, k, :], pws[:, k, :], ALU.mult, ["spw", K_], [K_])
            self.ts("dve", pws[:, k + 1, :], P_("t0"), 2.0, ALU.mult, [K_], ["spw"])
        BR = self.sb("sBR", [128, 16, 16], F32)
        BI = self.sb("sBI", [128, 16, 16], F32)
        QR = self.sb("sQR", [128, 16, 16], F32)
        QI = self.sb("sQI", [128, 16, 16], F32)
        TM = self.sb("sTM", [128, 16, 16], F32)
        self.dma(BR[:], I["ssm_b_re"][l].rearrange("(j a) n c -> (a n) j c", a=2), [], ["sB"])
        self.dma(BI[:], I["ssm_b_im"][l].rearrange("(j a) n c -> (a n) j c", a=2), [], ["sB"])
        crb = P_("cr").unsqueeze(2).to_broadcast([128, 16, 16])
        cib = P_("ci").unsqueeze(2).to_broadcast([128, 16, 16])
        self.tt("dve", QR[:], BR[:], crb, ALU.mult, ["sB", K_], ["sQ"])
        self.tt("dve", TM[:], BI[:], cib, ALU.mult, ["sB", K_], ["sTM"])
        self.tt("dve", QR[:], QR[:], TM[:], ALU.subtract, ["sQ", "sTM"], ["sQ"])
        self.tt("dve", QI[:], BI[:], crb, ALU.mult, ["sB", K_], ["sQ"])
        self.tt("dve", TM[:], BR[:], cib, ALU.mult, ["sB", K_, "sQ"], ["sTM"])
        self.tt("dve", QI[:], QI[:], TM[:], ALU.add, ["sQ", "sTM"], ["sQ"])
        LB = [self.sb("sLB%d" % q, [128, 16, 128], F32) for q in range(2)]
        WC = [self.sb("sWC%d" % q, [128, 16, 128], F32) for q in range(2)]
        Z = self.sb("sZ", [128, 128], F32)
        for q, Q in enumerate((QR, QI)):
            for j in range(16):
                c0 = 32 * (j % 4)
                self.memset("pool", Z[:], 0.0, ["sZ"])
                self.cp("pool", Z[0:64, c0:c0 + 16], Q[0:64, j, :], ["sQ", "sZ"], ["sZ"])
                self.cp("pool", Z[64:128, c0 + 16:c0 + 32], Q[64:128, j, :], ["sQ", "sZ"], ["sZ"])
                ps, pk = self.psum()
                self.tr(ps[:, 0:128], Z[:], ["sZ"], [pk])
                self.cp("dve", LB[q][:, j, :], ps[:, 0:128], [pk], ["sLB"])
        V = self.sb("sV", [32, 16, 128], F32)
        for q, nm in enumerate(("ssm_c_re", "ssm_c_im")):
            self.memset("pool", V[:], 0.0, ["sV"])
            self.memset("pool", WC[q][:], 0.0, ["sWC"])
            cv = I[nm][l].rearrange("(j a) c n -> a c j n", a=2)
            self.dma(V[0:16, :, 0:64], cv[0], ["sV"], ["sV"])
            self.dma(V[16:32, :, 64:128], cv[1], ["sV"], ["sV"])
            for j in range(16):
                c0 = 32 * (j % 4)
                ps, pk = self.psum()
                self.tr(ps[:, 0:32], V[:, j, :], ["sV"], [pk])
                if q == 0:
                    self.cp("dve", WC[q][:, j, c0:c0 + 32], ps[:, 0:32], [pk, "sWC"], ["sWC"])
                else:
                    self.ts("dve", WC[q][:, j, c0:c0 + 32], ps[:, 0:32], -1.0, ALU.mult, [pk, "sWC"], ["sWC"])
        dcol = self.sb("sdc", [128, 8], F32)
        self.colload(dcol[:, 0:4], I["ssm_d"][l].rearrange("(f p) -> f p", p=128), 4, "sdc")
        self.colload(dcol[:, 4:8], I["ssm_b_glu"][l].rearrange("(f p) -> f p", p=128), 4, "sdc")
        T = TP
        big = {n: self.sb("s_" + n, [128, T], F32) for n in ("xr", "xi", "ec", "es", "t1", "t2", "tm")}
        u = self.sb("s_u", [128, TT], F32)
        yg = self.sb("s_yg", [128, 4, TT], F32)
        ygb = self.sb("s_ygb", [128, 4, TT], BF16)
        xr, xi, ec, es, t1, t2, tm = (big[n] for n in ("xr", "xi", "ec", "es", "t1", "t2", "tm"))
        for yi in range(4):
            self.dma(u[:], self.d["ssmuT"][yi * 128:(yi + 1) * 128, :], [], ["s_u"])
            for gi, (t0, T) in enumerate(GRP):
                pre = "ps"[gi] + "_"
                nch = [(c0, min(512, T - c0)) for c0 in range(0, T, 512)]
                for jj in range(4):
                    j = 4 * yi + jj
                    for (dst, q, kk) in ((xr, 0, "s_xr"), (xi, 1, "s_xi")):
                        for (c0, cw) in nch:
                            ps, pk = self.psum()
                            self.mm(ps[:, 0:cw], LB[q][:, j, :], u[:, t0 + c0:t0 + c0 + cw], True, True, ["sLB", "s_u"], [pk])
                            self.cp("act", dst[:, c0:c0 + cw], ps[:, 0:cw], [pk], [kk])
                    self.memset("pool", ec[:, 0:1], 1.0, ["s_ec"])
                    self.memset("pool", es[:, 0:1], 0.0, ["s_es"])
                    n = 1
                    k = 0
                    while n < T:
                        cn, sn = pwc[:, k, j:j + 1], pws[:, k, j:j + 1]
                        self.ts("dve", tm[:, 0:n], es[:, 0:n], sn, ALU.mult, ["s_es", "spw"], ["s_tm"])
                        self.stt(ec[:, n:2 * n], ec[:, 0:n], cn, tm[:, 0:n], ALU.mult, ALU.subtract, ["s_ec", "s_tm", "spw"], ["s_ec2"])
                        self.ts("dve", tm[:, 0:n], ec[:, 0:n], sn, ALU.mult, ["s_ec", "spw", "s_ec2"], ["s_tm"])
                        self.stt(es[:, n:2 * n], es[:, 0:n], cn, tm[:, 0:n], ALU.mult, ALU.add, ["s_es", "s_tm", "spw"], ["s_es"])
                        self.P.last_w["s_ec"] = self.P.last_w["s_ec2"]
                        n *= 2
                        k += 1
                    self.tt("pool", t1[:, 0:T], ec[:, 0:T], xr[:, 0:T], ALU.mult, ["s_ec", "s_xr"], ["s_t1"])
                    self.tt("dve", tm[:, 0:T], es[:, 0:T], xi[:, 0:T], ALU.mult, ["s_es", "s_xi"], ["s_tm"])
                    self.tt("pool", t1[:, 0:T], t1[:, 0:T], tm[:, 0:T], ALU.add, ["s_t1", "s_tm"], ["s_t1"])
                    self.tt("pool", t2[:, 0:T], ec[:, 0:T], xi[:, 0:T], ALU.mult, ["s_ec", "s_xi"], ["s_t2"])
                    self.tt("dve", tm[:, 0:T], es[:, 0:T], xr[:, 0:T], ALU.mult, ["s_es", "s_xr", "s_t1"], ["s_tm"])
                    self.tt("dve", t2[:, 0:T], t2[:, 0:T], tm[:, 0:T], ALU.subtract, ["s_t2", "s_tm"], ["s_t2"])
                    rho = P_("mag")[:, j:j + 1].to_broadcast([128, T])
                    ir = 0.0 if gi == 0 else P_("g0r")[:, j:j + 1]
                    ii = 0.0 if gi == 0 else P_("g0i")[:, j:j + 1]
                    self.scan(xr[:, 0:T], rho, t1[:, 0:T], ir, [K_, "s_t1", "s_xr"], ["s_xr"])
                    self.scan(xi[:, 0:T], rho, t2[:, 0:T], ii, [K_, "s_t2", "s_xi"], ["s_xi"])
                    self.tt("pool", t1[:, 0:T], ec[:, 0:T], xr[:, 0:T], ALU.mult, ["s_ec", "s_xr", "s_t1"], ["s_t1"])
                    self.tt("dve", tm[:, 0:T], es[:, 0:T], xi[:, 0:T], ALU.mult, ["s_es", "s_xi", "s_t2"], ["s_tm"])
                    self.tt("pool", t1[:, 0:T], t1[:, 0:T], tm[:, 0:T], ALU.subtract, ["s_t1", "s_tm"], ["s_t1"])
                    self.tt("pool", t2[:, 0:T], ec[:, 0:T], xi[:, 0:T], ALU.mult, ["s_ec", "s_xi", "s_t2"], ["s_t2"])
                    self.tt("dve", tm[:, 0:T], es[:, 0:T], xr[:, 0:T], ALU.mult, ["s_es", "s_xr", "s_t1"], ["s_tm"])
                    self.tt("dve", t2[:, 0:T], t2[:, 0:T], tm[:, 0:T], ALU.add, ["s_t2", "s_tm"], ["s_t2"])
                    for (src, nm, kk) in ((t1, "ssm_re", "s_t1"), (t2, "ssm_im", "s_t2")):
                        o = self.o[pre + nm][l].rearrange("g n -> (g n)")[j * 128:(j + 1) * 128].rearrange("(p o) -> p o", o=1)
                        self.dma(o, src[:, T - 1:T], [kk], [self.o[pre + nm].name])
                    for ci, (c0, cw) in enumerate(nch):
                        yps, yk = self.ps[4 + ci], ("ps", 4 + ci)
                        self.mm(yps[:, 0:cw], WC[0][:, j, :], t1[:, c0:c0 + cw], jj == 0, False, ["sWC", "s_t1"], [yk])
                        self.mm(yps[:, 0:cw], WC[1][:, j, :], t2[:, c0:c0 + cw], False, jj == 3, ["sWC", "s_t2"], [yk])
                for ci, (c0, cw) in enumerate(nch):
                    yps, yk = self.ps[4 + ci], ("ps", 4 + ci)
                    self.stt(yg[:, yi, t0 + c0:t0 + c0 + cw], u[:, t0 + c0:t0 + c0 + cw], dcol[:, yi:yi + 1], yps[:, 0:cw],
                             ALU.mult, ALU.add, ["s_u", "sdc", yk], [("s_yg", yi, gi)])
                self.gelu(yg[:, yi, t0:t0 + T], yg[:, yi, t0:t0 + T], tm[:, 0:T], ("s_yg", yi, gi), "s_tm", ("s_yg", yi, gi))
                self.cp("pool", ygb[:, yi, t0:t0 + T], yg[:, yi, t0:t0 + T], [("s_yg", yi, gi)], ["s_ygb"])
        ob = [self.sb("s_ob", [128, 512], BF16) for _ in range(2)]
        sg = [self.sb("s_sg", [128, 512], F32) for _ in range(2)]
        self.gli = 0

        def glu(c, mw, t0, tn, ps, pk):
            i = c // 128
            b = self.gli % 2
            self.gli += 1
            self.act(sg[b][:, 0:tn], ps[:, 0:tn], AF.Sigmoid, [pk, "sdc"], [("s_sg", b)], bias=dcol[:, 4 + i:5 + i])
            gi = 0 if t0 < TP else 1
            self.tt("dve", ob[b][:, 0:tn], sg[b][:, 0:tn], yg[:, i, t0:t0 + tn], ALU.mult, [("s_sg", b), ("s_yg", i, gi)], [("s_ob", b)])
            self.dma(self.d["mixT"][512 + c:512 + c + mw, t0:t0 + tn], ob[b][:, 0:tn], [("s_ob", b)], ["mixT"])
        self.dense(I["ssm_w_glu"][l], 4, [(0, 256), (256, 256)], ygb, "s_ygb", [(t0, tn, t0) for (t0, tn) in CH], glu, wv=2)
        self.P.barrier()

    def stage_dn(self, l):
        self.off = self.lbase
        I = self.i
        self.psr = list(range(8))
        cw = self.sb("dcw", [128, 48], F32)
        self.colload(cw[:, 0:48], I["dn_conv_w"][l].rearrange("j (f p) -> (j f) p", p=128), 48, "dcw")
        ngc_ = self.sb("dng", [128, 1], F32)
        self.dma(ngc_[:], I["dn_norm_g"][l].rearrange("(p o) -> p o", o=1), [], ["dng"])
        hp = self.sb("dhp", [4, 4], F32)
        self.dma(hp[:, 0:1], I["dn_a_log"][l].rearrange("(h o) -> h o", o=1), [], ["dhp"])
        self.dma(hp[:, 1:2], I["dn_dt_bias"][l].rearrange("(h o) -> h o", o=1), [], ["dhp"])
        self.act(hp[:, 2:3], hp[:, 0:1], AF.Exp, ["dhp"], ["dhp"])
        self.ts("dve", hp[:, 2:3], hp[:, 2:3], -1.0, ALU.mult, ["dhp"], ["dhp"])
        T = TP
        rows = {n: self.sb("dr_" + n, [4, T], F32) for n in ("b", "g", "gc")}
        rows["ngc"] = rows["g"]
        colB = self.sb("dcB", [128, 16, 4], F32)
        colG = self.sb("dcG", [128, 16, 4], F32)
        colNB = self.sb("dcNB", [128, 16, 4], F32)
        colNEB = self.sb("dcNEB", [128, 16, 4], F32)
        qT = [self.sb("dq%d" % h, [128, T], F32) for h in range(4)]
        kT = [self.sb("dk%d" % h, [128, T], F32) for h in range(4)]
        vT = [self.sb("dv%d" % h, [128, T], F32) for h in range(4)]
        oT = vT
        S = [self.sb("dS%d" % h, [128, 128], F32) for h in range(4)]
        xpad = self.sb("dxp", [128, 3 + T], F32)
        rs = self.sb("drs", [128, 512], F32)
        sq = self.sb("dsq", [128, 512], F32)
        wn = ("dec", "decT", "N", "NT", "X0", "X1", "Y0", "Y1", "AT", "vb", "kd", "R", "vn", "QK", "qg")
        wt = [{n: self.sb("dw%d%s" % (h, n), [128, 128], F32) for n in wn} for h in range(4)]
        egl = [self.sb("degl%d" % h, [128, 1], F32) for h in range(4)]
        ob = self.sb("dob", [128, 512], BF16)
        for gi, (t0, T) in enumerate(GRP):
            pre = "ps"[gi] + "_"
            C = 128 if gi == 0 else 8
            NCH = T // C
            L = int(round(math.log2(C))) - 1
            b, g, gc, ngc = (rows[n] for n in ("b", "g", "gc", "ngc"))
            self.dma(b[:, 0:T], self.d["dnbaT"][0:4, t0:t0 + T], [], ["dr_b"])
            self.dma(g[:, 0:T], self.d["dnbaT"][4:8, t0:t0 + T], [], ["dr_g"])
            self.act(b[:, 0:T], b[:, 0:T], AF.Sigmoid, ["dr_b"], ["dr_b"])
            self.act(g[:, 0:T], g[:, 0:T], AF.Exp, ["dr_g", "dhp"], ["dr_g"], bias=hp[:, 1:2])
            self.act(g[:, 0:T], g[:, 0:T], AF.Ln, ["dr_g"], ["dr_g"], bias=self.onec[0:4, 0:1])
            self.ts("dve", g[:, 0:T], g[:, 0:T], hp[:, 2:3], ALU.mult, ["dr_g", "dhp"], ["dr_g"])
            for n in range(NCH):
                self.scan(gc[:, n * C:(n + 1) * C], self.ones[0:4, 0:C], g[:, n * C:(n + 1) * C], 0.0, ["dr_g", "dr_gc"], ["dr_gc"])
            self.ts("dve", ngc[:, 0:T], gc[:, 0:T], -1.0, ALU.mult, ["dr_gc"], ["dr_g"])
            for n in range(NCH):
                ps, pk = self.psum()
                self.tr(ps[0:C, 0:4], b[:, n * C:(n + 1) * C], ["dr_b"], [pk])
                self.tr(ps[0:C, 4:8], gc[:, n * C:(n + 1) * C], ["dr_gc"], [pk])
                self.cp("dve", colB[0:C, n, :], ps[0:C, 0:4], [pk], ["dcol"])
                self.cp("dve", colG[0:C, n, :], ps[0:C, 4:8], [pk], ["dcol"])
            self.act(colG[0:C, 0:NCH, :], colG[0:C, 0:NCH, :], AF.Exp, ["dcol"], ["dcol"])
            self.ts("dve", colNB[0:C, 0:NCH, :], colB[0:C, 0:NCH, :], -1.0, ALU.mult, ["dcol"], ["dcol"])
            self.tt("dve", colNEB[0:C, 0:NCH, :], colG[0:C, 0:NCH, :], colNB[0:C, 0:NCH, :], ALU.mult, ["dcol"], ["dcol"])
            for h in range(4):
                for (dst, part, kk) in ((qT[h], 0, ("dq", h)), (kT[h], 1, ("dk", h)), (vT[h], 2, ("dv", h))):
                    f = part * 4 + h
                    if gi == 0:
                        self.memset("pool", xpad[:, 0:3], 0.0, ["dxp"])
                    else:
                        self.colload(xpad[:, 0:3], I["state_delta_conv"][l][:, f * 128:(f + 1) * 128], 3, "dxp")
                    self.dma(xpad[:, 3:3 + T], self.d["dnqkvT"][f * 128:(f + 1) * 128, t0:t0 + T], [], ["dxp"])
                    self.ts("dve", dst[:, 0:T], xpad[:, 0:T], cw[:, f:f + 1], ALU.mult, ["dxp", "dcw"], [kk])
                    for j in range(1, 4):
                        self.stt(dst[:, 0:T], xpad[:, j:j + T], cw[:, 12 * j + f:12 * j + f + 1], dst[:, 0:T], ALU.mult, ALU.add,
                                 ["dxp", "dcw", kk], [kk])
                    self.act(dst[:, 0:T], dst[:, 0:T], AF.Silu, [kk], [kk])
                    if part < 2:
                        for c0 in range(0, T, 512):
                            w = min(512, T - c0)
                            self.act(sq[:, 0:w], dst[:, c0:c0 + w], AF.Square, [kk], ["dsq"])
                            ps, pk = self.psum()
                            self.mm(ps[:, 0:w], self.ones[:], sq[:, 0:w], True, True, ["dsq"], [pk])
                            self.act(rs[:, 0:w], ps[:, 0:w], AF.Sqrt, [pk], ["drs"], bias=self.epsc[:, 0:1])
                            self.recip(rs[:, 0:w], rs[:, 0:w], ["drs"], ["drs"])
                            self.stt(dst[:, c0:c0 + w], dst[:, c0:c0 + w], (128 ** -0.5) if part == 0 else 1.0, rs[:, 0:w],
                                     ALU.mult, ALU.mult, [kk, "drs"], [kk])
                if gi == 0:
                    self.memset("pool", S[h][:], 0.0, [("dS", h)])
                else:
                    self.dma(S[h][:], I["state_delta"][l][h], [], [("dS", h)])
            for n in range(NCH):
                cs = slice(n * C, (n + 1) * C)
                for h in range(4):
                    w = wt[h]
                    W = lambda nm: ("dw", h, nm)
                    selM = self.sel[0:4, h * 128:h * 128 + C]
                    sel128 = self.sel[0:4, h * 128:(h + 1) * 128]
                    kq, kk_, kv = ("dq", h), ("dk", h), ("dv", h)
                    ps, pk = self.psum()
                    self.mm(ps[0:C, 0:C], gc[0:4, cs], selM, True, False, ["dr_gc"], [pk])
                    self.mm(ps[0:C, 0:C], selM, ngc[0:4, cs], False, False, ["dr_g"], [pk])
                    self.mm(ps[0:C, 0:C], self.ident[0:C, 0:C], self.neglow[0:C, 0:C], False, True, [], [pk])
                    self.act(w["dec"][0:C, 0:C], ps[0:C, 0:C], AF.Exp, [pk], [W("dec")])
                    ps, pk = self.psum()
                    self.mm(ps[0:C, 0:C], selM, gc[0:4, cs], True, False, ["dr_gc"], [pk])
                    self.mm(ps[0:C, 0:C], ngc[0:4, cs], selM, False, False, ["dr_g"], [pk])
                    self.mm(ps[0:C, 0:C], self.ident[0:C, 0:C], self.negup[0:C, 0:C], False, True, [], [pk])
                    self.act(w["decT"][0:C, 0:C], ps[0:C, 0:C], AF.Exp, [pk], [W("decT")])
                    ps, pk = self.psum()
                    self.mm(ps[0:C, 0:C], kT[h][:, cs], kT[h][:, cs], True, True, [kk_], [pk])
                    self.tt("dve", w["N"][0:C, 0:C], ps[0:C, 0:C], w["dec"][0:C, 0:C], ALU.mult, [pk, W("dec")], [W("N")])
                    self.stt(w["N"][0:C, 0:C], w["N"][0:C, 0:C], colNB[0:C, n, h:h + 1], self.strict[0:C, 0:C], ALU.mult, ALU.mult,
                             [W("N"), "dcol"], [W("N")])
                    ps, pk = self.psum()
                    self.tr(ps[0:C, 0:C], w["N"][0:C, 0:C], [W("N")], [pk])
                    self.cp("act", w["NT"][0:C, 0:C], ps[0:C, 0:C], [pk], [W("NT")])
                    self.tt("dve", w["AT"][0:C, 0:C], w["NT"][0:C, 0:C], self.ident[0:C, 0:C], ALU.add, [W("NT")], [W("AT")])
                    Xc, Yc = "N", "NT"
                    for i in range(L):
                        Xn, Yn = "X%d" % (i % 2), "Y%d" % (i % 2)
                        ps, pk = self.psum()
                        self.mm(ps[0:C, 0:C], w[Yc][0:C, 0:C], w[Xc][0:C, 0:C], True, True, [W(Xc), W(Yc)], [pk])
                        self.cp("act", w[Xn][0:C, 0:C], ps[0:C, 0:C], [pk], [W(Xn)])
                        if i < L - 1:
                            ps, pk = self.psum()
                            self.mm(ps[0:C, 0:C], w[Xc][0:C, 0:C], w[Yc][0:C, 0:C], True, True, [W(Xc), W(Yc)], [pk])
                            self.cp("dve", w[Yn][0:C, 0:C], ps[0:C, 0:C], [pk], [W(Yn)])
                        ps, pk = self.psum()
                        self.mm(ps[0:C, 0:C], w[Xn][0:C, 0:C], w["AT"][0:C, 0:C], True, True, [W(Xn), W("AT")], [pk])
                        self.tt("dve", w["AT"][0:C, 0:C], w["AT"][0:C, 0:C], ps[0:C, 0:C], ALU.add, [pk, W("AT")], [W("AT")])
                        Xc, Yc = Xn, Yn
                    ps, pk = self.psum()
                    self.tr(ps[0:C, 0:128], vT[h][:, cs], [kv], [pk])
                    self.ts("dve", w["vb"][0:C, :], ps[0:C, 0:128], colB[0:C, n, h:h + 1], ALU.mult, [pk, "dcol"], [W("vb")])
                    ps, pk = self.psum()
                    self.tr(ps[0:C, 0:128], kT[h][:, cs], [kk_], [pk])
                    self.ts("dve", w["kd"][0:C, :], ps[0:C, 0:128], w["decT"][0:C, C - 1:C], ALU.mult, [pk, W("decT")], [W("kd")])
                    ps, pk = self.psum()
                    self.mm(ps[0:C, 0:128], kT[h][:, cs], S[h][:], True, True, [kk_, ("dS", h)], [pk])
                    self.stt(w["R"][0:C, :], ps[0:C, 0:128], colNEB[0:C, n, h:h + 1], w["vb"][0:C, :], ALU.mult, ALU.add,
                             [pk, "dcol", W("vb")], [W("R")])
                    ps, pk = self.psum()
                    self.mm(ps[0:C, 0:128], w["AT"][0:C, 0:C], w["R"][0:C, :], True, True, [W("AT"), W("R")], [pk])
                    self.cp("act", w["vn"][0:C, :], ps[0:C, 0:128], [pk], [W("vn")])
                    ps, pk = self.psum()
                    self.mm(ps[:, 0:C], sel128, gc[0:4, cs], True, True, ["dr_gc"], [pk])
                    self.act(w["qg"][:, 0:C], ps[:, 0:C], AF.Exp, [pk], [W("qg")])
                    self.tt("dve", w["qg"][:, 0:C], qT[h][:, cs], w["qg"][:, 0:C], ALU.mult, [kq, W("qg")], [W("qg")])
                    ps, pk = self.psum()
                    self.mm(ps[0:C, 0:C], kT[h][:, cs], qT[h][:, cs], True, True, [kk_, kq], [pk])
                    self.tt("dve", w["QK"][0:C, 0:C], ps[0:C, 0:C], w["decT"][0:C, 0:C], ALU.mult, [pk, W("decT")], [W("QK")])
                    ps, pk = self.psum()
                    self.mm(ps[:, 0:C], S[h][:], w["qg"][:, 0:C], True, False, [("dS", h), W("qg")], [pk])
                    self.mm(ps[:, 0:C], w["vn"][0:C, :], w["QK"][0:C, 0:C], False, True, [W("vn"), W("QK")], [pk])
                    self.cp("act", oT[h][:, cs], ps[:, 0:C], [pk], [("dv", h)])
                    ps, pk = self.psum()
                    self.mm(ps[:, 0:1], sel128, gc[0:4, (n + 1) * C - 1:(n + 1) * C], True, True, ["dr_gc"], [pk])
                    self.act(egl[h][:], ps[:, 0:1], AF.Exp, [pk], [("degl", h)])
                    ps, pk = self.psum()
                    self.mm(ps[:, 0:128], w["kd"][0:C, :], w["vn"][0:C, :], True, True, [W("kd"), W("vn")], [pk])
                    self.stt(S[h][:], S[h][:], egl[h][:, 0:1], ps[:, 0:128], ALU.mult, ALU.add, [("dS", h), ("degl", h), pk], [("dS", h)])
            for h in range(4):
                self.dma(self.o[pre + "delta"][l][h], S[h][:], [("dS", h)], [self.o[pre + "delta"].name])
                z = xpad
                self.dma(z[:, 0:T], self.d["dnzT"][h * 128:(h + 1) * 128, t0:t0 + T], [], ["dxp"])
                self.act(z[:, 0:T], z[:, 0:T], AF.Silu, ["dxp"], ["dxp"])
                for c0 in range(0, T, 512):
                    wd = min(512, T - c0)
                    self.act(sq[:, 0:wd], oT[h][:, c0:c0 + wd], AF.Square, [("dv", h)], ["dsq"])
                    ps, pk = self.psum()
                    self.mm(ps[:, 0:wd], self.ones[:], sq[:, 0:wd], True, True, ["dsq"], [pk])
                    self.act(rs[:, 0:wd], ps[:, 0:wd], AF.Sqrt, [pk], ["drs"], scale=1.0 / 128, bias=self.epsc[:, 0:1])
                    self.recip(rs[:, 0:wd], rs[:, 0:wd], ["drs"], ["drs"])
                    self.stt(sq[:, 0:wd], oT[h][:, c0:c0 + wd], ngc_[:, 0:1], rs[:, 0:wd], ALU.mult, ALU.mult,
                             [("dv", h), "dng", "drs", "dsq"], ["dsq"])
                    self.tt("dve", ob[:, 0:wd], sq[:, 0:wd], z[:, c0:c0 + wd], ALU.mult, ["dsq", "dxp"], ["dob"])
                    self.dma(self.d["mixT"][h * 128:(h + 1) * 128, t0 + c0:t0 + c0 + wd], ob[:, 0:wd], ["dob"], ["mixT"])
        self.psr = [0, 1, 2, 3]
        self.P.barrier()

    def stage_bias(self):
        self.off = self.lbase
        I = self.i
        rb = self.sb("rb", [32, 8], F32)
        oh = self.sb("oh", [32, VL], F32)
        cv = self.sb("cv", [1, VL], F32)
        bv = self.sb("bv", [8, VL], F32)
        self.dma(rb[:], I["rel_bias"], [], ["rb"])
        self.dma(oh[:], I["c_oh"], [], ["oh"])
        self.dma(cv[:], I["c_cv"], [], ["cv"])
        for c0 in range(0, VL, 512):
            w = min(512, VL - c0)
            ps, pk = self.psum()
            self.mm(ps[0:8, 0:w], rb[:, :], oh[:, c0:c0 + w], True, False, ["rb", "oh"], [pk])
            self.mm(ps[0:8, 0:w], self.ones[0:1, 0:8], cv[0:1, c0:c0 + w], False, True, ["cv"], [pk])
            self.cp("dve", bv[:, c0:c0 + w], ps[0:8, 0:w], [pk], ["bv"])
        self.dma(self.d["bvec"], bv[:], ["bv"], ["bvec"])
        self.P.barrier()
        U = [self.sb("bU", [128, SW], F32) for _ in range(2)]
        Tt = [self.sb("bT", [128, SW], F32) for _ in range(2)]
        for h in range(8):
            u, uk = U[h % 2], ("bU", h % 2)
            t, tk = Tt[h % 2], ("bT", h % 2)
            self.dma(u[:], bass.AP(self.d["bvec"].tensor, h * VL, [[1, 128], [1, SW]]), [], [uk])
            for c0 in range(0, SW, 512):
                w = min(512, SW - c0)
                ps, pk = self.psum()
                self.mm(ps[:, 0:w], self.anti[:], u[:, c0:c0 + w], True, True, [uk], [pk])
                self.cp("act" if (c0 // 512) % 2 else "dve", t[:, c0:c0 + w], ps[:, 0:w], [pk], [tk])
            self.dma(self.d["strip"][h], t[:], [tk], ["strip"])
        self.P.barrier()

    def stage_swa(self, l):
        self.off = self.lbase
        I = self.i
        qT = self.sb("aq", [128, 4, TT], BF16)
        kT = self.sb("ak", [128, 4, TT], BF16)
        V = self.sb("av", [128, 16, 512], BF16)
        Vn = self.sb("avn", [8, 512], BF16)
        kcT = self.sb("akc", [128, 4, 2048], BF16)
        Vc = self.sb("avc", [128, 16, 512], BF16)
        strip = [self.sb("ast", [128, SW], F32) for _ in range(2)]
        Lw = [self.sb("aL", [128, 512], F32) for _ in range(4)]
        PT = [self.sb("aP", [128, 512], BF16) for _ in range(4)]
        rinv = self.sb("ari", [128, 512], F32)
        ob = [self.sb("aob", [128, 512], BF16) for _ in range(2)]
        cst = self.sb("acs", [128, 4, 512], F32)
        sw = self.d["swaT"]
        self.dma(qT[:], sw[0:512, :].rearrange("(f p) t -> p f t", p=128), [], ["aq"], eng="pool")
        self.dma(kT[:], sw[512:1024, :].rearrange("(f p) t -> p f t", p=128), [], ["ak"], eng="pool")
        self.dma(V[:], self.o["p_win_v"][l].rearrange("(j p) c -> p j c", p=128), [], ["av"], eng="pool")
        self.dma(Vn[:], self.o["s_win_v"][l], [], ["avn"], eng="pool")
        self.dma(Vc[:], I["cache_win_v"][l].rearrange("(j p) c -> p j c", p=128), [], ["avc"], eng="pool")
        for j4 in range(4):
            self.dma(cst[:], I["cache_win_k"][l][j4 * 512:(j4 + 1) * 512, :].rearrange("(j p) c -> p j c", p=128), [], ["acs"])
            for jj in range(4):
                j = j4 * 4 + jj
                ps, pk = self.psum()
                for i in range(4):
                    self.tr(ps[:, i * 128:(i + 1) * 128], cst[:, jj, i * 128:(i + 1) * 128], ["acs"], [pk])
                self.cp("act" if jj % 2 else "dve", kcT[:, :, j * 128:(j + 1) * 128],
                        ps[:].rearrange("p (i t) -> p i t", i=4), [pk], ["akc"])
        LA = 2
        tasks = []
        un = 0
        for h in range(8):
            i, hb = h // 2, (h % 2) * 64
            units = [(0, c * 512, 512, [(kT, V, j, 128, c * 512 - 128 * j) for j in range(4 * c + 4)]) for c in range(4)]
            units.append((1, TP, TS, [(kcT, Vc, j, 128, 2048 - 128 * j) for j in range(16)] + [(kT, Vn, None, 8, 0)]))
            for (gi, q0, N, keys) in units:
                for ki, key in enumerate(keys):
                    tasks.append((h, i, hb, gi, q0, N, key, ki == 0, ki == len(keys) - 1, un, ki == 0 and q0 == 0))
                un += 1

        def A(ti):
            (h, i, hb, gi, q0, N, (KT, VV, j, kn, dl), first, last, u, newhead) = tasks[ti]
            st, sk = strip[h % 2], ("ast", h % 2)
            if newhead:
                self.dma(st[:], self.d["strip"][h], [], [sk])
            if j is None:
                lk, kkey = kT[hb:hb + 64, i, TP:TP + TS], "ak"
            else:
                lk, kkey = KT[hb:hb + 64, i, j * 128:(j + 1) * 128], ("ak" if gi == 0 else "akc")
            b = ti % 4
            ps, pk = self.psum()
            self.mm(ps[0:kn, 0:N], lk, qT[hb:hb + 64, i, q0:q0 + N], True, True, [kkey, "aq"], [pk])
            y0 = dl + 384
            self.stt(Lw[b][0:kn, 0:N], ps[0:kn, 0:N], 0.125, st[0:kn, y0:y0 + N], ALU.mult, ALU.add, [pk, sk], [("aL", b)])
            self.act(PT[b][0:kn, 0:N], Lw[b][0:kn, 0:N], AF.Exp, [("aL", b)], [("aP", b)])

        def B(ti):
            (h, i, hb, gi, q0, N, (KT, VV, j, kn, dl), first, last, u, newhead) = tasks[ti]
            b = ti % 4
            ops_, okk = self.ps[4 + 2 * (u % 2)], ("ps", 4 + 2 * (u % 2))
            sps, skk = self.ps[5 + 2 * (u % 2)], ("ps", 5 + 2 * (u % 2))
            if j is None:
                vv, vkey = Vn[0:kn, i * 128:(i + 1) * 128], "avn"
            else:
                vv, vkey = VV[:, j, i * 128:(i + 1) * 128], ("av" if gi == 0 else "avc")
            self.mm(ops_[:, 0:N], vv, PT[b][0:kn, 0:N], first, last, [vkey, ("aP", b)], [okk])
            self.mm(sps[:, 0:N], self.onesb[0:kn, :], PT[b][0:kn, 0:N], first, last, [("aP", b)], [skk])
            if last:
                self.recip(rinv[hb:hb + 64, 0:N], sps[hb:hb + 64, 0:N], [skk], ["ari"])
                o = ob[u % 2]
                self.tt("dve", o[hb:hb + 64, 0:N], ops_[hb:hb + 64, 0:N], rinv[hb:hb + 64, 0:N], ALU.mult, [okk, "ari"], [("aob", u % 2)])
                self.dma(self.d["mixT"][1024 + h * 64:1024 + (h + 1) * 64, q0:q0 + N], o[hb:hb + 64, 0:N], [("aob", u % 2)], ["mixT"])
        for ti in range(len(tasks) + LA):
            if ti < len(tasks):
                A(ti)
            if ti - LA >= 0:
                B(ti - LA)
        self.P.barrier()

    def resid(self):
        fifo = []

        def pre(c, mw, t0, tn):
            i = self.evi % 4
            self.evi += 1
            s, sk = self.evs[i], ("evs", i)
            rk = ("xTr", c, t0)
            self.dma(s[0:mw, 0:tn], self.d["xT"][c:c + mw, t0:t0 + tn], [rk], [sk])
            fifo.append((s, sk, rk))

        def f(c, mw, t0, tn, ps, pk):
            s, sk, rk = fifo.pop(0)
            self.tt("dve", s[0:mw, 0:tn], s[0:mw, 0:tn], ps[0:mw, 0:tn], ALU.add, [sk, pk], [sk])
            self.dma(self.d["xT"][c:c + mw, t0:t0 + tn], s[0:mw, 0:tn], [sk], [rk])
        return pre, f

    def stage_wout(self, l):
        self.off = self.lbase
        mx = self.sb("mixS", [128, 16, TT], BF16)
        self.evs = [self.sb("evs", [128, 512], F32) for _ in range(4)]
        self.evi = 0
        self.dma(mx[:], self.d["mixT"].rearrange("(f p) t -> p f t", p=128), [], ["mixS"])
        rp, rc = self.resid()
        self.dense(self.i["w_out"][l], 16, [(g, 256) for g in range(0, D, 256)], mx, "mixS",
                   [(t0, tn, t0) for (t0, tn) in CH], rc, pre=rp)
        self.P.barrier()

    def stage_cross(self, l):
        self.off = self.lbase
        I = self.i
        xn = self.sb("xnT", [128, 16, TT], BF16)
        self.evs = [self.sb("evs", [128, 512], F32) for _ in range(4)]
        self.evi = 0
        qT = self.sb("cq", [128, 4, TT], BF16)
        at = self.sb("cat", [128, 4, TT], BF16)
        kT = [self.sb("ckT%d" % g, [128, 4, 256], BF16) for g in range(2)]
        Vm = [self.sb("cV%d" % g, [128, 2, 512], BF16) for g in range(2)]
        kst = self.sb("ckst", [128, 2, 512], F32)
        PT = [self.sb("cP", [128, 512], BF16) for _ in range(4)]
        rinv = self.sb("cri", [128, 512], F32)
        ksrc = [self.o["p_mem_k"][l], I["cache_mem_k"][l]]
        vsrc = [self.o["p_mem_v"][l], I["cache_mem_v"][l]]
        for g in range(2):
            self.dma(Vm[g][:], vsrc[g].rearrange("(m p) c -> p m c", p=128), [], [("cV", g)], eng="pool")
            self.dma(kst[:], ksrc[g].rearrange("(m p) c -> p m c", p=128), [], ["ckst"])
            for m in range(2):
                ps, pk = self.psum()
                for h in range(4):
                    self.tr(ps[:, h * 128:(h + 1) * 128], kst[:, m, h * 128:(h + 1) * 128], ["ckst"], [pk])
                self.cp("dve", kT[g][:, :, m * 128:(m + 1) * 128], ps[:].rearrange("p (h t) -> p h t", h=4), [pk], [("ckT", g)])
        self.norm_T(self.d["xT"], self.gcols[:, 1, :], CH, xn, "xnT")

        def qcons(c, mw, t0, tn, ps, pk):
            self.act(qT[:, c // 128, t0:t0 + tn], ps[:, 0:tn], AF.Copy, [pk], [("cq", t0)], scale=128 ** -0.5)
        self.dense(I["w_mem_q"][l], 16, [(0, 256), (256, 256)], xn, "xnT", [(t0, tn, t0) for (t0, tn) in CH], qcons)
        LA = 0
        tasks = []
        un = 0
        for (t0, tn) in CH:
            g = 0 if t0 < TP else 1
            for h in range(4):
                for m in range(2):
                    tasks.append((t0, tn, g, h, m, un))
                un += 1

        def A(ti):
            (t0, tn, g, h, m, u) = tasks[ti]
            b = ti % 4
            ps, pk = self.psum()
            self.mm(ps[:, 0:tn], kT[g][:, h, m * 128:(m + 1) * 128], qT[:, h, t0:t0 + tn], True, True, [("ckT", g), ("cq", t0)], [pk])
            self.act(PT[b][:, 0:tn], ps[:, 0:tn], AF.Exp, [pk], [("cP", b)])

        def B(ti):
            (t0, tn, g, h, m, u) = tasks[ti]
            b = ti % 4
            ops_, okk = self.ps[4 + 2 * (u % 2)], ("ps", 4 + 2 * (u % 2))
            sps, skk = self.ps[5 + 2 * (u % 2)], ("ps", 5 + 2 * (u % 2))
            self.mm(ops_[:, 0:tn], Vm[g][:, m, h * 128:(h + 1) * 128], PT[b][:, 0:tn], m == 0, m == 1, [("cV", g), ("cP", b)], [okk])
            self.mm(sps[:, 0:tn], self.onesb[:], PT[b][:, 0:tn], m == 0, m == 1, [("cP", b)], [skk])
            if m == 1:
                self.recip(rinv[:, 0:tn], sps[:, 0:tn], [skk], ["cri"])
                self.tt("dve", at[:, h, t0:t0 + tn], ops_[:, 0:tn], rinv[:, 0:tn], ALU.mult, [okk, "cri"], [("cat", t0)])
        for ti in range(len(tasks) + LA):
            if ti < len(tasks):
                A(ti)
            if ti - LA >= 0:
                B(ti - LA)
        rp, rc = self.resid()
        self.dense(I["w_mem_o"][l], 4, [(g_, 256) for g_ in range(0, D, 256)], at, "cat", [(t0, tn, t0) for (t0, tn) in CH],
                   rc, wv=2, pre=rp)
        self.P.barrier()

    def stage_ffn(self, l):
        self.off = self.lbase
        I = self.i
        NJ = DFF // 128
        cwc = self.sb("fcw", [128, 3, 88], F32)
        stc = self.sb("fst", [128, 2, 88], F32)
        for r in range(3):
            self.colload(cwc[:, r, :], I["ffn_conv_w"][l][r].rearrange("(f p) -> f p", p=128), 88, "fcw")
        for r in range(2):
            self.colload(stc[:, r, :], I["state_ffn_conv"][l][r].rearrange("(f p) -> f p", p=128), 88, "fst")
        tails = self.sb("ftl", [128, 88, 2], F32)
        otl = [self.sb("fot%d" % g, [128, 2, 88], F32) for g in range(2)]
        self.evs = [self.sb("evs", [128, 512], F32) for _ in range(4)]
        self.evi = 0
        GW = 1032
        xn = self.sb("fxn", [128, 16, GW], BF16)
        a_off = self.off
        actT = self.sb("fact", [128, NJ, GW], BF16)
        hub = [self.sb("fhub", [128, 2 + 512], F32) for _ in range(4)]
        cvt = [self.sb("fcv", [128, 512], F32) for _ in range(2)]
        self.ring_extra(FFN_XW)
        W = I["w_up"][l]
        hi = 0
        for (tb, chunks) in ((0, [(0, 512), (512, 512)]), (1024, [(1024, 512), (1536, 512), (2048, 8)])):
            e_off = self.off
            self.off = a_off
            self.P.barrier()
            self.norm_T(self.d["xT"], self.gcols[:, 2, :], chunks, xn, ("fxn", tb), tbase=tb)
            self.P.barrier()
            self.off = e_off
            def issue_w(jp_):
                wu_ = self.next_w()
                self.dma(wu_[0][:, 0:16, 0:256], W[:, jp_ * 256:(jp_ + 1) * 256].rearrange("(k p) c -> p k c", p=128), [], [wu_[3]], eng="pool")
                wg_ = self.next_w()
                self.dma(wg_[0][:, 0:16, 0:256], W[:, DFF + jp_ * 256:DFF + (jp_ + 1) * 256].rearrange("(k p) c -> p k c", p=128), [], [wg_[3]], eng="pool")
                return wu_, wg_
            wq = [issue_w(0)]
            for jp in range(NJ // 2):
                if jp + 1 < NJ // 2:
                    wq.append(issue_w(jp + 1))
                wu, wg = wq.pop(0)
                for jj in range(2):
                    j = 2 * jp + jj
                    for (t0, tn) in chunks:
                        a0 = t0 - tb
                        res = []
                        for half, wt_ in enumerate((wu, wg)):
                            ft = half * NJ + j
                            ps, pk = self.psum()
                            for k in range(16):
                                self.mm(ps[:, 0:tn], wt_[0][:, k, jj * 128:(jj + 1) * 128], xn[:, k, a0:a0 + tn], k == 0, k == 15,
                                        [wt_[3], (("fxn", tb), t0)], [pk])
                            hb_, hk = hub[hi % 4], ("fhub", hi % 4)
                            hi += 1
                            if t0 == 0:
                                self.memset("pool", hb_[:, 0:2], 0.0, [hk])
                            elif t0 == TP:
                                self.cp("pool", hb_[:, 0:2], stc[:, :, ft], ["fst", hk], [hk])
                            else:
                                self.cp("pool", hb_[:, 0:2], tails[:, ft, :], [("ftl", ft), hk], [hk])
                            self.cp("act", hb_[:, 2:2 + tn], ps[:, 0:tn], [pk, hk], [hk])
                            if t0 + tn == TP or t0 == TP:
                                self.cp("pool", otl[0 if t0 < TP else 1][:, :, ft], hb_[:, tn:tn + 2], [hk], [("fot", ft)])
                            if t0 < TP:
                                self.cp("pool", tails[:, ft, :], hb_[:, tn:tn + 2], [hk], [("ftl", ft)])
                            cv, ck = cvt[half], ("fcv", half)
                            self.ts("dve", cv[:, 0:tn], hb_[:, 0:tn], cwc[:, 0, ft:ft + 1], ALU.mult, [hk, "fcw"], [ck])
                            self.stt(cv[:, 0:tn], hb_[:, 1:1 + tn], cwc[:, 1, ft:ft + 1], cv[:, 0:tn], ALU.mult, ALU.add, [hk, "fcw", ck], [ck])
                            self.stt(cv[:, 0:tn], hb_[:, 2:2 + tn], cwc[:, 2, ft:ft + 1], cv[:, 0:tn], ALU.mult, ALU.add, [hk, "fcw", ck], [ck])
                        self.act(cvt[1][:, 0:tn], cvt[1][:, 0:tn], AF.Silu, [("fcv", 1)], [("fcv", 1)])
                        self.tt("dve", actT[:, j, a0:a0 + tn], cvt[1][:, 0:tn], cvt[0][:, 0:tn], ALU.mult, [("fcv", 0), ("fcv", 1)], [(("fact", tb), t0)])
            rp, rc = self.resid()
            self.dense(I["w_down"][l], NJ, [(g_, 128) for g_ in range(0, D, 128)], actT, ("fact", tb),
                       [(t0, tn, t0 - tb) for (t0, tn) in chunks], rc, wv=1, pre=rp)
        for g, pre in enumerate(("p_", "s_")):
            for r in range(2):
                ps, pk = self.psum()
                self.tr(ps[0:88, 0:128], otl[g][:, r, :], [("fot", ft) for ft in range(88)], [pk])
                i = self.evi % 4
                self.evi += 1
                sv, sk = self.evs[i], ("evs", i)
                self.cp("dve", sv[0:88, 0:128], ps[0:88, 0:128], [pk], [sk])
                self.dma(self.o[pre + "ffn_conv"][l][r].rearrange("(f p) -> f p", p=128), sv[0:88, 0:128], [sk], [self.o[pre + "ffn_conv"].name])
        self.P.barrier()
        self.wring = self.wb

    def stage_final(self):
        self.off = self.lbase
        I = self.i
        gr = self.sb("zgr", [1, D], F32)
        gb = self.sb("zgb", [128, D], F32)
        self.dma(gr[:], I["g_final"], [], ["zgr"])
        for c in range(4):
            ps, pk = self.psum()
            self.mm(ps[:, :], self.ones[0:1, :], gr[0:1, c * 512:(c + 1) * 512], True, True, ["zgr"], [pk])
            self.cp("dve", gb[:, c * 512:(c + 1) * 512], ps[:, :], [pk], ["zgb"])
        xi = [self.sb("zxi", [128, 16, 128], F32) for _ in range(2)]
        xt = [self.sb("zxt", [128, D], F32) for _ in range(2)]
        sqt = self.sb("zsq", [128, D], F32)
        ss = self.sb("zss", [128, 2], F32)
        tiles = [(t * 128, 128, self.o["y_prompt"], t * 128) for t in range(16)] + [(TP, TS, self.o["y_sample"], 0)]
        for n, (t0, nt, dst, r0) in enumerate(tiles):
            a, ak = xi[n % 2], ("zxi", n % 2)
            b, bk = xt[n % 2], ("zxt", n % 2)
            self.dma(a[:, :, 0:nt], self.d["xT"].rearrange("(f p) t -> p f t", p=128)[:, :, t0:t0 + nt], [], [ak])
            for f4 in range(4):
                ps, pk = self.psum()
                for j in range(4):
                    self.tr(ps[0:nt, j * 128:(j + 1) * 128], a[:, f4 * 4 + j, 0:nt], [ak], [pk])
                self.cp("act" if f4 % 2 else "dve", b[0:nt, f4 * 512:(f4 + 1) * 512], ps[0:nt, :], [pk], [bk])
            self.act(sqt[0:nt, :], b[0:nt, :], AF.Square, [bk], ["zsq"])
            self.P.op("dve", (lambda o_, i_: (lambda e: e.reduce_sum(out=o_, in_=i_, axis=AX.X)))(ss[0:nt, 0:1], sqt[0:nt, :]), ["zsq"], ["zss"])
            self.act(ss[0:nt, 1:2], ss[0:nt, 0:1], AF.Sqrt, ["zss"], ["zss"], scale=1.0 / D, bias=self.epsc[0:nt, 0:1])
            self.recip(ss[0:nt, 1:2], ss[0:nt, 1:2], ["zss"], ["zss"])
            self.stt(b[0:nt, :], b[0:nt, :], ss[0:nt, 1:2], gb[0:nt, :], ALU.mult, ALU.mult, [bk, "zss", "zgb"], [bk])
            self.dma(dst[r0:r0 + nt, :], b[0:nt, :], [bk], [dst.name])
        self.P.barrier()

    def build(self):
        with contextlib.ExitStack() as st:
            self.stack = st
            self.setup()
            self.gcols = self.sb("gcols", [128, 4, 16], F32)
            self.epsc = self.sb("epsc", [128, 1], F32)
            self.memset("dve", self.epsc[:], EPS, ["epsc"])
            self.onec = self.sb("onec", [128, 1], F32)
            self.memset("dve", self.onec[:], 1.0, ["onec"])
            self.cl_st = [self.sb("clst", [128, 128], F32) for _ in range(2)]
            self.cl_i = 0
            self.lbase = self.off
            self.stage_input()
            self.stage_bias()
            for l in range(self.NL):
                self.load_gcols(l)
                self.P.barrier()
                self.stage_memkv(l)
                self.stage_in(l)
                self.stage_kvout(l)
                self.stage_lru(l)
                self.stage_ssm(l)
                self.stage_dn(l)
                self.stage_swa(l)
                if self.dbg:
                    self.dma(self.dbg["dbg_mix"], self.d["mixT"], [], [])
                    self.P.barrier()
                self.conv_tail_out(self.d["dnqkvT"], 1536, 3, self.o["p_delta_conv"], self.o["s_delta_conv"], l)
                self.conv_tail_out(self.d["lruxT"], 512, 3, self.o["p_lru_conv"], self.o["s_lru_conv"], l)
                self.stage_wout(l)
                self.stage_cross(l)
                self.stage_ffn(l)
            self.stage_final()
            self.P.emit()
        return self.nc


_CACHE = {}


def run(inputs, NL, stages="all", ncores=8):
    key = (NL, stages)
    if key not in _CACHE:
        _CACHE[key] = K(NL, stages).build()
    nc = _CACHE[key]
    consts = host_consts()
    f = lambda a: np.ascontiguousarray(np.asarray(a, dtype=np.float32))
    in_maps = []
    for c in range(ncores):
        b = c % 4
        m = dict(consts)
        for n, s in W_SHAPES.items():
            a = f(inputs[n])
            if n == "g_final":
                a = a.reshape(1, D)
            elif s[0] == "L":
                a = a[:NL]
            m[n] = np.ascontiguousarray(a)
        m["x_prompt"] = f(inputs["x_prompt"][b])
        m["x_sample"] = f(inputs["x_sample"][c])
        m["mem_prompt"] = f(inputs["mem_prompt"][b])
        for n, s in IN_SHAPES.items():
            if s[0] == "L":
                a = np.asarray(inputs[n])[:NL, c]
                m[n] = np.ascontiguousarray(a.reshape(_shape(s, NL)).astype(np.float32))
        in_maps.append(m)
    res = run_bass_kernel_spmd(nc, in_maps, core_ids=list(range(ncores)))
    return res.results


def assemble(results, NL):
    outs = []
    full = {"y_prompt": (4, TP, D), "y_sample": (8, TS, D)}
    for n, s in OUT_SHAPES:
        if n.startswith("y_p"):
            outs.append(np.stack([results[c][n] for c in range(4)]).reshape(4, TP, D))
        elif n.startswith("y_s"):
            outs.append(np.stack([results[c][n] for c in range(8)]).reshape(8, TS, D))
        else:
            nb = 4 if n.startswith("p_") else 8
            a = np.stack([results[c][n] for c in range(nb)], axis=1)
            outs.append(a)
    shp = {"mem_k": (256, 4, 128), "mem_v": (256, 4, 128), "win_k": (-1, 8, 64), "win_v": (-1, 8, 64)}
    res = []
    for (n, s), a in zip(OUT_SHAPES, outs):
        base = n[2:]
        if base in shp and not n.startswith("y_"):
            a = a.reshape(a.shape[0], a.shape[1], *([a.shape[2]] if shp[base][0] == -1 else [shp[base][0]]), *shp[base][1:])
        res.append(np.ascontiguousarray(a.astype(np.float32)))
    return tuple(res)


def kernel(**inputs):
    results = run(inputs, 4)
    return assemble(results, 4)
```

```python
import contextlib
import math
import numpy as np
import concourse.bass as bass
import concourse.mybir as mybir
from concourse.bass_utils import run_bass_kernel_spmd

F32 = mybir.dt.float32
BF16 = mybir.dt.bfloat16
AF = mybir.ActivationFunctionType
ALU = mybir.AluOpType
AX = mybir.AxisListType

ENGS = ("pe", "act", "dve", "pool", "sp")
NDMASEM = 16
NEG = -30000.0
D = 2048
TP = 2048
TS = 8
TT = TP + TS
DFF = 5632
EPS = 1e-6
CH = [(0, 512), (512, 512), (1024, 512), (1536, 512), (2048, 8)]
GRP = [(0, TP), (TP, TS)]
VL = 2600
SW = 2440
NWB = 3
FFN_XW = 2


class Prog:
    def __init__(self, nc):
        self.nc = nc
        self.ops = {e: [] for e in ENGS}
        self.last_w = {}
        self.readers = {}
        self.pend = {e: set() for e in ENGS}
        self.dmas = []

    def op(self, eng, fn, reads=(), writes=(), dma=False):
        idx = len(self.ops[eng])
        deps = set(self.pend[eng])
        self.pend[eng] = set()
        for k in reads:
            w = self.last_w.get(k)
            if w is not None:
                deps.add(w)
        for k in writes:
            w = self.last_w.get(k)
            if w is not None:
                deps.add(w)
            for e_, r in self.readers.get(k, {}).items():
                if e_ == "_dma":
                    deps.update(r)
                else:
                    deps.add(r)
        me = (eng, idx)
        deps.discard(me)
        if eng == "pe":
            deps = {d for d in deps if d[0] != "pe"}
        self.ops[eng].append(dict(fn=fn, deps=deps, dma=dma, sig=False))
        for k in reads:
            rd = self.readers.setdefault(k, {})
            if dma:
                rd.setdefault("_dma", []).append(me)
            else:
                rd[eng] = me
        for k in writes:
            self.last_w[k] = me
            self.readers[k] = {}
        if dma:
            self.dmas.append(me)
        return me

    def dma(self, eng, out, in_, reads=(), writes=(), **kw):
        return self.op(eng, lambda e: e.dma_start(out=out, in_=in_, **kw),
                       reads=reads, writes=writes, dma=True)

    def barrier(self):
        lasts = set(self.dmas)
        for e in ENGS:
            n = len(self.ops[e])
            if n:
                lasts.add((e, n - 1))
        for e in ENGS:
            self.pend[e] |= lasts
        self.dmas = []
        self.last_w = {}
        self.readers = {}

    def emit(self):
        nc = self.nc
        ops = self.ops
        for e in ENGS:
            for o in ops[e]:
                if o["dma"]:
                    o["sig"] = True
                for (pe_, pi) in o["deps"]:
                    ops[pe_][pi]["sig"] = True
        with contextlib.ExitStack() as st:
            EPOCH = 30000
            csem = {e: [st.enter_context(nc.semaphore("c_%s%d" % (e, i))) for i in range(4)] for e in ENGS}
            dsem = {e: [st.enter_context(nc.semaphore("d_%s%d" % (e, i))) for i in range(NDMASEM)]
                    for e in ("sp", "act", "pool")}
            finals = {}
            for e in ENGS:
                cnt = 0
                dcnt = [0] * NDMASEM
                rr = 0
                for o in ops[e]:
                    if not o["sig"]:
                        continue
                    if o["dma"]:
                        s = rr % NDMASEM
                        rr += 1
                        o["prev"] = (dsem[e][s], dcnt[s])
                        dcnt[s] += 16
                        o["sem"] = (dsem[e][s], dcnt[s], 16)
                    else:
                        o["sem"] = (csem[e][cnt // EPOCH], cnt % EPOCH + 1, 1)
                        cnt += 1
                finals[e] = [(dsem[e][s], dcnt[s]) for s in range(NDMASEM) if dcnt[s] > 0] if e in dsem else []
            block = st.enter_context(nc.Block())

            def run(e, eng):
                waited = {}

                def wait(sem, val):
                    if val <= 0:
                        return
                    key = id(sem)
                    if waited.get(key, 0) >= val:
                        return
                    waited[key] = val
                    eng.wait_ge(sem, val)

                for o in ops[e]:
                    mx = {}
                    for (pe_, pi) in o["deps"]:
                        s = ops[pe_][pi]["sem"]
                        if mx.get(id(s[0]), (None, 0))[1] < s[1]:
                            mx[id(s[0])] = (s[0], s[1])
                    for (s0, v) in mx.values():
                        wait(s0, v)
                    if o["dma"]:
                        wait(*o["prev"])
                    ins = o["fn"](eng)
                    if o["sig"]:
                        ins.then_inc(o["sem"][0], o["sem"][2])
                for (s, v) in finals[e]:
                    wait(s, v)

            @block.tensor
            def _(eng):
                run("pe", eng)

            @block.scalar
            def _(eng):
                run("act", eng)

            @block.vector
            def _(eng):
                run("dve", eng)

            @block.gpsimd
            def _(eng):
                run("pool", eng)

            @block.sync
            def _(eng):
                run("sp", eng)


def _t5_bucket(d):
    if d < 16:
        return d
    v = np.log(np.float32(d) / np.float32(16)) / np.float32(math.log(2048 / 16)) * np.float32(16)
    return min(31, 16 + int(np.float32(v).astype(np.int32)))


def host_consts():
    c = {}
    oh = np.zeros((32, VL), np.float32)
    cv = np.full((1, VL), NEG, np.float32)
    for y in range(VL):
        d = y - 511
        if d < 0 or d > 2048:
            continue
        m = sum(1 for (win, dil) in ((128, 1), (512, 4), (2048, 16)) if d % dil == 0 and d <= win)
        if m == 0:
            continue
        oh[_t5_bucket(d), y] = 1.0
        cv[0, y] = math.log(m)
    c["c_oh"] = oh
    c["c_cv"] = cv
    c["c_ident"] = np.eye(128, dtype=np.float32)
    c["c_anti"] = np.eye(128, dtype=np.float32)[::-1].copy()
    c["c_ones"] = np.ones((128, 128), np.float32)
    i = np.arange(128)
    c["c_neglow"] = np.where(i[None, :] <= i[:, None], 0.0, NEG).astype(np.float32)
    c["c_negup"] = np.where(i[:, None] <= i[None, :], 0.0, NEG).astype(np.float32)
    c["c_strict"] = (i[None, :] < i[:, None]).astype(np.float32)
    sel = np.zeros((4, 4, 128), np.float32)
    for h in range(4):
        sel[h, h, :] = 1.0
    c["c_sel"] = sel.reshape(4, 512)
    return c


CONST_SHAPES = {"c_oh": (32, VL), "c_cv": (1, VL), "c_ident": (128, 128), "c_anti": (128, 128),
                "c_ones": (128, 128), "c_neglow": (128, 128), "c_negup": (128, 128),
                "c_strict": (128, 128), "c_sel": (4, 512)}

W_SHAPES = {
    "rel_bias": (32, 8), "g_mix": ("L", D), "w_in": ("L", D, 5128), "dn_conv_w": ("L", 4, 1536),
    "dn_a_log": ("L", 4), "dn_dt_bias": ("L", 4), "dn_norm_g": ("L", 128),
    "ssm_a_re": ("L", 32, 64), "ssm_a_im": ("L", 32, 64), "ssm_log_dt": ("L", 32),
    "ssm_b_re": ("L", 32, 64, 16), "ssm_b_im": ("L", 32, 64, 16), "ssm_c_re": ("L", 32, 16, 64),
    "ssm_c_im": ("L", 32, 16, 64), "ssm_d": ("L", 512), "ssm_w_glu": ("L", 512, 512), "ssm_b_glu": ("L", 512),
    "lru_conv_w": ("L", 4, 512), "lru_conv_b": ("L", 512), "lru_w_a": ("L", 8, 64, 64), "lru_b_a": ("L", 512),
    "lru_w_x": ("L", 8, 64, 64), "lru_b_x": ("L", 512), "lru_lam": ("L", 512), "w_out": ("L", D, D),
    "g_cross": ("L", D), "g_mem": ("L", D), "w_mem_q": ("L", D, 512), "w_mem_k": ("L", D, 512),
    "w_mem_v": ("L", D, 512), "w_mem_o": ("L", 512, D), "g_ffn": ("L", D), "w_up": ("L", D, 2 * DFF),
    "ffn_conv_w": ("L", 3, 2 * DFF), "w_down": ("L", DFF, D), "g_final": (1, D),
}
IN_SHAPES = {
    "x_prompt": (TP, D), "x_sample": (TS, D), "mem_prompt": (256, D),
    "cache_mem_k": ("L", 256, 512), "cache_mem_v": ("L", 256, 512),
    "cache_win_k": ("L", 2048, 512), "cache_win_v": ("L", 2048, 512),
    "state_delta": ("L", 4, 128, 128), "state_delta_conv": ("L", 3, 1536),
    "state_ssm_re": ("L", 32, 64), "state_ssm_im": ("L", 32, 64), "state_lru": ("L", 512),
    "state_lru_conv": ("L", 3, 512), "state_ffn_conv": ("L", 2, 2 * DFF),
}
OUT_SHAPES = [
    ("y_prompt", (TP, D)), ("y_sample", (TS, D)),
    ("p_mem_k", ("L", 256, 512)), ("p_mem_v", ("L", 256, 512)),
    ("p_win_k", ("L", 2048, 512)), ("p_win_v", ("L", 2048, 512)),
    ("p_delta", ("L", 4, 128, 128)), ("p_delta_conv", ("L", 3, 1536)),
    ("p_ssm_re", ("L", 32, 64)), ("p_ssm_im", ("L", 32, 64)), ("p_lru", ("L", 512)),
    ("p_lru_conv", ("L", 3, 512)), ("p_ffn_conv", ("L", 2, 2 * DFF)),
    ("s_win_k", ("L", 8, 512)), ("s_win_v", ("L", 8, 512)),
    ("s_delta", ("L", 4, 128, 128)), ("s_delta_conv", ("L", 3, 1536)),
    ("s_ssm_re", ("L", 32, 64)), ("s_ssm_im", ("L", 32, 64)), ("s_lru", ("L", 512)),
    ("s_lru_conv", ("L", 3, 512)), ("s_ffn_conv", ("L", 2, 2 * DFF)),
]


def _shape(s, L):
    return [L if v == "L" else v for v in s]


class K:
    def __init__(self, NL, stages="all"):
        self.NL = NL
        self.stages = stages
        nc = self.nc = bass.Bass("TRN2", target_bir_lowering=False)
        self.P = Prog(nc)
        self.i = {}
        for n, s in list(W_SHAPES.items()) + list(IN_SHAPES.items()) + list(CONST_SHAPES.items()):
            self.i[n] = nc.dram_tensor(n, _shape(s, NL), F32, kind="ExternalInput").ap()
        self.o = {}
        for n, s in OUT_SHAPES:
            self.o[n] = nc.dram_tensor(n, _shape(s, NL), F32, kind="ExternalOutput").ap()
        self.dbg = {}
        if stages == "dbg":
            for n, shp in [("dbg_xT", [D, TT]), ("dbg_swaT", [1536, TT]), ("dbg_rs", [128, 512]), ("dbg_xn", [128, 512]), ("dbg_g", [128, 64])]:
                self.dbg[n] = nc.dram_tensor(n, shp, F32, kind="ExternalOutput").ap()
            self.dbg["dbg_mix"] = nc.dram_tensor("dbg_mix", [D, TT], BF16, kind="ExternalOutput").ap()
        self.d = {}
        for n, s, dt in [("xT", [D, TT], F32), ("dnqkvT", [1536, TT], F32), ("dnzT", [512, TT], F32),
                         ("dnbaT", [8, TT], F32), ("ssmuT", [512, TT], F32), ("swaT", [1536, TT], F32),
                         ("lruxT", [512, TT], F32), ("lrugT", [512, TT], F32), ("mixT", [D, TT], BF16),
                         ("memT", [D, 256], F32), ("bvec", [8, VL], F32), ("strip", [8, 128, SW], F32)]:
            self.d[n] = nc.dram_tensor("scr_" + n, s, dt, kind="Internal").ap()
        self.off = 16384
        self.seq = 0

    def sb(self, name, shape, dt):
        nb = int(np.prod(shape[1:])) * (4 if dt == F32 else 2)
        nb = (nb + 63) // 64 * 64
        self.seq += 1
        t = self.nc.alloc_sbuf_tensor_at("%s_%d" % (name, self.seq), shape, dt, offset=self.off)
        self.off += nb
        assert self.off <= 229376, (name, self.off)
        return t

    def mm(self, out, lhsT, rhs, start, stop, reads, writes):
        self.P.op("pe", lambda e: e.matmul(out, lhsT=lhsT, rhs=rhs, start=start, stop=stop), reads, writes)

    def tr(self, out, in_, reads, writes):
        ident = self.ident
        n = in_.shape[0]
        self.P.op("pe", lambda e: e.transpose(out=out, in_=in_, identity=ident[0:n, 0:n]), reads, writes)

    def act(self, out, in_, func, reads, writes, scale=1.0, bias=0.0, accum_out=None, eng="act"):
        kw = {}
        if accum_out is not None:
            kw["accum_out"] = accum_out
        self.P.op("act", lambda e: e.activation(out=out, in_=in_, func=func, scale=scale, bias=bias, **kw), reads, writes)

    def cp(self, eng, out, in_, reads, writes):
        if eng == "act":
            self.P.op("act", lambda e: e.activation(out=out, in_=in_, func=AF.Copy), reads, writes)
        else:
            self.P.op(eng, lambda e: e.tensor_copy(out=out, in_=in_), reads, writes)

    def tt(self, eng, out, in0, in1, op, reads, writes):
        self.P.op(eng, lambda e: e.tensor_tensor(out=out, in0=in0, in1=in1, op=op), reads, writes)

    def ts(self, eng, out, in0, s1, op0, reads, writes, s2=None, op1=None):
        if op1 is None:
            self.P.op(eng, lambda e: e.tensor_scalar(out=out, in0=in0, scalar1=s1, scalar2=None, op0=op0), reads, writes)
        else:
            self.P.op(eng, lambda e: e.tensor_scalar(out=out, in0=in0, scalar1=s1, scalar2=s2, op0=op0, op1=op1), reads, writes)

    def stt(self, out, in0, scalar, in1, op0, op1, reads, writes):
        self.P.op("dve", lambda e: e.scalar_tensor_tensor(out=out, in0=in0, scalar=scalar, in1=in1, op0=op0, op1=op1), reads, writes)

    def recip(self, out, in_, reads, writes):
        self.P.op("dve", lambda e: e.reciprocal(out=out, in_=in_), reads, writes)

    def scan(self, out, d0, d1, init, reads, writes):
        self.P.op("dve", lambda e: e.tensor_tensor_scan(out=out, data0=d0, data1=d1, initial=init, op0=ALU.mult, op1=ALU.add), reads, writes)

    def memset(self, eng, ap, v, writes):
        self.P.op(eng, lambda e: e.memset(ap, v), (), writes)

    def dma(self, out, in_, reads, writes, eng="sp", **kw):
        self.P.dma(eng, out, in_, reads, writes, **kw)

    def psum(self):
        self.psi = (self.psi + 1) % len(self.psr)
        b = self.psr[self.psi]
        return self.ps[b], ("ps", b)

    def setup(self):
        nc = self.nc
        self.ps = [self.stack.enter_context(nc.psum_tensor("ps%d" % b, [128, 512], F32)) for b in range(8)]
        self.psr = [0, 1, 2, 3]
        self.psi = 0
        self.ident = self.sb("ident", [128, 128], F32)
        self.anti = self.sb("anti", [128, 128], F32)
        self.ones = self.sb("ones", [128, 128], F32)
        self.onesb = self.sb("onesb", [128, 128], BF16)
        self.neglow = self.sb("neglow", [128, 128], F32)
        self.negup = self.sb("negup", [128, 128], F32)
        self.strict = self.sb("strict", [128, 128], F32)
        self.sel = self.sb("sel", [4, 512], F32)
        for t, n in [(self.ident, "c_ident"), (self.anti, "c_anti"), (self.ones, "c_ones"), (self.neglow, "c_neglow"),
                     (self.negup, "c_negup"), (self.strict, "c_strict"), (self.sel, "c_sel")]:
            self.dma(t[:], self.i[n], [], [n])
        self.cp("dve", self.onesb[:], self.ones[:], ["c_ones"], ["onesb"])
        self.wb = []
        for j in range(NWB):
            o = self.off
            a = nc.alloc_sbuf_tensor_at("wA%d" % j, [128, 16, 256], BF16, offset=o)
            b = nc.alloc_sbuf_tensor_at("wB%d" % j, [128, 44, 128], BF16, offset=o)
            c = nc.alloc_sbuf_tensor_at("wC%d" % j, [128, 4, 256], BF16, offset=o)
            self.wb.append((a, b, c, ("w", j)))
            self.off += 11264
        self.wbi = 0
        self.wring = self.wb
        self.base = self.off
        self.P.barrier()

    def next_w(self):
        self.wbi = (self.wbi + 1) % len(self.wring)
        return self.wring[self.wbi]

    def ring_extra(self, n):
        self.wring = list(self.wb)
        for j in range(n):
            o = self.off
            self.seq += 1
            a = self.nc.alloc_sbuf_tensor_at("wAx%d_%d" % (j, self.seq), [128, 16, 256], BF16, offset=o)
            b = self.nc.alloc_sbuf_tensor_at("wBx%d_%d" % (j, self.seq), [128, 44, 128], BF16, offset=o)
            c = self.nc.alloc_sbuf_tensor_at("wCx%d_%d" % (j, self.seq), [128, 4, 256], BF16, offset=o)
            self.wring.append((a, b, c, ("w", NWB + j)))
            self.off += 11264
            assert self.off <= 229376, ("ring_extra", self.off)

    def colload(self, dst, src_rows, n, key):
        st = self.cl_st[self.cl_i % 2]
        sk = ("clst", self.cl_i % 2)
        self.cl_i += 1
        self.dma(st[0:n, :], src_rows, [], [sk])
        ps, pk = self.psum()
        self.tr(ps[:, 0:n], st[0:n, :], [sk, "c_ident"], [pk])
        self.cp("dve", dst, ps[:, 0:n], [pk], [key])

    def stage_input(self):
        self.off = self.lbase
        xin = [self.sb("xin", [128, D], F32) for _ in range(2)]
        xo = [self.sb("xo", [128, 16, 128], F32) for _ in range(2)]
        tiles = [(self.i["x_prompt"], t * 128, 128, t * 128, "xT") for t in range(16)]
        tiles.append((self.i["x_sample"], 0, TS, TP, "xT"))
        tiles += [(self.i["mem_prompt"], t * 128, 128, t * 128, "memT") for t in range(2)]
        for n, (src, r0, nr, c0, dst) in enumerate(tiles):
            xi, xk = xin[n % 2], ("xin", n % 2)
            xot, ok = xo[n % 2], ("xo", n % 2)
            self.dma(xi[0:nr, :], src[r0:r0 + nr, :], [], [xk])
            for f4 in range(4):
                ps, pk = self.psum()
                for j in range(4):
                    f = f4 * 4 + j
                    self.tr(ps[:, j * 128:j * 128 + nr], xi[0:nr, f * 128:(f + 1) * 128], [xk, "c_ident"], [pk])
                src_ps = ps[:].rearrange("p (j t) -> p j t", j=4)[:, :, 0:nr]
                self.cp("dve" if f4 % 2 else "act", xot[:, f4 * 4:f4 * 4 + 4, 0:nr], src_ps, [pk], [ok])
            self.dma(self.d[dst].rearrange("(f p) t -> p f t", p=128)[:, :, c0:c0 + nr], xot[:, :, 0:nr], [ok], [self.d[dst].name])
        self.P.barrier()

    def norm_T(self, src, gcol, chunks, dst, dkey, tbase=0):
        NS = 256
        xl = [self.sb("nxl", [128, 16, NS], F32) for _ in range(2)]
        sq = [self.sb("nsq", [128, NS], F32) for _ in range(2)]
        rs = self.sb("nrs", [128, NS], F32)
        ci = 0
        for (T0, TN) in chunks:
            for t0 in range(T0, T0 + TN, NS):
                tn = min(NS, T0 + TN - t0)
                x, xk = xl[ci % 2], ("nxl", ci % 2)
                ci += 1
                self.dma(x[:, :, 0:tn], src.rearrange("(f p) t -> p f t", p=128)[:, :, t0:t0 + tn], [src.name], [xk])
                ps, pk = self.psum()
                for f in range(16):
                    s, sk = sq[f % 2], ("nsq", f % 2)
                    self.act(s[:, 0:tn], x[:, f, 0:tn], AF.Square, [xk], [sk])
                    self.mm(ps[:, 0:tn], self.ones[:], s[:, 0:tn], f == 0, f == 15, [sk, "c_ones"], [pk])
                self.act(rs[:, 0:tn], ps[:, 0:tn], AF.Sqrt, [pk], ["nrs"], scale=1.0 / D, bias=self.epsc[:, 0:1])
                self.recip(rs[:, 0:tn], rs[:, 0:tn], ["nrs"], ["nrs"])
                for f in range(16):
                    self.stt(dst[:, f, t0 - tbase:t0 - tbase + tn], x[:, f, 0:tn], gcol[:, f:f + 1], rs[:, 0:tn],
                             ALU.mult, ALU.mult, [xk, "nrs", "gcols"], [(dkey, T0)])

    def dense(self, Wd, nK, groups, act, akey, chunks, consume, wv=0, pre=None):
        for (c0, w) in groups:
            wt = self.next_w()
            wtile, wk = wt[wv], wt[3]
            self.dma(wtile[:, 0:nK, 0:w], Wd[:, c0:c0 + w].rearrange("(k p) c -> p k c", p=128), [], [wk], eng="pool")
            tiles = [(m0, min(128, w - m0), t0, tn, a0) for m0 in range(0, w, 128) for (t0, tn, a0) in chunks]
            if pre is not None:
                pre(c0 + tiles[0][0], tiles[0][1], tiles[0][2], tiles[0][3])
            for ti, (m0, mw, t0, tn, a0) in enumerate(tiles):
                ps, pk = self.psum()
                for k in range(nK):
                    self.mm(ps[0:mw, 0:tn], wtile[:, k, m0:m0 + mw], act[:, k, a0:a0 + tn], k == 0, k == nK - 1,
                            [wk, (akey, t0), akey], [pk])
                if pre is not None and ti + 1 < len(tiles):
                    n_ = tiles[ti + 1]
                    pre(c0 + n_[0], n_[1], n_[2], n_[3])
                consume(c0 + m0, mw, t0, tn, ps, pk)

    def to_dram(self, dst, r0):
        def f(c, mw, t0, tn, ps, pk):
            i = self.evi % 4
            self.evi += 1
            s, sk = self.evs[i], ("evs", i)
            self.cp("act" if i % 2 else "dve", s[0:mw, 0:tn], ps[0:mw, 0:tn], [pk], [sk])
            self.dma(dst[c - r0:c - r0 + mw, t0:t0 + tn], s[0:mw, 0:tn], [sk], [dst.name])
        return f

    def load_gcols(self, l):
        for j, n in enumerate(["g_mix", "g_cross", "g_ffn", "g_mem"]):
            self.colload(self.gcols[:, j, :], self.i[n][l].rearrange("(f p) -> f p", p=128), 16, "gcols")

    def stage_in(self, l):
        self.off = self.lbase
        xn = self.sb("xnT", [128, 16, TT], BF16)
        self.evs = [self.sb("evs", [128, 512], F32) for _ in range(4)]
        self.evi = 0
        self.norm_T(self.d["xT"], self.gcols[:, 0, :], CH, xn, "xnT")
        if self.dbg:
            self.P.barrier()
            dt = self.sb("dbgt", [128, 512], F32)
            self.cp("dve", dt[:], xn[:, 0, 0:512], [], ["dbgt"])
            self.dma(self.dbg["dbg_xn"], dt[:], ["dbgt"], [])
            self.dma(self.dbg["dbg_rs"], self.last_rs[:], [], [])
            self.dma(self.dbg["dbg_g"], self.gcols[:].rearrange("p a b -> p (a b)"), [], [])
            self.dma(self.dbg["dbg_xT"], self.d["xT"], [], [])
            self.P.barrier()
        W = self.i["w_in"][l]
        ch3 = [(t0, tn, t0) for (t0, tn) in CH]
        segs = [(0, 1536, "dnqkvT"), (1536, 512, "dnzT"), (2048, 8, "dnbaT"), (2056, 512, "ssmuT"),
                (2568, 1536, "swaT"), (4104, 512, "lruxT"), (4616, 512, "lrugT")]
        for (c0, n, dn) in segs:
            groups = [(c0 + g, min(256, n - g)) for g in range(0, n, 256)]
            self.dense(W, 16, groups, xn, "xnT", ch3, self.to_dram(self.d[dn], c0))
        self.P.barrier()
        if self.dbg:
            self.dma(self.dbg["dbg_swaT"], self.d["swaT"], [], [])
            self.P.barrier()

    def tok_major_out(self, srcT, r0, ncol, t0, nt, dst, c0, extra=None):
        assert ncol % 128 == 0
        nf = ncol // 128
        st = [self.sb("tmi", [128, nf, 512], F32) for _ in range(2)]
        so = [self.sb("tmo", [128, ncol], F32) for _ in range(2)]
        n = 0
        for tc in range(0, nt, 512):
            tw = min(512, nt - tc)
            s, sk = st[(tc // 512) % 2], ("tmi", (tc // 512) % 2)
            self.dma(s[:, :, 0:tw], srcT[r0:r0 + ncol, :].rearrange("(f p) t -> p f t", p=128)[:, :, t0 + tc:t0 + tc + tw],
                     [srcT.name], [sk])
            for tb in range(0, tw, 128):
                bw = min(128, tw - tb)
                o, ok = so[n % 2], ("tmo", n % 2)
                n += 1
                for f4 in range(0, nf, 4):
                    ps, pk = self.psum()
                    for j in range(min(4, nf - f4)):
                        self.tr(ps[0:bw, j * 128:(j + 1) * 128], s[:, f4 + j, tb:tb + bw], [sk, "c_ident"], [pk])
                    w = min(4, nf - f4) * 128
                    self.cp("act" if (f4 // 4) % 2 else "dve", o[0:bw, f4 * 128:f4 * 128 + w], ps[0:bw, 0:w], [pk], [ok])
                self.dma(dst[tc + tb:tc + tb + bw, c0:c0 + ncol], o[0:bw, :], [ok], [dst.name])
                if extra is not None:
                    extra(tc + tb, bw, o, ok)

    def stage_kvout(self, l):
        self.off = self.lbase
        for (t0, nt), pre in zip(GRP, ("p", "s")):
            self.tok_major_out(self.d["swaT"], 512, 512, t0, nt, self.o[pre + "_win_k"][l], 0)
            self.off = self.lbase
            self.tok_major_out(self.d["swaT"], 1024, 512, t0, nt, self.o[pre + "_win_v"][l], 0)
            self.off = self.lbase
        self.P.barrier()

    def conv_tail_out(self, srcT, nrow, ntail, dstp, dsts, l):
        for (t0, nt), dst in zip(GRP, (dstp, dsts)):
            self.off = self.lbase
            for c0 in range(0, nrow, 512):
                self.tok_major_out(srcT, c0, min(512, nrow - c0), t0 + nt - ntail, ntail, dst[l], c0)
                self.off = self.lbase
        self.P.barrier()

    def stage_memkv(self, l):
        self.off = self.lbase
        mn = self.sb("mnT", [128, 16, 256], BF16)
        self.norm_T(self.d["memT"], self.gcols[:, 3, :], [(0, 256)], mn, "mnT")
        so = [self.sb("mko", [128, 512], F32) for _ in range(2)]
        n = 0
        for wname, oname in (("w_mem_k", "p_mem_k"), ("w_mem_v", "p_mem_v")):
            W = self.i[wname][l]
            wts = []
            for half in range(2):
                wt = self.next_w()
                self.dma(wt[0][:, 0:16, 0:256], W[:, half * 256:(half + 1) * 256].rearrange("(k p) c -> p k c", p=128),
                         [], [wt[3]], eng="pool")
                wts.append(wt)
            for tt in range(2):
                o, ok = so[n % 2], ("mko", n % 2)
                n += 1
                for half in range(2):
                    ps, pk = self.psum()
                    for k in range(16):
                        self.mm(ps[:, 0:256], mn[:, k, tt * 128:(tt + 1) * 128], wts[half][0][:, k, 0:256], k == 0, k == 15,
                                [wts[half][3], ("mnT", 0)], [pk])
                    self.cp("dve" if half else "act", o[:, half * 256:(half + 1) * 256], ps[:, 0:256], [pk], [ok])
                self.dma(self.o[oname][l][tt * 128:(tt + 1) * 128, :], o[:], [ok], [self.o[oname].name])
        self.P.barrier()

    def stage_lru(self, l):
        self.off = self.lbase
        I = self.i
        lp = self.sb("lrup", [128, 40], F32)
        self.colload(lp[:, 0:16], I["lru_conv_w"][l].rearrange("j (f p) -> (j f) p", p=128), 16, "lrup")
        for j, n in enumerate(["lru_conv_b", "lru_b_a", "lru_b_x", "lru_lam"]):
            self.colload(lp[:, 16 + 4 * j:20 + 4 * j], I[n][l].rearrange("(f p) -> f p", p=128), 4, "lrup")
        self.act(lp[:, 32:36], lp[:, 28:32], AF.Exp, ["lrup"], ["lrup"], scale=-1.0)
        self.act(lp[:, 32:36], lp[:, 32:36], AF.Ln, ["lrup"], ["lrup"], bias=self.onec[:, 0:1])
        self.ts("dve", lp[:, 32:36], lp[:, 32:36], -8.0, ALU.mult, ["lrup"], ["lrup"])
        WA = self.sb("lruWA", [128, 4, 128], F32)
        WX = self.sb("lruWX", [128, 4, 128], F32)
        self.memset("dve", WA[:], 0.0, ["lruW"])
        self.memset("dve", WX[:], 0.0, ["lruW"])
        for k in range(8):
            hb = (k % 2) * 64
            self.dma(WA[hb:hb + 64, k // 2, hb:hb + 64], I["lru_w_a"][l][k], [], ["lruW"])
            self.dma(WX[hb:hb + 64, k // 2, hb:hb + 64], I["lru_w_x"][l][k], [], ["lruW"])
        T = TP
        xpad = self.sb("l_xpad", [128, 3 + T], F32)
        xc = self.sb("l_xc", [128, T], F32)
        r = self.sb("l_r", [128, T], F32)
        ig = self.sb("l_ig", [128, T], F32)
        a = self.sb("l_a", [128, T], F32)
        bx = self.sb("l_bx", [128, T], F32)
        g = self.sb("l_g", [128, T], F32)
        u = self.sb("l_u", [128, T], F32)
        yb = self.sb("l_yb", [128, T], BF16)
        h0 = self.sb("l_h0", [128, 4], F32)
        self.colload(h0[:, 0:4], I["state_lru"][l].rearrange("(f p) -> f p", p=128), 4, "l_h0")
        for i in range(4):
            for gi, (t0, T) in enumerate(GRP):
                pre = "ps"[gi] + "_"
                if gi == 0:
                    self.memset("pool", xpad[:, 0:3], 0.0, ["l_xpad"])
                else:
                    self.colload(xpad[:, 0:3], I["state_lru_conv"][l][:, i * 128:(i + 1) * 128], 3, "l_xpad")
                self.dma(xpad[:, 3:3 + T], self.d["lruxT"][i * 128:(i + 1) * 128, t0:t0 + T], [], ["l_xpad"])
                self.dma(g[:, 0:T], self.d["lrugT"][i * 128:(i + 1) * 128, t0:t0 + T], [], ["l_g"])
                self.ts("dve", xc[:, 0:T], xpad[:, 0:T], lp[:, i:i + 1], ALU.mult, ["l_xpad", "lrup"], ["l_xc"],
                        s2=lp[:, 16 + i:17 + i], op1=ALU.add)
                for j in range(1, 4):
                    self.stt(xc[:, 0:T], xpad[:, j:j + T], lp[:, 4 * j + i:4 * j + i + 1], xc[:, 0:T], ALU.mult, ALU.add,
                             ["l_xpad", "lrup", "l_xc"], ["l_xc"])
                for c0 in range(0, T, 512):
                    cw = min(512, T - c0)
                    ps, pk = self.psum()
                    self.mm(ps[:, 0:cw], WA[:, i, :], xc[:, c0:c0 + cw], True, True, ["lruW", "l_xc"], [pk])
                    self.act(r[:, c0:c0 + cw], ps[:, 0:cw], AF.Sigmoid, [pk, "lrup"], ["l_r"], bias=lp[:, 20 + i:21 + i])
                    ps, pk = self.psum()
                    self.mm(ps[:, 0:cw], WX[:, i, :], xc[:, c0:c0 + cw], True, True, ["lruW", "l_xc"], [pk])
                    self.act(ig[:, c0:c0 + cw], ps[:, 0:cw], AF.Sigmoid, [pk, "lrup"], ["l_ig"], bias=lp[:, 24 + i:25 + i])
                self.act(a[:, 0:T], r[:, 0:T], AF.Exp, ["l_r", "lrup"], ["l_a"], scale=lp[:, 32 + i:33 + i])
                self.tt("pool", bx[:, 0:T], a[:, 0:T], a[:, 0:T], ALU.mult, ["l_a"], ["l_bx"])
                self.ts("dve", bx[:, 0:T], bx[:, 0:T], -1.0, ALU.mult, ["l_bx"], ["l_bx"], s2=1.0, op1=ALU.add)
                self.ts("dve", bx[:, 0:T], bx[:, 0:T], 0.0, ALU.max, ["l_bx"], ["l_bx"])
                self.act(bx[:, 0:T], bx[:, 0:T], AF.Sqrt, ["l_bx"], ["l_bx"])
                self.tt("pool", ig[:, 0:T], ig[:, 0:T], xc[:, 0:T], ALU.mult, ["l_ig", "l_xc"], ["l_ig"])
                self.tt("dve", bx[:, 0:T], bx[:, 0:T], ig[:, 0:T], ALU.mult, ["l_bx", "l_ig"], ["l_bx"])
                init = 0.0 if gi == 0 else h0[:, i:i + 1]
                self.scan(r[:, 0:T], a[:, 0:T], bx[:, 0:T], init, ["l_a", "l_bx", "l_h0", "l_r"], ["l_r"])
                self.dma(self.o[pre + "lru"][l][i * 128:(i + 1) * 128].rearrange("(p o) -> p o", o=1), r[:, T - 1:T],
                         ["l_r"], [self.o[pre + "lru"].name])
                self.tt("pool", u[:, 0:T], g[:, 0:T], g[:, 0:T], ALU.mult, ["l_g"], ["l_u"])
                self.ts("dve", u[:, 0:T], u[:, 0:T], 0.044715, ALU.mult, ["l_u"], ["l_u"], s2=1.0, op1=ALU.add)
                self.tt("dve", u[:, 0:T], u[:, 0:T], g[:, 0:T], ALU.mult, ["l_u", "l_g"], ["l_u"])
                self.act(u[:, 0:T], u[:, 0:T], AF.Sigmoid, ["l_u"], ["l_u"], scale=1.5957691216057308)
                self.tt("pool", u[:, 0:T], u[:, 0:T], g[:, 0:T], ALU.mult, ["l_u", "l_g"], ["l_u"])
                self.tt("dve", yb[:, 0:T], u[:, 0:T], r[:, 0:T], ALU.mult, ["l_u", "l_r"], ["l_yb"])
                self.dma(self.d["mixT"][1536 + i * 128:1536 + (i + 1) * 128, t0:t0 + T], yb[:, 0:T], ["l_yb"], ["mixT"])
        self.P.barrier()

    def gelu(self, out, x, tmp, kx, kt, ko):
        self.tt("pool", tmp, x, x, ALU.mult, [kx], [kt])
        self.ts("dve", tmp, tmp, 0.044715, ALU.mult, [kt], [kt], s2=1.0, op1=ALU.add)
        self.tt("dve", tmp, tmp, x, ALU.mult, [kt, kx], [kt])
        self.act(tmp, tmp, AF.Sigmoid, [kt], [kt], scale=1.5957691216057308)
        self.tt("dve", out, tmp, x, ALU.mult, [kt, kx], [ko] if ko != kt else [kt])

    def stage_ssm(self, l):
        self.off = self.lbase
        I = self.i
        PI = math.pi
        sp = self.sb("ssp", [128, 24, 16], F32)
        names = ["are", "aim", "ldt", "step", "ars", "ais", "mag", "c1", "s1", "abr", "abi", "den", "cr", "ci",
                 "t0", "t1", "h0r", "h0i", "g0r", "g0i", "t2", "t3"]
        ix = {n: k for k, n in enumerate(names)}
        P_ = lambda n: sp[:, ix[n], :]
        K_ = "ssp"
        self.colload(P_("are"), I["ssm_a_re"][l].rearrange("(j a) n -> j (a n)", a=2), 16, K_)
        self.colload(P_("aim"), I["ssm_a_im"][l].rearrange("(j a) n -> j (a n)", a=2), 16, K_)
        self.colload(P_("h0r"), I["state_ssm_re"][l].rearrange("(j a) n -> j (a n)", a=2), 16, K_)
        self.colload(P_("h0i"), I["state_ssm_im"][l].rearrange("(j a) n -> j (a n)", a=2), 16, K_)
        ldr = self.sb("ldr", [1, 32], F32)
        self.dma(ldr[:], I["ssm_log_dt"][l:l + 1, :], [], ["ldr"])
        ps, pk = self.psum()
        self.mm(ps[:, 0:32], self.ones[0:1, :], ldr[0:1, :], True, True, ["ldr"], [pk])
        pv = ps[:, 0:32].rearrange("p (j a) -> p j a", a=2)
        self.cp("dve", P_("ldt")[0:64, :], pv[0:64, :, 0], [pk], [K_])
        self.cp("dve", P_("ldt")[64:128, :], pv[64:128, :, 1], [pk], [K_])
        self.act(P_("step"), P_("ldt"), AF.Exp, [K_], [K_])
        self.tt("dve", P_("ars"), P_("are"), P_("step"), ALU.mult, [K_], [K_])
        self.tt("dve", P_("ais"), P_("aim"), P_("step"), ALU.mult, [K_], [K_])
        self.act(P_("mag"), P_("ars"), AF.Exp, [K_], [K_])
        it = self.sb("ssi", [128, 16], mybir.dt.int32)

        def sin_of(dst, src, shift):
            self.ts("dve", P_("t0"), src, shift, ALU.add, [K_], [K_])
            self.ts("dve", P_("t1"), P_("t0"), 1.0 / (2 * PI), ALU.mult, [K_], [K_])
            self.cp("dve", it[:], P_("t1"), [K_], ["ssi"])
            self.cp("dve", P_("t1"), it[:], ["ssi"], [K_])
            self.stt(P_("t0"), P_("t1"), -2 * PI, P_("t0"), ALU.mult, ALU.add, [K_], [K_])
            self.ts("dve", P_("t1"), P_("t0"), PI, ALU.is_gt, [K_], [K_])
            self.stt(P_("t0"), P_("t1"), -2 * PI, P_("t0"), ALU.mult, ALU.add, [K_], [K_])
            self.ts("dve", P_("t1"), P_("t0"), -PI, ALU.is_lt, [K_], [K_])
            self.stt(P_("t0"), P_("t1"), 2 * PI, P_("t0"), ALU.mult, ALU.add, [K_], [K_])
            self.act(dst, P_("t0"), AF.Sin, [K_], [K_])
        sin_of(P_("s1"), P_("ais"), 0.0)
        sin_of(P_("c1"), P_("ais"), PI / 2)
        self.tt("dve", P_("abr"), P_("mag"), P_("c1"), ALU.mult, [K_], [K_])
        self.tt("dve", P_("abi"), P_("mag"), P_("s1"), ALU.mult, [K_], [K_])
        self.tt("dve", P_("den"), P_("are"), P_("are"), ALU.mult, [K_], [K_])
        self.tt("dve", P_("t0"), P_("aim"), P_("aim"), ALU.mult, [K_], [K_])
        self.tt("dve", P_("den"), P_("den"), P_("t0"), ALU.add, [K_], [K_])
        self.recip(P_("den"), P_("den"), [K_], [K_])
        self.ts("dve", P_("t2"), P_("abr"), -1.0, ALU.add, [K_], [K_])
        self.tt("dve", P_("t0"), P_("t2"), P_("are"), ALU.mult, [K_], [K_])
        self.tt("dve", P_("t1"), P_("abi"), P_("aim"), ALU.mult, [K_], [K_])
        self.tt("dve", P_("cr"), P_("t0"), P_("t1"), ALU.add, [K_], [K_])
        self.tt("dve", P_("cr"), P_("cr"), P_("den"), ALU.mult, [K_], [K_])
        self.tt("dve", P_("t0"), P_("abi"), P_("are"), ALU.mult, [K_], [K_])
        self.tt("dve", P_("t1"), P_("t2"), P_("aim"), ALU.mult, [K_], [K_])
        self.tt("dve", P_("ci"), P_("t0"), P_("t1"), ALU.subtract, [K_], [K_])
        self.tt("dve", P_("ci"), P_("ci"), P_("den"), ALU.mult, [K_], [K_])
        self.tt("dve", P_("t0"), P_("c1"), P_("h0r"), ALU.mult, [K_], [K_])
        self.tt("dve", P_("t1"), P_("s1"), P_("h0i"), ALU.mult, [K_], [K_])
        self.tt("dve", P_("g0r"), P_("t0"), P_("t1"), ALU.subtract, [K_], [K_])
        self.tt("dve", P_("t0"), P_("c1"), P_("h0i"), ALU.mult, [K_], [K_])
        self.tt("dve", P_("t1"), P_("s1"), P_("h0r"), ALU.mult, [K_], [K_])
        self.tt("dve", P_("g0i"), P_("t0"), P_("t1"), ALU.add, [K_], [K_])
        pwc = self.sb("spwc", [128, 12, 16], F32)
        pws = self.sb("spws", [128, 12, 16], F32)
        self.cp("dve", pwc[:, 0, :], P_("c1"), [K_], ["spw"])
        self.cp("dve", pws[:, 0, :], P_("s1"), [K_], ["spw"])
        for k in range(11):
            self.tt("dve", P_("t0"), pwc[:, k, :], pwc[:, k, :], ALU.mult, ["spw", K_], [K_])
            self.tt("dve", P_("t1"), pws[:, k, :], pws[:, k, :], ALU.mult, ["spw", K_], [K_])
            self.tt("dve", pwc[:, k + 1, :], P_("t0"), P_("t1"), ALU.subtract, [K_], ["spw"])
            self.tt("dve", P_("t0"), pwc[:, k, :], pws[:, k, :], ALU.mult, ["spw", K_], [K_])
            self.ts("dve", pws[:, k + 1, :], P_("t0"), 2.0, ALU.mult, [K_], ["spw"])
        BR = self.sb("sBR", [128, 16, 16], F32)
        BI = self.sb("sBI", [128, 16, 16], F32)
        QR = self.sb("sQR", [128, 16, 16], F32)
        QI = self.sb("sQI", [128, 16, 16], F32)
        TM = self.sb("sTM", [128, 16, 16], F32)
        self.dma(BR[:], I["ssm_b_re"][l].rearrange("(j a) n c -> (a n) j c", a=2), [], ["sB"])
        self.dma(BI[:], I["ssm_b_im"][l].rearrange("(j a) n c -> (a n) j c", a=2), [], ["sB"])
        crb = P_("cr").unsqueeze(2).to_broadcast([128, 16, 16])
        cib = P_("ci").unsqueeze(2).to_broadcast([128, 16, 16])
        self.tt("dve", QR[:], BR[:], crb, ALU.mult, ["sB", K_], ["sQ"])
        self.tt("dve", TM[:], BI[:], cib, ALU.mult, ["sB", K_], ["sTM"])
        self.tt("dve", QR[:], QR[:], TM[:], ALU.subtract, ["sQ", "sTM"], ["sQ"])
        self.tt("dve", QI[:], BI[:], crb, ALU.mult, ["sB", K_], ["sQ"])
        self.tt("dve", TM[:], BR[:], cib, ALU.mult, ["sB", K_, "sQ"], ["sTM"])
        self.tt("dve", QI[:], QI[:], TM[:], ALU.add, ["sQ", "sTM"], ["sQ"])
        LB = [self.sb("sLB%d" % q, [128, 16, 128], F32) for q in range(2)]
        WC = [self.sb("sWC%d" % q, [128, 16, 128], F32) for q in range(2)]
        Z = self.sb("sZ", [128, 128], F32)
        for q, Q in enumerate((QR, QI)):
            for j in range(16):
                c0 = 32 * (j % 4)
                self.memset("pool", Z[:], 0.0, ["sZ"])
                self.cp("pool", Z[0:64, c0:c0 + 16], Q[0:64, j, :], ["sQ", "sZ"], ["sZ"])
                self.cp("pool", Z[64:128, c0 + 16:c0 + 32], Q[64:128, j, :], ["sQ", "sZ"], ["sZ"])
                ps, pk = self.psum()
                self.tr(ps[:, 0:128], Z[:], ["sZ"], [pk])
                self.cp("dve", LB[q][:, j, :], ps[:, 0:128], [pk], ["sLB"])
        V = self.sb("sV", [32, 16, 128], F32)
        for q, nm in enumerate(("ssm_c_re", "ssm_c_im")):
            self.memset("pool", V[:], 0.0, ["sV"])
            self.memset("pool", WC[q][:], 0.0, ["sWC"])
            cv = I[nm][l].rearrange("(j a) c n -> a c j n", a=2)
            self.dma(V[0:16, :, 0:64], cv[0], ["sV"], ["sV"])
            self.dma(V[16:32, :, 64:128], cv[1], ["sV"], ["sV"])
            for j in range(16):
                c0 = 32 * (j % 4)
                ps, pk = self.psum()
                self.tr(ps[:, 0:32], V[:, j, :], ["sV"], [pk])
                if q == 0:
                    self.cp("dve", WC[q][:, j, c0:c0 + 32], ps[:, 0:32], [pk, "sWC"], ["sWC"])
                else:
                    self.ts("dve", WC[q][:, j, c0:c0 + 32], ps[:, 0:32], -1.0, ALU.mult, [pk, "sWC"], ["sWC"])
        dcol = self.sb("sdc", [128, 8], F32)
        self.colload(dcol[:, 0:4], I["ssm_d"][l].rearrange("(f p) -> f p", p=128), 4, "sdc")
        self.colload(dcol[:, 4:8], I["ssm_b_glu"][l].rearrange("(f p) -> f p", p=128), 4, "sdc")
        T = TP
        big = {n: self.sb("s_" + n, [128, T], F32) for n in ("xr", "xi", "ec", "es", "t1", "t2", "tm")}
        u = self.sb("s_u", [128, TT], F32)
        yg = self.sb("s_yg", [128, 4, TT], F32)
        ygb = self.sb("s_ygb", [128, 4, TT], BF16)
        xr, xi, ec, es, t1, t2, tm = (big[n] for n in ("xr", "xi", "ec", "es", "t1", "t2", "tm"))
        for yi in range(4):
            self.dma(u[:], self.d["ssmuT"][yi * 128:(yi + 1) * 128, :], [], ["s_u"])
            for gi, (t0, T) in enumerate(GRP):
                pre = "ps"[gi] + "_"
                nch = [(c0, min(512, T - c0)) for c0 in range(0, T, 512)]
                for jj in range(4):
                    j = 4 * yi + jj
                    for (dst, q, kk) in ((xr, 0, "s_xr"), (xi, 1, "s_xi")):
                        for (c0, cw) in nch:
                            ps, pk = self.psum()
                            self.mm(ps[:, 0:cw], LB[q][:, j, :], u[:, t0 + c0:t0 + c0 + cw], True, True, ["sLB", "s_u"], [pk])
                            self.cp("act", dst[:, c0:c0 + cw], ps[:, 0:cw], [pk], [kk])
                    self.memset("pool", ec[:, 0:1], 1.0, ["s_ec"])
                    self.memset("pool", es[:, 0:1], 0.0, ["s_es"])
                    n = 1
                    k = 0
                    while n < T:
                        cn, sn = pwc[:, k, j:j + 1], pws[:, k, j:j + 1]
                        self.ts("dve", tm[:, 0:n], es[:, 0:n], sn, ALU.mult, ["s_es", "spw"], ["s_tm"])
                        self.stt(ec[:, n:2 * n], ec[:, 0:n], cn, tm[:, 0:n], ALU.mult, ALU.subtract, ["s_ec", "s_tm", "spw"], ["s_ec2"])
                        self.ts("dve", tm[:, 0:n], ec[:, 0:n], sn, ALU.mult, ["s_ec", "spw", "s_ec2"], ["s_tm"])
                        self.stt(es[:, n:2 * n], es[:, 0:n], cn, tm[:, 0:n], ALU.mult, ALU.add, ["s_es", "s_tm", "spw"], ["s_es"])
                        self.P.last_w["s_ec"] = self.P.last_w["s_ec2"]
                        n *= 2
                        k += 1
                    self.tt("pool", t1[:, 0:T], ec[:, 0:T], xr[:, 0:T], ALU.mult, ["s_ec", "s_xr"], ["s_t1"])
                    self.tt("dve", tm[:, 0:T], es[:, 0:T], xi[:, 0:T], ALU.mult, ["s_es", "s_xi"], ["s_tm"])
                    self.tt("pool", t1[:, 0:T], t1[:, 0:T], tm[:, 0:T], ALU.add, ["s_t1", "s_tm"], ["s_t1"])
                    self.tt("pool", t2[:, 0:T], ec[:, 0:T], xi[:, 0:T], ALU.mult, ["s_ec", "s_xi"], ["s_t2"])
                    self.tt("dve", tm[:, 0:T], es[:, 0:T], xr[:, 0:T], ALU.mult, ["s_es", "s_xr", "s_t1"], ["s_tm"])
                    self.tt("dve", t2[:, 0:T], t2[:, 0:T], tm[:, 0:T], ALU.subtract, ["s_t2", "s_tm"], ["s_t2"])
                    rho = P_("mag")[:, j:j + 1].to_broadcast([128, T])
                    ir = 0.0 if gi == 0 else P_("g0r")[:, j:j + 1]
                    ii = 0.0 if gi == 0 else P_("g0i")[:, j:j + 1]
                    self.scan(xr[:, 0:T], rho, t1[:, 0:T], ir, [K_, "s_t1", "s_xr"], ["s_xr"])
                    self.scan(xi[:, 0:T], rho, t2[:, 0:T], ii, [K_, "s_t2", "s_xi"], ["s_xi"])
                    self.tt("pool", t1[:, 0:T], ec[:, 0:T], xr[:, 0:T], ALU.mult, ["s_ec", "s_xr", "s_t1"], ["s_t1"])
                    self.tt("dve", tm[:, 0:T], es[:, 0:T], xi[:, 0:T], ALU.mult, ["s_es", "s_xi", "s_t2"], ["s_tm"])
                    self.tt("pool", t1[:, 0:T], t1[:, 0:T], tm[:, 0:T], ALU.subtract, ["s_t1", "s_tm"], ["s_t1"])
                    self.tt("pool", t2[:, 0:T], ec[:, 0:T], xi[:, 0:T], ALU.mult, ["s_ec", "s_xi", "s_t2"], ["s_t2"])
                    self.tt("dve", tm[:, 0:T], es[:, 0:T], xr[:, 0:T], ALU.mult, ["s_es", "s_xr", "s_t1"], ["s_tm"])
                    self.tt("dve", t2[:, 0:T], t2[:, 0:T], tm[:, 0:T], ALU.add, ["s_t2", "s_tm"], ["s_t2"])
                    for (src, nm, kk) in ((t1, "ssm_re", "s_t1"), (t2, "ssm_im", "s_t2")):
                        o = self.o[pre + nm][l].rearrange("g n -> (g n)")[j * 128:(j + 1) * 128].rearrange("(p o) -> p o", o=1)
                        self.dma(o, src[:, T - 1:T], [kk], [self.o[pre + nm].name])
                    for ci, (c0, cw) in enumerate(nch):
                        yps, yk = self.ps[4 + ci], ("ps", 4 + ci)
                        self.mm(yps[:, 0:cw], WC[0][:, j, :], t1[:, c0:c0 + cw], jj == 0, False, ["sWC", "s_t1"], [yk])
                        self.mm(yps[:, 0:cw], WC[1][:, j, :], t2[:, c0:c0 + cw], False, jj == 3, ["sWC", "s_t2"], [yk])
                for ci, (c0, cw) in enumerate(nch):
                    yps, yk = self.ps[4 + ci], ("ps", 4 + ci)
                    self.stt(yg[:, yi, t0 + c0:t0 + c0 + cw], u[:, t0 + c0:t0 + c0 + cw], dcol[:, yi:yi + 1], yps[:, 0:cw],
                             ALU.mult, ALU.add, ["s_u", "sdc", yk], [("s_yg", yi, gi)])
                self.gelu(yg[:, yi, t0:t0 + T], yg[:, yi, t0:t0 + T], tm[:, 0:T], ("s_yg", yi, gi), "s_tm", ("s_yg", yi, gi))
                self.cp("pool", ygb[:, yi, t0:t0 + T], yg[:, yi, t0:t0 + T], [("s_yg", yi, gi)], ["s_ygb"])
        ob = [self.sb("s_ob", [128, 512], BF16) for _ in range(2)]
        sg = [self.sb("s_sg", [128, 512], F32) for _ in range(2)]
        self.gli = 0

        def glu(c, mw, t0, tn, ps, pk):
            i = c // 128
            b = self.gli % 2
            self.gli += 1
            self.act(sg[b][:, 0:tn], ps[:, 0:tn], AF.Sigmoid, [pk, "sdc"], [("s_sg", b)], bias=dcol[:, 4 + i:5 + i])
            gi = 0 if t0 < TP else 1
            self.tt("dve", ob[b][:, 0:tn], sg[b][:, 0:tn], yg[:, i, t0:t0 + tn], ALU.mult, [("s_sg", b), ("s_yg", i, gi)], [("s_ob", b)])
            self.dma(self.d["mixT"][512 + c:512 + c + mw, t0:t0 + tn], ob[b][:, 0:tn], [("s_ob", b)], ["mixT"])
        self.dense(I["ssm_w_glu"][l], 4, [(0, 256), (256, 256)], ygb, "s_ygb", [(t0, tn, t0) for (t0, tn) in CH], glu, wv=2)
        self.P.barrier()

    def stage_dn(self, l):
        self.off = self.lbase
        I = self.i
        self.psr = list(range(8))
        cw = self.sb("dcw", [128, 48], F32)
        self.colload(cw[:, 0:48], I["dn_conv_w"][l].rearrange("j (f p) -> (j f) p", p=128), 48, "dcw")
        ngc_ = self.sb("dng", [128, 1], F32)
        self.dma(ngc_[:], I["dn_norm_g"][l].rearrange("(p o) -> p o", o=1), [], ["dng"])
        hp = self.sb("dhp", [4, 4], F32)
        self.dma(hp[:, 0:1], I["dn_a_log"][l].rearrange("(h o) -> h o", o=1), [], ["dhp"])
        self.dma(hp[:, 1:2], I["dn_dt_bias"][l].rearrange("(h o) -> h o", o=1), [], ["dhp"])
        self.act(hp[:, 2:3], hp[:, 0:1], AF.Exp, ["dhp"], ["dhp"])
        self.ts("dve", hp[:, 2:3], hp[:, 2:3], -1.0, ALU.mult, ["dhp"], ["dhp"])
        T = TP
        rows = {n: self.sb("dr_" + n, [4, T], F32) for n in ("b", "g", "gc")}
        rows["ngc"] = rows["g"]
        colB = self.sb("dcB", [128, 16, 4], F32)
        colG = self.sb("dcG", [128, 16, 4], F32)
        colNB = self.sb("dcNB", [128, 16, 4], F32)
        colNEB = self.sb("dcNEB", [128, 16, 4], F32)
        qT = [self.sb("dq%d" % h, [128, T], F32) for h in range(4)]
        kT = [self.sb("dk%d" % h, [128, T], F32) for h in range(4)]
        vT = [self.sb("dv%d" % h, [128, T], F32) for h in range(4)]
        oT = vT
        S = [self.sb("dS%d" % h, [128, 128], F32) for h in range(4)]
        xpad = self.sb("dxp", [128, 3 + T], F32)
        rs = self.sb("drs", [128, 512], F32)
        sq = self.sb("dsq", [128, 512], F32)
        wn = ("dec", "decT", "N", "NT", "X0", "X1", "Y0", "Y1", "AT", "vb", "kd", "R", "vn", "QK", "qg")
        wt = [{n: self.sb("dw%d%s" % (h, n), [128, 128], F32) for n in wn} for h in range(4)]
        egl = [self.sb("degl%d" % h, [128, 1], F32) for h in range(4)]
        ob = self.sb("dob", [128, 512], BF16)
        for gi, (t0, T) in enumerate(GRP):
            pre = "ps"[gi] + "_"
            C = 128 if gi == 0 else 8
            NCH = T // C
            L = int(round(math.log2(C))) - 1
            b, g, gc, ngc = (rows[n] for n in ("b", "g", "gc", "ngc"))
            self.dma(b[:, 0:T], self.d["dnbaT"][0:4, t0:t0 + T], [], ["dr_b"])
            self.dma(g[:, 0:T], self.d["dnbaT"][4:8, t0:t0 + T], [], ["dr_g"])
            self.act(b[:, 0:T], b[:, 0:T], AF.Sigmoid, ["dr_b"], ["dr_b"])
            self.act(g[:, 0:T], g[:, 0:T], AF.Exp, ["dr_g", "dhp"], ["dr_g"], bias=hp[:, 1:2])
            self.act(g[:, 0:T], g[:, 0:T], AF.Ln, ["dr_g"], ["dr_g"], bias=self.onec[0:4, 0:1])
            self.ts("dve", g[:, 0:T], g[:, 0:T], hp[:, 2:3], ALU.mult, ["dr_g", "dhp"], ["dr_g"])
            for n in range(NCH):
                self.scan(gc[:, n * C:(n + 1) * C], self.ones[0:4, 0:C], g[:, n * C:(n + 1) * C], 0.0, ["dr_g", "dr_gc"], ["dr_gc"])
            self.ts("dve", ngc[:, 0:T], gc[:, 0:T], -1.0, ALU.mult, ["dr_gc"], ["dr_g"])
            for n in range(NCH):
                ps, pk = self.psum()
                self.tr(ps[0:C, 0:4], b[:, n * C:(n + 1) * C], ["dr_b"], [pk])
                self.tr(ps[0:C, 4:8], gc[:, n * C:(n + 1) * C], ["dr_gc"], [pk])
                self.cp("dve", colB[0:C, n, :], ps[0:C, 0:4], [pk], ["dcol"])
                self.cp("dve", colG[0:C, n, :], ps[0:C, 4:8], [pk], ["dcol"])
            self.act(colG[0:C, 0:NCH, :], colG[0:C, 0:NCH, :], AF.Exp, ["dcol"], ["dcol"])
            self.ts("dve", colNB[0:C, 0:NCH, :], colB[0:C, 0:NCH, :], -1.0, ALU.mult, ["dcol"], ["dcol"])
            self.tt("dve", colNEB[0:C, 0:NCH, :], colG[0:C, 0:NCH, :], colNB[0:C, 0:NCH, :], ALU.mult, ["dcol"], ["dcol"])
            for h in range(4):
                for (dst, part, kk) in ((qT[h], 0, ("dq", h)), (kT[h], 1, ("dk", h)), (vT[h], 2, ("dv", h))):
                    f = part * 4 + h
                    if gi == 0:
                        self.memset("pool", xpad[:, 0:3], 0.0, ["dxp"])
                    else:
                        self.colload(xpad[:, 0:3], I["state_delta_conv"][l][:, f * 128:(f + 1) * 128], 3, "dxp")
                    self.dma(xpad[:, 3:3 + T], self.d["dnqkvT"][f * 128:(f + 1) * 128, t0:t0 + T], [], ["dxp"])
                    self.ts("dve", dst[:, 0:T], xpad[:, 0:T], cw[:, f:f + 1], ALU.mult, ["dxp", "dcw"], [kk])
                    for j in range(1, 4):
                        self.stt(dst[:, 0:T], xpad[:, j:j + T], cw[:, 12 * j + f:12 * j + f + 1], dst[:, 0:T], ALU.mult, ALU.add,
                                 ["dxp", "dcw", kk], [kk])
                    self.act(dst[:, 0:T], dst[:, 0:T], AF.Silu, [kk], [kk])
                    if part < 2:
                        for c0 in range(0, T, 512):
                            w = min(512, T - c0)
                            self.act(sq[:, 0:w], dst[:, c0:c0 + w], AF.Square, [kk], ["dsq"])
                            ps, pk = self.psum()
                            self.mm(ps[:, 0:w], self.ones[:], sq[:, 0:w], True, True, ["dsq"], [pk])
                            self.act(rs[:, 0:w], ps[:, 0:w], AF.Sqrt, [pk], ["drs"], bias=self.epsc[:, 0:1])
                            self.recip(rs[:, 0:w], rs[:, 0:w], ["drs"], ["drs"])
                            self.stt(dst[:, c0:c0 + w], dst[:, c0:c0 + w], (128 ** -0.5) if part == 0 else 1.0, rs[:, 0:w],
                                     ALU.mult, ALU.mult, [kk, "drs"], [kk])
                if gi == 0:
                    self.memset("pool", S[h][:], 0.0, [("dS", h)])
                else:
                    self.dma(S[h][:], I["state_delta"][l][h], [], [("dS", h)])
            for n in range(NCH):
                cs = slice(n * C, (n + 1) * C)
                for h in range(4):
                    w = wt[h]
                    W = lambda nm: ("dw", h, nm)
                    selM = self.sel[0:4, h * 128:h * 128 + C]
                    sel128 = self.sel[0:4, h * 128:(h + 1) * 128]
                    kq, kk_, kv = ("dq", h), ("dk", h), ("dv", h)
                    ps, pk = self.psum()
                    self.mm(ps[0:C, 0:C], gc[0:4, cs], selM, True, False, ["dr_gc"], [pk])
                    self.mm(ps[0:C, 0:C], selM, ngc[0:4, cs], False, False, ["dr_g"], [pk])
                    self.mm(ps[0:C, 0:C], self.ident[0:C, 0:C], self.neglow[0:C, 0:C], False, True, [], [pk])
                    self.act(w["dec"][0:C, 0:C], ps[0:C, 0:C], AF.Exp, [pk], [W("dec")])
                    ps, pk = self.psum()
                    self.mm(ps[0:C, 0:C], selM, gc[0:4, cs], True, False, ["dr_gc"], [pk])
                    self.mm(ps[0:C, 0:C], ngc[0:4, cs], selM, False, False, ["dr_g"], [pk])
                    self.mm(ps[0:C, 0:C], self.ident[0:C, 0:C], self.negup[0:C, 0:C], False, True, [], [pk])
                    self.act(w["decT"][0:C, 0:C], ps[0:C, 0:C], AF.Exp, [pk], [W("decT")])
                    ps, pk = self.psum()
                    self.mm(ps[0:C, 0:C], kT[h][:, cs], kT[h][:, cs], True, True, [kk_], [pk])
                    self.tt("dve", w["N"][0:C, 0:C], ps[0:C, 0:C], w["dec"][0:C, 0:C], ALU.mult, [pk, W("dec")], [W("N")])
                    self.stt(w["N"][0:C, 0:C], w["N"][0:C, 0:C], colNB[0:C, n, h:h + 1], self.strict[0:C, 0:C], ALU.mult, ALU.mult,
                             [W("N"), "dcol"], [W("N")])
                    ps, pk = self.psum()
                    self.tr(ps[0:C, 0:C], w["N"][0:C, 0:C], [W("N")], [pk])
                    self.cp("act", w["NT"][0:C, 0:C], ps[0:C, 0:C], [pk], [W("NT")])
                    self.tt("dve", w["AT"][0:C, 0:C], w["NT"][0:C, 0:C], self.ident[0:C, 0:C], ALU.add, [W("NT")], [W("AT")])
                    Xc, Yc = "N", "NT"
                    for i in range(L):
                        Xn, Yn = "X%d" % (i % 2), "Y%d" % (i % 2)
                        ps, pk = self.psum()
                        self.mm(ps[0:C, 0:C], w[Yc][0:C, 0:C], w[Xc][0:C, 0:C], True, True, [W(Xc), W(Yc)], [pk])
                        self.cp("act", w[Xn][0:C, 0:C], ps[0:C, 0:C], [pk], [W(Xn)])
                        if i < L - 1:
                            ps, pk = self.psum()
                            self.mm(ps[0:C, 0:C], w[Xc][0:C, 0:C], w[Yc][0:C, 0:C], True, True, [W(Xc), W(Yc)], [pk])
                            self.cp("dve", w[Yn][0:C, 0:C], ps[0:C, 0:C], [pk], [W(Yn)])
                        ps, pk = self.psum()
                        self.mm(ps[0:C, 0:C], w[Xn][0:C, 0:C], w["AT"][0:C, 0:C], True, True, [W(Xn), W("AT")], [pk])
                        self.tt("dve", w["AT"][0:C, 0:C], w["AT"][0:C, 0:C], ps[0:C, 0:C], ALU.add, [pk, W("AT")], [W("AT")])
                        Xc, Yc = Xn, Yn
                    ps, pk = self.psum()
                    self.tr(ps[0:C, 0:128], vT[h][:, cs], [kv], [pk])
                    self.ts("dve", w["vb"][0:C, :], ps[0:C, 0:128], colB[0:C, n, h:h + 1], ALU.mult, [pk, "dcol"], [W("vb")])
                    ps, pk = self.psum()
                    self.tr(ps[0:C, 0:128], kT[h][:, cs], [kk_], [pk])
                    self.ts("dve", w["kd"][0:C, :], ps[0:C, 0:128], w["decT"][0:C, C - 1:C], ALU.mult, [pk, W("decT")], [W("kd")])
                    ps, pk = self.psum()
                    self.mm(ps[0:C, 0:128], kT[h][:, cs], S[h][:], True, True, [kk_, ("dS", h)], [pk])
                    self.stt(w["R"][0:C, :], ps[0:C, 0:128], colNEB[0:C, n, h:h + 1], w["vb"][0:C, :], ALU.mult, ALU.add,
                             [pk, "dcol", W("vb")], [W("R")])
                    ps, pk = self.psum()
                    self.mm(ps[0:C, 0:128], w["AT"][0:C, 0:C], w["R"][0:C, :], True, True, [W("AT"), W("R")], [pk])
                    self.cp("act", w["vn"][0:C, :], ps[0:C, 0:128], [pk], [W("vn")])
                    ps, pk = self.psum()
                    self.mm(ps[:, 0:C], sel128, gc[0:4, cs], True, True, ["dr_gc"], [pk])
                    self.act(w["qg"][:, 0:C], ps[:, 0:C], AF.Exp, [pk], [W("qg")])
                    self.tt("dve", w["qg"][:, 0:C], qT[h][:, cs], w["qg"][:, 0:C], ALU.mult, [kq, W("qg")], [W("qg")])
                    ps, pk = self.psum()
                    self.mm(ps[0:C, 0:C], kT[h][:, cs], qT[h][:, cs], True, True, [kk_, kq], [pk])
                    self.tt("dve", w["QK"][0:C, 0:C], ps[0:C, 0:C], w["decT"][0:C, 0:C], ALU.mult, [pk, W("decT")], [W("QK")])
                    ps, pk = self.psum()
                    self.mm(ps[:, 0:C], S[h][:], w["qg"][:, 0:C], True, False, [("dS", h), W("qg")], [pk])
                    self.mm(ps[:, 0:C], w["vn"][0:C, :], w["QK"][0:C, 0:C], False, True, [W("vn"), W("QK")], [pk])
                    self.cp("act", oT[h][:, cs], ps[:, 0:C], [pk], [("dv", h)])
                    ps, pk = self.psum()
                    self.mm(ps[:, 0:1], sel128, gc[0:4, (n + 1) * C - 1:(n + 1) * C], True, True, ["dr_gc"], [pk])
                    self.act(egl[h][:], ps[:, 0:1], AF.Exp, [pk], [("degl", h)])
                    ps, pk = self.psum()
                    self.mm(ps[:, 0:128], w["kd"][0:C, :], w["vn"][0:C, :], True, True, [W("kd"), W("vn")], [pk])
                    self.stt(S[h][:], S[h][:], egl[h][:, 0:1], ps[:, 0:128], ALU.mult, ALU.add, [("dS", h), ("degl", h), pk], [("dS", h)])
            for h in range(4):
                self.dma(self.o[pre + "delta"][l][h], S[h][:], [("dS", h)], [self.o[pre + "delta"].name])
                z = xpad
                self.dma(z[:, 0:T], self.d["dnzT"][h * 128:(h + 1) * 128, t0:t0 + T], [], ["dxp"])
                self.act(z[:, 0:T], z[:, 0:T], AF.Silu, ["dxp"], ["dxp"])
                for c0 in range(0, T, 512):
                    wd = min(512, T - c0)
                    self.act(sq[:, 0:wd], oT[h][:, c0:c0 + wd], AF.Square, [("dv", h)], ["dsq"])
                    ps, pk = self.psum()
                    self.mm(ps[:, 0:wd], self.ones[:], sq[:, 0:wd], True, True, ["dsq"], [pk])
                    self.act(rs[:, 0:wd], ps[:, 0:wd], AF.Sqrt, [pk], ["drs"], scale=1.0 / 128, bias=self.epsc[:, 0:1])
                    self.recip(rs[:, 0:wd], rs[:, 0:wd], ["drs"], ["drs"])
                    self.stt(sq[:, 0:wd], oT[h][:, c0:c0 + wd], ngc_[:, 0:1], rs[:, 0:wd], ALU.mult, ALU.mult,
                             [("dv", h), "dng", "drs", "dsq"], ["dsq"])
                    self.tt("dve", ob[:, 0:wd], sq[:, 0:wd], z[:, c0:c0 + wd], ALU.mult, ["dsq", "dxp"], ["dob"])
                    self.dma(self.d["mixT"][h * 128:(h + 1) * 128, t0 + c0:t0 + c0 + wd], ob[:, 0:wd], ["dob"], ["mixT"])
        self.psr = [0, 1, 2, 3]
        self.P.barrier()

    def stage_bias(self):
        self.off = self.lbase
        I = self.i
        rb = self.sb("rb", [32, 8], F32)
        oh = self.sb("oh", [32, VL], F32)
        cv = self.sb("cv", [1, VL], F32)
        bv = self.sb("bv", [8, VL], F32)
        self.dma(rb[:], I["rel_bias"], [], ["rb"])
        self.dma(oh[:], I["c_oh"], [], ["oh"])
        self.dma(cv[:], I["c_cv"], [], ["cv"])
        for c0 in range(0, VL, 512):
            w = min(512, VL - c0)
            ps, pk = self.psum()
            self.mm(ps[0:8, 0:w], rb[:, :], oh[:, c0:c0 + w], True, False, ["rb", "oh"], [pk])
            self.mm(ps[0:8, 0:w], self.ones[0:1, 0:8], cv[0:1, c0:c0 + w], False, True, ["cv"], [pk])
            self.cp("dve", bv[:, c0:c0 + w], ps[0:8, 0:w], [pk], ["bv"])
        self.dma(self.d["bvec"], bv[:], ["bv"], ["bvec"])
        self.P.barrier()
        U = [self.sb("bU", [128, SW], F32) for _ in range(2)]
        Tt = [self.sb("bT", [128, SW], F32) for _ in range(2)]
        for h in range(8):
            u, uk = U[h % 2], ("bU", h % 2)
            t, tk = Tt[h % 2], ("bT", h % 2)
            self.dma(u[:], bass.AP(self.d["bvec"].tensor, h * VL, [[1, 128], [1, SW]]), [], [uk])
            for c0 in range(0, SW, 512):
                w = min(512, SW - c0)
                ps, pk = self.psum()
                self.mm(ps[:, 0:w], self.anti[:], u[:, c0:c0 + w], True, True, [uk], [pk])
                self.cp("act" if (c0 // 512) % 2 else "dve", t[:, c0:c0 + w], ps[:, 0:w], [pk], [tk])
            self.dma(self.d["strip"][h], t[:], [tk], ["strip"])
        self.P.barrier()

    def stage_swa(self, l):
        self.off = self.lbase
        I = self.i
        qT = self.sb("aq", [128, 4, TT], BF16)
        kT = self.sb("ak", [128, 4, TT], BF16)
        V = self.sb("av", [128, 16, 512], BF16)
        Vn = self.sb("avn", [8, 512], BF16)
        kcT = self.sb("akc", [128, 4, 2048], BF16)
        Vc = self.sb("avc", [128, 16, 512], BF16)
        strip = [self.sb("ast", [128, SW], F32) for _ in range(2)]
        Lw = [self.sb("aL", [128, 512], F32) for _ in range(4)]
        PT = [self.sb("aP", [128, 512], BF16) for _ in range(4)]
        rinv = self.sb("ari", [128, 512], F32)
        ob = [self.sb("aob", [128, 512], BF16) for _ in range(2)]
        cst = self.sb("acs", [128, 4, 512], F32)
        sw = self.d["swaT"]
        self.dma(qT[:], sw[0:512, :].rearrange("(f p) t -> p f t", p=128), [], ["aq"], eng="pool")
        self.dma(kT[:], sw[512:1024, :].rearrange("(f p) t -> p f t", p=128), [], ["ak"], eng="pool")
        self.dma(V[:], self.o["p_win_v"][l].rearrange("(j p) c -> p j c", p=128), [], ["av"], eng="pool")
        self.dma(Vn[:], self.o["s_win_v"][l], [], ["avn"], eng="pool")
        self.dma(Vc[:], I["cache_win_v"][l].rearrange("(j p) c -> p j c", p=128), [], ["avc"], eng="pool")
        for j4 in range(4):
            self.dma(cst[:], I["cache_win_k"][l][j4 * 512:(j4 + 1) * 512, :].rearrange("(j p) c -> p j c", p=128), [], ["acs"])
            for jj in range(4):
                j = j4 * 4 + jj
                ps, pk = self.psum()
                for i in range(4):
                    self.tr(ps[:, i * 128:(i + 1) * 128], cst[:, jj, i * 128:(i + 1) * 128], ["acs"], [pk])
                self.cp("act" if jj % 2 else "dve", kcT[:, :, j * 128:(j + 1) * 128],
                        ps[:].rearrange("p (i t) -> p i t", i=4), [pk], ["akc"])
        LA = 3
        tasks = []
        un = 0
        for h in range(8):
            i, hb = h // 2, (h % 2) * 64
            units = [(0, c * 512, 512, [(kT, V, j, 128, c * 512 - 128 * j) for j in range(4 * c + 4)]) for c in range(4)]
            units.append((1, TP, TS, [(kcT, Vc, j, 128, 2048 - 128 * j) for j in range(16)] + [(kT, Vn, None, 8, 0)]))
            for (gi, q0, N, keys) in units:
                for ki, key in enumerate(keys):
                    tasks.append((h, i, hb, gi, q0, N, key, ki == 0, ki == len(keys) - 1, un, ki == 0 and q0 == 0))
                un += 1

        def A(ti):
            (h, i, hb, gi, q0, N, (KT, VV, j, kn, dl), first, last, u, newhead) = tasks[ti]
            st, sk = strip[h % 2], ("ast", h % 2)
            if newhead:
                self.dma(st[:], self.d["strip"][h], [], [sk])
            if j is None:
                lk, kkey = kT[hb:hb + 64, i, TP:TP + TS], "ak"
            else:
                lk, kkey = KT[hb:hb + 64, i, j * 128:(j + 1) * 128], ("ak" if gi == 0 else "akc")
            b = ti % 4
            ps, pk = self.psum()
            self.mm(ps[0:kn, 0:N], lk, qT[hb:hb + 64, i, q0:q0 + N], True, True, [kkey, "aq"], [pk])
            y0 = dl + 384
            self.stt(Lw[b][0:kn, 0:N], ps[0:kn, 0:N], 0.125, st[0:kn, y0:y0 + N], ALU.mult, ALU.add, [pk, sk], [("aL", b)])
            self.act(PT[b][0:kn, 0:N], Lw[b][0:kn, 0:N], AF.Exp, [("aL", b)], [("aP", b)])

        def B(ti):
            (h, i, hb, gi, q0, N, (KT, VV, j, kn, dl), first, last, u, newhead) = tasks[ti]
            b = ti % 4
            ops_, okk = self.ps[4 + 2 * (u % 2)], ("ps", 4 + 2 * (u % 2))
            sps, skk = self.ps[5 + 2 * (u % 2)], ("ps", 5 + 2 * (u % 2))
            if j is None:
                vv, vkey = Vn[0:kn, i * 128:(i + 1) * 128], "avn"
            else:
                vv, vkey = VV[:, j, i * 128:(i + 1) * 128], ("av" if gi == 0 else "avc")
            self.mm(ops_[:, 0:N], vv, PT[b][0:kn, 0:N], first, last, [vkey, ("aP", b)], [okk])
            self.mm(sps[:, 0:N], self.onesb[0:kn, :], PT[b][0:kn, 0:N], first, last, [("aP", b)], [skk])
            if last:
                self.recip(rinv[hb:hb + 64, 0:N], sps[hb:hb + 64, 0:N], [skk], ["ari"])
                o = ob[u % 2]
                self.tt("dve", o[hb:hb + 64, 0:N], ops_[hb:hb + 64, 0:N], rinv[hb:hb + 64, 0:N], ALU.mult, [okk, "ari"], [("aob", u % 2)])
                self.dma(self.d["mixT"][1024 + h * 64:1024 + (h + 1) * 64, q0:q0 + N], o[hb:hb + 64, 0:N], [("aob", u % 2)], ["mixT"])
        for ti in range(len(tasks) + LA):
            if ti < len(tasks):
                A(ti)
            if ti - LA >= 0:
                B(ti - LA)
        self.P.barrier()

    def resid(self):
        fifo = []

        def pre(c, mw, t0, tn):
            i = self.evi % 4
            self.evi += 1
            s, sk = self.evs[i], ("evs", i)
            rk = ("xTr", c, t0)
            self.dma(s[0:mw, 0:tn], self.d["xT"][c:c + mw, t0:t0 + tn], [rk], [sk])
            fifo.append((s, sk, rk))

        def f(c, mw, t0, tn, ps, pk):
            s, sk, rk = fifo.pop(0)
            self.tt("dve", s[0:mw, 0:tn], s[0:mw, 0:tn], ps[0:mw, 0:tn], ALU.add, [sk, pk], [sk])
            self.dma(self.d["xT"][c:c + mw, t0:t0 + tn], s[0:mw, 0:tn], [sk], [rk])
        return pre, f

    def stage_wout(self, l):
        self.off = self.lbase
        mx = self.sb("mixS", [128, 16, TT], BF16)
        self.evs = [self.sb("evs", [128, 512], F32) for _ in range(4)]
        self.evi = 0
        self.dma(mx[:], self.d["mixT"].rearrange("(f p) t -> p f t", p=128), [], ["mixS"])
        rp, rc = self.resid()
        self.dense(self.i["w_out"][l], 16, [(g, 256) for g in range(0, D, 256)], mx, "mixS",
                   [(t0, tn, t0) for (t0, tn) in CH], rc, pre=rp)
        self.P.barrier()

    def stage_cross(self, l):
        self.off = self.lbase
        I = self.i
        xn = self.sb("xnT", [128, 16, TT], BF16)
        self.evs = [self.sb("evs", [128, 512], F32) for _ in range(4)]
        self.evi = 0
        qT = self.sb("cq", [128, 4, TT], BF16)
        at = self.sb("cat", [128, 4, TT], BF16)
        kT = [self.sb("ckT%d" % g, [128, 4, 256], BF16) for g in range(2)]
        Vm = [self.sb("cV%d" % g, [128, 2, 512], BF16) for g in range(2)]
        kst = self.sb("ckst", [128, 2, 512], F32)
        PT = [self.sb("cP", [128, 512], BF16) for _ in range(4)]
        rinv = self.sb("cri", [128, 512], F32)
        ksrc = [self.o["p_mem_k"][l], I["cache_mem_k"][l]]
        vsrc = [self.o["p_mem_v"][l], I["cache_mem_v"][l]]
        for g in range(2):
            self.dma(Vm[g][:], vsrc[g].rearrange("(m p) c -> p m c", p=128), [], [("cV", g)], eng="pool")
            self.dma(kst[:], ksrc[g].rearrange("(m p) c -> p m c", p=128), [], ["ckst"])
            for m in range(2):
                ps, pk = self.psum()
                for h in range(4):
                    self.tr(ps[:, h * 128:(h + 1) * 128], kst[:, m, h * 128:(h + 1) * 128], ["ckst"], [pk])
                self.cp("dve", kT[g][:, :, m * 128:(m + 1) * 128], ps[:].rearrange("p (h t) -> p h t", h=4), [pk], [("ckT", g)])
        self.norm_T(self.d["xT"], self.gcols[:, 1, :], CH, xn, "xnT")

        def qcons(c, mw, t0, tn, ps, pk):
            self.act(qT[:, c // 128, t0:t0 + tn], ps[:, 0:tn], AF.Copy, [pk], [("cq", t0)], scale=128 ** -0.5)
        self.dense(I["w_mem_q"][l], 16, [(0, 256), (256, 256)], xn, "xnT", [(t0, tn, t0) for (t0, tn) in CH], qcons)
        LA = 0
        tasks = []
        un = 0
        for (t0, tn) in CH:
            g = 0 if t0 < TP else 1
            for h in range(4):
                for m in range(2):
                    tasks.append((t0, tn, g, h, m, un))
                un += 1

        def A(ti):
            (t0, tn, g, h, m, u) = tasks[ti]
            b = ti % 4
            ps, pk = self.psum()
            self.mm(ps[:, 0:tn], kT[g][:, h, m * 128:(m + 1) * 128], qT[:, h, t0:t0 + tn], True, True, [("ckT", g), ("cq", t0)], [pk])
            self.act(PT[b][:, 0:tn], ps[:, 0:tn], AF.Exp, [pk], [("cP", b)])

        def B(ti):
            (t0, tn, g, h, m, u) = tasks[ti]
            b = ti % 4
            ops_, okk = self.ps[4 + 2 * (u % 2)], ("ps", 4 + 2 * (u % 2))
            sps, skk = self.ps[5 + 2 * (u % 2)], ("ps", 5 + 2 * (u % 2))
            self.mm(ops_[:, 0:tn], Vm[g][:, m, h * 128:(h + 1) * 128], PT[b][:, 0:tn], m == 0, m == 1, [("cV", g), ("cP", b)], [okk])
            self.mm(sps[:, 0:tn], self.onesb[:], PT[b][:, 0:tn], m == 0, m == 1, [("cP", b)], [skk])
            if m == 1:
                self.recip(rinv[:, 0:tn], sps[:, 0:tn], [skk], ["cri"])
                self.tt("dve", at[:, h, t0:t0 + tn], ops_[:, 0:tn], rinv[:, 0:tn], ALU.mult, [okk, "cri"], [("cat", t0)])
        for ti in range(len(tasks) + LA):
            if ti < len(tasks):
                A(ti)
            if ti - LA >= 0:
                B(ti - LA)
        rp, rc = self.resid()
        self.dense(I["w_mem_o"][l], 4, [(g_, 256) for g_ in range(0, D, 256)], at, "cat", [(t0, tn, t0) for (t0, tn) in CH],
                   rc, wv=2, pre=rp)
        self.P.barrier()

    def stage_ffn(self, l):
        self.off = self.lbase
        I = self.i
        NJ = DFF // 128
        cwc = self.sb("fcw", [128, 3, 88], F32)
        stc = self.sb("fst", [128, 2, 88], F32)
        for r in range(3):
            self.colload(cwc[:, r, :], I["ffn_conv_w"][l][r].rearrange("(f p) -> f p", p=128), 88, "fcw")
        for r in range(2):
            self.colload(stc[:, r, :], I["state_ffn_conv"][l][r].rearrange("(f p) -> f p", p=128), 88, "fst")
        tails = self.sb("ftl", [128, 88, 2], F32)
        otl = [self.sb("fot%d" % g, [128, 2, 88], F32) for g in range(2)]
        self.evs = [self.sb("evs", [128, 512], F32) for _ in range(4)]
        self.evi = 0
        GW = 1032
        xn = self.sb("fxn", [128, 16, GW], BF16)
        a_off = self.off
        actT = self.sb("fact", [128, NJ, GW], BF16)
        hub = [self.sb("fhub", [128, 2 + 512], F32) for _ in range(4)]
        cvt = [self.sb("fcv", [128, 512], F32) for _ in range(2)]
        self.ring_extra(FFN_XW)
        W = I["w_up"][l]
        hi = 0
        for (tb, chunks) in ((0, [(0, 512), (512, 512)]), (1024, [(1024, 512), (1536, 512), (2048, 8)])):
            e_off = self.off
            self.off = a_off
            self.P.barrier()
            self.norm_T(self.d["xT"], self.gcols[:, 2, :], chunks, xn, ("fxn", tb), tbase=tb)
            self.P.barrier()
            self.off = e_off
            def issue_w(jp_):
                wu_ = self.next_w()
                self.dma(wu_[0][:, 0:16, 0:256], W[:, jp_ * 256:(jp_ + 1) * 256].rearrange("(k p) c -> p k c", p=128), [], [wu_[3]], eng="pool")
                wg_ = self.next_w()
                self.dma(wg_[0][:, 0:16, 0:256], W[:, DFF + jp_ * 256:DFF + (jp_ + 1) * 256].rearrange("(k p) c -> p k c", p=128), [], [wg_[3]], eng="pool")
                return wu_, wg_
            wq = [issue_w(0)]
            for jp in range(NJ // 2):
                if jp + 1 < NJ // 2:
                    wq.append(issue_w(jp + 1))
                wu, wg = wq.pop(0)
                for jj in range(2):
                    j = 2 * jp + jj
                    for (t0, tn) in chunks:
                        a0 = t0 - tb
                        res = []
                        for half, wt_ in enumerate((wu, wg)):
                            ft = half * NJ + j
                            ps, pk = self.psum()
                            for k in range(16):
                                self.mm(ps[:, 0:tn], wt_[0][:, k, jj * 128:(jj + 1) * 128], xn[:, k, a0:a0 + tn], k == 0, k == 15,
                                        [wt_[3], (("fxn", tb), t0)], [pk])
                            hb_, hk = hub[hi % 4], ("fhub", hi % 4)
                            hi += 1
                            if t0 == 0:
                                self.memset("pool", hb_[:, 0:2], 0.0, [hk])
                            elif t0 == TP:
                                self.cp("pool", hb_[:, 0:2], stc[:, :, ft], ["fst", hk], [hk])
                            else:
                                self.cp("pool", hb_[:, 0:2], tails[:, ft, :], [("ftl", ft), hk], [hk])
                            self.cp("act", hb_[:, 2:2 + tn], ps[:, 0:tn], [pk, hk], [hk])
                            if t0 + tn == TP or t0 == TP:
                                self.cp("pool", otl[0 if t0 < TP else 1][:, :, ft], hb_[:, tn:tn + 2], [hk], [("fot", ft)])
                            if t0 < TP:
                                self.cp("pool", tails[:, ft, :], hb_[:, tn:tn + 2], [hk], [("ftl", ft)])
                            cv, ck = cvt[half], ("fcv", half)
                            self.ts("dve", cv[:, 0:tn], hb_[:, 0:tn], cwc[:, 0, ft:ft + 1], ALU.mult, [hk, "fcw"], [ck])
                            self.stt(cv[:, 0:tn], hb_[:, 1:1 + tn], cwc[:, 1, ft:ft + 1], cv[:, 0:tn], ALU.mult, ALU.add, [hk, "fcw", ck], [ck])
                            self.stt(cv[:, 0:tn], hb_[:, 2:2 + tn], cwc[:, 2, ft:ft + 1], cv[:, 0:tn], ALU.mult, ALU.add, [hk, "fcw", ck], [ck])
                        self.act(cvt[1][:, 0:tn], cvt[1][:, 0:tn], AF.Silu, [("fcv", 1)], [("fcv", 1)])
                        self.tt("dve", actT[:, j, a0:a0 + tn], cvt[1][:, 0:tn], cvt[0][:, 0:tn], ALU.mult, [("fcv", 0), ("fcv", 1)], [(("fact", tb), t0)])
            rp, rc = self.resid()
            self.dense(I["w_down"][l], NJ, [(g_, 128) for g_ in range(0, D, 128)], actT, ("fact", tb),
                       [(t0, tn, t0 - tb) for (t0, tn) in chunks], rc, wv=1, pre=rp)
        for g, pre in enumerate(("p_", "s_")):
            for r in range(2):
                ps, pk = self.psum()
                self.tr(ps[0:88, 0:128], otl[g][:, r, :], [("fot", ft) for ft in range(88)], [pk])
                i = self.evi % 4
                self.evi += 1
                sv, sk = self.evs[i], ("evs", i)
                self.cp("dve", sv[0:88, 0:128], ps[0:88, 0:128], [pk], [sk])
                self.dma(self.o[pre + "ffn_conv"][l][r].rearrange("(f p) -> f p", p=128), sv[0:88, 0:128], [sk], [self.o[pre + "ffn_conv"].name])
        self.P.barrier()
        self.wring = self.wb

    def stage_final(self):
        self.off = self.lbase
        I = self.i
        gr = self.sb("zgr", [1, D], F32)
        gb = self.sb("zgb", [128, D], F32)
        self.dma(gr[:], I["g_final"], [], ["zgr"])
        for c in range(4):
            ps, pk = self.psum()
            self.mm(ps[:, :], self.ones[0:1, :], gr[0:1, c * 512:(c + 1) * 512], True, True, ["zgr"], [pk])
            self.cp("dve", gb[:, c * 512:(c + 1) * 512], ps[:, :], [pk], ["zgb"])
        xi = [self.sb("zxi", [128, 16, 128], F32) for _ in range(2)]
        xt = [self.sb("zxt", [128, D], F32) for _ in range(2)]
        sqt = self.sb("zsq", [128, D], F32)
        ss = self.sb("zss", [128, 2], F32)
        tiles = [(t * 128, 128, self.o["y_prompt"], t * 128) for t in range(16)] + [(TP, TS, self.o["y_sample"], 0)]
        for n, (t0, nt, dst, r0) in enumerate(tiles):
            a, ak = xi[n % 2], ("zxi", n % 2)
            b, bk = xt[n % 2], ("zxt", n % 2)
            self.dma(a[:, :, 0:nt], self.d["xT"].rearrange("(f p) t -> p f t", p=128)[:, :, t0:t0 + nt], [], [ak])
            for f4 in range(4):
                ps, pk = self.psum()
                for j in range(4):
                    self.tr(ps[0:nt, j * 128:(j + 1) * 128], a[:, f4 * 4 + j, 0:nt], [ak], [pk])
                self.cp("act" if f4 % 2 else "dve", b[0:nt, f4 * 512:(f4 + 1) * 512], ps[0:nt, :], [pk], [bk])
            self.act(sqt[0:nt, :], b[0:nt, :], AF.Square, [bk], ["zsq"])
            self.P.op("dve", (lambda o_, i_: (lambda e: e.reduce_sum(out=o_, in_=i_, axis=AX.X)))(ss[0:nt, 0:1], sqt[0:nt, :]), ["zsq"], ["zss"])
            self.act(ss[0:nt, 1:2], ss[0:nt, 0:1], AF.Sqrt, ["zss"], ["zss"], scale=1.0 / D, bias=self.epsc[0:nt, 0:1])
            self.recip(ss[0:nt, 1:2], ss[0:nt, 1:2], ["zss"], ["zss"])
            self.stt(b[0:nt, :], b[0:nt, :], ss[0:nt, 1:2], gb[0:nt, :], ALU.mult, ALU.mult, [bk, "zss", "zgb"], [bk])
            self.dma(dst[r0:r0 + nt, :], b[0:nt, :], [bk], [dst.name])
        self.P.barrier()

    def build(self):
        with contextlib.ExitStack() as st:
            self.stack = st
            self.setup()
            self.gcols = self.sb("gcols", [128, 4, 16], F32)
            self.epsc = self.sb("epsc", [128, 1], F32)
            self.memset("dve", self.epsc[:], EPS, ["epsc"])
            self.onec = self.sb("onec", [128, 1], F32)
            self.memset("dve", self.onec[:], 1.0, ["onec"])
            self.cl_st = [self.sb("clst", [128, 128], F32) for _ in range(2)]
            self.cl_i = 0
            self.lbase = self.off
            self.stage_input()
            self.stage_bias()
            for l in range(self.NL):
                self.load_gcols(l)
                self.P.barrier()
                self.stage_memkv(l)
                self.stage_in(l)
                self.stage_kvout(l)
                self.stage_lru(l)
                self.stage_ssm(l)
                self.stage_dn(l)
                self.stage_swa(l)
                if self.dbg:
                    self.dma(self.dbg["dbg_mix"], self.d["mixT"], [], [])
                    self.P.barrier()
                self.conv_tail_out(self.d["dnqkvT"], 1536, 3, self.o["p_delta_conv"], self.o["s_delta_conv"], l)
                self.conv_tail_out(self.d["lruxT"], 512, 3, self.o["p_lru_conv"], self.o["s_lru_conv"], l)
                self.stage_wout(l)
                self.stage_cross(l)
                self.stage_ffn(l)
            self.stage_final()
            self.P.emit()
        return self.nc


_CACHE = {}


def run(inputs, NL, stages="all", ncores=8):
    key = (NL, stages)
    if key not in _CACHE:
        _CACHE[key] = K(NL, stages).build()
    nc = _CACHE[key]
    consts = host_consts()
    f = lambda a: np.ascontiguousarray(np.asarray(a, dtype=np.float32))
    in_maps = []
    for c in range(ncores):
        b = c % 4
        m = dict(consts)
        for n, s in W_SHAPES.items():
            a = f(inputs[n])
            if n == "g_final":
                a = a.reshape(1, D)
            elif s[0] == "L":
                a = a[:NL]
            m[n] = np.ascontiguousarray(a)
        m["x_prompt"] = f(inputs["x_prompt"][b])
        m["x_sample"] = f(inputs["x_sample"][c])
        m["mem_prompt"] = f(inputs["mem_prompt"][b])
        for n, s in IN_SHAPES.items():
            if s[0] == "L":
                a = np.asarray(inputs[n])[:NL, c]
                m[n] = np.ascontiguousarray(a.reshape(_shape(s, NL)).astype(np.float32))
        in_maps.append(m)
    res = run_bass_kernel_spmd(nc, in_maps, core_ids=list(range(ncores)))
    return res.results


def assemble(results, NL):
    outs = []
    full = {"y_prompt": (4, TP, D), "y_sample": (8, TS, D)}
    for n, s in OUT_SHAPES:
        if n.startswith("y_p"):
            outs.append(np.stack([results[c][n] for c in range(4)]).reshape(4, TP, D))
        elif n.startswith("y_s"):
            outs.append(np.stack([results[c][n] for c in range(8)]).reshape(8, TS, D))
        else:
            nb = 4 if n.startswith("p_") else 8
            a = np.stack([results[c][n] for c in range(nb)], axis=1)
            outs.append(a)
    shp = {"mem_k": (256, 4, 128), "mem_v": (256, 4, 128), "win_k": (-1, 8, 64), "win_v": (-1, 8, 64)}
    res = []
    for (n, s), a in zip(OUT_SHAPES, outs):
        base = n[2:]
        if base in shp and not n.startswith("y_"):
            a = a.reshape(a.shape[0], a.shape[1], *([a.shape[2]] if shp[base][0] == -1 else [shp[base][0]]), *shp[base][1:])
        res.append(np.ascontiguousarray(a.astype(np.float32)))
    return tuple(res)


def kernel(**inputs):
    results = run(inputs, 4)
    return assemble(results, 4)
```
